# Optimizing a Trainium2 kernel written in Bass

```python
import math
import functools
import jax
import jax.numpy as jnp
from jax import lax
import numpy as np

D_MODEL = 1024
BATCH = 4
SEQ = 4096
DEPTH = 1
DEC_BATCH = 128
DEC_SEQ = 4
PAST_LEN = 2048
PAGE_SIZE = 128

SSM_WIDTH = D_MODEL // 2
SSM_GROUP_CH = 16
SSM_GROUPS = SSM_WIDTH // SSM_GROUP_CH
SSM_STATE = 64
ATTN_WIDTH = D_MODEL - SSM_WIDTH
ATTN_HEAD_DIM = 64
ATTN_V_DIM = 2 * ATTN_HEAD_DIM
ATTN_HEADS = ATTN_WIDTH // ATTN_V_DIM
IN_PROJ_COLS = SSM_WIDTH + 3 * ATTN_WIDTH
Q_BLOCK = 128
N_MEM = 256
CA_HEADS = 4
CA_HEAD_DIM = D_MODEL // CA_HEADS
FFN_HIDDEN = ((8 * D_MODEL // 3 + 127) // 128) * 128
CONV_WIDTH = 3
NORM_EPS = 1e-6

kernel_name = 'hymba_s5_diffattn_convffn_step'


def rms_norm(x, g):
    xf = x.astype(jnp.float32)
    y = xf * lax.rsqrt(jnp.mean(xf * xf, axis=-1, keepdims=True) + NORM_EPS)
    return (y * g.astype(jnp.float32)).astype(x.dtype)


def alibi_slopes(n_heads):
    return 2.0 ** (-8.0 * jnp.arange(1, n_heads + 1, dtype=jnp.float32) / n_heads)


def lambda_init(layer):
    return 0.8 - 0.6 * math.exp(-0.3 * layer)


def _complex_affine_combine(e1, e2):
    a1r, a1i, b1r, b1i = e1
    a2r, a2i, b2r, b2i = e2
    return (a1r * a2r - a1i * a2i,
            a1r * a2i + a1i * a2r,
            a2r * b1r - a2i * b1i + b2r,
            a2r * b1i + a2i * b1r + b2i)


def s5_mixer(u, h0_re, h0_im, p):
    f32 = jnp.float32
    bsz, t, _ = u.shape
    uf = u.astype(f32).reshape(bsz, t, SSM_GROUPS, SSM_GROUP_CH)
    dt = jnp.exp(p['ssm_log_dt'].astype(f32))[:, None]
    a_re = p['ssm_a_re'].astype(f32)
    a_im = p['ssm_a_im'].astype(f32)
    mag = jnp.exp(a_re * dt)
    lb_re = mag * jnp.cos(a_im * dt)
    lb_im = mag * jnp.sin(a_im * dt)
    den = a_re * a_re + a_im * a_im
    n_re = lb_re - 1.0
    f_re = (n_re * a_re + lb_im * a_im) / den
    f_im = (lb_im * a_re - n_re * a_im) / den
    b_re = p['ssm_b_re'].astype(f32)
    b_im = p['ssm_b_im'].astype(f32)
    bb_re = f_re[..., None] * b_re - f_im[..., None] * b_im
    bb_im = f_re[..., None] * b_im + f_im[..., None] * b_re
    bu_re = jnp.einsum('btgc,gpc->btgp', uf, bb_re)
    bu_im = jnp.einsum('btgc,gpc->btgp', uf, bb_im)
    dec_re = jnp.broadcast_to(lb_re, bu_re.shape)
    dec_im = jnp.broadcast_to(lb_im, bu_im.shape)
    acc_re, acc_im, s_re, s_im = lax.associative_scan(
        _complex_affine_combine, (dec_re, dec_im, bu_re, bu_im), axis=1)
    h0r = h0_re.astype(f32)[:, None]
    h0i = h0_im.astype(f32)[:, None]
    h_re = acc_re * h0r - acc_im * h0i + s_re
    h_im = acc_re * h0i + acc_im * h0r + s_im
    y = (jnp.einsum('btgp,gcp->btgc', h_re, p['ssm_c_re'].astype(f32))
         - jnp.einsum('btgp,gcp->btgc', h_im, p['ssm_c_im'].astype(f32))
         + p['ssm_d'].astype(f32) * uf)
    g = jax.nn.gelu(y.reshape(bsz, t, SSM_WIDTH))
    out = g * jax.nn.sigmoid(g @ p['ssm_glu_w'].astype(f32))
    return out.astype(u.dtype), h_re[:, -1], h_im[:, -1]


def diff_attn_core(q, k, v, q_pos, k_pos, lam):
    s = jnp.einsum('bqhid,bkhid->bhiqk', q, k).astype(jnp.float32) * (ATTN_HEAD_DIM ** -0.5)
    rel = q_pos[:, None] - k_pos[None, :]
    bias = -alibi_slopes(ATTN_HEADS)[:, None, None, None] * rel.astype(jnp.float32)
    s = jnp.where(rel >= 0, s + bias, -jnp.inf)
    pr = jax.nn.softmax(s, axis=-1)
    a = pr[:, :, 0] - lam * pr[:, :, 1]
    return jnp.einsum('bhqk,bkhd->bqhd', a.astype(v.dtype), v)


def attend_prompt(q, k, v, lam):
    bsz, t = q.shape[:2]
    nb = t // Q_BLOCK
    pos = jnp.arange(t, dtype=jnp.int32)
    q_blocks = q.reshape((bsz, nb, Q_BLOCK) + q.shape[2:]).swapaxes(0, 1)
    pos_blocks = pos.reshape(nb, Q_BLOCK)
    out = lax.map(lambda qp: diff_attn_core(qp[0], k, v, qp[1], pos, lam), (q_blocks, pos_blocks))
    return out.swapaxes(0, 1).reshape(bsz, t, ATTN_HEADS, ATTN_V_DIM)


def attend_sample(q, k, v, lam, past_k, past_v):
    t = q.shape[1]
    past = past_k.shape[1]
    k_all = jnp.concatenate([past_k.astype(k.dtype), k], axis=1)
    v_all = jnp.concatenate([past_v.astype(v.dtype), v], axis=1)
    k_pos = jnp.arange(past + t, dtype=jnp.int32)
    q_pos = past + jnp.arange(t, dtype=jnp.int32)
    return diff_attn_core(q, k_all, v_all, q_pos, k_pos, lam)


def memory_kv(mem, p):
    bsz, m, _ = mem.shape
    mn = rms_norm(mem, p['mem_norm_g'])
    mk = rms_norm((mn @ p['ca_wk']).reshape(bsz, m, CA_HEADS, CA_HEAD_DIM), p['ca_k_norm_g'])
    mv = (mn @ p['ca_wv']).reshape(bsz, m, CA_HEADS, CA_HEAD_DIM)
    return mk, mv


def layer_forward(x, p, layer, h0_re, h0_im, conv_prev, mem_k, mem_v, attend):
    f32 = jnp.float32
    bsz, t, _ = x.shape
    xn = rms_norm(x, p['ln1_g'])
    proj = xn @ p['w_in']
    u = proj[..., :SSM_WIDTH]
    q = proj[..., SSM_WIDTH:SSM_WIDTH + ATTN_WIDTH].reshape(bsz, t, ATTN_HEADS, 2, ATTN_HEAD_DIM)
    k = proj[..., SSM_WIDTH + ATTN_WIDTH:SSM_WIDTH + 2 * ATTN_WIDTH].reshape(bsz, t, ATTN_HEADS, 2, ATTN_HEAD_DIM)
    v = proj[..., SSM_WIDTH + 2 * ATTN_WIDTH:].reshape(bsz, t, ATTN_HEADS, ATTN_V_DIM)
    ssm_out, h_re, h_im = s5_mixer(u, h0_re, h0_im, p)
    q = rms_norm(q, p['q_norm_g'])
    k = rms_norm(k, p['k_norm_g'])
    lam0 = lambda_init(layer)
    lam = (jnp.exp(jnp.sum(p['lam_q1'].astype(f32) * p['lam_k1'].astype(f32)))
           - jnp.exp(jnp.sum(p['lam_q2'].astype(f32) * p['lam_k2'].astype(f32))) + lam0)
    o = attend(q, k, v, lam)
    o = rms_norm(o, p['subln_g']) * (1.0 - lam0)
    mixed = jnp.concatenate([ssm_out, o.reshape(bsz, t, ATTN_WIDTH)], axis=-1)
    x = x + mixed @ p['w_out']
    xn = rms_norm(x, p['ln2_g'])
    cq = rms_norm((xn @ p['ca_wq']).reshape(bsz, t, CA_HEADS, CA_HEAD_DIM), p['ca_q_norm_g'])
    s = jnp.einsum('bqhd,bmhd->bhqm', cq, mem_k.astype(cq.dtype)).astype(f32) * (CA_HEAD_DIM ** -0.5)
    pr = jax.nn.softmax(s, axis=-1).astype(x.dtype)
    co = jnp.einsum('bhqm,bmhd->bqhd', pr, mem_v.astype(x.dtype)).reshape(bsz, t, D_MODEL)
    x = x + co @ p['ca_wo']
    xn = rms_norm(x, p['ln3_g'])
    hg = xn @ p['ffn_wg']
    buf = jnp.concatenate([conv_prev.astype(hg.dtype), hg], axis=1)
    w = p['ffn_conv_w']
    conv = p['ffn_conv_b'] + w[0] * buf[:, 0:t]
    for j in range(1, CONV_WIDTH):
        conv = conv + w[j] * buf[:, j:j + t]
    x = x + (jax.nn.silu(conv) * (xn @ p['ffn_wv'])) @ p['ffn_wd']
    return x, k, v, h_re, h_im, buf[:, t:]


def setup_inputs(seed: int = 0) -> dict:
    key = jax.random.key(seed)
    keys = jax.random.split(key, 48)
    counter = [0]

    def nxt():
        kk = keys[counter[0]]
        counter[0] += 1
        return kk

    def nrm(shape, scale):
        return scale * jax.random.normal(nxt(), shape, jnp.float32)

    def gain(n):
        return 1.0 + nrm((DEPTH, n), 0.02)

    n_pages = PAST_LEN // PAGE_SIZE
    n_phys = (5 * DEC_BATCH * n_pages + 3) // 4
    a_im0 = math.pi * jnp.arange(SSM_STATE, dtype=jnp.float32)
    return {
        'x_prompt': nrm((BATCH, SEQ, D_MODEL), 1.0),
        'x_sample': nrm((DEC_BATCH, DEC_SEQ, D_MODEL), 1.0),
        'mem_prompt': nrm((BATCH, N_MEM, D_MODEL), 1.0),
        'cache_k': nrm((DEPTH, n_phys, PAGE_SIZE, ATTN_HEADS, 2, ATTN_HEAD_DIM), 1.0),
        'cache_v': nrm((DEPTH, n_phys, PAGE_SIZE, ATTN_HEADS, ATTN_V_DIM), 1.0),
        'page_table': jax.random.permutation(nxt(), n_phys)[:DEC_BATCH * n_pages].reshape(DEC_BATCH, n_pages).astype(jnp.int32),
        'state_ssm_re': nrm((DEPTH, DEC_BATCH, SSM_GROUPS, SSM_STATE), 0.1),
        'state_ssm_im': nrm((DEPTH, DEC_BATCH, SSM_GROUPS, SSM_STATE), 0.1),
        'state_conv': nrm((DEPTH, DEC_BATCH, CONV_WIDTH - 1, FFN_HIDDEN), 1.0),
        'cache_mem_k': nrm((DEPTH, DEC_BATCH, N_MEM, CA_HEADS, CA_HEAD_DIM), 1.0),
        'cache_mem_v': nrm((DEPTH, DEC_BATCH, N_MEM, CA_HEADS, CA_HEAD_DIM), 1.0),
        'ln1_g': gain(D_MODEL),
        'w_in': nrm((DEPTH, D_MODEL, IN_PROJ_COLS), D_MODEL ** -0.5),
        'ssm_a_re': -0.5 + nrm((DEPTH, SSM_GROUPS, SSM_STATE), 0.01),
        'ssm_a_im': a_im0 + nrm((DEPTH, SSM_GROUPS, SSM_STATE), 0.01),
        'ssm_b_re': nrm((DEPTH, SSM_GROUPS, SSM_STATE, SSM_GROUP_CH), (2 * SSM_GROUP_CH) ** -0.5),
        'ssm_b_im': nrm((DEPTH, SSM_GROUPS, SSM_STATE, SSM_GROUP_CH), (2 * SSM_GROUP_CH) ** -0.5),
        'ssm_c_re': nrm((DEPTH, SSM_GROUPS, SSM_GROUP_CH, SSM_STATE), (2 * SSM_STATE) ** -0.5),
        'ssm_c_im': nrm((DEPTH, SSM_GROUPS, SSM_GROUP_CH, SSM_STATE), (2 * SSM_STATE) ** -0.5),
        'ssm_d': nrm((DEPTH, SSM_GROUPS, SSM_GROUP_CH), 1.0),
        'ssm_log_dt': jax.random.uniform(nxt(), (DEPTH, SSM_GROUPS), jnp.float32, math.log(1e-3), math.log(1e-1)),
        'ssm_glu_w': nrm((DEPTH, SSM_WIDTH, SSM_WIDTH), SSM_WIDTH ** -0.5),
        'q_norm_g': gain(ATTN_HEAD_DIM),
        'k_norm_g': gain(ATTN_HEAD_DIM),
        'lam_q1': nrm((DEPTH, ATTN_HEAD_DIM), 0.1),
        'lam_k1': nrm((DEPTH, ATTN_HEAD_DIM), 0.1),
        'lam_q2': nrm((DEPTH, ATTN_HEAD_DIM), 0.1),
        'lam_k2': nrm((DEPTH, ATTN_HEAD_DIM), 0.1),
        'subln_g': gain(ATTN_V_DIM),
        'w_out': nrm((DEPTH, SSM_WIDTH + ATTN_WIDTH, D_MODEL), (SSM_WIDTH + ATTN_WIDTH) ** -0.5),
        'ln2_g': gain(D_MODEL),
        'mem_norm_g': gain(D_MODEL),
        'ca_wq': nrm((DEPTH, D_MODEL, D_MODEL), D_MODEL ** -0.5),
        'ca_wk': nrm((DEPTH, D_MODEL, D_MODEL), D_MODEL ** -0.5),
        'ca_wv': nrm((DEPTH, D_MODEL, D_MODEL), D_MODEL ** -0.5),
        'ca_q_norm_g': gain(CA_HEAD_DIM),
        'ca_k_norm_g': gain(CA_HEAD_DIM),
        'ca_wo': nrm((DEPTH, D_MODEL, D_MODEL), D_MODEL ** -0.5),
        'ln3_g': gain(D_MODEL),
        'ffn_wg': nrm((DEPTH, D_MODEL, FFN_HIDDEN), D_MODEL ** -0.5),
        'ffn_wv': nrm((DEPTH, D_MODEL, FFN_HIDDEN), D_MODEL ** -0.5),
        'ffn_conv_w': nrm((DEPTH, CONV_WIDTH, FFN_HIDDEN), CONV_WIDTH ** -0.5),
        'ffn_conv_b': nrm((DEPTH, FFN_HIDDEN), 0.01),
        'ffn_wd': nrm((DEPTH, FFN_HIDDEN, D_MODEL), FFN_HIDDEN ** -0.5),
    }


def reference(x_prompt, x_sample, mem_prompt, cache_k, cache_v, page_table,
              state_ssm_re, state_ssm_im, state_conv, cache_mem_k, cache_mem_v,
              ln1_g, w_in, ssm_a_re, ssm_a_im, ssm_b_re, ssm_b_im, ssm_c_re, ssm_c_im,
              ssm_d, ssm_log_dt, ssm_glu_w, q_norm_g, k_norm_g, lam_q1, lam_k1, lam_q2, lam_k2,
              subln_g, w_out, ln2_g, mem_norm_g, ca_wq, ca_wk, ca_wv, ca_q_norm_g, ca_k_norm_g,
              ca_wo, ln3_g, ffn_wg, ffn_wv, ffn_conv_w, ffn_conv_b, ffn_wd):
    y_prompt, y_sample = x_prompt, x_sample
    n_prompt, n_dec = x_prompt.shape[0], x_sample.shape[0]
    (k_p, v_p, k_s, v_s, hr_p, hi_p, hr_s, hi_s, c_p, c_s, mk_p, mv_p) = ([] for _ in range(12))
    for l in range(DEPTH):
        p = {
            'ln1_g': ln1_g[l], 'w_in': w_in[l],
            'ssm_a_re': ssm_a_re[l], 'ssm_a_im': ssm_a_im[l],
            'ssm_b_re': ssm_b_re[l], 'ssm_b_im': ssm_b_im[l],
            'ssm_c_re': ssm_c_re[l], 'ssm_c_im': ssm_c_im[l],
            'ssm_d': ssm_d[l], 'ssm_log_dt': ssm_log_dt[l], 'ssm_glu_w': ssm_glu_w[l],
            'q_norm_g': q_norm_g[l], 'k_norm_g': k_norm_g[l],
            'lam_q1': lam_q1[l], 'lam_k1': lam_k1[l], 'lam_q2': lam_q2[l], 'lam_k2': lam_k2[l],
            'subln_g': subln_g[l], 'w_out': w_out[l],
            'ln2_g': ln2_g[l], 'mem_norm_g': mem_norm_g[l],
            'ca_wq': ca_wq[l], 'ca_wk': ca_wk[l], 'ca_wv': ca_wv[l],
            'ca_q_norm_g': ca_q_norm_g[l], 'ca_k_norm_g': ca_k_norm_g[l], 'ca_wo': ca_wo[l],
            'ln3_g': ln3_g[l], 'ffn_wg': ffn_wg[l], 'ffn_wv': ffn_wv[l],
            'ffn_conv_w': ffn_conv_w[l], 'ffn_conv_b': ffn_conv_b[l], 'ffn_wd': ffn_wd[l],
        }
        mem_k, mem_v = memory_kv(mem_prompt, p)
        zeros_h = jnp.zeros((n_prompt, SSM_GROUPS, SSM_STATE), jnp.float32)
        zeros_c = jnp.zeros((n_prompt, CONV_WIDTH - 1, FFN_HIDDEN), x_prompt.dtype)
        y_prompt, kp, vp, hrp, hip, cp = layer_forward(
            y_prompt, p, l, zeros_h, zeros_h, zeros_c, mem_k, mem_v, attend_prompt)
        past_k = cache_k[l][page_table].reshape(n_dec, -1, ATTN_HEADS, 2, ATTN_HEAD_DIM)
        past_v = cache_v[l][page_table].reshape(n_dec, -1, ATTN_HEADS, ATTN_V_DIM)
        attend = functools.partial(attend_sample, past_k=past_k, past_v=past_v)
        y_sample, ks, vs, hrs, his, cs = layer_forward(
            y_sample, p, l, state_ssm_re[l], state_ssm_im[l], state_conv[l],
            cache_mem_k[l], cache_mem_v[l], attend)
        k_p.append(kp)
        v_p.append(vp)
        k_s.append(ks)
        v_s.append(vs)
        hr_p.append(hrp)
        hi_p.append(hip)
        hr_s.append(hrs)
        hi_s.append(his)
        c_p.append(cp)
        c_s.append(cs)
        mk_p.append(mem_k)
        mv_p.append(mem_v)
    return (y_prompt, y_sample,
            jnp.stack(k_p), jnp.stack(v_p), jnp.stack(k_s), jnp.stack(v_s),
            jnp.stack(hr_p), jnp.stack(hi_p), jnp.stack(hr_s), jnp.stack(hi_s),
            jnp.stack(c_p), jnp.stack(c_s), jnp.stack(mk_p), jnp.stack(mv_p))
```

```python
import math
import os
PH = set(os.environ.get('KPH', 'W,M,A2,C,B,D,S,CS,DS').split(','))
import numpy as np
import ml_dtypes
import concourse.bass as bass
import concourse.mybir as mybir
from concourse.bass_utils import run_bass_kernel_spmd
from contextlib import ExitStack

F32 = mybir.dt.float32
BF16 = mybir.dt.bfloat16
I32 = mybir.dt.int32
U32 = mybir.dt.uint32
ALU = mybir.AluOpType
AF = mybir.ActivationFunctionType

ENGS = ("pe", "act", "dve", "pool", "sp")
N_DMA_SEMS = 24
EPS = 1e-6
NCORES = 8
T = 4096
NT = 512
NTILES = T // NT
L = 8
NCH = T // L
SC = 16
NSC = NCH // SC
F = 2816
NFC = F // 128
SLOPES = [2.0 ** (-8.0 * (h + 1) / 4) for h in range(4)]
LAM0 = 0.8 - 0.6 * math.exp(-0.3 * 0)
NS = 16
TS = 64
NPW = 25
PW_N = list(range(9)) + [8 * k for k in range(2, 17)] + [4]


class Buf:
    __slots__ = ("name", "last_w", "readers", "excl")

    def __init__(self, name):
        self.name = name
        self.excl = False
        self.last_w = None
        self.readers = []


class Sched:
    def __init__(self, nc, es):
        self.nc = nc
        self.ops = {e: [] for e in ENGS}
        self.count = {e: 0 for e in ENGS}
        self.sems = {e: es.enter_context(nc.semaphore("s_" + e)) for e in ENGS}
        self.dsems = [es.enter_context(nc.semaphore("d%d" % i)) for i in range(N_DMA_SEMS)]
        self.dcnt = [0] * N_DMA_SEMS
        self.drr = 0
        self.waited = {e: {} for e in ENGS}
        self.bufs = []

    def buf(self, name):
        b = Buf(name)
        self.bufs.append(b)
        return b

    def _collect(self, eng, reads, writes, is_dma):
        toks = []
        for b in reads:
            if b.last_w is not None:
                toks.append(b.last_w)
        for b in writes:
            if b.last_w is not None:
                toks.append(b.last_w)
            toks.extend(b.readers)
        need = {}
        for (k, v) in toks:
            if (not is_dma) and eng == "pe" and k == "pe":
                continue
            if need.get(k, -1) < v:
                need[k] = v
        waits = []
        w = self.waited[eng]
        for k, v in need.items():
            if w.get(k, -1) >= v:
                continue
            w[k] = v
            waits.append((k, v))
        return waits

    def _commit(self, tok, reads, writes):
        for b in reads:
            if b.excl:
                b.last_w = tok
                b.readers = []
            else:
                b.readers.append(tok)
        for b in writes:
            b.last_w = tok
            b.readers = []

    def op(self, eng, fn, reads=(), writes=()):
        waits = self._collect(eng, reads, writes, False)
        self.count[eng] += 1
        tok = (eng, self.count[eng])
        self.ops[eng].append((waits, fn, None))
        self._commit(tok, reads, writes)
        return tok

    def dmaf(self, eng, fn, reads=(), writes=()):
        waits = self._collect(eng, reads, writes, True)
        i = self.drr
        self.drr = (self.drr + 1) % N_DMA_SEMS
        k = ("d", i)
        prev = self.dcnt[i]
        w = self.waited[eng]
        if prev > 0 and w.get(k, -1) < prev:
            w[k] = prev
            waits.append((k, prev))
        self.dcnt[i] += 16
        tok = (k, self.dcnt[i])
        self.ops[eng].append((waits, fn, (i, 16)))
        self._commit(tok, reads, writes)
        return tok

    def dma(self, eng, out, in_, reads=(), writes=(), **kw):
        def fn(e, out=out, in_=in_, kw=kw):
            return e.dma_start(out=out, in_=in_, **kw)
        return self.dmaf(eng, fn, reads, writes)

    def barrier(self):
        targets = [(e, self.count[e]) for e in ENGS if self.count[e] > 0]
        targets += [(("d", i), c) for i, c in enumerate(self.dcnt) if c > 0]
        for e in ENGS:
            w = self.waited[e]
            waits = []
            for k, v in targets:
                if w.get(k, -1) < v:
                    w[k] = v
                    waits.append((k, v))
            if waits:
                self.ops[e].append((waits, None, None))
        for b in self.bufs:
            b.last_w = None
            b.readers = []

    def _sem(self, k):
        if isinstance(k, tuple):
            return self.dsems[k[1]]
        return self.sems[k]

    def emit(self):
        nc = self.nc
        with nc.Block() as block:
            def mk(ename):
                def body(e):
                    own = self.sems[ename]
                    for waits, fn, dinc in self.ops[ename]:
                        for (k, v) in waits:
                            e.wait_ge(self._sem(k), v)
                        if fn is None:
                            continue
                        ins = fn(e)
                        if dinc is not None:
                            ins.then_inc(self.dsems[dinc[0]], dinc[1])
                        else:
                            ins.then_inc(own, 1)
                return body
            block.tensor(mk("pe"))
            block.scalar(mk("act"))
            block.vector(mk("dve"))
            block.gpsimd(mk("pool"))
            block.sync(mk("sp"))


class TB:
    def __init__(self, t, b):
        self.t = t
        self.b = b

    def __getitem__(self, k):
        return self.t[k]


def build_nc():
    nc = bass.Bass("TRN2", target_bir_lowering=False)

    def din(name, shape, dt=F32):
        return nc.dram_tensor(name, list(shape), dt, kind="ExternalInput").ap()

    def dout(name, shape, dt=F32):
        return nc.dram_tensor(name, list(shape), dt, kind="ExternalOutput").ap()

    def dscr(name, shape, dt):
        return nc.dram_tensor(name, list(shape), dt, kind="Internal").ap()

    xp = din("xp", [T, 1024])
    memp = din("memp", [256, 1024])
    ln1_g = din("ln1_g", [1024]); ln2_g = din("ln2_g", [1024]); ln3_g = din("ln3_g", [1024])
    memn_g = din("mem_norm_g", [1024])
    w_in = din("w_in", [1024, 2048])
    a_re = din("ssm_a_re", [32, 64]); a_im = din("ssm_a_im", [32, 64])
    b_re = din("ssm_b_re", [32, 64, 16]); b_im = din("ssm_b_im", [32, 64, 16])
    c_re = din("ssm_c_re", [32, 16, 64]); c_im = din("ssm_c_im", [32, 16, 64])
    ssm_d = din("ssm_d", [32, 16]); log_dt = din("ssm_log_dt", [32])
    glu_w = din("ssm_glu_w", [512, 512])
    qn_g = din("q_norm_g", [64]); kn_g = din("k_norm_g", [64])
    lq1 = din("lam_q1", [64]); lk1 = din("lam_k1", [64]); lq2 = din("lam_q2", [64]); lk2 = din("lam_k2", [64])
    subln_g = din("subln_g", [128])
    w_out = din("w_out", [1024, 1024])
    ca_wq = din("ca_wq", [1024, 1024]); ca_wk = din("ca_wk", [1024, 1024]); ca_wv = din("ca_wv", [1024, 1024])
    caq_g = din("ca_q_norm_g", [256]); cak_g = din("ca_k_norm_g", [256])
    ca_wo = din("ca_wo", [1024, 1024])
    ffn_wg = din("ffn_wg", [1024, F]); ffn_wv = din("ffn_wv", [1024, F]); ffn_wd = din("ffn_wd", [F, 1024])
    conv_w = din("ffn_conv_w", [3, F]); conv_b = din("ffn_conv_b", [F])
    xs = din("xs", [TS, 1024])
    st_re = din("st_re", [NS, 2048]); st_im = din("st_im", [NS, 2048])
    st_conv = din("st_conv", [NS * 2, F])
    cmk = din("cmk", [NS, 256, 1024]); cmv = din("cmv", [NS, 256, 1024])
    cache_k = din("cache_k", [2560 * 128, 512]); cache_v = din("cache_v", [2560 * 128, 512])
    ptab = din("ptab", [NS, 16], I32)
    c_ident = din("c_ident", [128, 128])
    c_bd64 = din("c_bd64", [128, 128])
    c_mask16 = din("c_mask16", [128, 128])
    c_caus = din("c_caus", [128, 128])
    c_pwn = din("c_pwn", [NPW])
    c_kpos = din("c_kpos", [128])
    c_g2m = din("c_g2m", [128, 2])

    y_p = dout("y_p", [T, 1024]); k_p = dout("k_p", [T, 512]); v_p = dout("v_p", [T, 512])
    hre_p = dout("hre_p", [16, 128]); him_p = dout("him_p", [16, 128])
    conv_p = dout("conv_p", [2, F])
    mk_p = dout("mk_p", [256, 1024]); mv_p = dout("mv_p", [256, 1024])

    y_s = dout("y_s", [TS, 1024]); k_s = dout("k_s", [TS, 512]); v_s = dout("v_s", [TS, 512])
    hre_s = dout("hre_s", [NS, 2048]); him_s = dout("him_s", [NS, 2048])
    conv_s = dout("conv_s", [NS * 2, F])
    wg_s = dscr("wg_s", [1024, F], BF16); wv_s = dscr("wv_s", [1024, F], BF16); wd_s = dscr("wd_s", [F, 1024], BF16)
    mix_s = dscr("mix_s", [NTILES, 128, 8, NT], BF16)

    es = ExitStack()
    with es:
        S = Sched(nc, es)

        def sb(stack, name, shape, dt):
            return TB(stack.enter_context(nc.sbuf_tensor(name, list(shape), dt)), S.buf(name))

        PS = [TB(es.enter_context(nc.psum_tensor("ps%d" % i, [128, 512], F32)), S.buf("ps%d" % i)) for i in range(8)]
        for p_ in PS:
            p_.b.excl = True
        psrr = [0]

        def bank():
            b = PS[psrr[0]]
            psrr[0] = (psrr[0] + 1) % 8
            return b

        SLOW = dict(allow_slow_non_contiguous=True)

        def mm(bk, lhsT, rhs, start, stop, reads, out=None, **kw):
            o = bk[:] if out is None else out
            S.op("pe", lambda e: e.matmul(o, lhsT=lhsT, rhs=rhs, start=start, stop=stop, **kw),
                 reads=reads, writes=[bk.b])

        def tr(bk, out, in_, ident, reads):
            S.op("pe", lambda e: e.transpose(out=out, in_=in_, identity=ident[:]), reads=reads + [ident.b],
                 writes=[bk.b])

        def act(out, in_, func, reads, writes, **kw):
            S.op("act", lambda e: e.activation(out=out, in_=in_, func=func, **kw), reads=reads, writes=writes)

        def tt(eng, out, in0, in1, op, reads, writes):
            S.op(eng, lambda e: e.tensor_tensor(out=out, in0=in0, in1=in1, op=op), reads=reads, writes=writes)

        def ts(eng, out, in0, s1, s2, op0, op1, reads, writes):
            if op1 is None:
                S.op(eng, lambda e: e.tensor_scalar(out=out, in0=in0, scalar1=s1, scalar2=None, op0=op0),
                     reads=reads, writes=writes)
            else:
                S.op(eng, lambda e: e.tensor_scalar(out=out, in0=in0, scalar1=s1, scalar2=s2, op0=op0, op1=op1),
                     reads=reads, writes=writes)

        def stt(out, in0, scalar, in1, op0, op1, reads, writes):
            S.op("dve", lambda e: e.scalar_tensor_tensor(out=out, in0=in0, scalar=scalar, in1=in1, op0=op0, op1=op1),
                 reads=reads, writes=writes)

        def cp(eng, out, in_, reads, writes):
            if eng == "act":
                S.op("act", lambda e: e.copy(out=out, in_=in_), reads=reads, writes=writes)
            else:
                S.op(eng, lambda e: e.tensor_copy(out=out, in_=in_), reads=reads, writes=writes)

        def memset(eng, ap, val, writes):
            S.op(eng, lambda e: e.memset(ap, val), writes=writes)

        def recip(out, in_, reads, writes):
            S.op("dve", lambda e: e.reciprocal(out=out, in_=in_), reads=reads, writes=writes)

        IDF = sb(es, "IDF", [128, 128], F32); IDB = sb(es, "IDB", [128, 128], BF16)
        ON1024 = sb(es, "ON1024", [128, 128], BF16); ON256 = sb(es, "ON256", [128, 128], BF16)
        ON128 = sb(es, "ON128", [128, 128], BF16); ONE1 = sb(es, "ONE1", [128, 128], BF16)
        BD64 = sb(es, "BD64", [128, 128], BF16)
        CAUS = sb(es, "CAUS", [128, 128], BF16)
        G1 = sb(es, "G1", [128, 8], F32); G2 = sb(es, "G2", [128, 8], F32); G3 = sb(es, "G3", [128, 8], F32)
        GM = sb(es, "GM", [128, 8], F32)
        QG = sb(es, "QG", [128, 1], F32); KG = sb(es, "KG", [128, 1], F32)
        SUBG = sb(es, "SUBG", [128, 1], F32)
        CQG = sb(es, "CQG", [128, 2], F32); CKG = sb(es, "CKG", [128, 2], F32)
        CW = sb(es, "CW", [128, 3, NFC], F32); CB = sb(es, "CB", [128, NFC], F32)
        KPOS = sb(es, "KPOS", [128, 1], F32)
        NLAM = sb(es, "NLAM", [128, 1], F32)
        LT = sb(es, "LT", [64, 4], F32)

        S.dma("sp", IDF[:], c_ident, writes=[IDF.b])
        S.dma("pool", IDB[:], c_ident, writes=[IDB.b])
        S.dma("pool", BD64[:], c_bd64, writes=[BD64.b])
        S.dma("pool", CAUS[:], c_caus, writes=[CAUS.b])
        memset("dve", ON1024[:], 1.0 / 1024, [ON1024.b]); memset("dve", ON256[:], 1.0 / 256, [ON256.b])
        memset("dve", ON128[:], 1.0 / 128, [ON128.b]); memset("dve", ONE1[:], 1.0, [ONE1.b])
        for (Gt, gsrc) in ((G1, ln1_g), (G2, ln2_g), (G3, ln3_g), (GM, memn_g)):
            S.dma("sp", Gt[:], gsrc.rearrange("(c p) -> p c", p=128), writes=[Gt.b], **SLOW)
        for (Gt, gsrc) in ((QG, qn_g), (KG, kn_g)):
            for hh in range(2):
                S.dma("sp", Gt[hh * 64:(hh + 1) * 64, :], gsrc.rearrange("(p o) -> p o", o=1), writes=[Gt.b], **SLOW)
        S.dma("sp", SUBG[:], subln_g.rearrange("(p o) -> p o", o=1), writes=[SUBG.b], **SLOW)
        S.dma("sp", CQG[:], caq_g.rearrange("(c p) -> p c", p=128), writes=[CQG.b], **SLOW)
        S.dma("sp", CKG[:], cak_g.rearrange("(c p) -> p c", p=128), writes=[CKG.b], **SLOW)
        for j in range(3):
            S.dma("sp", CW[:, j, :], conv_w[j].rearrange("(c p) -> p c", p=128), writes=[CW.b], **SLOW)
        S.dma("sp", CB[:], conv_b.rearrange("(c p) -> p c", p=128), writes=[CB.b], **SLOW)
        S.dma("sp", KPOS[:], c_kpos.rearrange("(p o) -> p o", o=1), writes=[KPOS.b], **SLOW)
        for i, src in enumerate((lq1, lk1, lq2, lk2)):
            S.dma("sp", LT[:, i:i + 1], src.rearrange("(p o) -> p o", o=1), writes=[LT.b], **SLOW)
        ts("dve", SUBG[:], SUBG[:], 1.0 - LAM0, None, ALU.mult, None, [SUBG.b], [SUBG.b])
        LP = sb(es, "LP", [64, 2], BF16)
        tt("dve", LP[:, 0:1], LT[:, 0:1], LT[:, 1:2], ALU.mult, [LT.b], [LP.b])
        tt("dve", LP[:, 1:2], LT[:, 2:3], LT[:, 3:4], ALU.mult, [LT.b], [LP.b])
        bk = bank()
        mm(bk, ONE1[0:64, :], LP[:, :], True, True, [ONE1.b, LP.b], out=bk[:, 0:2])
        LE = sb(es, "LE", [128, 2], F32)
        act(LE[:], bk[:, 0:2], AF.Exp, [bk.b], [LE.b])
        stt(NLAM[:], LE[:, 1:2], -LAM0, LE[:, 0:1], ALU.add, ALU.subtract, [LE.b], [NLAM.b])

        for (dst, src, rows) in (((wg_s, ffn_wg, 1024), (wv_s, ffn_wv, 1024), (wd_s, ffn_wd, F)) if 'W' in PH else ()):
            step = 128
            for r0 in range(0, rows, step):
                S.dma("pool", dst[r0:r0 + step, :], src[r0:r0 + step, :], max_dma_last_dim=4096)

        def load_norm_tile(X, xsrc_ap, nblk, G, xnT, rs, ss, junk, raw_xT=None):
            S.dma("sp", X[:, 0:nblk, :], xsrc_ap, writes=[X.b])
            for j in range(nblk):
                act(junk[:], X[:, j, :], AF.Square, [X.b], [junk.b, ss.b], accum_out=ss[:, j:j + 1])
            act(rs[:, 0:nblk], ss[:, 0:nblk], AF.Sqrt, [ss.b], [rs.b], scale=1.0 / 1024, bias=EPS)
            recip(rs[:, 0:nblk], rs[:, 0:nblk], [rs.b], [rs.b])
            if raw_xT is not None:
                for c in range(8):
                    bk = bank()
                    for j in range(nblk):
                        tr(bk, bk[:, j * 128:(j + 1) * 128], X[:, j, c * 128:(c + 1) * 128], IDF, [X.b])
                    cp("act", raw_xT[:, c, 0:nblk * 128], bk[:, 0:nblk * 128], [bk.b], [raw_xT.b])
            for j in range(nblk):
                ts("dve", X[:, j, :], X[:, j, :], rs[:, j:j + 1], None, ALU.mult, None, [X.b, rs.b], [X.b])
            for c in range(8):
                bk = bank()
                for j in range(nblk):
                    tr(bk, bk[:, j * 128:(j + 1) * 128], X[:, j, c * 128:(c + 1) * 128], IDF, [X.b])
                ts("dve", xnT[:, c, 0:nblk * 128], bk[:, 0:nblk * 128], G[:, c:c + 1], None, ALU.mult, None,
                   [bk.b, G.b], [xnT.b])

        def ln_fm(xT, G, xnT, n, sq, rstd):
            bk = bank()
            for c in range(8):
                act(sq[:, c % 2, 0:n], xT[:, c, 0:n], AF.Square, [xT.b], [sq.b])
                mm(bk, ON1024[:], sq[:, c % 2, 0:n], c == 0, c == 7, [ON1024.b, sq.b], out=bk[:, 0:n])
            act(rstd[:, 0:n], bk[:, 0:n], AF.Sqrt, [bk.b], [rstd.b], bias=EPS)
            recip(rstd[:, 0:n], rstd[:, 0:n], [rstd.b], [rstd.b])
            for c in range(8):
                stt(xnT[:, c, 0:n], xT[:, c, 0:n], G[:, c:c + 1], rstd[:, 0:n], ALU.mult, ALU.mult,
                    [xT.b, G.b, rstd.b], [xnT.b])

        def store_tm(src_fm, nchunks, n, dst_ap, stage, ident=IDF):
            nblk = (n + 127) // 128
            for j in range(nblk):
                w = min(128, n - j * 128)
                for c0 in range(0, nchunks, 4):
                    bk = bank()
                    for c in range(c0, min(c0 + 4, nchunks)):
                        tr(bk, bk[0:w, (c - c0) * 128:(c - c0 + 1) * 128], src_fm[:, c, j * 128:j * 128 + w], ident,
                           [src_fm.b])
                    nn = (min(c0 + 4, nchunks) - c0) * 128
                    cp("act", stage[0:w, j, c0 * 128:c0 * 128 + nn], bk[0:w, 0:nn], [bk.b], [stage.b])
            if n % 128 == 0:
                S.dma("sp", dst_ap.rearrange("(j p) f -> p j f", p=128), stage[:, 0:nblk, 0:nchunks * 128],
                      reads=[stage.b])
            else:
                S.dma("sp", dst_ap, stage[0:n, 0, 0:nchunks * 128], reads=[stage.b])


        xnTs = sb(es, "xnTs", [128, 8, TS], BF16)
        xsT = sb(es, "xsT", [128, 8, TS], F32)
        MIXs = sb(es, "MIXs", [128, 8, TS], BF16)
        usT = sb(es, "usT", [128, 4, TS], BF16)
        QTs = sb(es, "QTs", [128, 4, TS], BF16); KTs = sb(es, "KTs", [128, 4, TS], BF16)
        VN = sb(es, "VN", [4, NS, 4, 132], BF16)
        if 'S' in PH:
            with ExitStack() as ph:
                Xs_ = sb(ph, "Xs_", [128, 1024], F32)
                rs = sb(ph, "rs_s", [128, 1], F32); ss = sb(ph, "ss_s", [128, 1], F32)
                junk = sb(ph, "junk_s", [128, 1024], F32)
                S.dma("sp", Xs_[0:TS, :], xs, writes=[Xs_.b])
                act(junk[0:TS, :], Xs_[0:TS, :], AF.Square, [Xs_.b], [junk.b, ss.b], accum_out=ss[0:TS, 0:1])
                act(rs[0:TS, :], ss[0:TS, :], AF.Sqrt, [ss.b], [rs.b], scale=1.0 / 1024, bias=EPS)
                recip(rs[0:TS, :], rs[0:TS, :], [rs.b], [rs.b])
                for c in range(8):
                    bk = bank()
                    tr(bk, bk[:, 0:TS], Xs_[0:TS, c * 128:(c + 1) * 128], TB(IDF.t[0:TS, 0:TS], IDF.b), [Xs_.b])
                    cp("act", xsT[:, c, :], bk[:, 0:TS], [bk.b], [xsT.b])
                ts("dve", Xs_[0:TS, :], Xs_[0:TS, :], rs[0:TS, 0:1], None, ALU.mult, None, [Xs_.b, rs.b], [Xs_.b])
                for c in range(8):
                    bk = bank()
                    tr(bk, bk[:, 0:TS], Xs_[0:TS, c * 128:(c + 1) * 128], TB(IDF.t[0:TS, 0:TS], IDF.b), [Xs_.b])
                    ts("dve", xnTs[:, c, :], bk[:, 0:TS], G1[:, c:c + 1], None, ALU.mult, None, [bk.b, G1.b], [xnTs.b])
                S.barrier()

        MKT = sb(es, "MKT", [128, 8, 256], BF16)
        MV = sb(es, "MV", [128, 2, 1024], BF16)
        with ExitStack() as ph:
          if 'M' in PH:
            Xm = sb(ph, "Xm", [128, 2, 1024], F32)
            xnTm = sb(ph, "xnTm", [128, 8, 256], BF16)
            rs = sb(ph, "rs_m", [128, 4], F32); ss = sb(ph, "ss_m", [128, 4], F32)
            junk = sb(ph, "junk_m", [128, 1024], F32)
            Wk = sb(ph, "Wk", [128, 8, 1024], BF16); Wv = sb(ph, "Wv", [128, 8, 1024], BF16)
            mkraw = sb(ph, "mkraw", [128, 8, 256], F32)
            sqm = sb(ph, "sqm", [128, 2, 256], BF16); rstdm = sb(ph, "rstdm", [128, 256], F32)
            stg = sb(ph, "stg_m", [128, 2, 1024], F32)
            S.dma("pool", Wk[:], ca_wk.rearrange("(c p) n -> p c n", p=128), writes=[Wk.b])
            S.dma("pool", Wv[:], ca_wv.rearrange("(c p) n -> p c n", p=128), writes=[Wv.b])
            KS = int(os.environ.get('KSTOP', '9'))
            load_norm_tile(Xm, memp.rearrange("(j p) f -> p j f", p=128), 2, GM, xnTm, rs, ss, junk)
            for m in (range(8) if KS >= 2 else ()):
                bk = bank()
                for kc in range(8):
                    mm(bk, Wk[:, kc, m * 128:(m + 1) * 128], xnTm[:, kc, :], kc == 0, kc == 7, [Wk.b, xnTm.b],
                       out=bk[:, 0:256])
                cp("act", mkraw[:, m, :], bk[:, 0:256], [bk.b], [mkraw.b])
            for h in (range(4) if KS >= 3 else ()):
                bk = bank()
                for dh in range(2):
                    act(sqm[:, dh, :], mkraw[:, 2 * h + dh, :], AF.Square, [mkraw.b], [sqm.b])
                    mm(bk, ON256[:], sqm[:, dh, :], dh == 0, dh == 1, [ON256.b, sqm.b], out=bk[:, 0:256])
                act(rstdm[:], bk[:, 0:256], AF.Sqrt, [bk.b], [rstdm.b], bias=EPS)
                recip(rstdm[:], rstdm[:], [rstdm.b], [rstdm.b])
                for dh in range(2):
                    stt(mkraw[:, 2 * h + dh, :], mkraw[:, 2 * h + dh, :], CKG[:, dh:dh + 1], rstdm[:], ALU.mult,
                        ALU.mult, [mkraw.b, CKG.b, rstdm.b], [mkraw.b])
                    cp("act", MKT[:, 2 * h + dh, :], mkraw[:, 2 * h + dh, :], [mkraw.b], [MKT.b])
            if KS >= 4:
                store_tm(mkraw, 8, 256, mk_p, stg)
            for j in (range(2) if KS >= 5 else ()):
                for half in range(2):
                    bk = bank()
                    for kc in range(8):
                        mm(bk, xnTm[:, kc, j * 128:(j + 1) * 128], Wv[:, kc, half * 512:(half + 1) * 512], kc == 0,
                           kc == 7, [Wv.b, xnTm.b])
                    cp("act", stg[:, j, half * 512:(half + 1) * 512], bk[:], [bk.b], [stg.b])
                    cp("dve", MV[:, j, half * 512:(half + 1) * 512], bk[:], [bk.b], [MV.b])
            S.dma("sp", mv_p.rearrange("(j p) f -> p j f", p=128), stg[:], reads=[stg.b])
            S.barrier()


        s5 = ExitStack()
        if 'B' in PH:
            TWO_PI = 2.0 * math.pi
            PI_S = 3.1415925
            ZT = sb(s5, "ZT", [128, 4, 8, 2, 128], BF16)
            CLR = sb(s5, "CLR", [128, 16, 8, 32], BF16)
            CLI = sb(s5, "CLI", [128, 16, 8, 32], BF16)
            BDT = sb(s5, "BDT", [128, 4, 8, 128], BF16)
            LRE = sb(s5, "LRE", [128, 16, NPW], F32); LIM = sb(s5, "LIM", [128, 16, NPW], F32)
            with ExitStack() as tb:
                AR = sb(tb, "AR", [128, 16], F32); AI = sb(tb, "AI", [128, 16], F32)
                DT = sb(tb, "DT", [128, 16], F32)
                ARD = sb(tb, "ARD", [128, 16], F32); TH = sb(tb, "TH", [128, 16], F32)
                PWN = sb(tb, "PWN", [128, NPW], F32)
                ANG = sb(tb, "ANG", [128, 16, NPW], F32); R = sb(tb, "Rr", [128, 16, NPW], F32)
                KF = sb(tb, "KF", [128, 16, NPW], F32); KI = sb(tb, "KI", [128, 16, NPW], I32)
                MG = sb(tb, "MG", [128, 16, NPW], F32)
                S.dma("sp", AR[:], a_re.rearrange("(q g2) p -> (g2 p) q", g2=2), writes=[AR.b], **SLOW)
                S.dma("sp", AI[:], a_im.rearrange("(q g2) p -> (g2 p) q", g2=2), writes=[AI.b], **SLOW)
                for g2 in range(2):
                    S.dma("sp", DT[g2 * 64:(g2 + 1) * 64, :],
                          log_dt.rearrange("(q g2) -> g2 q", g2=2)[g2:g2 + 1, :].to_broadcast([64, 16]),
                          writes=[DT.b], **SLOW)
                S.dma("sp", PWN[:], c_pwn.rearrange("(o n) -> o n", o=1).to_broadcast([128, NPW]), writes=[PWN.b],
                      **SLOW)
                act(DT[:], DT[:], AF.Exp, [DT.b], [DT.b])
                tt("dve", ARD[:], AR[:], DT[:], ALU.mult, [AR.b, DT.b], [ARD.b])
                tt("dve", TH[:], AI[:], DT[:], ALU.mult, [AI.b, DT.b], [TH.b])
                bshape = [128, 16, NPW]
                tt("dve", MG[:], ARD[:, :].unsqueeze(2).to_broadcast(bshape), PWN[:, :].unsqueeze(1).to_broadcast(bshape),
                   ALU.mult, [ARD.b, PWN.b], [MG.b])
                act(MG[:], MG[:], AF.Exp, [MG.b], [MG.b])
                tt("dve", ANG[:], TH[:, :].unsqueeze(2).to_broadcast(bshape), PWN[:, :].unsqueeze(1).to_broadcast(bshape),
                   ALU.mult, [TH.b, PWN.b], [ANG.b])
                ts("dve", KF[:], ANG[:], 1.0 / TWO_PI, 0.5, ALU.mult, ALU.add, [ANG.b], [KF.b])
                cp("dve", KI[:], KF[:], [KF.b], [KI.b])
                cp("dve", KF[:], KI[:], [KI.b], [KF.b])
                stt(R[:], KF[:], -TWO_PI, ANG[:], ALU.mult, ALU.add, [KF.b, ANG.b], [R.b])

                def wrap(Rt):
                    ts("dve", KF[:], Rt[:], -math.pi, None, ALU.is_lt, None, [Rt.b], [KF.b])
                    stt(Rt[:], KF[:], TWO_PI, Rt[:], ALU.mult, ALU.add, [KF.b, Rt.b], [Rt.b])
                    ts("dve", KF[:], Rt[:], math.pi, None, ALU.is_gt, None, [Rt.b], [KF.b])
                    stt(Rt[:], KF[:], -TWO_PI, Rt[:], ALU.mult, ALU.add, [KF.b, Rt.b], [Rt.b])
                    ts("dve", Rt[:], Rt[:], -PI_S, PI_S, ALU.max, ALU.min, [Rt.b], [Rt.b])
                wrap(R)
                act(LIM[:], R[:], AF.Sin, [R.b], [LIM.b])
                ts("dve", R[:], R[:], math.pi / 2, None, ALU.add, None, [R.b], [R.b])
                wrap(R)
                act(LRE[:], R[:], AF.Sin, [R.b], [LRE.b])
                tt("dve", LRE[:], LRE[:], MG[:], ALU.mult, [LRE.b, MG.b], [LRE.b])
                tt("dve", LIM[:], LIM[:], MG[:], ALU.mult, [LIM.b, MG.b], [LIM.b])
                NRE = sb(tb, "NRE", [128, 16], F32); DEN = sb(tb, "DEN", [128, 16], F32)
                FRE = sb(tb, "FRE", [128, 16], F32); FIM = sb(tb, "FIM", [128, 16], F32)
                T1 = sb(tb, "T1", [128, 16], F32)
                ts("dve", NRE[:], LRE[:, :, 1], -1.0, None, ALU.add, None, [LRE.b], [NRE.b])
                tt("dve", DEN[:], AR[:], AR[:], ALU.mult, [AR.b], [DEN.b])
                tt("dve", T1[:], AI[:], AI[:], ALU.mult, [AI.b], [T1.b])
                tt("dve", DEN[:], DEN[:], T1[:], ALU.add, [DEN.b, T1.b], [DEN.b])
                recip(DEN[:], DEN[:], [DEN.b], [DEN.b])
                tt("dve", FRE[:], NRE[:], AR[:], ALU.mult, [NRE.b, AR.b], [FRE.b])
                tt("dve", T1[:], LIM[:, :, 1], AI[:], ALU.mult, [LIM.b, AI.b], [T1.b])
                tt("dve", FRE[:], FRE[:], T1[:], ALU.add, [FRE.b, T1.b], [FRE.b])
                tt("dve", FRE[:], FRE[:], DEN[:], ALU.mult, [FRE.b, DEN.b], [FRE.b])
                tt("dve", FIM[:], LIM[:, :, 1], AR[:], ALU.mult, [LIM.b, AR.b], [FIM.b])
                tt("dve", T1[:], NRE[:], AI[:], ALU.mult, [NRE.b, AI.b], [T1.b])
                tt("dve", FIM[:], FIM[:], T1[:], ALU.subtract, [FIM.b, T1.b], [FIM.b])
                tt("dve", FIM[:], FIM[:], DEN[:], ALU.mult, [FIM.b, DEN.b], [FIM.b])
                BR = sb(tb, "BR", [128, 16, 16], F32); BI = sb(tb, "BI", [128, 16, 16], F32)
                BBR = sb(tb, "BBR", [128, 16, 16], F32); BBI = sb(tb, "BBI", [128, 16, 16], F32)
                T2 = sb(tb, "T2", [128, 16, 16], F32)
                S.dma("sp", BR[:], b_re.rearrange("(q g2) p c -> (g2 p) q c", g2=2), writes=[BR.b], **SLOW)
                S.dma("sp", BI[:], b_im.rearrange("(q g2) p c -> (g2 p) q c", g2=2), writes=[BI.b], **SLOW)
                s3 = [128, 16, 16]
                fr = FRE[:, :].unsqueeze(2).to_broadcast(s3); fi = FIM[:, :].unsqueeze(2).to_broadcast(s3)
                tt("dve", BBR[:], BR[:], fr, ALU.mult, [BR.b, FRE.b], [BBR.b])
                tt("dve", T2[:], BI[:], fi, ALU.mult, [BI.b, FIM.b], [T2.b])
                tt("dve", BBR[:], BBR[:], T2[:], ALU.subtract, [BBR.b, T2.b], [BBR.b])
                tt("dve", BBI[:], BI[:], fr, ALU.mult, [BI.b, FRE.b], [BBI.b])
                tt("dve", T2[:], BR[:], fi, ALU.mult, [BR.b, FIM.b], [T2.b])
                tt("dve", BBI[:], BBI[:], T2[:], ALU.add, [BBI.b, T2.b], [BBI.b])
                ZR = sb(tb, "ZR", [128, 16, 8, 16], F32); ZI = sb(tb, "ZI", [128, 16, 8, 16], F32)
                T3 = sb(tb, "T3", [128, 16, 8, 16], F32)
                s4 = [128, 16, 8, 16]
                lr = LRE[:, :, 0:8].unsqueeze(3).to_broadcast(s4); li = LIM[:, :, 0:8].unsqueeze(3).to_broadcast(s4)
                br_ = BBR[:, :, :].unsqueeze(2).to_broadcast(s4); bi_ = BBI[:, :, :].unsqueeze(2).to_broadcast(s4)
                tt("dve", ZR[:], lr, br_, ALU.mult, [LRE.b, BBR.b], [ZR.b])
                tt("dve", T3[:], li, bi_, ALU.mult, [LIM.b, BBI.b], [T3.b])
                tt("dve", ZR[:], ZR[:], T3[:], ALU.subtract, [ZR.b, T3.b], [ZR.b])
                tt("dve", ZI[:], lr, bi_, ALU.mult, [LRE.b, BBI.b], [ZI.b])
                tt("dve", T3[:], li, br_, ALU.mult, [LIM.b, BBR.b], [T3.b])
                tt("dve", ZI[:], ZI[:], T3[:], ALU.add, [ZI.b, T3.b], [ZI.b])
                E4 = sb(tb, "E4", [128, 4, 8, 2, 128], F32)
                memset("pool", E4[:], 0.0, [E4.b])
                for ri, Zt in enumerate((ZR, ZI)):
                    for gc in range(4):
                        for g2 in range(2):
                            pr = slice(g2 * 64, (g2 + 1) * 64)
                            dst = E4[pr, gc, :, ri, :].rearrange("p t (j x) -> p t j x", x=32)[:, :, :, g2 * 16:(g2 + 1) * 16]
                            src = Zt[pr, gc * 4:(gc + 1) * 4, :, :].rearrange("p j t c -> p t j c")
                            cp("pool", dst, src, [Zt.b], [E4.b])
                for gc in range(4):
                    for ri in range(2):
                        for s0 in (0, 4):
                            bk = bank()
                            for sp_ in range(s0, s0 + 4):
                                tr(bk, bk[:, (sp_ - s0) * 128:(sp_ - s0 + 1) * 128], E4[:, gc, 7 - sp_, ri, :], IDF, [E4.b])
                            cp("act", ZT[:, gc, s0:s0 + 4, ri, :], bk[:].rearrange("p (s n) -> p s n", n=128), [bk.b],
                               [ZT.b])
                CN = sb(tb, "CN", [128, 2, 4, 64], F32)
                CE = sb(tb, "CE", [128, 2, 4, 2, 64], F32)
                CTR = sb(tb, "CTR", [128, 4, 128], F32); CTI = sb(tb, "CTI", [128, 4, 128], F32)
                CTIN = sb(tb, "CTIN", [128, 4, 128], F32)
                G2M = sb(tb, "G2M", [128, 2], F32)
                S.dma("sp", G2M[:], c_g2m, writes=[G2M.b])
                S.dma("sp", CN[:, 0, :, :], c_re.rearrange("(gc r) c p -> (r c) gc p", gc=4), writes=[CN.b], **SLOW)
                S.dma("sp", CN[:, 1, :, :], c_im.rearrange("(gc r) c p -> (r c) gc p", gc=4), writes=[CN.b], **SLOW)
                for ri in range(2):
                    for g2 in range(2):
                        ts("dve", CE[:, ri, :, g2, :], CN[:, ri, :, :], G2M[:, g2:g2 + 1], None, ALU.mult, None,
                           [CN.b, G2M.b], [CE.b])
                for ri, CTt in enumerate((CTR, CTI)):
                    bk = bank()
                    for gc in range(4):
                        tr(bk, bk[:, gc * 128:(gc + 1) * 128], CE[:, ri, gc, :, :].rearrange("p a b -> p (a b)"), IDF,
                           [CE.b])
                    cp("act", CTt[:], bk[:].rearrange("p (g n) -> p g n", n=128), [bk.b], [CTt.b])
                ts("dve", CTIN[:], CTI[:], -1.0, None, ALU.mult, None, [CTI.b], [CTIN.b])
                s5s = [128, 16, 8, 32]
                T4 = sb(tb, "T4", [128, 16, 8, 32], F32); T5 = sb(tb, "T5", [128, 16, 8, 32], F32)
                ctr = CTR[:, :, :].rearrange("p g (j x) -> p (g j) x", x=32).unsqueeze(2).to_broadcast(s5s)
                cti = CTI[:, :, :].rearrange("p g (j x) -> p (g j) x", x=32).unsqueeze(2).to_broadcast(s5s)
                l1r = LRE[:, :, 1:9].unsqueeze(3).to_broadcast(s5s); l1i = LIM[:, :, 1:9].unsqueeze(3).to_broadcast(s5s)
                tt("dve", T4[:], ctr, l1r, ALU.mult, [CTR.b, LRE.b], [T4.b])
                tt("dve", T5[:], cti, l1i, ALU.mult, [CTI.b, LIM.b], [T5.b])
                tt("dve", CLR[:], T4[:], T5[:], ALU.subtract, [T4.b, T5.b], [CLR.b])
                tt("dve", T4[:], ctr, l1i, ALU.mult, [CTR.b, LIM.b], [T4.b])
                tt("dve", T5[:], cti, l1r, ALU.mult, [CTI.b, LRE.b], [T5.b])
                tt("dve", T4[:], T4[:], T5[:], ALU.add, [T4.b, T5.b], [T4.b])
                ts("dve", CLI[:], T4[:], -1.0, None, ALU.mult, None, [T4.b], [CLI.b])
                MASK16 = sb(tb, "MASK16", [128, 128], F32); DCOL = sb(tb, "DCOL", [128, 4], F32)
                T6 = sb(tb, "T6", [128, 128], F32)
                S.dma("sp", MASK16[:], c_mask16, writes=[MASK16.b])
                S.dma("sp", DCOL[:], ssm_d.rearrange("(gc g8) c -> (g8 c) gc", gc=4), writes=[DCOL.b], **SLOW)
                for gc in range(4):
                    for tau in range(8):
                        bk = bank()
                        mm(bk, E4[:, gc, tau, 0, :], CTR[:, gc, :], True, False, [E4.b, CTR.b], out=bk[:, 0:128])
                        mm(bk, E4[:, gc, tau, 1, :], CTIN[:, gc, :], False, True, [E4.b, CTIN.b], out=bk[:, 0:128])
                        if tau == 0:
                            tt("dve", T6[:], bk[:, 0:128], MASK16[:], ALU.mult, [bk.b, MASK16.b], [T6.b])
                            stt(BDT[:, gc, 0, :], IDF[:], DCOL[:, gc:gc + 1], T6[:], ALU.mult, ALU.add,
                                [IDF.b, DCOL.b, T6.b], [BDT.b])
                        else:
                            tt("dve", BDT[:, gc, tau, :], bk[:, 0:128], MASK16[:], ALU.mult, [bk.b, MASK16.b], [BDT.b])
                S.barrier()

            pb = ExitStack()
            uT = sb(pb, "uT", [128, 4, T], BF16)
            with ExitStack() as ph:
                X = sb(ph, "Xu", [128, 4, 1024], F32)
                xnT = sb(ph, "xnTu", [128, 8, NT], BF16)
                rs = sb(ph, "rsu", [128, 4], F32); ss = sb(ph, "ssu", [128, 4], F32)
                junk = sb(ph, "junku", [128, 1024], F32)
                Wu = sb(ph, "Wu", [128, 8, 512], BF16)
                S.dma("pool", Wu[:], w_in[:, 0:512].rearrange("(c p) n -> p c n", p=128), writes=[Wu.b])
                for it in range(NTILES):
                    t0 = it * NT
                    load_norm_tile(X, xp[t0:t0 + NT, :].rearrange("(j p) f -> p j f", p=128), 4, G1, xnT, rs, ss, junk)
                    for m in range(4):
                        bk = bank()
                        for kc in range(8):
                            mm(bk, Wu[:, kc, m * 128:(m + 1) * 128], xnT[:, kc, :], kc == 0, kc == 7, [Wu.b, xnT.b])
                        cp("act", uT[:, m, t0:t0 + NT], bk[:], [bk.b], [uT.b])
                if 'S' in PH:
                    for m in range(4):
                        bk = bank()
                        for kc in range(8):
                            mm(bk, Wu[:, kc, m * 128:(m + 1) * 128], xnTs[:, kc, :], kc == 0, kc == 7, [Wu.b, xnTs.b],
                               out=bk[:, 0:TS])
                        cp("act", usT[:, m, :], bk[:, 0:TS], [bk.b], [usT.b])
                S.barrier()
            if 'S' in PH:
              with ExitStack() as ph:
                STt = sb(ph, "STt", [NS, 2, 2048], F32)
                S0 = sb(ph, "S0", [128, 2, 16, NS], F32); S0b = sb(ph, "S0b", [128, 2, 16, NS], BF16)
                SN = sb(ph, "SN", [128, 2, 16, NS], F32)
                W1 = sb(ph, "W1", [128, 16, NS], F32); W2 = sb(ph, "W2", [128, 16, NS], F32)
                hst = sb(ph, "hst", [NS, 2, 2048], F32)
                Y2s = sb(ph, "Y2s", [128, TS], F32); Y3s = sb(ph, "Y3s", [128, TS], F32)
                Gts = sb(ph, "Gts", [128, 4, TS], F32); Gbs = sb(ph, "Gbs", [128, 4, TS], BF16)
                GLUWs = sb(ph, "GLUWs", [128, 4, 512], BF16)
                S.dma("pool", GLUWs[:], glu_w.rearrange("(c p) n -> p c n", p=128), writes=[GLUWs.b])
                S.dma("sp", STt[:, 0, :], st_re, writes=[STt.b])
                S.dma("sp", STt[:, 1, :], st_im, writes=[STt.b])
                ID16 = TB(IDF.t[0:NS, 0:NS], IDF.b)
                for ri in range(2):
                    bk = bank()
                    for q16 in range(16):
                        tr(bk, bk[:, q16 * NS:(q16 + 1) * NS], STt[:, ri, q16 * 128:(q16 + 1) * 128], ID16, [STt.b])
                    cp("act", S0[:, ri, :, :], bk[:, 0:16 * NS].rearrange("p (q s) -> p q s", s=NS), [bk.b], [S0.b])
                    cp("dve", S0b[:, ri, :, :], S0[:, ri, :, :], [S0.b], [S0b.b])
                for q16 in range(16):
                    gc, j = q16 // 4, q16 % 4
                    pr = slice(32 * j, 32 * j + 32)
                    uv = usT[:, gc, :].rearrange("p (n l) -> p l n", l=4)
                    bk = bank()
                    for ri in range(2):
                        for sp_ in range(4):
                            mm(bk, ZT[pr, gc, 4 + sp_, ri, :], uv[pr, sp_, :], sp_ == 0, sp_ == 3, [ZT.b, usT.b],
                               out=bk[:, ri * NS:(ri + 1) * NS], tile_position=(32 * j, 0))
                    cp("act", SN[:, :, q16, :], bk[:, 0:2 * NS].rearrange("p (r s) -> p r s", s=NS), [bk.b], [SN.b])
                s3 = [128, 16, NS]
                l4r = LRE[:, :, 4:5].to_broadcast(s3); l4i = LIM[:, :, 4:5].to_broadcast(s3)
                tt("dve", W1[:], S0[:, 0, :, :], l4r, ALU.mult, [S0.b, LRE.b], [W1.b])
                tt("dve", W2[:], S0[:, 1, :, :], l4i, ALU.mult, [S0.b, LIM.b], [W2.b])
                tt("dve", W1[:], W1[:], W2[:], ALU.subtract, [W1.b, W2.b], [W1.b])
                tt("dve", SN[:, 0, :, :], SN[:, 0, :, :], W1[:], ALU.add, [SN.b, W1.b], [SN.b])
                tt("dve", W1[:], S0[:, 0, :, :], l4i, ALU.mult, [S0.b, LIM.b], [W1.b])
                tt("dve", W2[:], S0[:, 1, :, :], l4r, ALU.mult, [S0.b, LRE.b], [W2.b])
                tt("dve", W1[:], W1[:], W2[:], ALU.add, [W1.b, W2.b], [W1.b])
                tt("dve", SN[:, 1, :, :], SN[:, 1, :, :], W1[:], ALU.add, [SN.b, W1.b], [SN.b])
                for ri in range(2):
                    for q0 in range(0, 16, 4):
                        bk = bank()
                        for q16 in range(q0, q0 + 4):
                            tr(bk, bk[0:NS, (q16 - q0) * 128:(q16 - q0 + 1) * 128], SN[:, ri, q16, :], IDF, [SN.b])
                        cp("act", hst[:, ri, q0 * 128:(q0 + 4) * 128], bk[0:NS, :], [bk.b], [hst.b])
                S.dma("sp", hre_s, hst[:, 0, :], reads=[hst.b])
                S.dma("sp", him_s, hst[:, 1, :], reads=[hst.b])
                for gc in range(4):
                    bk = bank()
                    uv = usT[:, gc, :].rearrange("p (n l) -> p l n", l=4)
                    ov = bk[:, 0:TS].rearrange("p (n l) -> p l n", l=4)
                    for tp in range(4):
                        for sp_ in range(tp + 1):
                            mm(bk, BDT[:, gc, tp - sp_, :], uv[:, sp_, :], sp_ == 0, False, [BDT.b, usT.b],
                               out=ov[:, tp, :])
                        for j in range(4):
                            q16 = gc * 4 + j
                            for ri, CLt in enumerate((CLR, CLI)):
                                mm(bk, CLt[:, q16, tp, :], S0b[:, ri, q16, :], False, (j == 3 and ri == 1),
                                   [CLt.b, S0b.b], out=ov[32 * j:32 * j + 32, tp, :], tile_position=(0, 32 * j))
                    act(Y2s[:], bk[:, 0:TS], AF.Square, [bk.b], [Y2s.b])
                    ts("dve", Y2s[:], Y2s[:], 0.044715, 1.0, ALU.mult, ALU.add, [Y2s.b], [Y2s.b])
                    tt("dve", Y3s[:], Y2s[:], bk[:, 0:TS], ALU.mult, [Y2s.b, bk.b], [Y3s.b])
                    act(Y3s[:], Y3s[:], AF.Sigmoid, [Y3s.b], [Y3s.b], scale=2.0 * math.sqrt(2.0 / math.pi))
                    tt("dve", Gts[:, gc, :], Y3s[:], bk[:, 0:TS], ALU.mult, [Y3s.b, bk.b], [Gts.b])
                    cp("act", Gbs[:, gc, :], Gts[:, gc, :], [Gts.b], [Gbs.b])
                for m in range(4):
                    bk = bank()
                    for kc in range(4):
                        mm(bk, GLUWs[:, kc, m * 128:(m + 1) * 128], Gbs[:, kc, :], kc == 0, kc == 3, [GLUWs.b, Gbs.b],
                           out=bk[:, 0:TS])
                    act(Y3s[:], bk[:, 0:TS], AF.Sigmoid, [bk.b], [Y3s.b])
                    tt("dve", MIXs[:, m, :], Y3s[:], Gts[:, m, :], ALU.mult, [Y3s.b, Gts.b], [MIXs.b])
                S.barrier()
            SW = sb(pb, "SW", [128, 2, 8, NCH], F32)
            SP = sb(pb, "SPv", [128, 2, 16, NCH], BF16)
            EE = sb(pb, "EE", [128, 2, 16, NSC], F32)
            TA = sb(pb, "TA", [128, 8, NSC], F32); TBt = sb(pb, "TBt", [128, 8, NSC], F32)
            U1 = sb(pb, "U1", [128, 8], F32); U2 = sb(pb, "U2", [128, 8], F32)
            V1 = sb(pb, "V1", [128, 8, SC], F32); V2 = sb(pb, "V2", [128, 8, SC], F32)
            memset("dve", SP[:, :, :, 0:1], 0.0, [SP.b])
            for hq in range(2):
                qs = slice(8 * hq, 8 * hq + 8)
                for q8 in range(8):
                    q16 = 8 * hq + q8
                    gc, j = q16 // 4, q16 % 4
                    pr = slice(32 * j, 32 * j + 32)
                    uv = uT[:, gc, :].rearrange("p (n l) -> p l n", l=L)
                    for ri in range(2):
                        bk = bank()
                        for sp_ in range(L):
                            mm(bk, ZT[pr, gc, sp_, ri, :], uv[pr, sp_, :], sp_ == 0, sp_ == L - 1, [ZT.b, uT.b],
                               tile_position=(32 * j, 0))
                        cp("act" if ri == 0 else "dve", SW[:, ri, q8, :], bk[:], [bk.b], [SW.b])
                Sv = SW[:, :, :, :].rearrange("p r q (s i) -> p r q s i", i=SC)
                sA = [128, 8, NSC]
                a8r = LRE[:, qs, 8:9].to_broadcast(sA); a8i = LIM[:, qs, 8:9].to_broadcast(sA)
                for i in range(1, SC):
                    pre_r = Sv[:, 0, :, :, i - 1]; pre_i = Sv[:, 1, :, :, i - 1]
                    tt("dve", TA[:], pre_r, a8r, ALU.mult, [SW.b, LRE.b], [TA.b])
                    tt("dve", TBt[:], pre_i, a8i, ALU.mult, [SW.b, LIM.b], [TBt.b])
                    tt("dve", TA[:], TA[:], TBt[:], ALU.subtract, [TA.b, TBt.b], [TA.b])
                    tt("dve", TBt[:], pre_i, a8r, ALU.mult, [SW.b, LRE.b], [TBt.b])
                    tt("dve", Sv[:, 0, :, :, i], Sv[:, 0, :, :, i], TA[:], ALU.add, [SW.b, TA.b], [SW.b])
                    tt("dve", TA[:], pre_r, a8i, ALU.mult, [SW.b, LIM.b], [TA.b])
                    tt("dve", TA[:], TA[:], TBt[:], ALU.add, [TA.b, TBt.b], [TA.b])
                    tt("dve", Sv[:, 1, :, :, i], Sv[:, 1, :, :, i], TA[:], ALU.add, [SW.b, TA.b], [SW.b])
                cp("dve", EE[:, :, qs, 0], Sv[:, :, :, 0, SC - 1], [SW.b], [EE.b])
                for sc in range(1, NSC):
                    er = EE[:, 0, qs, sc - 1]; ei = EE[:, 1, qs, sc - 1]
                    tt("dve", U1[:], er, LRE[:, qs, 23], ALU.mult, [EE.b, LRE.b], [U1.b])
                    tt("dve", U2[:], ei, LIM[:, qs, 23], ALU.mult, [EE.b, LIM.b], [U2.b])
                    tt("dve", U1[:], U1[:], U2[:], ALU.subtract, [U1.b, U2.b], [U1.b])
                    tt("dve", EE[:, 0, qs, sc], U1[:], Sv[:, 0, :, sc, SC - 1], ALU.add, [U1.b, SW.b], [EE.b])
                    tt("dve", U1[:], er, LIM[:, qs, 23], ALU.mult, [EE.b, LIM.b], [U1.b])
                    tt("dve", U2[:], ei, LRE[:, qs, 23], ALU.mult, [EE.b, LRE.b], [U2.b])
                    tt("dve", U1[:], U1[:], U2[:], ALU.add, [U1.b, U2.b], [U1.b])
                    tt("dve", EE[:, 1, qs, sc], U1[:], Sv[:, 1, :, sc, SC - 1], ALU.add, [U1.b, SW.b], [EE.b])
                cp("act", SP[:, 0, qs, 1:SC], SW[:, 0, :, 0:SC - 1], [SW.b], [SP.b])
                cp("act", SP[:, 1, qs, 1:SC], SW[:, 1, :, 0:SC - 1], [SW.b], [SP.b])
                sC = [128, 8, SC]
                PR = LRE[:, qs, 8:24]; PI = LIM[:, qs, 8:24]
                for sc in range(1, NSC):
                    er = EE[:, 0, qs, sc - 1:sc].to_broadcast(sC); ei = EE[:, 1, qs, sc - 1:sc].to_broadcast(sC)
                    tt("dve", V1[:], PR, er, ALU.mult, [LRE.b, EE.b], [V1.b])
                    tt("dve", V2[:], PI, ei, ALU.mult, [LIM.b, EE.b], [V2.b])
                    tt("dve", V1[:], V1[:], V2[:], ALU.subtract, [V1.b, V2.b], [V1.b])
                    tt("dve", V1[:], V1[:], Sv[:, 0, :, sc, :], ALU.add, [V1.b, SW.b], [V1.b])
                    tt("dve", V2[:], PR, ei, ALU.mult, [LRE.b, EE.b], [V2.b])
                    n_ = SC if sc < NSC - 1 else SC - 1
                    cp("act", SP[:, 0, qs, sc * SC + 1:sc * SC + 1 + n_], V1[:, :, 0:n_], [V1.b], [SP.b])
                    tt("dve", V1[:], PI, er, ALU.mult, [LIM.b, EE.b], [V1.b])
                    tt("dve", V2[:], V2[:], V1[:], ALU.add, [V1.b, V2.b], [V2.b])
                    tt("dve", V2[:], V2[:], Sv[:, 1, :, sc, :], ALU.add, [V2.b, SW.b], [V2.b])
                    cp("act", SP[:, 1, qs, sc * SC + 1:sc * SC + 1 + n_], V2[:, :, 0:n_], [V2.b], [SP.b])
                for sc in range(1, NSC):
                    cp("act", SP[:, :, qs, sc * SC], EE[:, :, qs, sc - 1], [EE.b], [SP.b])
            S.dma("sp", hre_p.rearrange("q x -> x q"), EE[:, 0, :, NSC - 1], reads=[EE.b], **SLOW)
            S.dma("sp", him_p.rearrange("q x -> x q"), EE[:, 1, :, NSC - 1], reads=[EE.b], **SLOW)
            GLUW = sb(pb, "GLUW", [128, 4, 512], BF16)
            S.dma("pool", GLUW[:], glu_w.rearrange("(c p) n -> p c n", p=128), writes=[GLUW.b])
            Y2 = sb(pb, "Y2", [128, NT], F32); Y3 = sb(pb, "Y3", [128, NT], F32)
            Gt = sb(pb, "Gt", [128, 4, NT], F32); Gb = sb(pb, "Gb", [128, 4, NT], BF16)
            MS = [sb(pb, "MS%d" % i, [128, 4, NT], BF16) for i in range(2)]
            CG = math.sqrt(2.0 / math.pi)
            NCT = NT // L
            for it in range(NTILES):
                t0 = it * NT
                c0 = it * NCT
                for gc in range(4):
                    bk = bank()
                    uv = uT[:, gc, t0:t0 + NT].rearrange("p (n l) -> p l n", l=L)
                    ov = bk[:].rearrange("p (n l) -> p l n", l=L)
                    for tp in range(L):
                        for sp_ in range(tp + 1):
                            mm(bk, BDT[:, gc, tp - sp_, :], uv[:, sp_, :], sp_ == 0, False, [BDT.b, uT.b],
                               out=ov[:, tp, :])
                        for j in range(4):
                            q16 = gc * 4 + j
                            for ri, CLt in enumerate((CLR, CLI)):
                                mm(bk, CLt[:, q16, tp, :], SP[:, ri, q16, c0:c0 + NCT], False,
                                   (j == 3 and ri == 1), [CLt.b, SP.b], out=ov[32 * j:32 * j + 32, tp, :],
                                   tile_position=(0, 32 * j))
                    act(Y2[:], bk[:], AF.Square, [bk.b], [Y2.b])
                    ts("dve", Y2[:], Y2[:], 0.044715, 1.0, ALU.mult, ALU.add, [Y2.b], [Y2.b])
                    tt("dve", Y3[:], Y2[:], bk[:], ALU.mult, [Y2.b, bk.b], [Y3.b])
                    act(Y3[:], Y3[:], AF.Sigmoid, [Y3.b], [Y3.b], scale=2.0 * CG)
                    tt("dve", Gt[:, gc, :], Y3[:], bk[:], ALU.mult, [Y3.b, bk.b], [Gt.b])
                    cp("act", Gb[:, gc, :], Gt[:, gc, :], [Gt.b], [Gb.b])
                M_ = MS[it % 2]
                for m in range(4):
                    bk = bank()
                    for kc in range(4):
                        mm(bk, GLUW[:, kc, m * 128:(m + 1) * 128], Gb[:, kc, :], kc == 0, kc == 3, [GLUW.b, Gb.b])
                    act(Y3[:], bk[:], AF.Sigmoid, [bk.b], [Y3.b])
                    tt("dve", M_[:, m, :], Y3[:], Gt[:, m, :], ALU.mult, [Y3.b, Gt.b], [M_.b])
                S.dma("sp", mix_s[it, :, 0:4, :], M_[:], reads=[M_.b])
            S.barrier()
            pb.close()
        s5.close()

        att = ExitStack()
        QT = sb(att, "QT", [128, 4, T], BF16)
        KT = sb(att, "KT", [128, 4, T], BF16)
        V = sb(att, "V", [128, T // 128, 512], BF16)
        with ExitStack() as ph:
          if 'A2' in PH:
            X = sb(ph, "X", [128, 4, 1024], F32)
            xnT = sb(ph, "xnT", [128, 8, NT], BF16)
            rs = sb(ph, "rs", [128, 4], F32); ss = sb(ph, "ss", [128, 4], F32)
            junk = sb(ph, "junk", [128, 1024], F32)
            W = sb(ph, "Wqkv", [128, 8, 1536], BF16)
            sq = sb(ph, "sq", [128, NT], BF16); rstd = sb(ph, "rstd", [128, NT], F32)
            knT = sb(ph, "knT", [128, 4, NT], F32)
            stg = sb(ph, "stg", [128, 4, 512], F32)
            S.dma("pool", W[:], w_in[:, 512:2048].rearrange("(c p) n -> p c n", p=128), writes=[W.b])
            for it in range(NTILES):
                t0 = it * NT
                load_norm_tile(X, xp[t0:t0 + NT, :].rearrange("(j p) f -> p j f", p=128), 4, G1, xnT, rs, ss, junk)
                for qk in range(2):
                    for m in range(4):
                        bk = bank()
                        for kc in range(8):
                            mm(bk, W[:, kc, qk * 512 + m * 128:qk * 512 + (m + 1) * 128], xnT[:, kc, :], kc == 0,
                               kc == 7, [W.b, xnT.b])
                        act(sq[:], bk[:], AF.Square, [bk.b], [sq.b])
                        b2 = bank()
                        mm(b2, BD64[:], sq[:], True, True, [BD64.b, sq.b])
                        act(rstd[:], b2[:], AF.Sqrt, [b2.b], [rstd.b], bias=EPS)
                        recip(rstd[:], rstd[:], [rstd.b], [rstd.b])
                        if qk == 0:
                            stt(QT[:, m, t0:t0 + NT], bk[:], QG[:, 0:1], rstd[:], ALU.mult, ALU.mult,
                                [bk.b, QG.b, rstd.b], [QT.b])
                        else:
                            stt(knT[:, m, :], bk[:], KG[:, 0:1], rstd[:], ALU.mult, ALU.mult,
                                [bk.b, KG.b, rstd.b], [knT.b])
                            cp("act", KT[:, m, t0:t0 + NT], knT[:, m, :], [knT.b], [KT.b])
                store_tm(knT, 4, NT, k_p[t0:t0 + NT, :], stg)
                for j in range(4):
                    bk = bank()
                    for kc in range(8):
                        mm(bk, xnT[:, kc, j * 128:(j + 1) * 128], W[:, kc, 1024:1536], kc == 0, kc == 7,
                           [W.b, xnT.b])
                    cp("act", stg[:, j, :], bk[:], [bk.b], [stg.b])
                    cp("dve", V[:, it * 4 + j, :], bk[:], [bk.b], [V.b])
                S.dma("sp", v_p[t0:t0 + NT, :].rearrange("(j p) f -> p j f", p=128), stg[:], reads=[stg.b])
            if 'S' in PH:
                for qk in range(2):
                    for m in range(4):
                        bk = bank()
                        for kc in range(8):
                            mm(bk, W[:, kc, qk * 512 + m * 128:qk * 512 + (m + 1) * 128], xnTs[:, kc, :], kc == 0,
                               kc == 7, [W.b, xnTs.b], out=bk[:, 0:TS])
                        act(sq[:, 0:TS], bk[:, 0:TS], AF.Square, [bk.b], [sq.b])
                        b2 = bank()
                        mm(b2, BD64[:], sq[:, 0:TS], True, True, [BD64.b, sq.b], out=b2[:, 0:TS])
                        act(rstd[:, 0:TS], b2[:, 0:TS], AF.Sqrt, [b2.b], [rstd.b], bias=EPS)
                        recip(rstd[:, 0:TS], rstd[:, 0:TS], [rstd.b], [rstd.b])
                        if qk == 0:
                            stt(QTs[:, m, :], bk[:, 0:TS], QG[:, 0:1], rstd[:, 0:TS], ALU.mult, ALU.mult,
                                [bk.b, QG.b, rstd.b], [QTs.b])
                        else:
                            stt(knT[:, m, 0:TS], bk[:, 0:TS], KG[:, 0:1], rstd[:, 0:TS], ALU.mult, ALU.mult,
                                [bk.b, KG.b, rstd.b], [knT.b])
                            cp("act", KTs[:, m, :], knT[:, m, 0:TS], [knT.b], [KTs.b])
                store_tm(knT, 4, TS, k_s, stg)
                memset("dve", VN[:], 1.0, [VN.b])
                for sq_ in range(NS):
                    bk = bank()
                    for kc in range(8):
                        mm(bk, xnTs[:, kc, sq_ * 4:(sq_ + 1) * 4], W[:, kc, 1024:1536], kc == 0, kc == 7,
                           [W.b, xnTs.b], out=bk[0:4, :])
                    cp("act", stg[0:4, sq_ % 4, :], bk[0:4, :], [bk.b], [stg.b])
                    cp("dve", VN[:, sq_, :, 0:128], bk[0:4, :].rearrange("p (h d) -> p h d", d=128), [bk.b], [VN.b])
                    if sq_ % 4 == 3:
                        sg = sq_ // 4
                        S.dma("sp", v_s.rearrange("(s t) f -> t s f", t=4)[:, sg * 4:(sg + 1) * 4, :], stg[0:4, :, :],
                              reads=[stg.b])
            S.barrier()

        with ExitStack() as ph:
          if 'C' in PH:
            PT = [sb(ph, "PT%d" % i, [128, NT], BF16) for i in range(4)]
            BIAS = sb(ph, "BIAS", [128, 4, 34], F32)
            for h in range(4):
                for bi in range(1, 34):
                    ts("dve", BIAS[:, h, bi:bi + 1], KPOS[:, 0:1], float(1 - 128 * bi), SLOPES[h], ALU.add, ALU.mult,
                       [KPOS.b], [BIAS.b])
            On = [sb(ph, "On%d" % i, [128, NT], F32) for i in range(2)]
            rden = sb(ph, "rden", [128, NT], F32)
            sq = sb(ph, "sqa", [128, NT], BF16); rstd = sb(ph, "rstda", [128, NT], F32)
            AO = sb(ph, "AO", [128, 4, NT], BF16)
            ptrr = 0
            for it in range(NTILES):
                q0 = it * NT
                for h in range(4):
                    slope = SLOPES[h]
                    wq = 256 if slope * 511 > 64 else 512
                    nkb = (q0 + NT) // 128
                    Ob = [PS[0], PS[1]]
                    Db = [PS[2], PS[3]]
                    for kb in range(nkb):
                        k0 = kb * 128
                        a = max(0, (k0 - q0) // 128)
                        c0 = a * 128
                        Sb = [PS[4 + (kb % 2) * 2], PS[5 + (kb % 2) * 2]]
                        for mp in range(2):
                            pr = slice(mp * 64, (mp + 1) * 64)
                            mm(Sb[mp], KT[pr, h, k0:k0 + 128], QT[pr, h, q0 + c0:q0 + NT], True, True,
                               [KT.b, QT.b], out=Sb[mp][:, c0:NT])
                        for mp in range(2):
                            P = PT[ptrr % 4]; ptrr += 1
                            for g0 in range((c0 // wq) * wq, NT, wq):
                                lo = max(g0, c0)
                                hi = g0 + wq
                                bi = (q0 + hi - k0) // 128
                                act(P[:, lo:hi], Sb[mp][:, lo:hi], AF.Exp, [Sb[mp].b, BIAS.b], [P.b],
                                    scale=0.125, bias=BIAS[:, h, bi:bi + 1])
                            if k0 >= q0:
                                tt("dve", P[:, c0:c0 + 128], P[:, c0:c0 + 128], CAUS[:], ALU.mult, [P.b, CAUS.b],
                                   [P.b])
                            mm(Ob[mp], V[:, kb, h * 128:(h + 1) * 128], P[:, c0:NT], kb == 0, kb == nkb - 1,
                               [V.b, P.b], out=Ob[mp][:, c0:NT])
                            mm(Db[mp], ONE1[:], P[:, c0:NT], kb == 0, kb == nkb - 1, [ONE1.b, P.b],
                               out=Db[mp][:, c0:NT])
                    for mp in range(2):
                        recip(rden[:], Db[mp][:], [Db[mp].b], [rden.b])
                        tt("dve", On[mp][:], Ob[mp][:], rden[:], ALU.mult, [Ob[mp].b, rden.b], [On[mp].b])
                    stt(On[0][:], On[1][:], NLAM[:, 0:1], On[0][:], ALU.mult, ALU.add, [On[0].b, On[1].b, NLAM.b],
                        [On[0].b])
                    act(sq[:], On[0][:], AF.Square, [On[0].b], [sq.b])
                    b2 = PS[4]
                    mm(b2, ON128[:], sq[:], True, True, [ON128.b, sq.b])
                    act(rstd[:], b2[:], AF.Sqrt, [b2.b], [rstd.b], bias=EPS)
                    recip(rstd[:], rstd[:], [rstd.b], [rstd.b])
                    stt(AO[:, h, :], On[0][:], SUBG[:, 0:1], rstd[:], ALU.mult, ALU.mult,
                        [On[0].b, SUBG.b, rstd.b], [AO.b])
                S.dma("sp", mix_s[it, :, 4:8, :], AO[:], reads=[AO.b])
            S.barrier()
        att.close()


        with ExitStack() as ph:
          if 'S' in PH and 'CS' in PH:
            NKB = 4
            PTI = sb(ph, "PTI", [128, NS * 16], I32)
            IDX = sb(ph, "IDX", [128, NS * 16], U32)
            KPB = [sb(ph, "KPB%d" % i, [128, 512], F32) for i in range(NKB)]
            VPF = [sb(ph, "VPF%d" % i, [128, 512], F32) for i in range(NKB)]
            VPB = [sb(ph, "VPB%d" % i, [128, 4, 132], BF16) for i in range(32)]
            KpT = [sb(ph, "KpT%d" % i, [128, 4, 128], BF16) for i in range(2)]
            QB = sb(ph, "QB", [128, 4, NS, 2, 4], BF16)
            BIASS = sb(ph, "BIASS", [128, 16, 4, 8], F32)
            BIASN = sb(ph, "BIASN", [4, 4, 8], F32)
            TMPs = sb(ph, "TMPs", [128, 512], F32)
            PBs = [sb(ph, "PBs%d" % i, [128, 512], BF16) for i in range(2)]
            TNs = sb(ph, "TNs", [4, 32], F32); PNs = sb(ph, "PNs", [4, 32], BF16)
            RD = sb(ph, "RDs", [8, 4], F32)
            ONs = sb(ph, "ONs", [8, 4, 128], F32)
            CMB = sb(ph, "CMB", [8, 4], F32)
            OD4 = sb(ph, "OD4", [4, 512], F32)
            ODT = sb(ph, "ODT", [128, 4, TS], F32)
            sqs = sb(ph, "sqs", [128, TS], BF16); rstds = sb(ph, "rstds", [128, TS], F32)
            S.dma("sp", PTI[:], ptab.rearrange("s g -> (s g)").rearrange("(o n) -> o n", o=1).to_broadcast([128, NS * 16]),
                  writes=[PTI.b], **SLOW)
            ts("dve", IDX[:], PTI[:], 128.0, KPOS[:, 0:1], ALU.mult, ALU.add, [PTI.b, KPOS.b], [IDX.b])
            memset("dve", QB[:], 0.0, [QB.b])
            for mp in range(2):
                pr = slice(mp * 64, (mp + 1) * 64)
                cp("dve", QB[pr, :, :, mp, :], QTs[pr, :, :].rearrange("p h (s t) -> p h s t", t=4), [QTs.b], [QB.b])
            for pg in range(16):
                for h in range(4):
                    ts("dve", BIASS[:, pg, h, :], KPOS[:, 0:1].to_broadcast([128, 8]), float(pg * 128 - 2048), SLOPES[h],
                       ALU.add, ALU.mult, [KPOS.b], [BIASS.b])
            for h in range(4):
                ts("dve", BIASN[:, h, :], KPOS[0:4, 0:1].to_broadcast([4, 8]), SLOPES[h], None, ALU.mult, None, [KPOS.b],
                   [BIASN.b])
            for i in range(32):
                memset("dve", VPB[i][:], 1.0, [VPB[i].b])
            stt(CMB[:], IDF[0:8, 4:8], NLAM[0:8, 0:1], IDF[0:8, 0:4], ALU.mult, ALU.add, [IDF.b, NLAM.b], [CMB.b])
            ID4 = TB(IDF.t[0:4, 0:4], IDF.b)
            kcnt = 0
            ck_rows = cache_k
            cv_rows = cache_v
            for sq_ in range(NS):
                SBk = PS[sq_ % 2]
                PB_ = PBs[sq_ % 2]
                Ob = [PS[2], PS[3], PS[4], PS[5]]
                pend = []
                for pg in range(16):
                    Kp = KPB[kcnt % NKB]; Vf = VPF[kcnt % NKB]; Vp = VPB[kcnt % 32]; kcnt += 1
                    col = sq_ * 16 + pg
                    S.dmaf("pool", (lambda e, Kp=Kp, col=col: e.indirect_dma_start(
                        out=Kp[:, :], out_offset=None, in_=ck_rows,
                        in_offset=bass.IndirectOffsetOnAxis(ap=IDX[:, col:col + 1], axis=0))),
                        reads=[IDX.b], writes=[Kp.b])
                    S.dmaf("pool", (lambda e, Vf=Vf, col=col: e.indirect_dma_start(
                        out=Vf[:, :], out_offset=None, in_=cv_rows,
                        in_offset=bass.IndirectOffsetOnAxis(ap=IDX[:, col:col + 1], axis=0))),
                        reads=[IDX.b], writes=[Vf.b])
                    bkT = PS[6 + (pg % 2)]
                    KT_ = KpT[pg % 2]
                    for h in range(4):
                        tr(bkT, bkT[:, h * 128:(h + 1) * 128], Kp[:, h * 128:(h + 1) * 128], IDF, [Kp.b])
                    cp("act", KT_[:, :, :], bkT[:].rearrange("p (h n) -> p h n", n=128), [bkT.b], [KT_.b])
                    for h in range(4):
                        mm(SBk, KT_[:, h, :], QB[:, h, sq_, :, :].rearrange("p a b -> p (a b)"), True, True,
                           [KT_.b, QB.b], out=SBk[:, (pg * 4 + h) * 8:(pg * 4 + h + 1) * 8])
                    cp("dve", Vp[:, :, 0:128], Vf[:, :].rearrange("p (h d) -> p h d", d=128), [Vf.b], [Vp.b])
                    pend.append(Vp)
                NB = PS[6]
                for h in range(4):
                    mm(NB, KTs[:, h, sq_ * 4:(sq_ + 1) * 4], QB[:, h, sq_, :, :].rearrange("p a b -> p (a b)"), True, True,
                       [KTs.b, QB.b], out=NB[0:4, h * 8:(h + 1) * 8])
                stt(TMPs[:], SBk[:], 0.125, BIASS[:, :, :, :].rearrange("p a b c -> p (a b c)"), ALU.mult, ALU.add,
                    [SBk.b, BIASS.b], [TMPs.b])
                act(PB_[:], TMPs[:], AF.Exp, [TMPs.b], [PB_.b])
                stt(TNs[:], NB[0:4, 0:32], 0.125, BIASN[:, :, :].rearrange("p a b -> p (a b)"), ALU.mult, ALU.add,
                    [NB.b, BIASN.b], [TNs.b])
                act(TNs[:], TNs[:], AF.Exp, [TNs.b], [TNs.b])
                tt("dve", PNs[:, :].rearrange("p (a t) -> p a t", t=4), TNs[:, :].rearrange("p (a t) -> p a t", t=4),
                   CAUS[0:4, 0:4].unsqueeze(1).to_broadcast([4, 8, 4]), ALU.mult, [TNs.b, CAUS.b], [PNs.b])
                for pg in range(16):
                    Vp = pend[pg]
                    for h in range(4):
                        mm(Ob[h], PB_[:, (pg * 4 + h) * 8:(pg * 4 + h + 1) * 8], Vp[:, h, 0:129], pg == 0, False,
                           [PB_.b, Vp.b], out=Ob[h][0:8, 0:129])
                for h in range(4):
                    mm(Ob[h], PNs[0:4, h * 8:(h + 1) * 8], VN[0:4, sq_, h, 0:129], False, True, [PNs.b, VN.b],
                       out=Ob[h][0:8, 0:129])
                for h in range(4):
                    recip(RD[:, h:h + 1], Ob[h][0:8, 128:129], [Ob[h].b], [RD.b])
                    ts("dve", ONs[:, h, :], Ob[h][0:8, 0:128], RD[:, h:h + 1], None, ALU.mult, None, [Ob[h].b, RD.b],
                       [ONs.b])
                DBk = PS[7]
                mm(DBk, CMB[:, :], ONs[:, :, :].rearrange("p h d -> p (h d)"), True, True, [CMB.b, ONs.b],
                   out=DBk[0:4, :])
                cp("act", OD4[:], DBk[0:4, :], [DBk.b], [OD4.b])
                TBk = PS[6]
                for h in range(4):
                    tr(TBk, TBk[:, h * 4:(h + 1) * 4], OD4[0:4, h * 128:(h + 1) * 128], ID4, [OD4.b])
                cp("act", ODT[:, :, sq_ * 4:(sq_ + 1) * 4], TBk[:, 0:16].rearrange("p (h t) -> p h t", t=4), [TBk.b],
                   [ODT.b])
            for h in range(4):
                act(sqs[:], ODT[:, h, :], AF.Square, [ODT.b], [sqs.b])
                b2 = PS[7]
                mm(b2, ON128[:], sqs[:], True, True, [ON128.b, sqs.b], out=b2[:, 0:TS])
                act(rstds[:], b2[:, 0:TS], AF.Sqrt, [b2.b], [rstds.b], bias=EPS)
                recip(rstds[:], rstds[:], [rstds.b], [rstds.b])
                stt(MIXs[:, 4 + h, :], ODT[:, h, :], SUBG[:, 0:1], rstds[:], ALU.mult, ALU.mult,
                    [ODT.b, SUBG.b, rstds.b], [MIXs.b])
            S.barrier()

        with ExitStack() as ph:
          if 'D' in PH:
            X = sb(ph, "Xd", [128, 4, 1024], F32)
            xT = sb(ph, "xT", [128, 8, NT], F32)
            MIX = [sb(ph, "MIX%d" % i, [128, 8, NT], BF16) for i in range(2)]
            A = sb(ph, "actA", [128, 8, NT], BF16); Bb = sb(ph, "actB", [128, 8, NT], BF16)
            Wo = sb(ph, "Wo", [128, 8, 1024], BF16); Wq = sb(ph, "Wq", [128, 8, 1024], BF16)
            Wo2 = sb(ph, "Wo2", [128, 8, 1024], BF16)
            sq = sb(ph, "sqd", [128, 2, NT], BF16); rstd = sb(ph, "rstdd", [128, NT], F32)
            Pm = [sb(ph, "Pm%d" % i, [128, NT], BF16) for i in range(2)]
            hT = sb(ph, "hT", [128, NFC, NT], BF16)
            HG = sb(ph, "HG", [128, NT + 2], F32); CARRY = sb(ph, "CARRY", [128, NFC, 2], F32)
            cv = sb(ph, "cv", [128, NT], F32)
            rden = sb(ph, "rdend", [128, NT], F32)
            WG = [sb(ph, "WG%d" % i, [128, 8, 128], BF16) for i in range(2)]
            WV = [sb(ph, "WV%d" % i, [128, 8, 128], BF16) for i in range(2)]
            WD = [sb(ph, "WD%d" % i, [128, NFC, 128], BF16) for i in range(2)]
            S.dma("pool", Wo[:], w_out.rearrange("(c p) n -> p c n", p=128), writes=[Wo.b])
            S.dma("pool", Wq[:], ca_wq.rearrange("(c p) n -> p c n", p=128), writes=[Wq.b])
            S.dma("pool", Wo2[:], ca_wo.rearrange("(c p) n -> p c n", p=128), writes=[Wo2.b])
            memset("dve", CARRY[:], 0.0, [CARRY.b])
            if 'B' not in PH or os.environ.get('KNOB'):
                memset("dve", hT[:, 0:4, :], 0.0, [hT.b])
                for it in range(NTILES):
                    S.dma("sp", mix_s[it, :, 0:4, :], hT[:, 0:4, :], reads=[hT.b])
                S.barrier()
            wrr = 0
            for it in range(NTILES):
                t0 = it * NT
                M_ = MIX[it % 2]
                S.dma("sp", M_[:], mix_s[it], writes=[M_.b])
                S.dma("sp", X[:], xp[t0:t0 + NT, :].rearrange("(j p) f -> p j f", p=128), writes=[X.b])
                for c in range(8):
                    bk = bank()
                    for j in range(4):
                        tr(bk, bk[:, j * 128:(j + 1) * 128], X[:, j, c * 128:(c + 1) * 128], IDF, [X.b])
                    cp("act", xT[:, c, :], bk[:], [bk.b], [xT.b])
                for m in range(8):
                    bk = bank()
                    for kc in range(8):
                        mm(bk, Wo[:, kc, m * 128:(m + 1) * 128], M_[:, kc, :], kc == 0, kc == 7, [Wo.b, M_.b])
                    tt("dve", xT[:, m, :], bk[:], xT[:, m, :], ALU.add, [bk.b, xT.b], [xT.b])
                ln_fm(xT, G2, A, NT, sq, rstd)
                for h in range(4):
                    bq = [bank(), bank()]
                    for dh in range(2):
                        for kc in range(8):
                            mm(bq[dh], Wq[:, kc, (2 * h + dh) * 128:(2 * h + dh + 1) * 128], A[:, kc, :], kc == 0,
                               kc == 7, [Wq.b, A.b])
                    b2 = bank()
                    for dh in range(2):
                        act(sq[:, dh, :], bq[dh][:], AF.Square, [bq[dh].b], [sq.b])
                        mm(b2, ON256[:], sq[:, dh, :], dh == 0, dh == 1, [ON256.b, sq.b])
                    act(rstd[:], b2[:], AF.Sqrt, [b2.b], [rstd.b], bias=EPS)
                    recip(rstd[:], rstd[:], [rstd.b], [rstd.b])
                    for dh in range(2):
                        stt(Bb[:, 2 * h + dh, :], bq[dh][:], CQG[:, dh:dh + 1], rstd[:], ALU.mult, ALU.mult,
                            [bq[dh].b, CQG.b, rstd.b], [Bb.b])
                for h in range(4):
                    for mb in range(2):
                        bs = bank()
                        for dh in range(2):
                            mm(bs, MKT[:, 2 * h + dh, mb * 128:(mb + 1) * 128], Bb[:, 2 * h + dh, :], dh == 0, dh == 1,
                               [MKT.b, Bb.b])
                        act(Pm[mb][:], bs[:], AF.Exp, [bs.b], [Pm[mb].b], scale=1.0 / 16)
                    bd = bank()
                    for mb in range(2):
                        mm(bd, ONE1[:], Pm[mb][:], mb == 0, mb == 1, [ONE1.b, Pm[mb].b])
                    recip(rden[:], bd[:], [bd.b], [rden.b])
                    for dvc in range(2):
                        bo = bank()
                        for mb in range(2):
                            mm(bo, MV[:, mb, (2 * h + dvc) * 128:(2 * h + dvc + 1) * 128], Pm[mb][:], mb == 0, mb == 1,
                               [MV.b, Pm[mb].b])
                        tt("dve", A[:, 2 * h + dvc, :], bo[:], rden[:], ALU.mult, [bo.b, rden.b], [A.b])
                for m in range(8):
                    bk = bank()
                    for kc in range(8):
                        mm(bk, Wo2[:, kc, m * 128:(m + 1) * 128], A[:, kc, :], kc == 0, kc == 7, [Wo2.b, A.b])
                    tt("dve", xT[:, m, :], bk[:], xT[:, m, :], ALU.add, [bk.b, xT.b], [xT.b])
                ln_fm(xT, G3, Bb, NT, sq, rstd)
                for fc in range(NFC):
                    wg = WG[wrr % 2]; wv = WV[wrr % 2]; wrr += 1
                    S.dma("sp", wg[:], wg_s[:, fc * 128:(fc + 1) * 128].rearrange("(c p) n -> p c n", p=128),
                          writes=[wg.b])
                    S.dma("sp", wv[:], wv_s[:, fc * 128:(fc + 1) * 128].rearrange("(c p) n -> p c n", p=128),
                          writes=[wv.b])
                    bg = bank(); bv = bank()
                    for kc in range(8):
                        mm(bg, wg[:, kc, :], Bb[:, kc, :], kc == 0, kc == 7, [wg.b, Bb.b])
                    for kc in range(8):
                        mm(bv, wv[:, kc, :], Bb[:, kc, :], kc == 0, kc == 7, [wv.b, Bb.b])
                    cp("dve", HG[:, 0:2], CARRY[:, fc, :], [CARRY.b], [HG.b])
                    cp("act", HG[:, 2:NT + 2], bg[:], [bg.b], [HG.b])
                    cp("dve", CARRY[:, fc, :], HG[:, NT:NT + 2], [HG.b], [CARRY.b])
                    act(cv[:], HG[:, 2:NT + 2], AF.Identity, [HG.b, CW.b, CB.b], [cv.b], scale=CW[:, 2, fc:fc + 1],
                        bias=CB[:, fc:fc + 1])
                    stt(cv[:], HG[:, 1:NT + 1], CW[:, 1, fc:fc + 1], cv[:], ALU.mult, ALU.add, [HG.b, CW.b, cv.b],
                        [cv.b])
                    stt(cv[:], HG[:, 0:NT], CW[:, 0, fc:fc + 1], cv[:], ALU.mult, ALU.add, [HG.b, CW.b, cv.b], [cv.b])
                    act(cv[:], cv[:], AF.Silu, [cv.b], [cv.b])
                    tt("dve", hT[:, fc, :], cv[:], bv[:], ALU.mult, [cv.b, bv.b], [hT.b])
                for m in range(8):
                    wd = WD[m % 2]
                    S.dma("sp", wd[:], wd_s[:, m * 128:(m + 1) * 128].rearrange("(c p) n -> p c n", p=128),
                          writes=[wd.b])
                    bk = bank()
                    for fc in range(NFC):
                        mm(bk, wd[:, fc, :], hT[:, fc, :], fc == 0, fc == NFC - 1, [wd.b, hT.b])
                    tt("dve", xT[:, m, :], bk[:], xT[:, m, :], ALU.add, [bk.b, xT.b], [xT.b])
                store_tm(xT, 8, NT, y_p[t0:t0 + NT, :], X)
            for j in range(2):
                S.dma("sp", conv_p[j].rearrange("(c p) -> p c", p=128), CARRY[:, :, j], reads=[CARRY.b], **SLOW)
            S.barrier()

        with ExitStack() as ph:
          if 'S' in PH and 'DS' in PH:
            X = sb(ph, "Xds", [128, 1, 1024], F32)
            xT = sb(ph, "xTs", [128, 8, TS], F32)
            A = sb(ph, "actAs", [128, 8, TS], BF16); Bb = sb(ph, "actBs", [128, 8, TS], BF16)
            Wo = sb(ph, "Wos", [128, 8, 1024], BF16); Wq = sb(ph, "Wqs", [128, 8, 1024], BF16)
            Wo2 = sb(ph, "Wo2s", [128, 8, 1024], BF16)
            sq = sb(ph, "sqds", [128, 2, TS], BF16); rstd = sb(ph, "rstdds", [128, TS], F32)
            hT = sb(ph, "hTs", [128, NFC, TS], BF16)
            WG = [sb(ph, "WGs%d" % i, [128, 8, 128], BF16) for i in range(2)]
            WV = [sb(ph, "WVs%d" % i, [128, 8, 128], BF16) for i in range(2)]
            WD = [sb(ph, "WDs%d" % i, [128, NFC, 128], BF16) for i in range(2)]
            S.dma("pool", Wo[:], w_out.rearrange("(c p) n -> p c n", p=128), writes=[Wo.b])
            S.dma("pool", Wq[:], ca_wq.rearrange("(c p) n -> p c n", p=128), writes=[Wq.b])
            S.dma("pool", Wo2[:], ca_wo.rearrange("(c p) n -> p c n", p=128), writes=[Wo2.b])
            wrr = 0
            n = TS
            CMKt = sb(ph, "CMKt", [128, 2, 1024], F32)
            CMVb = [sb(ph, "CMVb%d" % i, [128, 2, 4, 260], BF16) for i in range(2)]
            MKs = sb(ph, "MKs", [128, 8, 256], BF16)
            PCs = sb(ph, "PCs", [128, 32], BF16)
            RDc = sb(ph, "RDc", [4, 4], F32)
            COn = sb(ph, "COn", [4, 4, 256], F32)
            SCt = sb(ph, "SCt", [32, F], F32)
            SCV = sb(ph, "SCV", [128, NFC, 32], F32)
            CVS = sb(ph, "CVS", [128, NFC, 32], F32)
            HGs = sb(ph, "HGs", [128, NS, 6], F32)
            cvs = sb(ph, "cvs", [128, NS, 4], F32)
            ID4 = TB(IDF.t[0:4, 0:4], IDF.b); ID32 = TB(IDF.t[0:32, 0:32], IDF.b)
            for i in range(2):
                memset("dve", CMVb[i][:], 1.0, [CMVb[i].b])
            S.dma("sp", SCt[:], st_conv, writes=[SCt.b])
            for f0 in range(0, NFC, 16):
                bk = bank()
                for fc in range(f0, min(f0 + 16, NFC)):
                    tr(bk, bk[:, (fc - f0) * 32:(fc - f0 + 1) * 32], SCt[0:32, fc * 128:(fc + 1) * 128], ID32, [SCt.b])
                nf = min(f0 + 16, NFC) - f0
                cp("act", SCV[:, f0:f0 + nf, :], bk[:, 0:nf * 32].rearrange("p (f x) -> p f x", x=32), [bk.b], [SCV.b])
            for c in range(8):
                cp("dve", xT[:, c, 0:n], xsT[:, c, :], [xsT.b], [xT.b])
            for m in range(8):
                bk = bank()
                for kc in range(8):
                    mm(bk, Wo[:, kc, m * 128:(m + 1) * 128], MIXs[:, kc, :], kc == 0, kc == 7, [Wo.b, MIXs.b],
                       out=bk[:, 0:n])
                tt("dve", xT[:, m, 0:n], bk[:, 0:n], xT[:, m, 0:n], ALU.add, [bk.b, xT.b], [xT.b])
            ln_fm(xT, G2, A, n, sq, rstd)
            for h in range(4):
                bq = [bank(), bank()]
                for dh in range(2):
                    for kc in range(8):
                        mm(bq[dh], Wq[:, kc, (2 * h + dh) * 128:(2 * h + dh + 1) * 128], A[:, kc, 0:n], kc == 0,
                           kc == 7, [Wq.b, A.b], out=bq[dh][:, 0:n])
                b2 = bank()
                for dh in range(2):
                    act(sq[:, dh, 0:n], bq[dh][:, 0:n], AF.Square, [bq[dh].b], [sq.b])
                    mm(b2, ON256[:], sq[:, dh, 0:n], dh == 0, dh == 1, [ON256.b, sq.b], out=b2[:, 0:n])
                act(rstd[:, 0:n], b2[:, 0:n], AF.Sqrt, [b2.b], [rstd.b], bias=EPS)
                recip(rstd[:, 0:n], rstd[:, 0:n], [rstd.b], [rstd.b])
                for dh in range(2):
                    stt(Bb[:, 2 * h + dh, 0:n], bq[dh][:, 0:n], CQG[:, dh:dh + 1], rstd[:, 0:n], ALU.mult, ALU.mult,
                        [bq[dh].b, CQG.b, rstd.b], [Bb.b])
            for sq_ in range(NS):
                CMV_ = CMVb[sq_ % 2]
                S.dma("sp", CMKt[:], cmk[sq_].rearrange("(mb p) f -> p mb f", p=128), writes=[CMKt.b])
                for mb in range(2):
                    S.dma("pool", CMV_[:, mb, :, 0:256], cmv[sq_, mb * 128:(mb + 1) * 128, :].rearrange(
                        "p (h d) -> p h d", d=256), writes=[CMV_.b])
                for mb in range(2):
                    for c0 in (0, 4):
                        bk = bank()
                        for c8 in range(c0, c0 + 4):
                            tr(bk, bk[:, (c8 - c0) * 128:(c8 - c0 + 1) * 128], CMKt[:, mb, c8 * 128:(c8 + 1) * 128], IDF,
                               [CMKt.b])
                        cp("act" if c0 == 0 else "dve", MKs[:, c0:c0 + 4, mb * 128:(mb + 1) * 128],
                           bk[:].rearrange("p (c n) -> p c n", n=128), [bk.b], [MKs.b])
                SBc = bank()
                for mb in range(2):
                    for h in range(4):
                        for dh in range(2):
                            mm(SBc, MKs[:, 2 * h + dh, mb * 128:(mb + 1) * 128], Bb[:, 2 * h + dh, sq_ * 4:(sq_ + 1) * 4],
                               dh == 0, dh == 1, [MKs.b, Bb.b], out=SBc[:, (mb * 4 + h) * 4:(mb * 4 + h + 1) * 4])
                act(PCs[:], SBc[:, 0:32], AF.Exp, [SBc.b], [PCs.b], scale=1.0 / 16)
                Oc = [bank(), bank(), bank(), bank()]
                for h in range(4):
                    for mb in range(2):
                        mm(Oc[h], PCs[:, (mb * 4 + h) * 4:(mb * 4 + h + 1) * 4], CMV_[:, mb, h, 0:257], mb == 0, mb == 1,
                           [PCs.b, CMV_.b], out=Oc[h][0:4, 0:257])
                for h in range(4):
                    recip(RDc[:, h:h + 1], Oc[h][0:4, 256:257], [Oc[h].b], [RDc.b])
                    ts("dve", COn[:, h, :], Oc[h][0:4, 0:256], RDc[:, h:h + 1], None, ALU.mult, None,
                       [Oc[h].b, RDc.b], [COn.b])
                TBk = bank()
                cof = COn[:, :, :].rearrange("p h d -> p (h d)")
                for c8 in range(8):
                    tr(TBk, TBk[:, c8 * 4:(c8 + 1) * 4], cof[0:4, c8 * 128:(c8 + 1) * 128], ID4, [COn.b])
                cp("act", A[:, :, sq_ * 4:(sq_ + 1) * 4], TBk[:, 0:32].rearrange("p (c t) -> p c t", t=4), [TBk.b],
                   [A.b])
            for m in range(8):
                bk = bank()
                for kc in range(8):
                    mm(bk, Wo2[:, kc, m * 128:(m + 1) * 128], A[:, kc, 0:n], kc == 0, kc == 7, [Wo2.b, A.b],
                       out=bk[:, 0:n])
                tt("dve", xT[:, m, 0:n], bk[:, 0:n], xT[:, m, 0:n], ALU.add, [bk.b, xT.b], [xT.b])
            ln_fm(xT, G3, Bb, n, sq, rstd)
            for fc in range(NFC):
                wg = WG[wrr % 2]; wv = WV[wrr % 2]; wrr += 1
                S.dma("sp", wg[:], wg_s[:, fc * 128:(fc + 1) * 128].rearrange("(c p) n -> p c n", p=128),
                      writes=[wg.b])
                S.dma("sp", wv[:], wv_s[:, fc * 128:(fc + 1) * 128].rearrange("(c p) n -> p c n", p=128),
                      writes=[wv.b])
                bg = bank(); bv = bank()
                for kc in range(8):
                    mm(bg, wg[:, kc, :], Bb[:, kc, 0:n], kc == 0, kc == 7, [wg.b, Bb.b], out=bg[:, 0:n])
                for kc in range(8):
                    mm(bv, wv[:, kc, :], Bb[:, kc, 0:n], kc == 0, kc == 7, [wv.b, Bb.b], out=bv[:, 0:n])
                cp("dve", HGs[:, :, 0:2], SCV[:, fc, :].rearrange("p (s j) -> p s j", j=2), [SCV.b], [HGs.b])
                cp("act", HGs[:, :, 2:6], bg[:, 0:n].rearrange("p (s t) -> p s t", t=4), [bg.b], [HGs.b])
                cp("dve", CVS[:, fc, :].rearrange("p (s j) -> p s j", j=2), HGs[:, :, 4:6], [HGs.b], [CVS.b])
                act(cvs[:], HGs[:, :, 2:6], AF.Identity, [HGs.b, CW.b, CB.b], [cvs.b], scale=CW[:, 2, fc:fc + 1],
                    bias=CB[:, fc:fc + 1])
                stt(cvs[:], HGs[:, :, 1:5], CW[:, 1, fc:fc + 1], cvs[:], ALU.mult, ALU.add, [HGs.b, CW.b, cvs.b],
                    [cvs.b])
                stt(cvs[:], HGs[:, :, 0:4], CW[:, 0, fc:fc + 1], cvs[:], ALU.mult, ALU.add, [HGs.b, CW.b, cvs.b],
                    [cvs.b])
                act(cvs[:], cvs[:], AF.Silu, [cvs.b], [cvs.b])
                tt("dve", hT[:, fc, 0:n], cvs[:, :, :].rearrange("p s t -> p (s t)"), bv[:, 0:n], ALU.mult,
                   [cvs.b, bv.b], [hT.b])
            for m in range(8):
                wd = WD[m % 2]
                S.dma("sp", wd[:], wd_s[:, m * 128:(m + 1) * 128].rearrange("(c p) n -> p c n", p=128),
                      writes=[wd.b])
                bk = bank()
                for fc in range(NFC):
                    mm(bk, wd[:, fc, :], hT[:, fc, 0:n], fc == 0, fc == NFC - 1, [wd.b, hT.b], out=bk[:, 0:n])
                tt("dve", xT[:, m, 0:n], bk[:, 0:n], xT[:, m, 0:n], ALU.add, [bk.b, xT.b], [xT.b])
            store_tm(xT, 8, n, y_s, X)
            for f0 in range(0, NFC, 4):
                bk = bank()
                nf = min(f0 + 4, NFC) - f0
                for fc in range(f0, f0 + nf):
                    tr(bk, bk[0:32, (fc - f0) * 128:(fc - f0 + 1) * 128], CVS[:, fc, :], IDF, [CVS.b])
                cp("act", SCt[0:32, f0 * 128:(f0 + nf) * 128], bk[0:32, 0:nf * 128], [bk.b], [SCt.b])
            S.dma("sp", conv_s, SCt[:], reads=[SCt.b])

            S.barrier()

        S.barrier()
        S.emit()
    return nc


_NC_CACHE = {}


def _consts():
    ident = np.eye(128, dtype=np.float32)
    bd64 = np.zeros((128, 128), np.float32); bd64[:64, :64] = 1 / 64; bd64[64:, 64:] = 1 / 64
    m16 = np.kron(np.eye(8, dtype=np.float32), np.ones((16, 16), np.float32))
    caus = np.triu(np.ones((128, 128), np.float32))
    pwn = np.array(PW_N, np.float32)
    kpos = np.arange(128, dtype=np.float32)
    g2m = np.zeros((128, 2), np.float32)
    for p in range(128):
        g2m[p, (p // 16) % 2] = 1.0
    return dict(c_ident=ident, c_bd64=bd64, c_mask16=m16, c_caus=caus, c_pwn=pwn, c_kpos=kpos, c_g2m=g2m)


WNAMES = ["ln1_g", "ln2_g", "ln3_g", "mem_norm_g", "w_in", "ssm_a_re", "ssm_a_im", "ssm_b_re", "ssm_b_im",
          "ssm_c_re", "ssm_c_im", "ssm_d", "ssm_log_dt", "ssm_glu_w", "q_norm_g", "k_norm_g", "lam_q1", "lam_k1",
          "lam_q2", "lam_k2", "subln_g", "w_out", "ca_wq", "ca_wk", "ca_wv", "ca_q_norm_g", "ca_k_norm_g", "ca_wo",
          "ffn_wg", "ffn_wv", "ffn_wd", "ffn_conv_w", "ffn_conv_b"]


def make_in_maps(inp, cores):
    cst = _consts()
    shared = {n: np.ascontiguousarray(np.asarray(inp[n])[0]) for n in WNAMES}
    maps = []
    for c in cores:
        b = c % 4
        m = dict(shared)
        m.update(cst)
        m["xp"] = np.ascontiguousarray(np.asarray(inp["x_prompt"])[b])
        m["memp"] = np.ascontiguousarray(np.asarray(inp["mem_prompt"])[b])
        sl = slice(c * NS, (c + 1) * NS)
        m["xs"] = np.ascontiguousarray(np.asarray(inp["x_sample"])[sl]).reshape(TS, 1024)
        m["st_re"] = np.ascontiguousarray(np.asarray(inp["state_ssm_re"])[0, sl]).reshape(NS, 2048)
        m["st_im"] = np.ascontiguousarray(np.asarray(inp["state_ssm_im"])[0, sl]).reshape(NS, 2048)
        m["st_conv"] = np.ascontiguousarray(np.asarray(inp["state_conv"])[0, sl]).reshape(NS * 2, F)
        m["cmk"] = np.ascontiguousarray(np.asarray(inp["cache_mem_k"])[0, sl]).reshape(NS, 256, 1024)
        m["cmv"] = np.ascontiguousarray(np.asarray(inp["cache_mem_v"])[0, sl]).reshape(NS, 256, 1024)
        m["cache_k"] = np.asarray(inp["cache_k"]).reshape(2560 * 128, 512)
        m["cache_v"] = np.asarray(inp["cache_v"]).reshape(2560 * 128, 512)
        m["ptab"] = np.ascontiguousarray(np.asarray(inp["page_table"])[sl]).astype(np.int32)
        maps.append(m)
    return maps


def kernel(**inp):
    nc = build_nc()
    cores = list(range(NCORES))
    maps = make_in_maps(inp, cores)
    res = run_bass_kernel_spmd(nc, maps, core_ids=cores)
    R = res.results
    f32 = np.float32
    y_prompt = np.stack([R[b]["y_p"] for b in range(4)]).astype(f32)
    k_prompt = np.stack([R[b]["k_p"] for b in range(4)]).reshape(1, 4, T, 4, 2, 64).astype(f32)
    v_prompt = np.stack([R[b]["v_p"] for b in range(4)]).reshape(1, 4, T, 4, 128).astype(f32)
    hre = np.stack([R[b]["hre_p"] for b in range(4)]).reshape(1, 4, 32, 64).astype(f32)
    him = np.stack([R[b]["him_p"] for b in range(4)]).reshape(1, 4, 32, 64).astype(f32)
    conv_prompt = np.stack([R[b]["conv_p"] for b in range(4)]).reshape(1, 4, 2, F).astype(f32)
    mk = np.stack([R[b]["mk_p"] for b in range(4)]).reshape(1, 4, 256, 4, 256).astype(f32)
    mv = np.stack([R[b]["mv_p"] for b in range(4)]).reshape(1, 4, 256, 4, 256).astype(f32)
    y_sample = np.concatenate([R[c]["y_s"] for c in range(NCORES)]).reshape(128, 4, 1024).astype(f32)
    k_sample = np.concatenate([R[c]["k_s"] for c in range(NCORES)]).reshape(1, 128, 4, 4, 2, 64).astype(f32)
    v_sample = np.concatenate([R[c]["v_s"] for c in range(NCORES)]).reshape(1, 128, 4, 4, 128).astype(f32)
    hre_s = np.concatenate([R[c]["hre_s"] for c in range(NCORES)]).reshape(1, 128, 32, 64).astype(f32)
    him_s = np.concatenate([R[c]["him_s"] for c in range(NCORES)]).reshape(1, 128, 32, 64).astype(f32)
    conv_s = np.concatenate([R[c]["conv_s"] for c in range(NCORES)]).reshape(1, 128, 2, F).astype(f32)
    return (y_prompt, y_sample, k_prompt, v_prompt, k_sample, v_sample, hre, him, hre_s, him_s,
            conv_prompt, conv_s, mk, mv)
```

```python
import math
import os
PH = set(os.environ.get('KPH', 'W,M,A2,C,B,D,S,CS,DS').split(','))
import numpy as np
import ml_dtypes
import concourse.bass as bass
import concourse.mybir as mybir
from concourse.bass_utils import run_bass_kernel_spmd
from contextlib import ExitStack

F32 = mybir.dt.float32
BF16 = mybir.dt.bfloat16
I32 = mybir.dt.int32
U32 = mybir.dt.uint32
ALU = mybir.AluOpType
AF = mybir.ActivationFunctionType

ENGS = ("pe", "act", "dve", "pool", "sp")
N_DMA_SEMS = 24
EPS = 1e-6
NCORES = 8
T = 4096
NT = 512
NTILES = T // NT
L = 8
NCH = T // L
SC = 16
NSC = NCH // SC
F = 2816
NFC = F // 128
SLOPES = [2.0 ** (-8.0 * (h + 1) / 4) for h in range(4)]
LAM0 = 0.8 - 0.6 * math.exp(-0.3 * 0)
NS = 16
TS = 64
NPW = 25
PW_N = list(range(9)) + [8 * k for k in range(2, 17)] + [4]


class Buf:
    __slots__ = ("name", "last_w", "readers", "excl")

    def __init__(self, name):
        self.name = name
        self.excl = False
        self.last_w = None
        self.readers = []


class Sched:
    def __init__(self, nc, es):
        self.nc = nc
        self.ops = {e: [] for e in ENGS}
        self.count = {e: 0 for e in ENGS}
        self.sems = {e: es.enter_context(nc.semaphore("s_" + e)) for e in ENGS}
        self.dsems = [es.enter_context(nc.semaphore("d%d" % i)) for i in range(N_DMA_SEMS)]
        self.dcnt = [0] * N_DMA_SEMS
        self.drr = 0
        self.waited = {e: {} for e in ENGS}
        self.bufs = []

    def buf(self, name):
        b = Buf(name)
        self.bufs.append(b)
        return b

    def _collect(self, eng, reads, writes, is_dma):
        toks = []
        for b in reads:
            if b.last_w is not None:
                toks.append(b.last_w)
        for b in writes:
            if b.last_w is not None:
                toks.append(b.last_w)
            toks.extend(b.readers)
        need = {}
        for (k, v) in toks:
            if (not is_dma) and eng == "pe" and k == "pe":
                continue
            if need.get(k, -1) < v:
                need[k] = v
        waits = []
        w = self.waited[eng]
        for k, v in need.items():
            if w.get(k, -1) >= v:
                continue
            w[k] = v
            waits.append((k, v))
        return waits

    def _commit(self, tok, reads, writes):
        for b in reads:
            if b.excl:
                b.last_w = tok
                b.readers = []
            else:
                b.readers.append(tok)
        for b in writes:
            b.last_w = tok
            b.readers = []

    def op(self, eng, fn, reads=(), writes=()):
        waits = self._collect(eng, reads, writes, False)
        self.count[eng] += 1
        tok = (eng, self.count[eng])
        self.ops[eng].append((waits, fn, None))
        self._commit(tok, reads, writes)
        return tok

    def dmaf(self, eng, fn, reads=(), writes=()):
        waits = self._collect(eng, reads, writes, True)
        i = self.drr
        self.drr = (self.drr + 1) % N_DMA_SEMS
        k = ("d", i)
        prev = self.dcnt[i]
        w = self.waited[eng]
        if prev > 0 and w.get(k, -1) < prev:
            w[k] = prev
            waits.append((k, prev))
        self.dcnt[i] += 16
        tok = (k, self.dcnt[i])
        self.ops[eng].append((waits, fn, (i, 16)))
        self._commit(tok, reads, writes)
        return tok

    def dma(self, eng, out, in_, reads=(), writes=(), **kw):
        def fn(e, out=out, in_=in_, kw=kw):
            return e.dma_start(out=out, in_=in_, **kw)
        return self.dmaf(eng, fn, reads, writes)

    def barrier(self):
        targets = [(e, self.count[e]) for e in ENGS if self.count[e] > 0]
        targets += [(("d", i), c) for i, c in enumerate(self.dcnt) if c > 0]
        for e in ENGS:
            w = self.waited[e]
            waits = []
            for k, v in targets:
                if w.get(k, -1) < v:
                    w[k] = v
                    waits.append((k, v))
            if waits:
                self.ops[e].append((waits, None, None))
        for b in self.bufs:
            b.last_w = None
            b.readers = []

    def _sem(self, k):
        if isinstance(k, tuple):
            return self.dsems[k[1]]
        return self.sems[k]

    def emit(self):
        nc = self.nc
        with nc.Block() as block:
            def mk(ename):
                def body(e):
                    own = self.sems[ename]
                    for waits, fn, dinc in self.ops[ename]:
                        for (k, v) in waits:
                            e.wait_ge(self._sem(k), v)
                        if fn is None:
                            continue
                        ins = fn(e)
                        if dinc is not None:
                            ins.then_inc(self.dsems[dinc[0]], dinc[1])
                        else:
                            ins.then_inc(own, 1)
                return body
            block.tensor(mk("pe"))
            block.scalar(mk("act"))
            block.vector(mk("dve"))
            block.gpsimd(mk("pool"))
            block.sync(mk("sp"))


class TB:
    def __init__(self, t, b):
        self.t = t
        self.b = b

    def __getitem__(self, k):
        return self.t[k]


def build_nc():
    nc = bass.Bass("TRN2", target_bir_lowering=False)

    def din(name, shape, dt=F32):
        return nc.dram_tensor(name, list(shape), dt, kind="ExternalInput").ap()

    def dout(name, shape, dt=F32):
        return nc.dram_tensor(name, list(shape), dt, kind="ExternalOutput").ap()

    def dscr(name, shape, dt):
        return nc.dram_tensor(name, list(shape), dt, kind="Internal").ap()

    xp = din("xp", [T, 1024])
    memp = din("memp", [256, 1024])
    ln1_g = din("ln1_g", [1024]); ln2_g = din("ln2_g", [1024]); ln3_g = din("ln3_g", [1024])
    memn_g = din("mem_norm_g", [1024])
    w_in = din("w_in", [1024, 2048])
    a_re = din("ssm_a_re", [32, 64]); a_im = din("ssm_a_im", [32, 64])
    b_re = din("ssm_b_re", [32, 64, 16]); b_im = din("ssm_b_im", [32, 64, 16])
    c_re = din("ssm_c_re", [32, 16, 64]); c_im = din("ssm_c_im", [32, 16, 64])
    ssm_d = din("ssm_d", [32, 16]); log_dt = din("ssm_log_dt", [32])
    glu_w = din("ssm_glu_w", [512, 512])
    qn_g = din("q_norm_g", [64]); kn_g = din("k_norm_g", [64])
    lq1 = din("lam_q1", [64]); lk1 = din("lam_k1", [64]); lq2 = din("lam_q2", [64]); lk2 = din("lam_k2", [64])
    subln_g = din("subln_g", [128])
    w_out = din("w_out", [1024, 1024])
    ca_wq = din("ca_wq", [1024, 1024]); ca_wk = din("ca_wk", [1024, 1024]); ca_wv = din("ca_wv", [1024, 1024])
    caq_g = din("ca_q_norm_g", [256]); cak_g = din("ca_k_norm_g", [256])
    ca_wo = din("ca_wo", [1024, 1024])
    ffn_wg = din("ffn_wg", [1024, F]); ffn_wv = din("ffn_wv", [1024, F]); ffn_wd = din("ffn_wd", [F, 1024])
    conv_w = din("ffn_conv_w", [3, F]); conv_b = din("ffn_conv_b", [F])
    xs = din("xs", [TS, 1024])
    st_re = din("st_re", [NS, 2048]); st_im = din("st_im", [NS, 2048])
    st_conv = din("st_conv", [NS * 2, F])
    cmk = din("cmk", [NS, 256, 1024]); cmv = din("cmv", [NS, 256, 1024])
    cache_k = din("cache_k", [2560 * 128, 512]); cache_v = din("cache_v", [2560 * 128, 512])
    ptab = din("ptab", [NS, 16], I32)
    c_ident = din("c_ident", [128, 128])
    c_bd64 = din("c_bd64", [128, 128])
    c_mask16 = din("c_mask16", [128, 128])
    c_caus = din("c_caus", [128, 128])
    c_pwn = din("c_pwn", [NPW])
    c_kpos = din("c_kpos", [128])
    c_g2m = din("c_g2m", [128, 2])

    y_p = dout("y_p", [T, 1024]); k_p = dout("k_p", [T, 512]); v_p = dout("v_p", [T, 512])
    hre_p = dout("hre_p", [16, 128]); him_p = dout("him_p", [16, 128])
    conv_p = dout("conv_p", [2, F])
    mk_p = dout("mk_p", [256, 1024]); mv_p = dout("mv_p", [256, 1024])

    y_s = dout("y_s", [TS, 1024]); k_s = dout("k_s", [TS, 512]); v_s = dout("v_s", [TS, 512])
    hre_s = dout("hre_s", [NS, 2048]); him_s = dout("him_s", [NS, 2048])
    conv_s = dout("conv_s", [NS * 2, F])
    wg_s = dscr("wg_s", [128, NFC, 8, 128], BF16); wv_s = dscr("wv_s", [128, NFC, 8, 128], BF16)
    wd_s = dscr("wd_s", [128, 8, NFC, 128], BF16)
    mix_s = dscr("mix_s", [NTILES, 128, 8, NT], BF16)

    es = ExitStack()
    with es:
        S = Sched(nc, es)

        def sb(stack, name, shape, dt):
            return TB(stack.enter_context(nc.sbuf_tensor(name, list(shape), dt)), S.buf(name))

        PS = [TB(es.enter_context(nc.psum_tensor("ps%d" % i, [128, 512], F32)), S.buf("ps%d" % i)) for i in range(8)]
        for p_ in PS:
            p_.b.excl = True
        psrr = [0]

        def bank():
            b = PS[psrr[0]]
            psrr[0] = (psrr[0] + 1) % 8
            return b

        SLOW = dict(allow_slow_non_contiguous=True)

        def mm(bk, lhsT, rhs, start, stop, reads, out=None, **kw):
            o = bk[:] if out is None else out
            S.op("pe", lambda e: e.matmul(o, lhsT=lhsT, rhs=rhs, start=start, stop=stop, **kw),
                 reads=reads, writes=[bk.b])

        def tr(bk, out, in_, ident, reads):
            S.op("pe", lambda e: e.transpose(out=out, in_=in_, identity=ident[:]), reads=reads + [ident.b],
                 writes=[bk.b])

        def act(out, in_, func, reads, writes, **kw):
            S.op("act", lambda e: e.activation(out=out, in_=in_, func=func, **kw), reads=reads, writes=writes)

        def tt(eng, out, in0, in1, op, reads, writes):
            S.op(eng, lambda e: e.tensor_tensor(out=out, in0=in0, in1=in1, op=op), reads=reads, writes=writes)

        def ts(eng, out, in0, s1, s2, op0, op1, reads, writes):
            if op1 is None:
                S.op(eng, lambda e: e.tensor_scalar(out=out, in0=in0, scalar1=s1, scalar2=None, op0=op0),
                     reads=reads, writes=writes)
            else:
                S.op(eng, lambda e: e.tensor_scalar(out=out, in0=in0, scalar1=s1, scalar2=s2, op0=op0, op1=op1),
                     reads=reads, writes=writes)

        def stt(out, in0, scalar, in1, op0, op1, reads, writes):
            S.op("dve", lambda e: e.scalar_tensor_tensor(out=out, in0=in0, scalar=scalar, in1=in1, op0=op0, op1=op1),
                 reads=reads, writes=writes)

        def cp(eng, out, in_, reads, writes):
            if eng == "act":
                S.op("act", lambda e: e.copy(out=out, in_=in_), reads=reads, writes=writes)
            else:
                S.op(eng, lambda e: e.tensor_copy(out=out, in_=in_), reads=reads, writes=writes)

        def memset(eng, ap, val, writes):
            S.op(eng, lambda e: e.memset(ap, val), writes=writes)

        def recip(out, in_, reads, writes):
            S.op("dve", lambda e: e.reciprocal(out=out, in_=in_), reads=reads, writes=writes)

        IDF = sb(es, "IDF", [128, 128], F32); IDB = sb(es, "IDB", [128, 128], BF16)
        ON1024 = sb(es, "ON1024", [128, 128], BF16); ON256 = sb(es, "ON256", [128, 128], BF16)
        ON128 = sb(es, "ON128", [128, 128], BF16); ONE1 = sb(es, "ONE1", [128, 128], BF16)
        BD64 = sb(es, "BD64", [128, 128], BF16)
        CAUS = sb(es, "CAUS", [128, 128], BF16)
        G1 = sb(es, "G1", [128, 8], F32); G2 = sb(es, "G2", [128, 8], F32); G3 = sb(es, "G3", [128, 8], F32)
        GM = sb(es, "GM", [128, 8], F32)
        QG = sb(es, "QG", [128, 1], F32); KG = sb(es, "KG", [128, 1], F32)
        SUBG = sb(es, "SUBG", [128, 1], F32)
        CQG = sb(es, "CQG", [128, 2], F32); CKG = sb(es, "CKG", [128, 2], F32)
        CW = sb(es, "CW", [128, 3, NFC], F32); CB = sb(es, "CB", [128, NFC], F32)
        KPOS = sb(es, "KPOS", [128, 1], F32)
        NLAM = sb(es, "NLAM", [128, 1], F32)
        LT = sb(es, "LT", [64, 4], F32)

        S.dma("sp", IDF[:], c_ident, writes=[IDF.b])
        S.dma("pool", IDB[:], c_ident, writes=[IDB.b])
        S.dma("pool", BD64[:], c_bd64, writes=[BD64.b])
        S.dma("pool", CAUS[:], c_caus, writes=[CAUS.b])
        memset("dve", ON1024[:], 1.0 / 1024, [ON1024.b]); memset("dve", ON256[:], 1.0 / 256, [ON256.b])
        memset("dve", ON128[:], 1.0 / 128, [ON128.b]); memset("dve", ONE1[:], 1.0, [ONE1.b])
        for (Gt, gsrc) in ((G1, ln1_g), (G2, ln2_g), (G3, ln3_g), (GM, memn_g)):
            S.dma("sp", Gt[:], gsrc.rearrange("(c p) -> p c", p=128), writes=[Gt.b], **SLOW)
        for (Gt, gsrc) in ((QG, qn_g), (KG, kn_g)):
            for hh in range(2):
                S.dma("sp", Gt[hh * 64:(hh + 1) * 64, :], gsrc.rearrange("(p o) -> p o", o=1), writes=[Gt.b], **SLOW)
        S.dma("sp", SUBG[:], subln_g.rearrange("(p o) -> p o", o=1), writes=[SUBG.b], **SLOW)
        S.dma("sp", CQG[:], caq_g.rearrange("(c p) -> p c", p=128), writes=[CQG.b], **SLOW)
        S.dma("sp", CKG[:], cak_g.rearrange("(c p) -> p c", p=128), writes=[CKG.b], **SLOW)
        for j in range(3):
            S.dma("sp", CW[:, j, :], conv_w[j].rearrange("(c p) -> p c", p=128), writes=[CW.b], **SLOW)
        S.dma("sp", CB[:], conv_b.rearrange("(c p) -> p c", p=128), writes=[CB.b], **SLOW)
        S.dma("sp", KPOS[:], c_kpos.rearrange("(p o) -> p o", o=1), writes=[KPOS.b], **SLOW)
        for i, src in enumerate((lq1, lk1, lq2, lk2)):
            S.dma("sp", LT[:, i:i + 1], src.rearrange("(p o) -> p o", o=1), writes=[LT.b], **SLOW)
        ts("dve", SUBG[:], SUBG[:], 1.0 - LAM0, None, ALU.mult, None, [SUBG.b], [SUBG.b])
        LP = sb(es, "LP", [64, 2], BF16)
        tt("dve", LP[:, 0:1], LT[:, 0:1], LT[:, 1:2], ALU.mult, [LT.b], [LP.b])
        tt("dve", LP[:, 1:2], LT[:, 2:3], LT[:, 3:4], ALU.mult, [LT.b], [LP.b])
        bk = bank()
        mm(bk, ONE1[0:64, :], LP[:, :], True, True, [ONE1.b, LP.b], out=bk[:, 0:2])
        LE = sb(es, "LE", [128, 2], F32)
        act(LE[:], bk[:, 0:2], AF.Exp, [bk.b], [LE.b])
        stt(NLAM[:], LE[:, 1:2], -LAM0, LE[:, 0:1], ALU.add, ALU.subtract, [LE.b], [NLAM.b])

        def convert_ffn_weights(stack):
            CIN = sb(stack, "CIN", [128, 4096], F32)
            COUT = sb(stack, "COUT", [128, 4096], BF16)
            for (dst, src) in ((wg_s, ffn_wg), (wv_s, ffn_wv)):
                for f0 in range(0, NFC, 4):
                    nf = min(4, NFC - f0)
                    cin = CIN[:, 0:8 * nf * 128].rearrange("p (kc x) -> p kc x", kc=8)
                    S.dma("pool", cin, src[:, f0 * 128:(f0 + nf) * 128].rearrange("(kc p) x -> p kc x", p=128),
                          writes=[CIN.b])
                    cout = COUT[:, 0:nf * 1024].rearrange("p (fc kc n) -> p fc kc n", kc=8, n=128)
                    cp("pool", cout, cin.rearrange("p kc (fc n) -> p fc kc n", n=128), [CIN.b], [COUT.b])
                    S.dma("pool", dst[:, f0:f0 + nf].rearrange("p fc kc n -> p (fc kc n)"), COUT[:, 0:nf * 1024],
                          reads=[COUT.b])
            for m in range(8):
                for f0 in range(0, NFC, 8):
                    f1 = min(f0 + 8, NFC)
                    S.dma("pool", CIN[:, f0 * 128:f1 * 128].rearrange("p (fc n) -> p fc n", n=128),
                          ffn_wd[f0 * 128:f1 * 128, m * 128:(m + 1) * 128].rearrange("(fc p) n -> p fc n", p=128),
                          writes=[CIN.b])
                cp("pool", COUT[:, 0:NFC * 128], CIN[:, 0:NFC * 128], [CIN.b], [COUT.b])
                S.dma("pool", wd_s[:, m].rearrange("p fc n -> p (fc n)"), COUT[:, 0:NFC * 128], reads=[COUT.b])

        def load_norm_tile(X, xsrc_ap, nblk, G, xnT, rs, ss, junk, raw_xT=None):
            S.dma("sp", X[:, 0:nblk, :], xsrc_ap, writes=[X.b])
            for j in range(nblk):
                act(junk[:], X[:, j, :], AF.Square, [X.b], [junk.b, ss.b], accum_out=ss[:, j:j + 1])
            act(rs[:, 0:nblk], ss[:, 0:nblk], AF.Sqrt, [ss.b], [rs.b], scale=1.0 / 1024, bias=EPS)
            recip(rs[:, 0:nblk], rs[:, 0:nblk], [rs.b], [rs.b])
            if raw_xT is not None:
                for c in range(8):
                    bk = bank()
                    for j in range(nblk):
                        tr(bk, bk[:, j * 128:(j + 1) * 128], X[:, j, c * 128:(c + 1) * 128], IDF, [X.b])
                    cp("act", raw_xT[:, c, 0:nblk * 128], bk[:, 0:nblk * 128], [bk.b], [raw_xT.b])
            for j in range(nblk):
                ts("dve", X[:, j, :], X[:, j, :], rs[:, j:j + 1], None, ALU.mult, None, [X.b, rs.b], [X.b])
            for c in range(8):
                bk = bank()
                for j in range(nblk):
                    tr(bk, bk[:, j * 128:(j + 1) * 128], X[:, j, c * 128:(c + 1) * 128], IDF, [X.b])
                ts("dve", xnT[:, c, 0:nblk * 128], bk[:, 0:nblk * 128], G[:, c:c + 1], None, ALU.mult, None,
                   [bk.b, G.b], [xnT.b])

        def ln_fm(xT, G, xnT, n, sq, rstd):
            bk = bank()
            for c in range(8):
                act(sq[:, c % 2, 0:n], xT[:, c, 0:n], AF.Square, [xT.b], [sq.b])
                mm(bk, ON1024[:], sq[:, c % 2, 0:n], c == 0, c == 7, [ON1024.b, sq.b], out=bk[:, 0:n])
            act(rstd[:, 0:n], bk[:, 0:n], AF.Sqrt, [bk.b], [rstd.b], bias=EPS)
            recip(rstd[:, 0:n], rstd[:, 0:n], [rstd.b], [rstd.b])
            for c in range(8):
                stt(xnT[:, c, 0:n], xT[:, c, 0:n], G[:, c:c + 1], rstd[:, 0:n], ALU.mult, ALU.mult,
                    [xT.b, G.b, rstd.b], [xnT.b])

        def store_tm(src_fm, nchunks, n, dst_ap, stage, ident=IDF):
            nblk = (n + 127) // 128
            for j in range(nblk):
                w = min(128, n - j * 128)
                for c0 in range(0, nchunks, 4):
                    bk = bank()
                    for c in range(c0, min(c0 + 4, nchunks)):
                        tr(bk, bk[0:w, (c - c0) * 128:(c - c0 + 1) * 128], src_fm[:, c, j * 128:j * 128 + w], ident,
                           [src_fm.b])
                    nn = (min(c0 + 4, nchunks) - c0) * 128
                    cp("act", stage[0:w, j, c0 * 128:c0 * 128 + nn], bk[0:w, 0:nn], [bk.b], [stage.b])
            if n % 128 == 0:
                S.dma("sp", dst_ap.rearrange("(j p) f -> p j f", p=128), stage[:, 0:nblk, 0:nchunks * 128],
                      reads=[stage.b])
            else:
                S.dma("sp", dst_ap, stage[0:n, 0, 0:nchunks * 128], reads=[stage.b])


        xnTs = sb(es, "xnTs", [128, 8, TS], BF16)
        xsT = sb(es, "xsT", [128, 8, TS], F32)
        MIXs = sb(es, "MIXs", [128, 8, TS], BF16)
        usT = sb(es, "usT", [128, 4, TS], BF16)
        QTs = sb(es, "QTs", [128, 4, TS], BF16); KTs = sb(es, "KTs", [128, 4, TS], BF16)
        VN = sb(es, "VN", [4, NS, 4, 132], BF16)
        if 'S' in PH:
            with ExitStack() as ph:
                Xs_ = sb(ph, "Xs_", [128, 1024], F32)
                rs = sb(ph, "rs_s", [128, 1], F32); ss = sb(ph, "ss_s", [128, 1], F32)
                junk = sb(ph, "junk_s", [128, 1024], F32)
                S.dma("sp", Xs_[0:TS, :], xs, writes=[Xs_.b])
                act(junk[0:TS, :], Xs_[0:TS, :], AF.Square, [Xs_.b], [junk.b, ss.b], accum_out=ss[0:TS, 0:1])
                act(rs[0:TS, :], ss[0:TS, :], AF.Sqrt, [ss.b], [rs.b], scale=1.0 / 1024, bias=EPS)
                recip(rs[0:TS, :], rs[0:TS, :], [rs.b], [rs.b])
                for c in range(8):
                    bk = bank()
                    tr(bk, bk[:, 0:TS], Xs_[0:TS, c * 128:(c + 1) * 128], TB(IDF.t[0:TS, 0:TS], IDF.b), [Xs_.b])
                    cp("act", xsT[:, c, :], bk[:, 0:TS], [bk.b], [xsT.b])
                ts("dve", Xs_[0:TS, :], Xs_[0:TS, :], rs[0:TS, 0:1], None, ALU.mult, None, [Xs_.b, rs.b], [Xs_.b])
                for c in range(8):
                    bk = bank()
                    tr(bk, bk[:, 0:TS], Xs_[0:TS, c * 128:(c + 1) * 128], TB(IDF.t[0:TS, 0:TS], IDF.b), [Xs_.b])
                    ts("dve", xnTs[:, c, :], bk[:, 0:TS], G1[:, c:c + 1], None, ALU.mult, None, [bk.b, G1.b], [xnTs.b])
                S.barrier()

        MKT = sb(es, "MKT", [128, 8, 256], BF16)
        MV = sb(es, "MV", [128, 2, 1024], BF16)
        with ExitStack() as ph:
          if 'M' in PH:
            Xm = sb(ph, "Xm", [128, 2, 1024], F32)
            xnTm = sb(ph, "xnTm", [128, 8, 256], BF16)
            rs = sb(ph, "rs_m", [128, 4], F32); ss = sb(ph, "ss_m", [128, 4], F32)
            junk = sb(ph, "junk_m", [128, 1024], F32)
            Wk = sb(ph, "Wk", [128, 8, 1024], BF16); Wv = sb(ph, "Wv", [128, 8, 1024], BF16)
            mkraw = sb(ph, "mkraw", [128, 8, 256], F32)
            sqm = sb(ph, "sqm", [128, 2, 256], BF16); rstdm = sb(ph, "rstdm", [128, 256], F32)
            stg = sb(ph, "stg_m", [128, 2, 1024], F32)
            S.dma("pool", Wk[:], ca_wk.rearrange("(c p) n -> p c n", p=128), writes=[Wk.b])
            S.dma("pool", Wv[:], ca_wv.rearrange("(c p) n -> p c n", p=128), writes=[Wv.b])
            KS = int(os.environ.get('KSTOP', '9'))
            load_norm_tile(Xm, memp.rearrange("(j p) f -> p j f", p=128), 2, GM, xnTm, rs, ss, junk)
            for m in (range(8) if KS >= 2 else ()):
                bk = bank()
                for kc in range(8):
                    mm(bk, Wk[:, kc, m * 128:(m + 1) * 128], xnTm[:, kc, :], kc == 0, kc == 7, [Wk.b, xnTm.b],
                       out=bk[:, 0:256])
                cp("act", mkraw[:, m, :], bk[:, 0:256], [bk.b], [mkraw.b])
            for h in (range(4) if KS >= 3 else ()):
                bk = bank()
                for dh in range(2):
                    act(sqm[:, dh, :], mkraw[:, 2 * h + dh, :], AF.Square, [mkraw.b], [sqm.b])
                    mm(bk, ON256[:], sqm[:, dh, :], dh == 0, dh == 1, [ON256.b, sqm.b], out=bk[:, 0:256])
                act(rstdm[:], bk[:, 0:256], AF.Sqrt, [bk.b], [rstdm.b], bias=EPS)
                recip(rstdm[:], rstdm[:], [rstdm.b], [rstdm.b])
                for dh in range(2):
                    stt(mkraw[:, 2 * h + dh, :], mkraw[:, 2 * h + dh, :], CKG[:, dh:dh + 1], rstdm[:], ALU.mult,
                        ALU.mult, [mkraw.b, CKG.b, rstdm.b], [mkraw.b])
                    cp("act", MKT[:, 2 * h + dh, :], mkraw[:, 2 * h + dh, :], [mkraw.b], [MKT.b])
            if KS >= 4:
                store_tm(mkraw, 8, 256, mk_p, stg)
            for j in (range(2) if KS >= 5 else ()):
                for half in range(2):
                    bk = bank()
                    for kc in range(8):
                        mm(bk, xnTm[:, kc, j * 128:(j + 1) * 128], Wv[:, kc, half * 512:(half + 1) * 512], kc == 0,
                           kc == 7, [Wv.b, xnTm.b])
                    cp("act", stg[:, j, half * 512:(half + 1) * 512], bk[:], [bk.b], [stg.b])
                    cp("dve", MV[:, j, half * 512:(half + 1) * 512], bk[:], [bk.b], [MV.b])
            S.dma("sp", mv_p.rearrange("(j p) f -> p j f", p=128), stg[:], reads=[stg.b])
            S.barrier()


        s5 = ExitStack()
        if 'B' in PH:
            TWO_PI = 2.0 * math.pi
            PI_S = 3.1415925
            ZT = sb(s5, "ZT", [128, 4, 8, 2, 128], BF16)
            CLR = sb(s5, "CLR", [128, 16, 8, 32], BF16)
            CLI = sb(s5, "CLI", [128, 16, 8, 32], BF16)
            BDT = sb(s5, "BDT", [128, 4, 8, 128], BF16)
            LRE = sb(s5, "LRE", [128, 16, NPW], F32); LIM = sb(s5, "LIM", [128, 16, NPW], F32)
            with ExitStack() as tb:
                AR = sb(tb, "AR", [128, 16], F32); AI = sb(tb, "AI", [128, 16], F32)
                DT = sb(tb, "DT", [128, 16], F32)
                ARD = sb(tb, "ARD", [128, 16], F32); TH = sb(tb, "TH", [128, 16], F32)
                PWN = sb(tb, "PWN", [128, NPW], F32)
                ANG = sb(tb, "ANG", [128, 16, NPW], F32); R = sb(tb, "Rr", [128, 16, NPW], F32)
                KF = sb(tb, "KF", [128, 16, NPW], F32); KI = sb(tb, "KI", [128, 16, NPW], I32)
                MG = sb(tb, "MG", [128, 16, NPW], F32)
                S.dma("sp", AR[:], a_re.rearrange("(q g2) p -> (g2 p) q", g2=2), writes=[AR.b], **SLOW)
                S.dma("sp", AI[:], a_im.rearrange("(q g2) p -> (g2 p) q", g2=2), writes=[AI.b], **SLOW)
                for g2 in range(2):
                    S.dma("sp", DT[g2 * 64:(g2 + 1) * 64, :],
                          log_dt.rearrange("(q g2) -> g2 q", g2=2)[g2:g2 + 1, :].to_broadcast([64, 16]),
                          writes=[DT.b], **SLOW)
                S.dma("sp", PWN[:], c_pwn.rearrange("(o n) -> o n", o=1).to_broadcast([128, NPW]), writes=[PWN.b],
                      **SLOW)
                act(DT[:], DT[:], AF.Exp, [DT.b], [DT.b])
                tt("dve", ARD[:], AR[:], DT[:], ALU.mult, [AR.b, DT.b], [ARD.b])
                tt("dve", TH[:], AI[:], DT[:], ALU.mult, [AI.b, DT.b], [TH.b])
                bshape = [128, 16, NPW]
                tt("dve", MG[:], ARD[:, :].unsqueeze(2).to_broadcast(bshape), PWN[:, :].unsqueeze(1).to_broadcast(bshape),
                   ALU.mult, [ARD.b, PWN.b], [MG.b])
                act(MG[:], MG[:], AF.Exp, [MG.b], [MG.b])
                tt("dve", ANG[:], TH[:, :].unsqueeze(2).to_broadcast(bshape), PWN[:, :].unsqueeze(1).to_broadcast(bshape),
                   ALU.mult, [TH.b, PWN.b], [ANG.b])
                ts("dve", KF[:], ANG[:], 1.0 / TWO_PI, 0.5, ALU.mult, ALU.add, [ANG.b], [KF.b])
                cp("dve", KI[:], KF[:], [KF.b], [KI.b])
                cp("dve", KF[:], KI[:], [KI.b], [KF.b])
                stt(R[:], KF[:], -TWO_PI, ANG[:], ALU.mult, ALU.add, [KF.b, ANG.b], [R.b])

                def wrap(Rt):
                    ts("dve", KF[:], Rt[:], -math.pi, None, ALU.is_lt, None, [Rt.b], [KF.b])
                    stt(Rt[:], KF[:], TWO_PI, Rt[:], ALU.mult, ALU.add, [KF.b, Rt.b], [Rt.b])
                    ts("dve", KF[:], Rt[:], math.pi, None, ALU.is_gt, None, [Rt.b], [KF.b])
                    stt(Rt[:], KF[:], -TWO_PI, Rt[:], ALU.mult, ALU.add, [KF.b, Rt.b], [Rt.b])
                    ts("dve", Rt[:], Rt[:], -PI_S, PI_S, ALU.max, ALU.min, [Rt.b], [Rt.b])
                wrap(R)
                act(LIM[:], R[:], AF.Sin, [R.b], [LIM.b])
                ts("dve", R[:], R[:], math.pi / 2, None, ALU.add, None, [R.b], [R.b])
                wrap(R)
                act(LRE[:], R[:], AF.Sin, [R.b], [LRE.b])
                tt("dve", LRE[:], LRE[:], MG[:], ALU.mult, [LRE.b, MG.b], [LRE.b])
                tt("dve", LIM[:], LIM[:], MG[:], ALU.mult, [LIM.b, MG.b], [LIM.b])
                NRE = sb(tb, "NRE", [128, 16], F32); DEN = sb(tb, "DEN", [128, 16], F32)
                FRE = sb(tb, "FRE", [128, 16], F32); FIM = sb(tb, "FIM", [128, 16], F32)
                T1 = sb(tb, "T1", [128, 16], F32)
                ts("dve", NRE[:], LRE[:, :, 1], -1.0, None, ALU.add, None, [LRE.b], [NRE.b])
                tt("dve", DEN[:], AR[:], AR[:], ALU.mult, [AR.b], [DEN.b])
                tt("dve", T1[:], AI[:], AI[:], ALU.mult, [AI.b], [T1.b])
                tt("dve", DEN[:], DEN[:], T1[:], ALU.add, [DEN.b, T1.b], [DEN.b])
                recip(DEN[:], DEN[:], [DEN.b], [DEN.b])
                tt("dve", FRE[:], NRE[:], AR[:], ALU.mult, [NRE.b, AR.b], [FRE.b])
                tt("dve", T1[:], LIM[:, :, 1], AI[:], ALU.mult, [LIM.b, AI.b], [T1.b])
                tt("dve", FRE[:], FRE[:], T1[:], ALU.add, [FRE.b, T1.b], [FRE.b])
                tt("dve", FRE[:], FRE[:], DEN[:], ALU.mult, [FRE.b, DEN.b], [FRE.b])
                tt("dve", FIM[:], LIM[:, :, 1], AR[:], ALU.mult, [LIM.b, AR.b], [FIM.b])
                tt("dve", T1[:], NRE[:], AI[:], ALU.mult, [NRE.b, AI.b], [T1.b])
                tt("dve", FIM[:], FIM[:], T1[:], ALU.subtract, [FIM.b, T1.b], [FIM.b])
                tt("dve", FIM[:], FIM[:], DEN[:], ALU.mult, [FIM.b, DEN.b], [FIM.b])
                BR = sb(tb, "BR", [128, 16, 16], F32); BI = sb(tb, "BI", [128, 16, 16], F32)
                BBR = sb(tb, "BBR", [128, 16, 16], F32); BBI = sb(tb, "BBI", [128, 16, 16], F32)
                T2 = sb(tb, "T2", [128, 16, 16], F32)
                S.dma("sp", BR[:], b_re.rearrange("(q g2) p c -> (g2 p) q c", g2=2), writes=[BR.b], **SLOW)
                S.dma("sp", BI[:], b_im.rearrange("(q g2) p c -> (g2 p) q c", g2=2), writes=[BI.b], **SLOW)
                s3 = [128, 16, 16]
                fr = FRE[:, :].unsqueeze(2).to_broadcast(s3); fi = FIM[:, :].unsqueeze(2).to_broadcast(s3)
                tt("dve", BBR[:], BR[:], fr, ALU.mult, [BR.b, FRE.b], [BBR.b])
                tt("dve", T2[:], BI[:], fi, ALU.mult, [BI.b, FIM.b], [T2.b])
                tt("dve", BBR[:], BBR[:], T2[:], ALU.subtract, [BBR.b, T2.b], [BBR.b])
                tt("dve", BBI[:], BI[:], fr, ALU.mult, [BI.b, FRE.b], [BBI.b])
                tt("dve", T2[:], BR[:], fi, ALU.mult, [BR.b, FIM.b], [T2.b])
                tt("dve", BBI[:], BBI[:], T2[:], ALU.add, [BBI.b, T2.b], [BBI.b])
                ZR = sb(tb, "ZR", [128, 16, 8, 16], F32); ZI = sb(tb, "ZI", [128, 16, 8, 16], F32)
                T3 = sb(tb, "T3", [128, 16, 8, 16], F32)
                s4 = [128, 16, 8, 16]
                lr = LRE[:, :, 0:8].unsqueeze(3).to_broadcast(s4); li = LIM[:, :, 0:8].unsqueeze(3).to_broadcast(s4)
                br_ = BBR[:, :, :].unsqueeze(2).to_broadcast(s4); bi_ = BBI[:, :, :].unsqueeze(2).to_broadcast(s4)
                tt("dve", ZR[:], lr, br_, ALU.mult, [LRE.b, BBR.b], [ZR.b])
                tt("dve", T3[:], li, bi_, ALU.mult, [LIM.b, BBI.b], [T3.b])
                tt("dve", ZR[:], ZR[:], T3[:], ALU.subtract, [ZR.b, T3.b], [ZR.b])
                tt("dve", ZI[:], lr, bi_, ALU.mult, [LRE.b, BBI.b], [ZI.b])
                tt("dve", T3[:], li, br_, ALU.mult, [LIM.b, BBR.b], [T3.b])
                tt("dve", ZI[:], ZI[:], T3[:], ALU.add, [ZI.b, T3.b], [ZI.b])
                E4 = sb(tb, "E4", [128, 4, 8, 2, 128], F32)
                memset("pool", E4[:], 0.0, [E4.b])
                for ri, Zt in enumerate((ZR, ZI)):
                    for gc in range(4):
                        for g2 in range(2):
                            pr = slice(g2 * 64, (g2 + 1) * 64)
                            dst = E4[pr, gc, :, ri, :].rearrange("p t (j x) -> p t j x", x=32)[:, :, :, g2 * 16:(g2 + 1) * 16]
                            src = Zt[pr, gc * 4:(gc + 1) * 4, :, :].rearrange("p j t c -> p t j c")
                            cp("pool", dst, src, [Zt.b], [E4.b])
                for gc in range(4):
                    for ri in range(2):
                        for s0 in (0, 4):
                            bk = bank()
                            for sp_ in range(s0, s0 + 4):
                                tr(bk, bk[:, (sp_ - s0) * 128:(sp_ - s0 + 1) * 128], E4[:, gc, 7 - sp_, ri, :], IDF, [E4.b])
                            cp("act", ZT[:, gc, s0:s0 + 4, ri, :], bk[:].rearrange("p (s n) -> p s n", n=128), [bk.b],
                               [ZT.b])
                CN = sb(tb, "CN", [128, 2, 4, 64], F32)
                CE = sb(tb, "CE", [128, 2, 4, 2, 64], F32)
                CTR = sb(tb, "CTR", [128, 4, 128], F32); CTI = sb(tb, "CTI", [128, 4, 128], F32)
                CTIN = sb(tb, "CTIN", [128, 4, 128], F32)
                G2M = sb(tb, "G2M", [128, 2], F32)
                S.dma("sp", G2M[:], c_g2m, writes=[G2M.b])
                S.dma("sp", CN[:, 0, :, :], c_re.rearrange("(gc r) c p -> (r c) gc p", gc=4), writes=[CN.b], **SLOW)
                S.dma("sp", CN[:, 1, :, :], c_im.rearrange("(gc r) c p -> (r c) gc p", gc=4), writes=[CN.b], **SLOW)
                for ri in range(2):
                    for g2 in range(2):
                        ts("dve", CE[:, ri, :, g2, :], CN[:, ri, :, :], G2M[:, g2:g2 + 1], None, ALU.mult, None,
                           [CN.b, G2M.b], [CE.b])
                for ri, CTt in enumerate((CTR, CTI)):
                    bk = bank()
                    for gc in range(4):
                        tr(bk, bk[:, gc * 128:(gc + 1) * 128], CE[:, ri, gc, :, :].rearrange("p a b -> p (a b)"), IDF,
                           [CE.b])
                    cp("act", CTt[:], bk[:].rearrange("p (g n) -> p g n", n=128), [bk.b], [CTt.b])
                ts("dve", CTIN[:], CTI[:], -1.0, None, ALU.mult, None, [CTI.b], [CTIN.b])
                s5s = [128, 16, 8, 32]
                T4 = sb(tb, "T4", [128, 16, 8, 32], F32); T5 = sb(tb, "T5", [128, 16, 8, 32], F32)
                ctr = CTR[:, :, :].rearrange("p g (j x) -> p (g j) x", x=32).unsqueeze(2).to_broadcast(s5s)
                cti = CTI[:, :, :].rearrange("p g (j x) -> p (g j) x", x=32).unsqueeze(2).to_broadcast(s5s)
                l1r = LRE[:, :, 1:9].unsqueeze(3).to_broadcast(s5s); l1i = LIM[:, :, 1:9].unsqueeze(3).to_broadcast(s5s)
                tt("dve", T4[:], ctr, l1r, ALU.mult, [CTR.b, LRE.b], [T4.b])
                tt("dve", T5[:], cti, l1i, ALU.mult, [CTI.b, LIM.b], [T5.b])
                tt("dve", CLR[:], T4[:], T5[:], ALU.subtract, [T4.b, T5.b], [CLR.b])
                tt("dve", T4[:], ctr, l1i, ALU.mult, [CTR.b, LIM.b], [T4.b])
                tt("dve", T5[:], cti, l1r, ALU.mult, [CTI.b, LRE.b], [T5.b])
                tt("dve", T4[:], T4[:], T5[:], ALU.add, [T4.b, T5.b], [T4.b])
                ts("dve", CLI[:], T4[:], -1.0, None, ALU.mult, None, [T4.b], [CLI.b])
                MASK16 = sb(tb, "MASK16", [128, 128], F32); DCOL = sb(tb, "DCOL", [128, 4], F32)
                T6 = sb(tb, "T6", [128, 128], F32)
                S.dma("sp", MASK16[:], c_mask16, writes=[MASK16.b])
                S.dma("sp", DCOL[:], ssm_d.rearrange("(gc g8) c -> (g8 c) gc", gc=4), writes=[DCOL.b], **SLOW)
                for gc in range(4):
                    for tau in range(8):
                        bk = bank()
                        mm(bk, E4[:, gc, tau, 0, :], CTR[:, gc, :], True, False, [E4.b, CTR.b], out=bk[:, 0:128])
                        mm(bk, E4[:, gc, tau, 1, :], CTIN[:, gc, :], False, True, [E4.b, CTIN.b], out=bk[:, 0:128])
                        if tau == 0:
                            tt("dve", T6[:], bk[:, 0:128], MASK16[:], ALU.mult, [bk.b, MASK16.b], [T6.b])
                            stt(BDT[:, gc, 0, :], IDF[:], DCOL[:, gc:gc + 1], T6[:], ALU.mult, ALU.add,
                                [IDF.b, DCOL.b, T6.b], [BDT.b])
                        else:
                            tt("dve", BDT[:, gc, tau, :], bk[:, 0:128], MASK16[:], ALU.mult, [bk.b, MASK16.b], [BDT.b])
                S.barrier()

            pb = ExitStack()
            uT = sb(pb, "uT", [128, 4, T], BF16)
            with ExitStack() as ph:
                X = sb(ph, "Xu", [128, 4, 1024], F32)
                xnT = sb(ph, "xnTu", [128, 8, NT], BF16)
                rs = sb(ph, "rsu", [128, 4], F32); ss = sb(ph, "ssu", [128, 4], F32)
                junk = sb(ph, "junku", [128, 1024], F32)
                Wu = sb(ph, "Wu", [128, 8, 512], BF16)
                S.dma("pool", Wu[:], w_in[:, 0:512].rearrange("(c p) n -> p c n", p=128), writes=[Wu.b])
                for it in range(NTILES):
                    t0 = it * NT
                    load_norm_tile(X, xp[t0:t0 + NT, :].rearrange("(j p) f -> p j f", p=128), 4, G1, xnT, rs, ss, junk)
                    for m in range(4):
                        bk = bank()
                        for kc in range(8):
                            mm(bk, Wu[:, kc, m * 128:(m + 1) * 128], xnT[:, kc, :], kc == 0, kc == 7, [Wu.b, xnT.b])
                        cp("act", uT[:, m, t0:t0 + NT], bk[:], [bk.b], [uT.b])
                if 'S' in PH:
                    for m in range(4):
                        bk = bank()
                        for kc in range(8):
                            mm(bk, Wu[:, kc, m * 128:(m + 1) * 128], xnTs[:, kc, :], kc == 0, kc == 7, [Wu.b, xnTs.b],
                               out=bk[:, 0:TS])
                        cp("act", usT[:, m, :], bk[:, 0:TS], [bk.b], [usT.b])
                S.barrier()
            if 'S' in PH:
              with ExitStack() as ph:
                STt = sb(ph, "STt", [NS, 2, 2048], F32)
                S0 = sb(ph, "S0", [128, 2, 16, NS], F32); S0b = sb(ph, "S0b", [128, 2, 16, NS], BF16)
                SN = sb(ph, "SN", [128, 2, 16, NS], F32)
                W1 = sb(ph, "W1", [128, 16, NS], F32); W2 = sb(ph, "W2", [128, 16, NS], F32)
                hst = sb(ph, "hst", [NS, 2, 2048], F32)
                Y2s = sb(ph, "Y2s", [128, TS], F32); Y3s = sb(ph, "Y3s", [128, TS], F32)
                Gts = sb(ph, "Gts", [128, 4, TS], F32); Gbs = sb(ph, "Gbs", [128, 4, TS], BF16)
                GLUWs = sb(ph, "GLUWs", [128, 4, 512], BF16)
                S.dma("pool", GLUWs[:], glu_w.rearrange("(c p) n -> p c n", p=128), writes=[GLUWs.b])
                S.dma("sp", STt[:, 0, :], st_re, writes=[STt.b])
                S.dma("sp", STt[:, 1, :], st_im, writes=[STt.b])
                ID16 = TB(IDF.t[0:NS, 0:NS], IDF.b)
                for ri in range(2):
                    bk = bank()
                    for q16 in range(16):
                        tr(bk, bk[:, q16 * NS:(q16 + 1) * NS], STt[:, ri, q16 * 128:(q16 + 1) * 128], ID16, [STt.b])
                    cp("act", S0[:, ri, :, :], bk[:, 0:16 * NS].rearrange("p (q s) -> p q s", s=NS), [bk.b], [S0.b])
                    cp("dve", S0b[:, ri, :, :], S0[:, ri, :, :], [S0.b], [S0b.b])
                for q16 in range(16):
                    gc, j = q16 // 4, q16 % 4
                    pr = slice(32 * j, 32 * j + 32)
                    uv = usT[:, gc, :].rearrange("p (n l) -> p l n", l=4)
                    bk = bank()
                    for ri in range(2):
                        for sp_ in range(4):
                            mm(bk, ZT[pr, gc, 4 + sp_, ri, :], uv[pr, sp_, :], sp_ == 0, sp_ == 3, [ZT.b, usT.b],
                               out=bk[:, ri * NS:(ri + 1) * NS], tile_position=(32 * j, 0))
                    cp("act", SN[:, :, q16, :], bk[:, 0:2 * NS].rearrange("p (r s) -> p r s", s=NS), [bk.b], [SN.b])
                s3 = [128, 16, NS]
                l4r = LRE[:, :, 4:5].to_broadcast(s3); l4i = LIM[:, :, 4:5].to_broadcast(s3)
                tt("dve", W1[:], S0[:, 0, :, :], l4r, ALU.mult, [S0.b, LRE.b], [W1.b])
                tt("dve", W2[:], S0[:, 1, :, :], l4i, ALU.mult, [S0.b, LIM.b], [W2.b])
                tt("dve", W1[:], W1[:], W2[:], ALU.subtract, [W1.b, W2.b], [W1.b])
                tt("dve", SN[:, 0, :, :], SN[:, 0, :, :], W1[:], ALU.add, [SN.b, W1.b], [SN.b])
                tt("dve", W1[:], S0[:, 0, :, :], l4i, ALU.mult, [S0.b, LIM.b], [W1.b])
                tt("dve", W2[:], S0[:, 1, :, :], l4r, ALU.mult, [S0.b, LRE.b], [W2.b])
                tt("dve", W1[:], W1[:], W2[:], ALU.add, [W1.b, W2.b], [W1.b])
                tt("dve", SN[:, 1, :, :], SN[:, 1, :, :], W1[:], ALU.add, [SN.b, W1.b], [SN.b])
                for ri in range(2):
                    for q0 in range(0, 16, 4):
                        bk = bank()
                        for q16 in range(q0, q0 + 4):
                            tr(bk, bk[0:NS, (q16 - q0) * 128:(q16 - q0 + 1) * 128], SN[:, ri, q16, :], IDF, [SN.b])
                        cp("act", hst[:, ri, q0 * 128:(q0 + 4) * 128], bk[0:NS, :], [bk.b], [hst.b])
                S.dma("sp", hre_s, hst[:, 0, :], reads=[hst.b])
                S.dma("sp", him_s, hst[:, 1, :], reads=[hst.b])
                for gc in range(4):
                    bk = bank()
                    uv = usT[:, gc, :].rearrange("p (n l) -> p l n", l=4)
                    ov = bk[:, 0:TS].rearrange("p (n l) -> p l n", l=4)
                    for tp in range(4):
                        for sp_ in range(tp + 1):
                            mm(bk, BDT[:, gc, tp - sp_, :], uv[:, sp_, :], sp_ == 0, False, [BDT.b, usT.b],
                               out=ov[:, tp, :])
                        for j in range(4):
                            q16 = gc * 4 + j
                            for ri, CLt in enumerate((CLR, CLI)):
                                mm(bk, CLt[:, q16, tp, :], S0b[:, ri, q16, :], False, (j == 3 and ri == 1),
                                   [CLt.b, S0b.b], out=ov[32 * j:32 * j + 32, tp, :], tile_position=(0, 32 * j))
                    act(Y2s[:], bk[:, 0:TS], AF.Square, [bk.b], [Y2s.b])
                    ts("dve", Y2s[:], Y2s[:], 0.044715, 1.0, ALU.mult, ALU.add, [Y2s.b], [Y2s.b])
                    tt("dve", Y3s[:], Y2s[:], bk[:, 0:TS], ALU.mult, [Y2s.b, bk.b], [Y3s.b])
                    act(Y3s[:], Y3s[:], AF.Sigmoid, [Y3s.b], [Y3s.b], scale=2.0 * math.sqrt(2.0 / math.pi))
                    tt("dve", Gts[:, gc, :], Y3s[:], bk[:, 0:TS], ALU.mult, [Y3s.b, bk.b], [Gts.b])
                    cp("act", Gbs[:, gc, :], Gts[:, gc, :], [Gts.b], [Gbs.b])
                for m in range(4):
                    bk = bank()
                    for kc in range(4):
                        mm(bk, GLUWs[:, kc, m * 128:(m + 1) * 128], Gbs[:, kc, :], kc == 0, kc == 3, [GLUWs.b, Gbs.b],
                           out=bk[:, 0:TS])
                    act(Y3s[:], bk[:, 0:TS], AF.Sigmoid, [bk.b], [Y3s.b])
                    tt("dve", MIXs[:, m, :], Y3s[:], Gts[:, m, :], ALU.mult, [Y3s.b, Gts.b], [MIXs.b])
                S.barrier()
            SW = sb(pb, "SW", [128, 2, 8, NCH], F32)
            SP = sb(pb, "SPv", [128, 2, 16, NCH], BF16)
            EE = sb(pb, "EE", [128, 2, 16, NSC], F32)
            TA = sb(pb, "TA", [128, 8, NSC], F32); TBt = sb(pb, "TBt", [128, 8, NSC], F32)
            U1 = sb(pb, "U1", [128, 8], F32); U2 = sb(pb, "U2", [128, 8], F32)
            V1 = sb(pb, "V1", [128, 8, SC], F32); V2 = sb(pb, "V2", [128, 8, SC], F32)
            memset("dve", SP[:, :, :, 0:1], 0.0, [SP.b])
            for hq in range(2):
                qs = slice(8 * hq, 8 * hq + 8)
                for q8 in range(8):
                    q16 = 8 * hq + q8
                    gc, j = q16 // 4, q16 % 4
                    pr = slice(32 * j, 32 * j + 32)
                    uv = uT[:, gc, :].rearrange("p (n l) -> p l n", l=L)
                    for ri in range(2):
                        bk = bank()
                        for sp_ in range(L):
                            mm(bk, ZT[pr, gc, sp_, ri, :], uv[pr, sp_, :], sp_ == 0, sp_ == L - 1, [ZT.b, uT.b],
                               tile_position=(32 * j, 0))
                        cp("act" if ri == 0 else "dve", SW[:, ri, q8, :], bk[:], [bk.b], [SW.b])
                Sv = SW[:, :, :, :].rearrange("p r q (s i) -> p r q s i", i=SC)
                sA = [128, 8, NSC]
                a8r = LRE[:, qs, 8:9].to_broadcast(sA); a8i = LIM[:, qs, 8:9].to_broadcast(sA)
                for i in range(1, SC):
                    pre_r = Sv[:, 0, :, :, i - 1]; pre_i = Sv[:, 1, :, :, i - 1]
                    tt("dve", TA[:], pre_r, a8r, ALU.mult, [SW.b, LRE.b], [TA.b])
                    tt("dve", TBt[:], pre_i, a8i, ALU.mult, [SW.b, LIM.b], [TBt.b])
                    tt("dve", TA[:], TA[:], TBt[:], ALU.subtract, [TA.b, TBt.b], [TA.b])
                    tt("dve", TBt[:], pre_i, a8r, ALU.mult, [SW.b, LRE.b], [TBt.b])
                    tt("dve", Sv[:, 0, :, :, i], Sv[:, 0, :, :, i], TA[:], ALU.add, [SW.b, TA.b], [SW.b])
                    tt("dve", TA[:], pre_r, a8i, ALU.mult, [SW.b, LIM.b], [TA.b])
                    tt("dve", TA[:], TA[:], TBt[:], ALU.add, [TA.b, TBt.b], [TA.b])
                    tt("dve", Sv[:, 1, :, :, i], Sv[:, 1, :, :, i], TA[:], ALU.add, [SW.b, TA.b], [SW.b])
                cp("dve", EE[:, :, qs, 0], Sv[:, :, :, 0, SC - 1], [SW.b], [EE.b])
                for sc in range(1, NSC):
                    er = EE[:, 0, qs, sc - 1]; ei = EE[:, 1, qs, sc - 1]
                    tt("dve", U1[:], er, LRE[:, qs, 23], ALU.mult, [EE.b, LRE.b], [U1.b])
                    tt("dve", U2[:], ei, LIM[:, qs, 23], ALU.mult, [EE.b, LIM.b], [U2.b])
                    tt("dve", U1[:], U1[:], U2[:], ALU.subtract, [U1.b, U2.b], [U1.b])
                    tt("dve", EE[:, 0, qs, sc], U1[:], Sv[:, 0, :, sc, SC - 1], ALU.add, [U1.b, SW.b], [EE.b])
                    tt("dve", U1[:], er, LIM[:, qs, 23], ALU.mult, [EE.b, LIM.b], [U1.b])
                    tt("dve", U2[:], ei, LRE[:, qs, 23], ALU.mult, [EE.b, LRE.b], [U2.b])
                    tt("dve", U1[:], U1[:], U2[:], ALU.add, [U1.b, U2.b], [U1.b])
                    tt("dve", EE[:, 1, qs, sc], U1[:], Sv[:, 1, :, sc, SC - 1], ALU.add, [U1.b, SW.b], [EE.b])
                cp("act", SP[:, 0, qs, 1:SC], SW[:, 0, :, 0:SC - 1], [SW.b], [SP.b])
                cp("act", SP[:, 1, qs, 1:SC], SW[:, 1, :, 0:SC - 1], [SW.b], [SP.b])
                sC = [128, 8, SC]
                PR = LRE[:, qs, 8:24]; PI = LIM[:, qs, 8:24]
                for sc in range(1, NSC):
                    er = EE[:, 0, qs, sc - 1:sc].to_broadcast(sC); ei = EE[:, 1, qs, sc - 1:sc].to_broadcast(sC)
                    tt("dve", V1[:], PR, er, ALU.mult, [LRE.b, EE.b], [V1.b])
                    tt("dve", V2[:], PI, ei, ALU.mult, [LIM.b, EE.b], [V2.b])
                    tt("dve", V1[:], V1[:], V2[:], ALU.subtract, [V1.b, V2.b], [V1.b])
                    tt("dve", V1[:], V1[:], Sv[:, 0, :, sc, :], ALU.add, [V1.b, SW.b], [V1.b])
                    tt("dve", V2[:], PR, ei, ALU.mult, [LRE.b, EE.b], [V2.b])
                    n_ = SC if sc < NSC - 1 else SC - 1
                    cp("act", SP[:, 0, qs, sc * SC + 1:sc * SC + 1 + n_], V1[:, :, 0:n_], [V1.b], [SP.b])
                    tt("dve", V1[:], PI, er, ALU.mult, [LIM.b, EE.b], [V1.b])
                    tt("dve", V2[:], V2[:], V1[:], ALU.add, [V1.b, V2.b], [V2.b])
                    tt("dve", V2[:], V2[:], Sv[:, 1, :, sc, :], ALU.add, [V2.b, SW.b], [V2.b])
                    cp("act", SP[:, 1, qs, sc * SC + 1:sc * SC + 1 + n_], V2[:, :, 0:n_], [V2.b], [SP.b])
                for sc in range(1, NSC):
                    cp("act", SP[:, :, qs, sc * SC], EE[:, :, qs, sc - 1], [EE.b], [SP.b])
            S.dma("sp", hre_p.rearrange("q x -> x q"), EE[:, 0, :, NSC - 1], reads=[EE.b], **SLOW)
            S.dma("sp", him_p.rearrange("q x -> x q"), EE[:, 1, :, NSC - 1], reads=[EE.b], **SLOW)
            GLUW = sb(pb, "GLUW", [128, 4, 512], BF16)
            S.dma("pool", GLUW[:], glu_w.rearrange("(c p) n -> p c n", p=128), writes=[GLUW.b])
            Y2 = sb(pb, "Y2", [128, NT], F32); Y3 = sb(pb, "Y3", [128, NT], F32)
            Gt = sb(pb, "Gt", [128, 4, NT], F32); Gb = sb(pb, "Gb", [128, 4, NT], BF16)
            MS = [sb(pb, "MS%d" % i, [128, 4, NT], BF16) for i in range(2)]
            CG = math.sqrt(2.0 / math.pi)
            NCT = NT // L
            for it in range(NTILES):
                t0 = it * NT
                c0 = it * NCT
                for gc in range(4):
                    bk = bank()
                    uv = uT[:, gc, t0:t0 + NT].rearrange("p (n l) -> p l n", l=L)
                    ov = bk[:].rearrange("p (n l) -> p l n", l=L)
                    for tp in range(L):
                        for sp_ in range(tp + 1):
                            mm(bk, BDT[:, gc, tp - sp_, :], uv[:, sp_, :], sp_ == 0, False, [BDT.b, uT.b],
                               out=ov[:, tp, :])
                        for j in range(4):
                            q16 = gc * 4 + j
                            for ri, CLt in enumerate((CLR, CLI)):
                                mm(bk, CLt[:, q16, tp, :], SP[:, ri, q16, c0:c0 + NCT], False,
                                   (j == 3 and ri == 1), [CLt.b, SP.b], out=ov[32 * j:32 * j + 32, tp, :],
                                   tile_position=(0, 32 * j))
                    act(Y2[:], bk[:], AF.Square, [bk.b], [Y2.b])
                    ts("dve", Y2[:], Y2[:], 0.044715, 1.0, ALU.mult, ALU.add, [Y2.b], [Y2.b])
                    tt("dve", Y3[:], Y2[:], bk[:], ALU.mult, [Y2.b, bk.b], [Y3.b])
                    act(Y3[:], Y3[:], AF.Sigmoid, [Y3.b], [Y3.b], scale=2.0 * CG)
                    tt("dve", Gt[:, gc, :], Y3[:], bk[:], ALU.mult, [Y3.b, bk.b], [Gt.b])
                    cp("act", Gb[:, gc, :], Gt[:, gc, :], [Gt.b], [Gb.b])
                M_ = MS[it % 2]
                for m in range(4):
                    bk = bank()
                    for kc in range(4):
                        mm(bk, GLUW[:, kc, m * 128:(m + 1) * 128], Gb[:, kc, :], kc == 0, kc == 3, [GLUW.b, Gb.b])
                    act(Y3[:], bk[:], AF.Sigmoid, [bk.b], [Y3.b])
                    tt("dve", M_[:, m, :], Y3[:], Gt[:, m, :], ALU.mult, [Y3.b, Gt.b], [M_.b])
                S.dma("sp", mix_s[it, :, 0:4, :], M_[:], reads=[M_.b])
            S.barrier()
            pb.close()
        s5.close()

        att = ExitStack()
        QT = sb(att, "QT", [128, 4, T], BF16)
        KT = sb(att, "KT", [128, 4, T], BF16)
        V = sb(att, "V", [128, T // 128, 512], BF16)
        with ExitStack() as ph:
          if 'A2' in PH:
            X = sb(ph, "X", [128, 4, 1024], F32)
            xnT = sb(ph, "xnT", [128, 8, NT], BF16)
            rs = sb(ph, "rs", [128, 4], F32); ss = sb(ph, "ss", [128, 4], F32)
            junk = sb(ph, "junk", [128, 1024], F32)
            W = sb(ph, "Wqkv", [128, 8, 1536], BF16)
            sq = sb(ph, "sq", [128, NT], BF16); rstd = sb(ph, "rstd", [128, NT], F32)
            knT = sb(ph, "knT", [128, 4, NT], F32)
            stg = sb(ph, "stg", [128, 4, 512], F32)
            S.dma("pool", W[:], w_in[:, 512:2048].rearrange("(c p) n -> p c n", p=128), writes=[W.b])
            for it in range(NTILES):
                t0 = it * NT
                load_norm_tile(X, xp[t0:t0 + NT, :].rearrange("(j p) f -> p j f", p=128), 4, G1, xnT, rs, ss, junk)
                for qk in range(2):
                    for m in range(4):
                        bk = bank()
                        for kc in range(8):
                            mm(bk, W[:, kc, qk * 512 + m * 128:qk * 512 + (m + 1) * 128], xnT[:, kc, :], kc == 0,
                               kc == 7, [W.b, xnT.b])
                        act(sq[:], bk[:], AF.Square, [bk.b], [sq.b])
                        b2 = bank()
                        mm(b2, BD64[:], sq[:], True, True, [BD64.b, sq.b])
                        act(rstd[:], b2[:], AF.Sqrt, [b2.b], [rstd.b], bias=EPS)
                        recip(rstd[:], rstd[:], [rstd.b], [rstd.b])
                        if qk == 0:
                            stt(QT[:, m, t0:t0 + NT], bk[:], QG[:, 0:1], rstd[:], ALU.mult, ALU.mult,
                                [bk.b, QG.b, rstd.b], [QT.b])
                        else:
                            stt(knT[:, m, :], bk[:], KG[:, 0:1], rstd[:], ALU.mult, ALU.mult,
                                [bk.b, KG.b, rstd.b], [knT.b])
                            cp("act", KT[:, m, t0:t0 + NT], knT[:, m, :], [knT.b], [KT.b])
                store_tm(knT, 4, NT, k_p[t0:t0 + NT, :], stg)
                for j in range(4):
                    bk = bank()
                    for kc in range(8):
                        mm(bk, xnT[:, kc, j * 128:(j + 1) * 128], W[:, kc, 1024:1536], kc == 0, kc == 7,
                           [W.b, xnT.b])
                    cp("act", stg[:, j, :], bk[:], [bk.b], [stg.b])
                    cp("dve", V[:, it * 4 + j, :], bk[:], [bk.b], [V.b])
                S.dma("sp", v_p[t0:t0 + NT, :].rearrange("(j p) f -> p j f", p=128), stg[:], reads=[stg.b])
            if 'S' in PH:
                for qk in range(2):
                    for m in range(4):
                        bk = bank()
                        for kc in range(8):
                            mm(bk, W[:, kc, qk * 512 + m * 128:qk * 512 + (m + 1) * 128], xnTs[:, kc, :], kc == 0,
                               kc == 7, [W.b, xnTs.b], out=bk[:, 0:TS])
                        act(sq[:, 0:TS], bk[:, 0:TS], AF.Square, [bk.b], [sq.b])
                        b2 = bank()
                        mm(b2, BD64[:], sq[:, 0:TS], True, True, [BD64.b, sq.b], out=b2[:, 0:TS])
                        act(rstd[:, 0:TS], b2[:, 0:TS], AF.Sqrt, [b2.b], [rstd.b], bias=EPS)
                        recip(rstd[:, 0:TS], rstd[:, 0:TS], [rstd.b], [rstd.b])
                        if qk == 0:
                            stt(QTs[:, m, :], bk[:, 0:TS], QG[:, 0:1], rstd[:, 0:TS], ALU.mult, ALU.mult,
                                [bk.b, QG.b, rstd.b], [QTs.b])
                        else:
                            stt(knT[:, m, 0:TS], bk[:, 0:TS], KG[:, 0:1], rstd[:, 0:TS], ALU.mult, ALU.mult,
                                [bk.b, KG.b, rstd.b], [knT.b])
                            cp("act", KTs[:, m, :], knT[:, m, 0:TS], [knT.b], [KTs.b])
                store_tm(knT, 4, TS, k_s, stg)
                memset("dve", VN[:], 1.0, [VN.b])
                for sq_ in range(NS):
                    bk = bank()
                    for kc in range(8):
                        mm(bk, xnTs[:, kc, sq_ * 4:(sq_ + 1) * 4], W[:, kc, 1024:1536], kc == 0, kc == 7,
                           [W.b, xnTs.b], out=bk[0:4, :])
                    cp("act", stg[0:4, sq_ % 4, :], bk[0:4, :], [bk.b], [stg.b])
                    cp("dve", VN[:, sq_, :, 0:128], bk[0:4, :].rearrange("p (h d) -> p h d", d=128), [bk.b], [VN.b])
                    if sq_ % 4 == 3:
                        sg = sq_ // 4
                        S.dma("sp", v_s.rearrange("(s t) f -> t s f", t=4)[:, sg * 4:(sg + 1) * 4, :], stg[0:4, :, :],
                              reads=[stg.b])
            S.barrier()

        with ExitStack() as ph:
          if 'W' in PH and 'C' not in PH:
            convert_ffn_weights(ph)
            S.barrier()
          if 'C' in PH:
            if 'W' in PH:
                convert_ffn_weights(ph)
            PT = [sb(ph, "PT%d" % i, [128, NT], BF16) for i in range(6)]
            BIAS = sb(ph, "BIAS", [128, 4, 34], F32)
            for h in range(4):
                for bi in range(1, 34):
                    ts("dve", BIAS[:, h, bi:bi + 1], KPOS[:, 0:1], float(1 - 128 * bi), SLOPES[h], ALU.add, ALU.mult,
                       [KPOS.b], [BIAS.b])
            On = [sb(ph, "On%d" % i, [128, NT], F32) for i in range(2)]
            rden = [sb(ph, "rden%d" % i, [128, NT], F32) for i in range(2)]
            sq = sb(ph, "sqa", [128, NT], BF16); rstd = sb(ph, "rstda", [128, NT], F32)
            AO = [sb(ph, "AO%d" % i, [128, 4, NT], BF16) for i in range(2)]
            ptrr = [0]
            units = []
            for it in range(NTILES):
                for h in range(4):
                    nkb = (it * NT + NT) // 128
                    for kb in range(nkb):
                        units.append((it, h, kb, nkb))
            Ob = [PS[0], PS[1]]
            Db = [PS[2], PS[3]]

            def sbanks(ui):
                return [PS[4 + (ui % 2) * 2], PS[5 + (ui % 2) * 2]]

            def emit_qk(ui):
                it, h, kb, nkb = units[ui]
                q0 = it * NT
                k0 = kb * 128
                c0 = max(0, (k0 - q0) // 128) * 128
                Sb = sbanks(ui)
                for mp in range(2):
                    pr = slice(mp * 64, (mp + 1) * 64)
                    mm(Sb[mp], KT[pr, h, k0:k0 + 128], QT[pr, h, q0 + c0:q0 + NT], True, True,
                       [KT.b, QT.b], out=Sb[mp][:, c0:NT])

            def emit_rest(ui):
                it, h, kb, nkb = units[ui]
                q0 = it * NT
                k0 = kb * 128
                c0 = max(0, (k0 - q0) // 128) * 128
                slope = SLOPES[h]
                wq = 256 if slope * 511 > 64 else 512
                Sb = sbanks(ui)
                Ps = []
                for mp in range(2):
                    P = PT[ptrr[0] % len(PT)]; ptrr[0] += 1
                    Ps.append(P)
                    for g0 in range((c0 // wq) * wq, NT, wq):
                        lo = max(g0, c0)
                        hi = g0 + wq
                        bi = (q0 + hi - k0) // 128
                        act(P[:, lo:hi], Sb[mp][:, lo:hi], AF.Exp, [Sb[mp].b, BIAS.b], [P.b],
                            scale=0.125, bias=BIAS[:, h, bi:bi + 1])
                    if k0 >= q0:
                        tt("dve", P[:, c0:c0 + 128], P[:, c0:c0 + 128], CAUS[:], ALU.mult, [P.b, CAUS.b],
                           [P.b])
                for mp in range(2):
                    P = Ps[mp]
                    mm(Ob[mp], V[:, kb, h * 128:(h + 1) * 128], P[:, c0:NT], kb == 0, kb == nkb - 1,
                       [V.b, P.b], out=Ob[mp][:, c0:NT])
                    mm(Db[mp], ONE1[:], P[:, c0:NT], kb == 0, kb == nkb - 1, [ONE1.b, P.b],
                       out=Db[mp][:, c0:NT])

            def emit_epi(ui):
                it, h, kb, nkb = units[ui]
                for mp in range(2):
                    recip(rden[mp][:], Db[mp][:], [Db[mp].b], [rden[mp].b])
                    tt("dve", On[mp][:], Ob[mp][:], rden[mp][:], ALU.mult, [Ob[mp].b, rden[mp].b], [On[mp].b])
                stt(On[0][:], On[1][:], NLAM[:, 0:1], On[0][:], ALU.mult, ALU.add, [On[0].b, On[1].b, NLAM.b],
                    [On[0].b])
                act(sq[:], On[0][:], AF.Square, [On[0].b], [sq.b])
                b2 = sbanks(ui)[0]
                mm(b2, ON128[:], sq[:], True, True, [ON128.b, sq.b])
                act(rstd[:], b2[:], AF.Sqrt, [b2.b], [rstd.b], bias=EPS)
                recip(rstd[:], rstd[:], [rstd.b], [rstd.b])
                stt(AO[it % 2][:, h, :], On[0][:], SUBG[:, 0:1], rstd[:], ALU.mult, ALU.mult,
                    [On[0].b, SUBG.b, rstd.b], [AO[it % 2].b])
                if h == 3:
                    S.dma("sp", mix_s[it, :, 4:8, :], AO[it % 2][:], reads=[AO[it % 2].b])

            emit_qk(0)
            for ui in range(len(units)):
                if ui + 1 < len(units):
                    emit_qk(ui + 1)
                emit_rest(ui)
                if units[ui][2] == units[ui][3] - 1:
                    emit_epi(ui)
            S.barrier()
        att.close()


        with ExitStack() as ph:
          if 'S' in PH and 'CS' in PH:
            NKB = 4
            PTI = sb(ph, "PTI", [128, NS * 16], I32)
            IDX = sb(ph, "IDX", [128, NS * 16], U32)
            KPB = [sb(ph, "KPB%d" % i, [128, 512], F32) for i in range(NKB)]
            VPF = [sb(ph, "VPF%d" % i, [128, 512], F32) for i in range(NKB)]
            VPB = [sb(ph, "VPB%d" % i, [128, 4, 132], BF16) for i in range(32)]
            KpT = [sb(ph, "KpT%d" % i, [128, 4, 128], BF16) for i in range(2)]
            QB = sb(ph, "QB", [128, 4, NS, 2, 4], BF16)
            BIASS = sb(ph, "BIASS", [128, 16, 4, 8], F32)
            BIASN = sb(ph, "BIASN", [4, 4, 8], F32)
            TMPs = sb(ph, "TMPs", [128, 512], F32)
            PBs = [sb(ph, "PBs%d" % i, [128, 512], BF16) for i in range(2)]
            TNs = sb(ph, "TNs", [4, 32], F32); PNs = sb(ph, "PNs", [4, 32], BF16)
            RD = sb(ph, "RDs", [8, 4], F32)
            ONs = sb(ph, "ONs", [8, 4, 128], F32)
            CMB = sb(ph, "CMB", [8, 4], F32)
            OD4 = sb(ph, "OD4", [4, 512], F32)
            ODT = sb(ph, "ODT", [128, 4, TS], F32)
            sqs = sb(ph, "sqs", [128, TS], BF16); rstds = sb(ph, "rstds", [128, TS], F32)
            S.dma("sp", PTI[:], ptab.rearrange("s g -> (s g)").rearrange("(o n) -> o n", o=1).to_broadcast([128, NS * 16]),
                  writes=[PTI.b], **SLOW)
            ts("dve", IDX[:], PTI[:], 128.0, KPOS[:, 0:1], ALU.mult, ALU.add, [PTI.b, KPOS.b], [IDX.b])
            memset("dve", QB[:], 0.0, [QB.b])
            for mp in range(2):
                pr = slice(mp * 64, (mp + 1) * 64)
                cp("dve", QB[pr, :, :, mp, :], QTs[pr, :, :].rearrange("p h (s t) -> p h s t", t=4), [QTs.b], [QB.b])
            for pg in range(16):
                for h in range(4):
                    ts("dve", BIASS[:, pg, h, :], KPOS[:, 0:1].to_broadcast([128, 8]), float(pg * 128 - 2048), SLOPES[h],
                       ALU.add, ALU.mult, [KPOS.b], [BIASS.b])
            for h in range(4):
                ts("dve", BIASN[:, h, :], KPOS[0:4, 0:1].to_broadcast([4, 8]), SLOPES[h], None, ALU.mult, None, [KPOS.b],
                   [BIASN.b])
            for i in range(32):
                memset("dve", VPB[i][:], 1.0, [VPB[i].b])
            stt(CMB[:], IDF[0:8, 4:8], NLAM[0:8, 0:1], IDF[0:8, 0:4], ALU.mult, ALU.add, [IDF.b, NLAM.b], [CMB.b])
            ID4 = TB(IDF.t[0:4, 0:4], IDF.b)
            kcnt = 0
            ck_rows = cache_k
            cv_rows = cache_v
            for sq_ in range(NS):
                SBk = PS[sq_ % 2]
                PB_ = PBs[sq_ % 2]
                Ob = [PS[2], PS[3], PS[4], PS[5]]
                pend = []
                for pg in range(16):
                    Kp = KPB[kcnt % NKB]; Vf = VPF[kcnt % NKB]; Vp = VPB[kcnt % 32]; kcnt += 1
                    col = sq_ * 16 + pg
                    S.dmaf("pool", (lambda e, Kp=Kp, col=col: e.indirect_dma_start(
                        out=Kp[:, :], out_offset=None, in_=ck_rows,
                        in_offset=bass.IndirectOffsetOnAxis(ap=IDX[:, col:col + 1], axis=0))),
                        reads=[IDX.b], writes=[Kp.b])
                    S.dmaf("pool", (lambda e, Vf=Vf, col=col: e.indirect_dma_start(
                        out=Vf[:, :], out_offset=None, in_=cv_rows,
                        in_offset=bass.IndirectOffsetOnAxis(ap=IDX[:, col:col + 1], axis=0))),
                        reads=[IDX.b], writes=[Vf.b])
                    bkT = PS[6 + (pg % 2)]
                    KT_ = KpT[pg % 2]
                    for h in range(4):
                        tr(bkT, bkT[:, h * 128:(h + 1) * 128], Kp[:, h * 128:(h + 1) * 128], IDF, [Kp.b])
                    cp("act", KT_[:, :, :], bkT[:].rearrange("p (h n) -> p h n", n=128), [bkT.b], [KT_.b])
                    for h in range(4):
                        mm(SBk, KT_[:, h, :], QB[:, h, sq_, :, :].rearrange("p a b -> p (a b)"), True, True,
                           [KT_.b, QB.b], out=SBk[:, (pg * 4 + h) * 8:(pg * 4 + h + 1) * 8])
                    cp("dve", Vp[:, :, 0:128], Vf[:, :].rearrange("p (h d) -> p h d", d=128), [Vf.b], [Vp.b])
                    pend.append(Vp)
                NB = PS[6]
                for h in range(4):
                    mm(NB, KTs[:, h, sq_ * 4:(sq_ + 1) * 4], QB[:, h, sq_, :, :].rearrange("p a b -> p (a b)"), True, True,
                       [KTs.b, QB.b], out=NB[0:4, h * 8:(h + 1) * 8])
                stt(TMPs[:], SBk[:], 0.125, BIASS[:, :, :, :].rearrange("p a b c -> p (a b c)"), ALU.mult, ALU.add,
                    [SBk.b, BIASS.b], [TMPs.b])
                act(PB_[:], TMPs[:], AF.Exp, [TMPs.b], [PB_.b])
                stt(TNs[:], NB[0:4, 0:32], 0.125, BIASN[:, :, :].rearrange("p a b -> p (a b)"), ALU.mult, ALU.add,
                    [NB.b, BIASN.b], [TNs.b])
                act(TNs[:], TNs[:], AF.Exp, [TNs.b], [TNs.b])
                tt("dve", PNs[:, :].rearrange("p (a t) -> p a t", t=4), TNs[:, :].rearrange("p (a t) -> p a t", t=4),
                   CAUS[0:4, 0:4].unsqueeze(1).to_broadcast([4, 8, 4]), ALU.mult, [TNs.b, CAUS.b], [PNs.b])
                for pg in range(16):
                    Vp = pend[pg]
                    for h in range(4):
                        mm(Ob[h], PB_[:, (pg * 4 + h) * 8:(pg * 4 + h + 1) * 8], Vp[:, h, 0:129], pg == 0, False,
                           [PB_.b, Vp.b], out=Ob[h][0:8, 0:129])
                for h in range(4):
                    mm(Ob[h], PNs[0:4, h * 8:(h + 1) * 8], VN[0:4, sq_, h, 0:129], False, True, [PNs.b, VN.b],
                       out=Ob[h][0:8, 0:129])
                for h in range(4):
                    recip(RD[:, h:h + 1], Ob[h][0:8, 128:129], [Ob[h].b], [RD.b])
                    ts("dve", ONs[:, h, :], Ob[h][0:8, 0:128], RD[:, h:h + 1], None, ALU.mult, None, [Ob[h].b, RD.b],
                       [ONs.b])
                DBk = PS[7]
                mm(DBk, CMB[:, :], ONs[:, :, :].rearrange("p h d -> p (h d)"), True, True, [CMB.b, ONs.b],
                   out=DBk[0:4, :])
                cp("act", OD4[:], DBk[0:4, :], [DBk.b], [OD4.b])
                TBk = PS[6]
                for h in range(4):
                    tr(TBk, TBk[:, h * 4:(h + 1) * 4], OD4[0:4, h * 128:(h + 1) * 128], ID4, [OD4.b])
                cp("act", ODT[:, :, sq_ * 4:(sq_ + 1) * 4], TBk[:, 0:16].rearrange("p (h t) -> p h t", t=4), [TBk.b],
                   [ODT.b])
            for h in range(4):
                act(sqs[:], ODT[:, h, :], AF.Square, [ODT.b], [sqs.b])
                b2 = PS[7]
                mm(b2, ON128[:], sqs[:], True, True, [ON128.b, sqs.b], out=b2[:, 0:TS])
                act(rstds[:], b2[:, 0:TS], AF.Sqrt, [b2.b], [rstds.b], bias=EPS)
                recip(rstds[:], rstds[:], [rstds.b], [rstds.b])
                stt(MIXs[:, 4 + h, :], ODT[:, h, :], SUBG[:, 0:1], rstds[:], ALU.mult, ALU.mult,
                    [ODT.b, SUBG.b, rstds.b], [MIXs.b])
            S.barrier()

        with ExitStack() as ph:
          if 'D' in PH:
            X = sb(ph, "Xd", [128, 4, 1024], F32)
            xT = sb(ph, "xT", [128, 8, NT], F32)
            MIX = [sb(ph, "MIX%d" % i, [128, 8, NT], BF16) for i in range(2)]
            A = sb(ph, "actA", [128, 8, NT], BF16); Bb = sb(ph, "actB", [128, 8, NT], BF16)
            Wo = sb(ph, "Wo", [128, 8, 1024], BF16); Wq = sb(ph, "Wq", [128, 8, 1024], BF16)
            Wo2 = sb(ph, "Wo2", [128, 8, 1024], BF16)
            sq = sb(ph, "sqd", [128, 2, NT], BF16); rstd = sb(ph, "rstdd", [128, NT], F32)
            Pm = [sb(ph, "Pm%d" % i, [128, NT], BF16) for i in range(2)]
            hT = sb(ph, "hT", [128, NFC, NT], BF16)
            HG = sb(ph, "HG", [128, NT + 2], F32); CARRY = sb(ph, "CARRY", [128, NFC, 2], F32)
            cv = sb(ph, "cv", [128, NT], F32)
            rden = sb(ph, "rdend", [128, NT], F32)
            WG = [sb(ph, "WG%d" % i, [128, 8, 128], BF16) for i in range(3)]
            WV = [sb(ph, "WV%d" % i, [128, 8, 128], BF16) for i in range(3)]
            WD = [sb(ph, "WD%d" % i, [128, NFC, 128], BF16) for i in range(2)]
            S.dma("pool", Wo[:], w_out.rearrange("(c p) n -> p c n", p=128), writes=[Wo.b])
            S.dma("pool", Wq[:], ca_wq.rearrange("(c p) n -> p c n", p=128), writes=[Wq.b])
            S.dma("pool", Wo2[:], ca_wo.rearrange("(c p) n -> p c n", p=128), writes=[Wo2.b])
            memset("dve", CARRY[:], 0.0, [CARRY.b])
            if 'B' not in PH or os.environ.get('KNOB'):
                memset("dve", hT[:, 0:4, :], 0.0, [hT.b])
                for it in range(NTILES):
                    S.dma("sp", mix_s[it, :, 0:4, :], hT[:, 0:4, :], reads=[hT.b])
                S.barrier()
            wrr = 0
            for it in range(NTILES):
                t0 = it * NT
                M_ = MIX[it % 2]
                S.dma("sp", M_[:], mix_s[it], writes=[M_.b])
                S.dma("sp", X[:], xp[t0:t0 + NT, :].rearrange("(j p) f -> p j f", p=128), writes=[X.b])
                for c in range(8):
                    bk = bank()
                    for j in range(4):
                        tr(bk, bk[:, j * 128:(j + 1) * 128], X[:, j, c * 128:(c + 1) * 128], IDF, [X.b])
                    cp("act", xT[:, c, :], bk[:], [bk.b], [xT.b])
                for m in range(8):
                    bk = bank()
                    for kc in range(8):
                        mm(bk, Wo[:, kc, m * 128:(m + 1) * 128], M_[:, kc, :], kc == 0, kc == 7, [Wo.b, M_.b])
                    tt("dve", xT[:, m, :], bk[:], xT[:, m, :], ALU.add, [bk.b, xT.b], [xT.b])
                ln_fm(xT, G2, A, NT, sq, rstd)
                for h in range(4):
                    bq = [bank(), bank()]
                    for dh in range(2):
                        for kc in range(8):
                            mm(bq[dh], Wq[:, kc, (2 * h + dh) * 128:(2 * h + dh + 1) * 128], A[:, kc, :], kc == 0,
                               kc == 7, [Wq.b, A.b])
                    b2 = bank()
                    for dh in range(2):
                        act(sq[:, dh, :], bq[dh][:], AF.Square, [bq[dh].b], [sq.b])
                        mm(b2, ON256[:], sq[:, dh, :], dh == 0, dh == 1, [ON256.b, sq.b])
                    act(rstd[:], b2[:], AF.Sqrt, [b2.b], [rstd.b], bias=EPS)
                    recip(rstd[:], rstd[:], [rstd.b], [rstd.b])
                    for dh in range(2):
                        stt(Bb[:, 2 * h + dh, :], bq[dh][:], CQG[:, dh:dh + 1], rstd[:], ALU.mult, ALU.mult,
                            [bq[dh].b, CQG.b, rstd.b], [Bb.b])
                for h in range(4):
                    for mb in range(2):
                        bs = bank()
                        for dh in range(2):
                            mm(bs, MKT[:, 2 * h + dh, mb * 128:(mb + 1) * 128], Bb[:, 2 * h + dh, :], dh == 0, dh == 1,
                               [MKT.b, Bb.b])
                        act(Pm[mb][:], bs[:], AF.Exp, [bs.b], [Pm[mb].b], scale=1.0 / 16)
                    bd = bank()
                    for mb in range(2):
                        mm(bd, ONE1[:], Pm[mb][:], mb == 0, mb == 1, [ONE1.b, Pm[mb].b])
                    recip(rden[:], bd[:], [bd.b], [rden.b])
                    for dvc in range(2):
                        bo = bank()
                        for mb in range(2):
                            mm(bo, MV[:, mb, (2 * h + dvc) * 128:(2 * h + dvc + 1) * 128], Pm[mb][:], mb == 0, mb == 1,
                               [MV.b, Pm[mb].b])
                        tt("dve", A[:, 2 * h + dvc, :], bo[:], rden[:], ALU.mult, [bo.b, rden.b], [A.b])
                for m in range(8):
                    bk = bank()
                    for kc in range(8):
                        mm(bk, Wo2[:, kc, m * 128:(m + 1) * 128], A[:, kc, :], kc == 0, kc == 7, [Wo2.b, A.b])
                    tt("dve", xT[:, m, :], bk[:], xT[:, m, :], ALU.add, [bk.b, xT.b], [xT.b])
                ln_fm(xT, G3, Bb, NT, sq, rstd)
                for fc in range(NFC):
                    wg = WG[wrr % 3]; wv = WV[wrr % 3]; wrr += 1
                    S.dma("sp", wg[:], wg_s[:, fc],
                          writes=[wg.b])
                    S.dma("sp", wv[:], wv_s[:, fc],
                          writes=[wv.b])
                    bg = bank(); bv = bank()
                    for kc in range(8):
                        mm(bg, wg[:, kc, :], Bb[:, kc, :], kc == 0, kc == 7, [wg.b, Bb.b])
                    for kc in range(8):
                        mm(bv, wv[:, kc, :], Bb[:, kc, :], kc == 0, kc == 7, [wv.b, Bb.b])
                    cp("dve", HG[:, 0:2], CARRY[:, fc, :], [CARRY.b], [HG.b])
                    cp("act", HG[:, 2:NT + 2], bg[:], [bg.b], [HG.b])
                    cp("dve", CARRY[:, fc, :], HG[:, NT:NT + 2], [HG.b], [CARRY.b])
                    act(cv[:], HG[:, 2:NT + 2], AF.Identity, [HG.b, CW.b, CB.b], [cv.b], scale=CW[:, 2, fc:fc + 1],
                        bias=CB[:, fc:fc + 1])
                    stt(cv[:], HG[:, 1:NT + 1], CW[:, 1, fc:fc + 1], cv[:], ALU.mult, ALU.add, [HG.b, CW.b, cv.b],
                        [cv.b])
                    stt(cv[:], HG[:, 0:NT], CW[:, 0, fc:fc + 1], cv[:], ALU.mult, ALU.add, [HG.b, CW.b, cv.b], [cv.b])
                    act(cv[:], cv[:], AF.Silu, [cv.b], [cv.b])
                    tt("dve", hT[:, fc, :], cv[:], bv[:], ALU.mult, [cv.b, bv.b], [hT.b])
                for m in range(8):
                    wd = WD[m % 2]
                    S.dma("sp", wd[:], wd_s[:, m],
                          writes=[wd.b])
                    bk = bank()
                    for fc in range(NFC):
                        mm(bk, wd[:, fc, :], hT[:, fc, :], fc == 0, fc == NFC - 1, [wd.b, hT.b])
                    tt("dve", xT[:, m, :], bk[:], xT[:, m, :], ALU.add, [bk.b, xT.b], [xT.b])
                store_tm(xT, 8, NT, y_p[t0:t0 + NT, :], X)
            for j in range(2):
                S.dma("sp", conv_p[j].rearrange("(c p) -> p c", p=128), CARRY[:, :, j], reads=[CARRY.b], **SLOW)
            S.barrier()

        with ExitStack() as ph:
          if 'S' in PH and 'DS' in PH:
            X = sb(ph, "Xds", [128, 1, 1024], F32)
            xT = sb(ph, "xTs", [128, 8, TS], F32)
            A = sb(ph, "actAs", [128, 8, TS], BF16); Bb = sb(ph, "actBs", [128, 8, TS], BF16)
            Wo = sb(ph, "Wos", [128, 8, 1024], BF16); Wq = sb(ph, "Wqs", [128, 8, 1024], BF16)
            Wo2 = sb(ph, "Wo2s", [128, 8, 1024], BF16)
            sq = sb(ph, "sqds", [128, 2, TS], BF16); rstd = sb(ph, "rstdds", [128, TS], F32)
            hT = sb(ph, "hTs", [128, NFC, TS], BF16)
            WG = [sb(ph, "WGs%d" % i, [128, 8, 128], BF16) for i in range(2)]
            WV = [sb(ph, "WVs%d" % i, [128, 8, 128], BF16) for i in range(2)]
            WD = [sb(ph, "WDs%d" % i, [128, NFC, 128], BF16) for i in range(2)]
            S.dma("pool", Wo[:], w_out.rearrange("(c p) n -> p c n", p=128), writes=[Wo.b])
            S.dma("pool", Wq[:], ca_wq.rearrange("(c p) n -> p c n", p=128), writes=[Wq.b])
            S.dma("pool", Wo2[:], ca_wo.rearrange("(c p) n -> p c n", p=128), writes=[Wo2.b])
            wrr = 0
            n = TS
            CMKt = sb(ph, "CMKt", [128, 2, 1024], F32)
            CMVb = [sb(ph, "CMVb%d" % i, [128, 2, 4, 260], BF16) for i in range(2)]
            MKs = sb(ph, "MKs", [128, 8, 256], BF16)
            PCs = sb(ph, "PCs", [128, 32], BF16)
            RDc = sb(ph, "RDc", [4, 4], F32)
            COn = sb(ph, "COn", [4, 4, 256], F32)
            SCt = sb(ph, "SCt", [32, F], F32)
            SCV = sb(ph, "SCV", [128, NFC, 32], F32)
            CVS = sb(ph, "CVS", [128, NFC, 32], F32)
            HGs = sb(ph, "HGs", [128, NS, 6], F32)
            cvs = sb(ph, "cvs", [128, NS, 4], F32)
            ID4 = TB(IDF.t[0:4, 0:4], IDF.b); ID32 = TB(IDF.t[0:32, 0:32], IDF.b)
            for i in range(2):
                memset("dve", CMVb[i][:], 1.0, [CMVb[i].b])
            S.dma("sp", SCt[:], st_conv, writes=[SCt.b])
            for f0 in range(0, NFC, 16):
                bk = bank()
                for fc in range(f0, min(f0 + 16, NFC)):
                    tr(bk, bk[:, (fc - f0) * 32:(fc - f0 + 1) * 32], SCt[0:32, fc * 128:(fc + 1) * 128], ID32, [SCt.b])
                nf = min(f0 + 16, NFC) - f0
                cp("act", SCV[:, f0:f0 + nf, :], bk[:, 0:nf * 32].rearrange("p (f x) -> p f x", x=32), [bk.b], [SCV.b])
            for c in range(8):
                cp("dve", xT[:, c, 0:n], xsT[:, c, :], [xsT.b], [xT.b])
            for m in range(8):
                bk = bank()
                for kc in range(8):
                    mm(bk, Wo[:, kc, m * 128:(m + 1) * 128], MIXs[:, kc, :], kc == 0, kc == 7, [Wo.b, MIXs.b],
                       out=bk[:, 0:n])
                tt("dve", xT[:, m, 0:n], bk[:, 0:n], xT[:, m, 0:n], ALU.add, [bk.b, xT.b], [xT.b])
            ln_fm(xT, G2, A, n, sq, rstd)
            for h in range(4):
                bq = [bank(), bank()]
                for dh in range(2):
                    for kc in range(8):
                        mm(bq[dh], Wq[:, kc, (2 * h + dh) * 128:(2 * h + dh + 1) * 128], A[:, kc, 0:n], kc == 0,
                           kc == 7, [Wq.b, A.b], out=bq[dh][:, 0:n])
                b2 = bank()
                for dh in range(2):
                    act(sq[:, dh, 0:n], bq[dh][:, 0:n], AF.Square, [bq[dh].b], [sq.b])
                    mm(b2, ON256[:], sq[:, dh, 0:n], dh == 0, dh == 1, [ON256.b, sq.b], out=b2[:, 0:n])
                act(rstd[:, 0:n], b2[:, 0:n], AF.Sqrt, [b2.b], [rstd.b], bias=EPS)
                recip(rstd[:, 0:n], rstd[:, 0:n], [rstd.b], [rstd.b])
                for dh in range(2):
                    stt(Bb[:, 2 * h + dh, 0:n], bq[dh][:, 0:n], CQG[:, dh:dh + 1], rstd[:, 0:n], ALU.mult, ALU.mult,
                        [bq[dh].b, CQG.b, rstd.b], [Bb.b])
            for sq_ in range(NS):
                CMV_ = CMVb[sq_ % 2]
                S.dma("sp", CMKt[:], cmk[sq_].rearrange("(mb p) f -> p mb f", p=128), writes=[CMKt.b])
                for mb in range(2):
                    S.dma("pool", CMV_[:, mb, :, 0:256], cmv[sq_, mb * 128:(mb + 1) * 128, :].rearrange(
                        "p (h d) -> p h d", d=256), writes=[CMV_.b])
                for mb in range(2):
                    for c0 in (0, 4):
                        bk = bank()
                        for c8 in range(c0, c0 + 4):
                            tr(bk, bk[:, (c8 - c0) * 128:(c8 - c0 + 1) * 128], CMKt[:, mb, c8 * 128:(c8 + 1) * 128], IDF,
                               [CMKt.b])
                        cp("act" if c0 == 0 else "dve", MKs[:, c0:c0 + 4, mb * 128:(mb + 1) * 128],
                           bk[:].rearrange("p (c n) -> p c n", n=128), [bk.b], [MKs.b])
                SBc = bank()
                for mb in range(2):
                    for h in range(4):
                        for dh in range(2):
                            mm(SBc, MKs[:, 2 * h + dh, mb * 128:(mb + 1) * 128], Bb[:, 2 * h + dh, sq_ * 4:(sq_ + 1) * 4],
                               dh == 0, dh == 1, [MKs.b, Bb.b], out=SBc[:, (mb * 4 + h) * 4:(mb * 4 + h + 1) * 4])
                act(PCs[:], SBc[:, 0:32], AF.Exp, [SBc.b], [PCs.b], scale=1.0 / 16)
                Oc = [bank(), bank(), bank(), bank()]
                for h in range(4):
                    for mb in range(2):
                        mm(Oc[h], PCs[:, (mb * 4 + h) * 4:(mb * 4 + h + 1) * 4], CMV_[:, mb, h, 0:257], mb == 0, mb == 1,
                           [PCs.b, CMV_.b], out=Oc[h][0:4, 0:257])
                for h in range(4):
                    recip(RDc[:, h:h + 1], Oc[h][0:4, 256:257], [Oc[h].b], [RDc.b])
                    ts("dve", COn[:, h, :], Oc[h][0:4, 0:256], RDc[:, h:h + 1], None, ALU.mult, None,
                       [Oc[h].b, RDc.b], [COn.b])
                TBk = bank()
                cof = COn[:, :, :].rearrange("p h d -> p (h d)")
                for c8 in range(8):
                    tr(TBk, TBk[:, c8 * 4:(c8 + 1) * 4], cof[0:4, c8 * 128:(c8 + 1) * 128], ID4, [COn.b])
                cp("act", A[:, :, sq_ * 4:(sq_ + 1) * 4], TBk[:, 0:32].rearrange("p (c t) -> p c t", t=4), [TBk.b],
                   [A.b])
            for m in range(8):
                bk = bank()
                for kc in range(8):
                    mm(bk, Wo2[:, kc, m * 128:(m + 1) * 128], A[:, kc, 0:n], kc == 0, kc == 7, [Wo2.b, A.b],
                       out=bk[:, 0:n])
                tt("dve", xT[:, m, 0:n], bk[:, 0:n], xT[:, m, 0:n], ALU.add, [bk.b, xT.b], [xT.b])
            ln_fm(xT, G3, Bb, n, sq, rstd)
            for fc in range(NFC):
                wg = WG[wrr % 2]; wv = WV[wrr % 2]; wrr += 1
                S.dma("sp", wg[:], wg_s[:, fc],
                      writes=[wg.b])
                S.dma("sp", wv[:], wv_s[:, fc],
                      writes=[wv.b])
                bg = bank(); bv = bank()
                for kc in range(8):
                    mm(bg, wg[:, kc, :], Bb[:, kc, 0:n], kc == 0, kc == 7, [wg.b, Bb.b], out=bg[:, 0:n])
                for kc in range(8):
                    mm(bv, wv[:, kc, :], Bb[:, kc, 0:n], kc == 0, kc == 7, [wv.b, Bb.b], out=bv[:, 0:n])
                cp("dve", HGs[:, :, 0:2], SCV[:, fc, :].rearrange("p (s j) -> p s j", j=2), [SCV.b], [HGs.b])
                cp("act", HGs[:, :, 2:6], bg[:, 0:n].rearrange("p (s t) -> p s t", t=4), [bg.b], [HGs.b])
                cp("dve", CVS[:, fc, :].rearrange("p (s j) -> p s j", j=2), HGs[:, :, 4:6], [HGs.b], [CVS.b])
                act(cvs[:], HGs[:, :, 2:6], AF.Identity, [HGs.b, CW.b, CB.b], [cvs.b], scale=CW[:, 2, fc:fc + 1],
                    bias=CB[:, fc:fc + 1])
                stt(cvs[:], HGs[:, :, 1:5], CW[:, 1, fc:fc + 1], cvs[:], ALU.mult, ALU.add, [HGs.b, CW.b, cvs.b],
                    [cvs.b])
                stt(cvs[:], HGs[:, :, 0:4], CW[:, 0, fc:fc + 1], cvs[:], ALU.mult, ALU.add, [HGs.b, CW.b, cvs.b],
                    [cvs.b])
                act(cvs[:], cvs[:], AF.Silu, [cvs.b], [cvs.b])
                tt("dve", hT[:, fc, 0:n], cvs[:, :, :].rearrange("p s t -> p (s t)"), bv[:, 0:n], ALU.mult,
                   [cvs.b, bv.b], [hT.b])
            for m in range(8):
                wd = WD[m % 2]
                S.dma("sp", wd[:], wd_s[:, m],
                      writes=[wd.b])
                bk = bank()
                for fc in range(NFC):
                    mm(bk, wd[:, fc, :], hT[:, fc, 0:n], fc == 0, fc == NFC - 1, [wd.b, hT.b], out=bk[:, 0:n])
                tt("dve", xT[:, m, 0:n], bk[:, 0:n], xT[:, m, 0:n], ALU.add, [bk.b, xT.b], [xT.b])
            store_tm(xT, 8, n, y_s, X)
            for f0 in range(0, NFC, 4):
                bk = bank()
                nf = min(f0 + 4, NFC) - f0
                for fc in range(f0, f0 + nf):
                    tr(bk, bk[0:32, (fc - f0) * 128:(fc - f0 + 1) * 128], CVS[:, fc, :], IDF, [CVS.b])
                cp("act", SCt[0:32, f0 * 128:(f0 + nf) * 128], bk[0:32, 0:nf * 128], [bk.b], [SCt.b])
            S.dma("sp", conv_s, SCt[:], reads=[SCt.b])

            S.barrier()

        S.barrier()
        S.emit()
    return nc


_NC_CACHE = {}


def _consts():
    ident = np.eye(128, dtype=np.float32)
    bd64 = np.zeros((128, 128), np.float32); bd64[:64, :64] = 1 / 64; bd64[64:, 64:] = 1 / 64
    m16 = np.kron(np.eye(8, dtype=np.float32), np.ones((16, 16), np.float32))
    caus = np.triu(np.ones((128, 128), np.float32))
    pwn = np.array(PW_N, np.float32)
    kpos = np.arange(128, dtype=np.float32)
    g2m = np.zeros((128, 2), np.float32)
    for p in range(128):
        g2m[p, (p // 16) % 2] = 1.0
    return dict(c_ident=ident, c_bd64=bd64, c_mask16=m16, c_caus=caus, c_pwn=pwn, c_kpos=kpos, c_g2m=g2m)


WNAMES = ["ln1_g", "ln2_g", "ln3_g", "mem_norm_g", "w_in", "ssm_a_re", "ssm_a_im", "ssm_b_re", "ssm_b_im",
          "ssm_c_re", "ssm_c_im", "ssm_d", "ssm_log_dt", "ssm_glu_w", "q_norm_g", "k_norm_g", "lam_q1", "lam_k1",
          "lam_q2", "lam_k2", "subln_g", "w_out", "ca_wq", "ca_wk", "ca_wv", "ca_q_norm_g", "ca_k_norm_g", "ca_wo",
          "ffn_wg", "ffn_wv", "ffn_wd", "ffn_conv_w", "ffn_conv_b"]


def make_in_maps(inp, cores):
    cst = _consts()
    shared = {n: np.ascontiguousarray(np.asarray(inp[n])[0]) for n in WNAMES}
    maps = []
    for c in cores:
        b = c % 4
        m = dict(shared)
        m.update(cst)
        m["xp"] = np.ascontiguousarray(np.asarray(inp["x_prompt"])[b])
        m["memp"] = np.ascontiguousarray(np.asarray(inp["mem_prompt"])[b])
        sl = slice(c * NS, (c + 1) * NS)
        m["xs"] = np.ascontiguousarray(np.asarray(inp["x_sample"])[sl]).reshape(TS, 1024)
        m["st_re"] = np.ascontiguousarray(np.asarray(inp["state_ssm_re"])[0, sl]).reshape(NS, 2048)
        m["st_im"] = np.ascontiguousarray(np.asarray(inp["state_ssm_im"])[0, sl]).reshape(NS, 2048)
        m["st_conv"] = np.ascontiguousarray(np.asarray(inp["state_conv"])[0, sl]).reshape(NS * 2, F)
        m["cmk"] = np.ascontiguousarray(np.asarray(inp["cache_mem_k"])[0, sl]).reshape(NS, 256, 1024)
        m["cmv"] = np.ascontiguousarray(np.asarray(inp["cache_mem_v"])[0, sl]).reshape(NS, 256, 1024)
        m["cache_k"] = np.asarray(inp["cache_k"]).reshape(2560 * 128, 512)
        m["cache_v"] = np.asarray(inp["cache_v"]).reshape(2560 * 128, 512)
        m["ptab"] = np.ascontiguousarray(np.asarray(inp["page_table"])[sl]).astype(np.int32)
        maps.append(m)
    return maps


def kernel(**inp):
    nc = build_nc()
    cores = list(range(NCORES))
    maps = make_in_maps(inp, cores)
    res = run_bass_kernel_spmd(nc, maps, core_ids=cores)
    R = res.results
    f32 = np.float32
    y_prompt = np.stack([R[b]["y_p"] for b in range(4)]).astype(f32)
    k_prompt = np.stack([R[b]["k_p"] for b in range(4)]).reshape(1, 4, T, 4, 2, 64).astype(f32)
    v_prompt = np.stack([R[b]["v_p"] for b in range(4)]).reshape(1, 4, T, 4, 128).astype(f32)
    hre = np.stack([R[b]["hre_p"] for b in range(4)]).reshape(1, 4, 32, 64).astype(f32)
    him = np.stack([R[b]["him_p"] for b in range(4)]).reshape(1, 4, 32, 64).astype(f32)
    conv_prompt = np.stack([R[b]["conv_p"] for b in range(4)]).reshape(1, 4, 2, F).astype(f32)
    mk = np.stack([R[b]["mk_p"] for b in range(4)]).reshape(1, 4, 256, 4, 256).astype(f32)
    mv = np.stack([R[b]["mv_p"] for b in range(4)]).reshape(1, 4, 256, 4, 256).astype(f32)
    y_sample = np.concatenate([R[c]["y_s"] for c in range(NCORES)]).reshape(128, 4, 1024).astype(f32)
    k_sample = np.concatenate([R[c]["k_s"] for c in range(NCORES)]).reshape(1, 128, 4, 4, 2, 64).astype(f32)
    v_sample = np.concatenate([R[c]["v_s"] for c in range(NCORES)]).reshape(1, 128, 4, 4, 128).astype(f32)
    hre_s = np.concatenate([R[c]["hre_s"] for c in range(NCORES)]).reshape(1, 128, 32, 64).astype(f32)
    him_s = np.concatenate([R[c]["him_s"] for c in range(NCORES)]).reshape(1, 128, 32, 64).astype(f32)
    conv_s = np.concatenate([R[c]["conv_s"] for c in range(NCORES)]).reshape(1, 128, 2, F).astype(f32)
    return (y_prompt, y_sample, k_prompt, v_prompt, k_sample, v_sample, hre, him, hre_s, him_s,
            conv_prompt, conv_s, mk, mv)
```

```python
import math
import os
PH = set(os.environ.get('KPH', 'W,M,A2,C,B,D,S,CS,DS').split(','))
import numpy as np
import ml_dtypes
import concourse.bass as bass
import concourse.mybir as mybir
from concourse.bass_utils import run_bass_kernel_spmd
from contextlib import ExitStack

F32 = mybir.dt.float32
BF16 = mybir.dt.bfloat16
I32 = mybir.dt.int32
U32 = mybir.dt.uint32
ALU = mybir.AluOpType
AF = mybir.ActivationFunctionType

ENGS = ("pe", "act", "dve", "pool", "sp")
N_DMA_SEMS = 24
EPS = 1e-6
NCORES = 8
T = 4096
NT = 512
NTILES = T // NT
L = 8
NCH = T // L
SC = 16
NSC = NCH // SC
F = 2816
NFC = F // 128
SLOPES = [2.0 ** (-8.0 * (h + 1) / 4) for h in range(4)]
LAM0 = 0.8 - 0.6 * math.exp(-0.3 * 0)
NS = 16
TS = 64
NPW = 25
PW_N = list(range(9)) + [8 * k for k in range(2, 17)] + [4]


class Buf:
    __slots__ = ("name", "last_w", "readers", "excl")

    def __init__(self, name):
        self.name = name
        self.excl = False
        self.last_w = None
        self.readers = []


class Sched:
    def __init__(self, nc, es):
        self.nc = nc
        self.ops = {e: [] for e in ENGS}
        self.count = {e: 0 for e in ENGS}
        self.sems = {e: es.enter_context(nc.semaphore("s_" + e)) for e in ENGS}
        self.dsems = [es.enter_context(nc.semaphore("d%d" % i)) for i in range(N_DMA_SEMS)]
        self.dcnt = [0] * N_DMA_SEMS
        self.drr = 0
        self.waited = {e: {} for e in ENGS}
        self.bufs = []

    def buf(self, name):
        b = Buf(name)
        self.bufs.append(b)
        return b

    def _collect(self, eng, reads, writes, is_dma):
        toks = []
        for b in reads:
            if b.last_w is not None:
                toks.append(b.last_w)
        for b in writes:
            if b.last_w is not None:
                toks.append(b.last_w)
            toks.extend(b.readers)
        need = {}
        for (k, v) in toks:
            if (not is_dma) and eng == "pe" and k == "pe":
                continue
            if need.get(k, -1) < v:
                need[k] = v
        waits = []
        w = self.waited[eng]
        for k, v in need.items():
            if w.get(k, -1) >= v:
                continue
            w[k] = v
            waits.append((k, v))
        return waits

    def _commit(self, tok, reads, writes):
        for b in reads:
            if b.excl:
                b.last_w = tok
                b.readers = []
            else:
                b.readers.append(tok)
        for b in writes:
            b.last_w = tok
            b.readers = []

    def op(self, eng, fn, reads=(), writes=()):
        waits = self._collect(eng, reads, writes, False)
        self.count[eng] += 1
        tok = (eng, self.count[eng])
        self.ops[eng].append((waits, fn, None))
        self._commit(tok, reads, writes)
        return tok

    def dmaf(self, eng, fn, reads=(), writes=()):
        waits = self._collect(eng, reads, writes, True)
        i = self.drr
        self.drr = (self.drr + 1) % N_DMA_SEMS
        k = ("d", i)
        prev = self.dcnt[i]
        w = self.waited[eng]
        if prev > 0 and w.get(k, -1) < prev:
            w[k] = prev
            waits.append((k, prev))
        self.dcnt[i] += 16
        tok = (k, self.dcnt[i])
        self.ops[eng].append((waits, fn, (i, 16)))
        self._commit(tok, reads, writes)
        return tok

    def dma(self, eng, out, in_, reads=(), writes=(), **kw):
        def fn(e, out=out, in_=in_, kw=kw):
            return e.dma_start(out=out, in_=in_, **kw)
        return self.dmaf(eng, fn, reads, writes)

    def barrier(self):
        targets = [(e, self.count[e]) for e in ENGS if self.count[e] > 0]
        targets += [(("d", i), c) for i, c in enumerate(self.dcnt) if c > 0]
        for e in ENGS:
            w = self.waited[e]
            waits = []
            for k, v in targets:
                if w.get(k, -1) < v:
                    w[k] = v
                    waits.append((k, v))
            if waits:
                self.ops[e].append((waits, None, None))
        for b in self.bufs:
            b.last_w = None
            b.readers = []

    def _sem(self, k):
        if isinstance(k, tuple):
            return self.dsems[k[1]]
        return self.sems[k]

    def emit(self):
        nc = self.nc
        with nc.Block() as block:
            def mk(ename):
                def body(e):
                    own = self.sems[ename]
                    for waits, fn, dinc in self.ops[ename]:
                        for (k, v) in waits:
                            e.wait_ge(self._sem(k), v)
                        if fn is None:
                            continue
                        ins = fn(e)
                        if dinc is not None:
                            ins.then_inc(self.dsems[dinc[0]], dinc[1])
                        else:
                            ins.then_inc(own, 1)
                return body
            block.tensor(mk("pe"))
            block.scalar(mk("act"))
            block.vector(mk("dve"))
            block.gpsimd(mk("pool"))
            block.sync(mk("sp"))


class TB:
    def __init__(self, t, b):
        self.t = t
        self.b = b

    def __getitem__(self, k):
        return self.t[k]


def build_nc():
    nc = bass.Bass("TRN2", target_bir_lowering=False)

    def din(name, shape, dt=F32):
        return nc.dram_tensor(name, list(shape), dt, kind="ExternalInput").ap()

    def dout(name, shape, dt=F32):
        return nc.dram_tensor(name, list(shape), dt, kind="ExternalOutput").ap()

    def dscr(name, shape, dt):
        return nc.dram_tensor(name, list(shape), dt, kind="Internal").ap()

    xp = din("xp", [T, 1024])
    memp = din("memp", [256, 1024])
    ln1_g = din("ln1_g", [1024]); ln2_g = din("ln2_g", [1024]); ln3_g = din("ln3_g", [1024])
    memn_g = din("mem_norm_g", [1024])
    w_in = din("w_in", [1024, 2048])
    a_re = din("ssm_a_re", [32, 64]); a_im = din("ssm_a_im", [32, 64])
    b_re = din("ssm_b_re", [32, 64, 16]); b_im = din("ssm_b_im", [32, 64, 16])
    c_re = din("ssm_c_re", [32, 16, 64]); c_im = din("ssm_c_im", [32, 16, 64])
    ssm_d = din("ssm_d", [32, 16]); log_dt = din("ssm_log_dt", [32])
    glu_w = din("ssm_glu_w", [512, 512])
    qn_g = din("q_norm_g", [64]); kn_g = din("k_norm_g", [64])
    lq1 = din("lam_q1", [64]); lk1 = din("lam_k1", [64]); lq2 = din("lam_q2", [64]); lk2 = din("lam_k2", [64])
    subln_g = din("subln_g", [128])
    w_out = din("w_out", [1024, 1024])
    ca_wq = din("ca_wq", [1024, 1024]); ca_wk = din("ca_wk", [1024, 1024]); ca_wv = din("ca_wv", [1024, 1024])
    caq_g = din("ca_q_norm_g", [256]); cak_g = din("ca_k_norm_g", [256])
    ca_wo = din("ca_wo", [1024, 1024])
    ffn_wg = din("ffn_wg", [1024, F]); ffn_wv = din("ffn_wv", [1024, F]); ffn_wd = din("ffn_wd", [F, 1024])
    conv_w = din("ffn_conv_w", [3, F]); conv_b = din("ffn_conv_b", [F])
    xs = din("xs", [TS, 1024])
    st_re = din("st_re", [NS, 2048]); st_im = din("st_im", [NS, 2048])
    st_conv = din("st_conv", [NS * 2, F])
    cmk = din("cmk", [NS, 256, 1024]); cmv = din("cmv", [NS, 256, 1024])
    cache_k = din("cache_k", [2560 * 128, 512]); cache_v = din("cache_v", [2560 * 128, 512])
    ptab = din("ptab", [NS, 16], I32)
    c_ident = din("c_ident", [128, 128])
    c_bd64 = din("c_bd64", [128, 128])
    c_mask16 = din("c_mask16", [128, 128])
    c_caus = din("c_caus", [128, 128])
    c_pwn = din("c_pwn", [NPW])
    c_kpos = din("c_kpos", [128])
    c_g2m = din("c_g2m", [128, 2])

    y_p = dout("y_p", [T, 1024]); k_p = dout("k_p", [T, 512]); v_p = dout("v_p", [T, 512])
    hre_p = dout("hre_p", [16, 128]); him_p = dout("him_p", [16, 128])
    conv_p = dout("conv_p", [2, F])
    mk_p = dout("mk_p", [256, 1024]); mv_p = dout("mv_p", [256, 1024])

    y_s = dout("y_s", [TS, 1024]); k_s = dout("k_s", [TS, 512]); v_s = dout("v_s", [TS, 512])
    hre_s = dout("hre_s", [NS, 2048]); him_s = dout("him_s", [NS, 2048])
    conv_s = dout("conv_s", [NS * 2, F])
    wg_s = dscr("wg_s", [128, NFC, 8, 128], BF16); wv_s = dscr("wv_s", [128, NFC, 8, 128], BF16)
    wd_s = dscr("wd_s", [128, 8, NFC, 128], BF16)
    mix_s = dscr("mix_s", [NTILES, 128, 8, NT], BF16)
    vn_s = dscr("vn_s", [4, NS * 4 * 132], BF16)

    es = ExitStack()
    with es:
        S = Sched(nc, es)

        def sb(stack, name, shape, dt):
            return TB(stack.enter_context(nc.sbuf_tensor(name, list(shape), dt)), S.buf(name))

        PS = [TB(es.enter_context(nc.psum_tensor("ps%d" % i, [128, 512], F32)), S.buf("ps%d" % i)) for i in range(8)]
        for p_ in PS:
            p_.b.excl = True
        psrr = [0]

        def bank():
            b = PS[psrr[0]]
            psrr[0] = (psrr[0] + 1) % 8
            return b

        SLOW = dict(allow_slow_non_contiguous=True)

        def mm(bk, lhsT, rhs, start, stop, reads, out=None, **kw):
            o = bk[:] if out is None else out
            S.op("pe", lambda e: e.matmul(o, lhsT=lhsT, rhs=rhs, start=start, stop=stop, **kw),
                 reads=reads, writes=[bk.b])

        def tr(bk, out, in_, ident, reads):
            S.op("pe", lambda e: e.transpose(out=out, in_=in_, identity=ident[:]), reads=reads + [ident.b],
                 writes=[bk.b])

        def act(out, in_, func, reads, writes, **kw):
            S.op("act", lambda e: e.activation(out=out, in_=in_, func=func, **kw), reads=reads, writes=writes)

        def tt(eng, out, in0, in1, op, reads, writes):
            S.op(eng, lambda e: e.tensor_tensor(out=out, in0=in0, in1=in1, op=op), reads=reads, writes=writes)

        def ts(eng, out, in0, s1, s2, op0, op1, reads, writes):
            if op1 is None:
                S.op(eng, lambda e: e.tensor_scalar(out=out, in0=in0, scalar1=s1, scalar2=None, op0=op0),
                     reads=reads, writes=writes)
            else:
                S.op(eng, lambda e: e.tensor_scalar(out=out, in0=in0, scalar1=s1, scalar2=s2, op0=op0, op1=op1),
                     reads=reads, writes=writes)

        def stt(out, in0, scalar, in1, op0, op1, reads, writes):
            S.op("dve", lambda e: e.scalar_tensor_tensor(out=out, in0=in0, scalar=scalar, in1=in1, op0=op0, op1=op1),
                 reads=reads, writes=writes)

        def cp(eng, out, in_, reads, writes):
            if eng == "act":
                S.op("act", lambda e: e.copy(out=out, in_=in_), reads=reads, writes=writes)
            else:
                S.op(eng, lambda e: e.tensor_copy(out=out, in_=in_), reads=reads, writes=writes)

        def memset(eng, ap, val, writes):
            S.op(eng, lambda e: e.memset(ap, val), writes=writes)

        def recip(out, in_, reads, writes):
            S.op("dve", lambda e: e.reciprocal(out=out, in_=in_), reads=reads, writes=writes)

        IDF = sb(es, "IDF", [128, 128], F32); IDB = sb(es, "IDB", [128, 128], BF16)
        ON1024 = sb(es, "ON1024", [128, 128], BF16); ON256 = sb(es, "ON256", [128, 128], BF16)
        ON128 = sb(es, "ON128", [128, 128], BF16); ONE1 = sb(es, "ONE1", [128, 128], BF16)
        BD64 = sb(es, "BD64", [128, 128], BF16)
        CAUS = sb(es, "CAUS", [128, 128], BF16)
        G1 = sb(es, "G1", [128, 8], F32); G2 = sb(es, "G2", [128, 8], F32); G3 = sb(es, "G3", [128, 8], F32)
        GM = sb(es, "GM", [128, 8], F32)
        QG = sb(es, "QG", [128, 1], F32); KG = sb(es, "KG", [128, 1], F32)
        SUBG = sb(es, "SUBG", [128, 1], F32)
        CQG = sb(es, "CQG", [128, 2], F32); CKG = sb(es, "CKG", [128, 2], F32)
        CW = sb(es, "CW", [128, 3, NFC], F32); CB = sb(es, "CB", [128, NFC], F32)
        KPOS = sb(es, "KPOS", [128, 1], F32)
        NLAM = sb(es, "NLAM", [128, 1], F32)
        LT = sb(es, "LT", [64, 4], F32)

        S.dma("sp", IDF[:], c_ident, writes=[IDF.b])
        S.dma("pool", IDB[:], c_ident, writes=[IDB.b])
        S.dma("pool", BD64[:], c_bd64, writes=[BD64.b])
        S.dma("pool", CAUS[:], c_caus, writes=[CAUS.b])
        memset("dve", ON1024[:], 1.0 / 1024, [ON1024.b]); memset("dve", ON256[:], 1.0 / 256, [ON256.b])
        memset("dve", ON128[:], 1.0 / 128, [ON128.b]); memset("dve", ONE1[:], 1.0, [ONE1.b])
        for (Gt, gsrc) in ((G1, ln1_g), (G2, ln2_g), (G3, ln3_g), (GM, memn_g)):
            S.dma("sp", Gt[:], gsrc.rearrange("(c p) -> p c", p=128), writes=[Gt.b], **SLOW)
        for (Gt, gsrc) in ((QG, qn_g), (KG, kn_g)):
            for hh in range(2):
                S.dma("sp", Gt[hh * 64:(hh + 1) * 64, :], gsrc.rearrange("(p o) -> p o", o=1), writes=[Gt.b], **SLOW)
        S.dma("sp", SUBG[:], subln_g.rearrange("(p o) -> p o", o=1), writes=[SUBG.b], **SLOW)
        S.dma("sp", CQG[:], caq_g.rearrange("(c p) -> p c", p=128), writes=[CQG.b], **SLOW)
        S.dma("sp", CKG[:], cak_g.rearrange("(c p) -> p c", p=128), writes=[CKG.b], **SLOW)
        for j in range(3):
            S.dma("sp", CW[:, j, :], conv_w[j].rearrange("(c p) -> p c", p=128), writes=[CW.b], **SLOW)
        S.dma("sp", CB[:], conv_b.rearrange("(c p) -> p c", p=128), writes=[CB.b], **SLOW)
        S.dma("sp", KPOS[:], c_kpos.rearrange("(p o) -> p o", o=1), writes=[KPOS.b], **SLOW)
        for i, src in enumerate((lq1, lk1, lq2, lk2)):
            S.dma("sp", LT[:, i:i + 1], src.rearrange("(p o) -> p o", o=1), writes=[LT.b], **SLOW)
        ts("dve", SUBG[:], SUBG[:], 1.0 - LAM0, None, ALU.mult, None, [SUBG.b], [SUBG.b])
        LP = sb(es, "LP", [64, 2], BF16)
        tt("dve", LP[:, 0:1], LT[:, 0:1], LT[:, 1:2], ALU.mult, [LT.b], [LP.b])
        tt("dve", LP[:, 1:2], LT[:, 2:3], LT[:, 3:4], ALU.mult, [LT.b], [LP.b])
        bk = bank()
        mm(bk, ONE1[0:64, :], LP[:, :], True, True, [ONE1.b, LP.b], out=bk[:, 0:2])
        LE = sb(es, "LE", [128, 2], F32)
        act(LE[:], bk[:, 0:2], AF.Exp, [bk.b], [LE.b])
        stt(NLAM[:], LE[:, 1:2], -LAM0, LE[:, 0:1], ALU.add, ALU.subtract, [LE.b], [NLAM.b])

        def convert_ffn_weights(stack):
            CIN = sb(stack, "CIN", [128, 4096], F32)
            COUT = sb(stack, "COUT", [128, 4096], BF16)
            for (dst, src) in ((wg_s, ffn_wg), (wv_s, ffn_wv)):
                for f0 in range(0, NFC, 4):
                    nf = min(4, NFC - f0)
                    cin = CIN[:, 0:8 * nf * 128].rearrange("p (kc x) -> p kc x", kc=8)
                    S.dma("pool", cin, src[:, f0 * 128:(f0 + nf) * 128].rearrange("(kc p) x -> p kc x", p=128),
                          writes=[CIN.b])
                    cout = COUT[:, 0:nf * 1024].rearrange("p (fc kc n) -> p fc kc n", kc=8, n=128)
                    cp("pool", cout, cin.rearrange("p kc (fc n) -> p fc kc n", n=128), [CIN.b], [COUT.b])
                    S.dma("pool", dst[:, f0:f0 + nf].rearrange("p fc kc n -> p (fc kc n)"), COUT[:, 0:nf * 1024],
                          reads=[COUT.b])
            for m in range(8):
                for f0 in range(0, NFC, 8):
                    f1 = min(f0 + 8, NFC)
                    S.dma("pool", CIN[:, f0 * 128:f1 * 128].rearrange("p (fc n) -> p fc n", n=128),
                          ffn_wd[f0 * 128:f1 * 128, m * 128:(m + 1) * 128].rearrange("(fc p) n -> p fc n", p=128),
                          writes=[CIN.b])
                cp("pool", COUT[:, 0:NFC * 128], CIN[:, 0:NFC * 128], [CIN.b], [COUT.b])
                S.dma("pool", wd_s[:, m].rearrange("p fc n -> p (fc n)"), COUT[:, 0:NFC * 128], reads=[COUT.b])

        def load_norm_tile(X, xsrc_ap, nblk, G, xnT, rs, ss, junk, raw_xT=None):
            S.dma("sp", X[:, 0:nblk, :], xsrc_ap, writes=[X.b])
            for j in range(nblk):
                act(junk[:], X[:, j, :], AF.Square, [X.b], [junk.b, ss.b], accum_out=ss[:, j:j + 1])
            act(rs[:, 0:nblk], ss[:, 0:nblk], AF.Sqrt, [ss.b], [rs.b], scale=1.0 / 1024, bias=EPS)
            recip(rs[:, 0:nblk], rs[:, 0:nblk], [rs.b], [rs.b])
            if raw_xT is not None:
                for c in range(8):
                    bk = bank()
                    for j in range(nblk):
                        tr(bk, bk[:, j * 128:(j + 1) * 128], X[:, j, c * 128:(c + 1) * 128], IDF, [X.b])
                    cp("act", raw_xT[:, c, 0:nblk * 128], bk[:, 0:nblk * 128], [bk.b], [raw_xT.b])
            for j in range(nblk):
                ts("dve", X[:, j, :], X[:, j, :], rs[:, j:j + 1], None, ALU.mult, None, [X.b, rs.b], [X.b])
            for c in range(8):
                bk = bank()
                for j in range(nblk):
                    tr(bk, bk[:, j * 128:(j + 1) * 128], X[:, j, c * 128:(c + 1) * 128], IDF, [X.b])
                ts("dve", xnT[:, c, 0:nblk * 128], bk[:, 0:nblk * 128], G[:, c:c + 1], None, ALU.mult, None,
                   [bk.b, G.b], [xnT.b])

        def ln_fm(xT, G, xnT, n, sq, rstd):
            bk = bank()
            for c in range(8):
                act(sq[:, c % 2, 0:n], xT[:, c, 0:n], AF.Square, [xT.b], [sq.b])
                mm(bk, ON1024[:], sq[:, c % 2, 0:n], c == 0, c == 7, [ON1024.b, sq.b], out=bk[:, 0:n])
            act(rstd[:, 0:n], bk[:, 0:n], AF.Sqrt, [bk.b], [rstd.b], bias=EPS)
            recip(rstd[:, 0:n], rstd[:, 0:n], [rstd.b], [rstd.b])
            for c in range(8):
                stt(xnT[:, c, 0:n], xT[:, c, 0:n], G[:, c:c + 1], rstd[:, 0:n], ALU.mult, ALU.mult,
                    [xT.b, G.b, rstd.b], [xnT.b])

        def store_tm(src_fm, nchunks, n, dst_ap, stage, ident=IDF):
            nblk = (n + 127) // 128
            for j in range(nblk):
                w = min(128, n - j * 128)
                for c0 in range(0, nchunks, 4):
                    bk = bank()
                    for c in range(c0, min(c0 + 4, nchunks)):
                        tr(bk, bk[0:w, (c - c0) * 128:(c - c0 + 1) * 128], src_fm[:, c, j * 128:j * 128 + w], ident,
                           [src_fm.b])
                    nn = (min(c0 + 4, nchunks) - c0) * 128
                    cp("act", stage[0:w, j, c0 * 128:c0 * 128 + nn], bk[0:w, 0:nn], [bk.b], [stage.b])
            if n % 128 == 0:
                S.dma("sp", dst_ap.rearrange("(j p) f -> p j f", p=128), stage[:, 0:nblk, 0:nchunks * 128],
                      reads=[stage.b])
            else:
                S.dma("sp", dst_ap, stage[0:n, 0, 0:nchunks * 128], reads=[stage.b])


        xnTs = sb(es, "xnTs", [128, 8, TS], BF16)
        xsT = sb(es, "xsT", [128, 8, TS], F32)
        MIXs = sb(es, "MIXs", [128, 8, TS], BF16)
        usT = sb(es, "usT", [128, 4, TS], BF16)
        QTs = sb(es, "QTs", [128, 4, TS], BF16); KTs = sb(es, "KTs", [128, 4, TS], BF16)
        if 'S' in PH:
            with ExitStack() as ph:
                Xs_ = sb(ph, "Xs_", [128, 1024], F32)
                rs = sb(ph, "rs_s", [128, 1], F32); ss = sb(ph, "ss_s", [128, 1], F32)
                junk = sb(ph, "junk_s", [128, 1024], F32)
                S.dma("sp", Xs_[0:TS, :], xs, writes=[Xs_.b])
                act(junk[0:TS, :], Xs_[0:TS, :], AF.Square, [Xs_.b], [junk.b, ss.b], accum_out=ss[0:TS, 0:1])
                act(rs[0:TS, :], ss[0:TS, :], AF.Sqrt, [ss.b], [rs.b], scale=1.0 / 1024, bias=EPS)
                recip(rs[0:TS, :], rs[0:TS, :], [rs.b], [rs.b])
                for c in range(8):
                    bk = bank()
                    tr(bk, bk[:, 0:TS], Xs_[0:TS, c * 128:(c + 1) * 128], TB(IDF.t[0:TS, 0:TS], IDF.b), [Xs_.b])
                    cp("act", xsT[:, c, :], bk[:, 0:TS], [bk.b], [xsT.b])
                ts("dve", Xs_[0:TS, :], Xs_[0:TS, :], rs[0:TS, 0:1], None, ALU.mult, None, [Xs_.b, rs.b], [Xs_.b])
                for c in range(8):
                    bk = bank()
                    tr(bk, bk[:, 0:TS], Xs_[0:TS, c * 128:(c + 1) * 128], TB(IDF.t[0:TS, 0:TS], IDF.b), [Xs_.b])
                    ts("dve", xnTs[:, c, :], bk[:, 0:TS], G1[:, c:c + 1], None, ALU.mult, None, [bk.b, G1.b], [xnTs.b])
                S.barrier()

        MKT = sb(es, "MKT", [128, 8, 256], BF16)
        MV = sb(es, "MV", [128, 2, 1024], BF16)
        with ExitStack() as ph:
          if 'M' in PH:
            Xm = sb(ph, "Xm", [128, 2, 1024], F32)
            xnTm = sb(ph, "xnTm", [128, 8, 256], BF16)
            rs = sb(ph, "rs_m", [128, 4], F32); ss = sb(ph, "ss_m", [128, 4], F32)
            junk = sb(ph, "junk_m", [128, 1024], F32)
            Wk = sb(ph, "Wk", [128, 8, 1024], BF16); Wv = sb(ph, "Wv", [128, 8, 1024], BF16)
            mkraw = sb(ph, "mkraw", [128, 8, 256], F32)
            sqm = sb(ph, "sqm", [128, 2, 256], BF16); rstdm = sb(ph, "rstdm", [128, 256], F32)
            stg = sb(ph, "stg_m", [128, 2, 1024], F32)
            S.dma("pool", Wk[:], ca_wk.rearrange("(c p) n -> p c n", p=128), writes=[Wk.b])
            S.dma("pool", Wv[:], ca_wv.rearrange("(c p) n -> p c n", p=128), writes=[Wv.b])
            KS = int(os.environ.get('KSTOP', '9'))
            load_norm_tile(Xm, memp.rearrange("(j p) f -> p j f", p=128), 2, GM, xnTm, rs, ss, junk)
            for m in (range(8) if KS >= 2 else ()):
                bk = bank()
                for kc in range(8):
                    mm(bk, Wk[:, kc, m * 128:(m + 1) * 128], xnTm[:, kc, :], kc == 0, kc == 7, [Wk.b, xnTm.b],
                       out=bk[:, 0:256])
                cp("act", mkraw[:, m, :], bk[:, 0:256], [bk.b], [mkraw.b])
            for h in (range(4) if KS >= 3 else ()):
                bk = bank()
                for dh in range(2):
                    act(sqm[:, dh, :], mkraw[:, 2 * h + dh, :], AF.Square, [mkraw.b], [sqm.b])
                    mm(bk, ON256[:], sqm[:, dh, :], dh == 0, dh == 1, [ON256.b, sqm.b], out=bk[:, 0:256])
                act(rstdm[:], bk[:, 0:256], AF.Sqrt, [bk.b], [rstdm.b], bias=EPS)
                recip(rstdm[:], rstdm[:], [rstdm.b], [rstdm.b])
                for dh in range(2):
                    stt(mkraw[:, 2 * h + dh, :], mkraw[:, 2 * h + dh, :], CKG[:, dh:dh + 1], rstdm[:], ALU.mult,
                        ALU.mult, [mkraw.b, CKG.b, rstdm.b], [mkraw.b])
                    cp("act", MKT[:, 2 * h + dh, :], mkraw[:, 2 * h + dh, :], [mkraw.b], [MKT.b])
            if KS >= 4:
                store_tm(mkraw, 8, 256, mk_p, stg)
            for j in (range(2) if KS >= 5 else ()):
                for half in range(2):
                    bk = bank()
                    for kc in range(8):
                        mm(bk, xnTm[:, kc, j * 128:(j + 1) * 128], Wv[:, kc, half * 512:(half + 1) * 512], kc == 0,
                           kc == 7, [Wv.b, xnTm.b])
                    cp("act", stg[:, j, half * 512:(half + 1) * 512], bk[:], [bk.b], [stg.b])
                    cp("dve", MV[:, j, half * 512:(half + 1) * 512], bk[:], [bk.b], [MV.b])
            S.dma("sp", mv_p.rearrange("(j p) f -> p j f", p=128), stg[:], reads=[stg.b])
            S.barrier()


        s5 = ExitStack()
        if 'B' in PH:
            TWO_PI = 2.0 * math.pi
            PI_S = 3.1415925
            ZT = sb(s5, "ZT", [128, 4, 8, 2, 128], BF16)
            CLR = sb(s5, "CLR", [128, 16, 8, 32], BF16)
            CLI = sb(s5, "CLI", [128, 16, 8, 32], BF16)
            BDT = sb(s5, "BDT", [128, 4, 8, 128], BF16)
            LRE = sb(s5, "LRE", [128, 16, NPW], F32); LIM = sb(s5, "LIM", [128, 16, NPW], F32)
            with ExitStack() as tb:
                AR = sb(tb, "AR", [128, 16], F32); AI = sb(tb, "AI", [128, 16], F32)
                DT = sb(tb, "DT", [128, 16], F32)
                ARD = sb(tb, "ARD", [128, 16], F32); TH = sb(tb, "TH", [128, 16], F32)
                PWN = sb(tb, "PWN", [128, NPW], F32)
                ANG = sb(tb, "ANG", [128, 16, NPW], F32); R = sb(tb, "Rr", [128, 16, NPW], F32)
                KF = sb(tb, "KF", [128, 16, NPW], F32); KI = sb(tb, "KI", [128, 16, NPW], I32)
                MG = sb(tb, "MG", [128, 16, NPW], F32)
                S.dma("sp", AR[:], a_re.rearrange("(q g2) p -> (g2 p) q", g2=2), writes=[AR.b], **SLOW)
                S.dma("sp", AI[:], a_im.rearrange("(q g2) p -> (g2 p) q", g2=2), writes=[AI.b], **SLOW)
                for g2 in range(2):
                    S.dma("sp", DT[g2 * 64:(g2 + 1) * 64, :],
                          log_dt.rearrange("(q g2) -> g2 q", g2=2)[g2:g2 + 1, :].to_broadcast([64, 16]),
                          writes=[DT.b], **SLOW)
                S.dma("sp", PWN[:], c_pwn.rearrange("(o n) -> o n", o=1).to_broadcast([128, NPW]), writes=[PWN.b],
                      **SLOW)
                act(DT[:], DT[:], AF.Exp, [DT.b], [DT.b])
                tt("dve", ARD[:], AR[:], DT[:], ALU.mult, [AR.b, DT.b], [ARD.b])
                tt("dve", TH[:], AI[:], DT[:], ALU.mult, [AI.b, DT.b], [TH.b])
                bshape = [128, 16, NPW]
                tt("dve", MG[:], ARD[:, :].unsqueeze(2).to_broadcast(bshape), PWN[:, :].unsqueeze(1).to_broadcast(bshape),
                   ALU.mult, [ARD.b, PWN.b], [MG.b])
                act(MG[:], MG[:], AF.Exp, [MG.b], [MG.b])
                tt("dve", ANG[:], TH[:, :].unsqueeze(2).to_broadcast(bshape), PWN[:, :].unsqueeze(1).to_broadcast(bshape),
                   ALU.mult, [TH.b, PWN.b], [ANG.b])
                ts("dve", KF[:], ANG[:], 1.0 / TWO_PI, 0.5, ALU.mult, ALU.add, [ANG.b], [KF.b])
                cp("dve", KI[:], KF[:], [KF.b], [KI.b])
                cp("dve", KF[:], KI[:], [KI.b], [KF.b])
                stt(R[:], KF[:], -TWO_PI, ANG[:], ALU.mult, ALU.add, [KF.b, ANG.b], [R.b])

                def wrap(Rt):
                    ts("dve", KF[:], Rt[:], -math.pi, None, ALU.is_lt, None, [Rt.b], [KF.b])
                    stt(Rt[:], KF[:], TWO_PI, Rt[:], ALU.mult, ALU.add, [KF.b, Rt.b], [Rt.b])
                    ts("dve", KF[:], Rt[:], math.pi, None, ALU.is_gt, None, [Rt.b], [KF.b])
                    stt(Rt[:], KF[:], -TWO_PI, Rt[:], ALU.mult, ALU.add, [KF.b, Rt.b], [Rt.b])
                    ts("dve", Rt[:], Rt[:], -PI_S, PI_S, ALU.max, ALU.min, [Rt.b], [Rt.b])
                wrap(R)
                act(LIM[:], R[:], AF.Sin, [R.b], [LIM.b])
                ts("dve", R[:], R[:], math.pi / 2, None, ALU.add, None, [R.b], [R.b])
                wrap(R)
                act(LRE[:], R[:], AF.Sin, [R.b], [LRE.b])
                tt("dve", LRE[:], LRE[:], MG[:], ALU.mult, [LRE.b, MG.b], [LRE.b])
                tt("dve", LIM[:], LIM[:], MG[:], ALU.mult, [LIM.b, MG.b], [LIM.b])
                NRE = sb(tb, "NRE", [128, 16], F32); DEN = sb(tb, "DEN", [128, 16], F32)
                FRE = sb(tb, "FRE", [128, 16], F32); FIM = sb(tb, "FIM", [128, 16], F32)
                T1 = sb(tb, "T1", [128, 16], F32)
                ts("dve", NRE[:], LRE[:, :, 1], -1.0, None, ALU.add, None, [LRE.b], [NRE.b])
                tt("dve", DEN[:], AR[:], AR[:], ALU.mult, [AR.b], [DEN.b])
                tt("dve", T1[:], AI[:], AI[:], ALU.mult, [AI.b], [T1.b])
                tt("dve", DEN[:], DEN[:], T1[:], ALU.add, [DEN.b, T1.b], [DEN.b])
                recip(DEN[:], DEN[:], [DEN.b], [DEN.b])
                tt("dve", FRE[:], NRE[:], AR[:], ALU.mult, [NRE.b, AR.b], [FRE.b])
                tt("dve", T1[:], LIM[:, :, 1], AI[:], ALU.mult, [LIM.b, AI.b], [T1.b])
                tt("dve", FRE[:], FRE[:], T1[:], ALU.add, [FRE.b, T1.b], [FRE.b])
                tt("dve", FRE[:], FRE[:], DEN[:], ALU.mult, [FRE.b, DEN.b], [FRE.b])
                tt("dve", FIM[:], LIM[:, :, 1], AR[:], ALU.mult, [LIM.b, AR.b], [FIM.b])
                tt("dve", T1[:], NRE[:], AI[:], ALU.mult, [NRE.b, AI.b], [T1.b])
                tt("dve", FIM[:], FIM[:], T1[:], ALU.subtract, [FIM.b, T1.b], [FIM.b])
                tt("dve", FIM[:], FIM[:], DEN[:], ALU.mult, [FIM.b, DEN.b], [FIM.b])
                BR = sb(tb, "BR", [128, 16, 16], F32); BI = sb(tb, "BI", [128, 16, 16], F32)
                BBR = sb(tb, "BBR", [128, 16, 16], F32); BBI = sb(tb, "BBI", [128, 16, 16], F32)
                T2 = sb(tb, "T2", [128, 16, 16], F32)
                S.dma("sp", BR[:], b_re.rearrange("(q g2) p c -> (g2 p) q c", g2=2), writes=[BR.b], **SLOW)
                S.dma("sp", BI[:], b_im.rearrange("(q g2) p c -> (g2 p) q c", g2=2), writes=[BI.b], **SLOW)
                s3 = [128, 16, 16]
                fr = FRE[:, :].unsqueeze(2).to_broadcast(s3); fi = FIM[:, :].unsqueeze(2).to_broadcast(s3)
                tt("dve", BBR[:], BR[:], fr, ALU.mult, [BR.b, FRE.b], [BBR.b])
                tt("dve", T2[:], BI[:], fi, ALU.mult, [BI.b, FIM.b], [T2.b])
                tt("dve", BBR[:], BBR[:], T2[:], ALU.subtract, [BBR.b, T2.b], [BBR.b])
                tt("dve", BBI[:], BI[:], fr, ALU.mult, [BI.b, FRE.b], [BBI.b])
                tt("dve", T2[:], BR[:], fi, ALU.mult, [BR.b, FIM.b], [T2.b])
                tt("dve", BBI[:], BBI[:], T2[:], ALU.add, [BBI.b, T2.b], [BBI.b])
                ZR = sb(tb, "ZR", [128, 16, 8, 16], F32); ZI = sb(tb, "ZI", [128, 16, 8, 16], F32)
                T3 = sb(tb, "T3", [128, 16, 8, 16], F32)
                s4 = [128, 16, 8, 16]
                lr = LRE[:, :, 0:8].unsqueeze(3).to_broadcast(s4); li = LIM[:, :, 0:8].unsqueeze(3).to_broadcast(s4)
                br_ = BBR[:, :, :].unsqueeze(2).to_broadcast(s4); bi_ = BBI[:, :, :].unsqueeze(2).to_broadcast(s4)
                tt("dve", ZR[:], lr, br_, ALU.mult, [LRE.b, BBR.b], [ZR.b])
                tt("dve", T3[:], li, bi_, ALU.mult, [LIM.b, BBI.b], [T3.b])
                tt("dve", ZR[:], ZR[:], T3[:], ALU.subtract, [ZR.b, T3.b], [ZR.b])
                tt("dve", ZI[:], lr, bi_, ALU.mult, [LRE.b, BBI.b], [ZI.b])
                tt("dve", T3[:], li, br_, ALU.mult, [LIM.b, BBR.b], [T3.b])
                tt("dve", ZI[:], ZI[:], T3[:], ALU.add, [ZI.b, T3.b], [ZI.b])
                E4 = sb(tb, "E4", [128, 4, 8, 2, 128], F32)
                memset("pool", E4[:], 0.0, [E4.b])
                for ri, Zt in enumerate((ZR, ZI)):
                    for gc in range(4):
                        for g2 in range(2):
                            pr = slice(g2 * 64, (g2 + 1) * 64)
                            dst = E4[pr, gc, :, ri, :].rearrange("p t (j x) -> p t j x", x=32)[:, :, :, g2 * 16:(g2 + 1) * 16]
                            src = Zt[pr, gc * 4:(gc + 1) * 4, :, :].rearrange("p j t c -> p t j c")
                            cp("pool", dst, src, [Zt.b], [E4.b])
                for gc in range(4):
                    for ri in range(2):
                        for s0 in (0, 4):
                            bk = bank()
                            for sp_ in range(s0, s0 + 4):
                                tr(bk, bk[:, (sp_ - s0) * 128:(sp_ - s0 + 1) * 128], E4[:, gc, 7 - sp_, ri, :], IDF, [E4.b])
                            cp("act", ZT[:, gc, s0:s0 + 4, ri, :], bk[:].rearrange("p (s n) -> p s n", n=128), [bk.b],
                               [ZT.b])
                CN = sb(tb, "CN", [128, 2, 4, 64], F32)
                CE = sb(tb, "CE", [128, 2, 4, 2, 64], F32)
                CTR = sb(tb, "CTR", [128, 4, 128], F32); CTI = sb(tb, "CTI", [128, 4, 128], F32)
                CTIN = sb(tb, "CTIN", [128, 4, 128], F32)
                G2M = sb(tb, "G2M", [128, 2], F32)
                S.dma("sp", G2M[:], c_g2m, writes=[G2M.b])
                S.dma("sp", CN[:, 0, :, :], c_re.rearrange("(gc r) c p -> (r c) gc p", gc=4), writes=[CN.b], **SLOW)
                S.dma("sp", CN[:, 1, :, :], c_im.rearrange("(gc r) c p -> (r c) gc p", gc=4), writes=[CN.b], **SLOW)
                for ri in range(2):
                    for g2 in range(2):
                        ts("dve", CE[:, ri, :, g2, :], CN[:, ri, :, :], G2M[:, g2:g2 + 1], None, ALU.mult, None,
                           [CN.b, G2M.b], [CE.b])
                for ri, CTt in enumerate((CTR, CTI)):
                    bk = bank()
                    for gc in range(4):
                        tr(bk, bk[:, gc * 128:(gc + 1) * 128], CE[:, ri, gc, :, :].rearrange("p a b -> p (a b)"), IDF,
                           [CE.b])
                    cp("act", CTt[:], bk[:].rearrange("p (g n) -> p g n", n=128), [bk.b], [CTt.b])
                ts("dve", CTIN[:], CTI[:], -1.0, None, ALU.mult, None, [CTI.b], [CTIN.b])
                s5s = [128, 16, 8, 32]
                T4 = sb(tb, "T4", [128, 16, 8, 32], F32); T5 = sb(tb, "T5", [128, 16, 8, 32], F32)
                ctr = CTR[:, :, :].rearrange("p g (j x) -> p (g j) x", x=32).unsqueeze(2).to_broadcast(s5s)
                cti = CTI[:, :, :].rearrange("p g (j x) -> p (g j) x", x=32).unsqueeze(2).to_broadcast(s5s)
                l1r = LRE[:, :, 1:9].unsqueeze(3).to_broadcast(s5s); l1i = LIM[:, :, 1:9].unsqueeze(3).to_broadcast(s5s)
                tt("dve", T4[:], ctr, l1r, ALU.mult, [CTR.b, LRE.b], [T4.b])
                tt("dve", T5[:], cti, l1i, ALU.mult, [CTI.b, LIM.b], [T5.b])
                tt("dve", CLR[:], T4[:], T5[:], ALU.subtract, [T4.b, T5.b], [CLR.b])
                tt("dve", T4[:], ctr, l1i, ALU.mult, [CTR.b, LIM.b], [T4.b])
                tt("dve", T5[:], cti, l1r, ALU.mult, [CTI.b, LRE.b], [T5.b])
                tt("dve", T4[:], T4[:], T5[:], ALU.add, [T4.b, T5.b], [T4.b])
                ts("dve", CLI[:], T4[:], -1.0, None, ALU.mult, None, [T4.b], [CLI.b])
                MASK16 = sb(tb, "MASK16", [128, 128], F32); DCOL = sb(tb, "DCOL", [128, 4], F32)
                T6 = sb(tb, "T6", [128, 128], F32)
                S.dma("sp", MASK16[:], c_mask16, writes=[MASK16.b])
                S.dma("sp", DCOL[:], ssm_d.rearrange("(gc g8) c -> (g8 c) gc", gc=4), writes=[DCOL.b], **SLOW)
                for gc in range(4):
                    for tau in range(8):
                        bk = bank()
                        mm(bk, E4[:, gc, tau, 0, :], CTR[:, gc, :], True, False, [E4.b, CTR.b], out=bk[:, 0:128])
                        mm(bk, E4[:, gc, tau, 1, :], CTIN[:, gc, :], False, True, [E4.b, CTIN.b], out=bk[:, 0:128])
                        if tau == 0:
                            tt("dve", T6[:], bk[:, 0:128], MASK16[:], ALU.mult, [bk.b, MASK16.b], [T6.b])
                            stt(BDT[:, gc, 0, :], IDF[:], DCOL[:, gc:gc + 1], T6[:], ALU.mult, ALU.add,
                                [IDF.b, DCOL.b, T6.b], [BDT.b])
                        else:
                            tt("dve", BDT[:, gc, tau, :], bk[:, 0:128], MASK16[:], ALU.mult, [bk.b, MASK16.b], [BDT.b])
                S.barrier()

            pb = ExitStack()
            uT = sb(pb, "uT", [128, 4, T], BF16)
            with ExitStack() as ph:
                X = sb(ph, "Xu", [128, 4, 1024], F32)
                xnT = sb(ph, "xnTu", [128, 8, NT], BF16)
                rs = sb(ph, "rsu", [128, 4], F32); ss = sb(ph, "ssu", [128, 4], F32)
                junk = sb(ph, "junku", [128, 1024], F32)
                Wu = sb(ph, "Wu", [128, 8, 512], BF16)
                S.dma("pool", Wu[:], w_in[:, 0:512].rearrange("(c p) n -> p c n", p=128), writes=[Wu.b])
                for it in range(NTILES):
                    t0 = it * NT
                    load_norm_tile(X, xp[t0:t0 + NT, :].rearrange("(j p) f -> p j f", p=128), 4, G1, xnT, rs, ss, junk)
                    for m in range(4):
                        bk = bank()
                        for kc in range(8):
                            mm(bk, Wu[:, kc, m * 128:(m + 1) * 128], xnT[:, kc, :], kc == 0, kc == 7, [Wu.b, xnT.b])
                        cp("act", uT[:, m, t0:t0 + NT], bk[:], [bk.b], [uT.b])
                if 'S' in PH:
                    for m in range(4):
                        bk = bank()
                        for kc in range(8):
                            mm(bk, Wu[:, kc, m * 128:(m + 1) * 128], xnTs[:, kc, :], kc == 0, kc == 7, [Wu.b, xnTs.b],
                               out=bk[:, 0:TS])
                        cp("act", usT[:, m, :], bk[:, 0:TS], [bk.b], [usT.b])
                S.barrier()
            if 'S' in PH:
              with ExitStack() as ph:
                STt = sb(ph, "STt", [NS, 2, 2048], F32)
                S0 = sb(ph, "S0", [128, 2, 16, NS], F32); S0b = sb(ph, "S0b", [128, 2, 16, NS], BF16)
                SN = sb(ph, "SN", [128, 2, 16, NS], F32)
                W1 = sb(ph, "W1", [128, 16, NS], F32); W2 = sb(ph, "W2", [128, 16, NS], F32)
                hst = sb(ph, "hst", [NS, 2, 2048], F32)
                Y2s = sb(ph, "Y2s", [128, TS], F32); Y3s = sb(ph, "Y3s", [128, TS], F32)
                Gts = sb(ph, "Gts", [128, 4, TS], F32); Gbs = sb(ph, "Gbs", [128, 4, TS], BF16)
                GLUWs = sb(ph, "GLUWs", [128, 4, 512], BF16)
                S.dma("pool", GLUWs[:], glu_w.rearrange("(c p) n -> p c n", p=128), writes=[GLUWs.b])
                S.dma("sp", STt[:, 0, :], st_re, writes=[STt.b])
                S.dma("sp", STt[:, 1, :], st_im, writes=[STt.b])
                ID16 = TB(IDF.t[0:NS, 0:NS], IDF.b)
                for ri in range(2):
                    bk = bank()
                    for q16 in range(16):
                        tr(bk, bk[:, q16 * NS:(q16 + 1) * NS], STt[:, ri, q16 * 128:(q16 + 1) * 128], ID16, [STt.b])
                    cp("act", S0[:, ri, :, :], bk[:, 0:16 * NS].rearrange("p (q s) -> p q s", s=NS), [bk.b], [S0.b])
                    cp("dve", S0b[:, ri, :, :], S0[:, ri, :, :], [S0.b], [S0b.b])
                for q16 in range(16):
                    gc, j = q16 // 4, q16 % 4
                    pr = slice(32 * j, 32 * j + 32)
                    uv = usT[:, gc, :].rearrange("p (n l) -> p l n", l=4)
                    bk = bank()
                    for ri in range(2):
                        for sp_ in range(4):
                            mm(bk, ZT[pr, gc, 4 + sp_, ri, :], uv[pr, sp_, :], sp_ == 0, sp_ == 3, [ZT.b, usT.b],
                               out=bk[:, ri * NS:(ri + 1) * NS], tile_position=(32 * j, 0))
                    cp("act", SN[:, :, q16, :], bk[:, 0:2 * NS].rearrange("p (r s) -> p r s", s=NS), [bk.b], [SN.b])
                s3 = [128, 16, NS]
                l4r = LRE[:, :, 4:5].to_broadcast(s3); l4i = LIM[:, :, 4:5].to_broadcast(s3)
                tt("dve", W1[:], S0[:, 0, :, :], l4r, ALU.mult, [S0.b, LRE.b], [W1.b])
                tt("dve", W2[:], S0[:, 1, :, :], l4i, ALU.mult, [S0.b, LIM.b], [W2.b])
                tt("dve", W1[:], W1[:], W2[:], ALU.subtract, [W1.b, W2.b], [W1.b])
                tt("dve", SN[:, 0, :, :], SN[:, 0, :, :], W1[:], ALU.add, [SN.b, W1.b], [SN.b])
                tt("dve", W1[:], S0[:, 0, :, :], l4i, ALU.mult, [S0.b, LIM.b], [W1.b])
                tt("dve", W2[:], S0[:, 1, :, :], l4r, ALU.mult, [S0.b, LRE.b], [W2.b])
                tt("dve", W1[:], W1[:], W2[:], ALU.add, [W1.b, W2.b], [W1.b])
                tt("dve", SN[:, 1, :, :], SN[:, 1, :, :], W1[:], ALU.add, [SN.b, W1.b], [SN.b])
                for ri in range(2):
                    for q0 in range(0, 16, 4):
                        bk = bank()
                        for q16 in range(q0, q0 + 4):
                            tr(bk, bk[0:NS, (q16 - q0) * 128:(q16 - q0 + 1) * 128], SN[:, ri, q16, :], IDF, [SN.b])
                        cp("act", hst[:, ri, q0 * 128:(q0 + 4) * 128], bk[0:NS, :], [bk.b], [hst.b])
                S.dma("sp", hre_s, hst[:, 0, :], reads=[hst.b])
                S.dma("sp", him_s, hst[:, 1, :], reads=[hst.b])
                for gc in range(4):
                    bk = bank()
                    uv = usT[:, gc, :].rearrange("p (n l) -> p l n", l=4)
                    ov = bk[:, 0:TS].rearrange("p (n l) -> p l n", l=4)
                    for tp in range(4):
                        for sp_ in range(tp + 1):
                            mm(bk, BDT[:, gc, tp - sp_, :], uv[:, sp_, :], sp_ == 0, False, [BDT.b, usT.b],
                               out=ov[:, tp, :])
                        for j in range(4):
                            q16 = gc * 4 + j
                            for ri, CLt in enumerate((CLR, CLI)):
                                mm(bk, CLt[:, q16, tp, :], S0b[:, ri, q16, :], False, (j == 3 and ri == 1),
                                   [CLt.b, S0b.b], out=ov[32 * j:32 * j + 32, tp, :], tile_position=(0, 32 * j))
                    act(Y2s[:], bk[:, 0:TS], AF.Square, [bk.b], [Y2s.b])
                    ts("dve", Y2s[:], Y2s[:], 0.044715, 1.0, ALU.mult, ALU.add, [Y2s.b], [Y2s.b])
                    tt("dve", Y3s[:], Y2s[:], bk[:, 0:TS], ALU.mult, [Y2s.b, bk.b], [Y3s.b])
                    act(Y3s[:], Y3s[:], AF.Sigmoid, [Y3s.b], [Y3s.b], scale=2.0 * math.sqrt(2.0 / math.pi))
                    tt("dve", Gts[:, gc, :], Y3s[:], bk[:, 0:TS], ALU.mult, [Y3s.b, bk.b], [Gts.b])
                    cp("act", Gbs[:, gc, :], Gts[:, gc, :], [Gts.b], [Gbs.b])
                for m in range(4):
                    bk = bank()
                    for kc in range(4):
                        mm(bk, GLUWs[:, kc, m * 128:(m + 1) * 128], Gbs[:, kc, :], kc == 0, kc == 3, [GLUWs.b, Gbs.b],
                           out=bk[:, 0:TS])
                    act(Y3s[:], bk[:, 0:TS], AF.Sigmoid, [bk.b], [Y3s.b])
                    tt("dve", MIXs[:, m, :], Y3s[:], Gts[:, m, :], ALU.mult, [Y3s.b, Gts.b], [MIXs.b])
                S.barrier()
            SW = sb(pb, "SW", [128, 2, 8, NCH], F32)
            SP = sb(pb, "SPv", [128, 2, 16, NCH], BF16)
            EE = sb(pb, "EE", [128, 2, 16, NSC], F32)
            TA = sb(pb, "TA", [128, 8, NSC], F32); TBt = sb(pb, "TBt", [128, 8, NSC], F32)
            U1 = sb(pb, "U1", [128, 8], F32); U2 = sb(pb, "U2", [128, 8], F32)
            V1 = sb(pb, "V1", [128, 8, SC], F32); V2 = sb(pb, "V2", [128, 8, SC], F32)
            memset("dve", SP[:, :, :, 0:1], 0.0, [SP.b])
            for hq in range(2):
                qs = slice(8 * hq, 8 * hq + 8)
                for q8 in range(8):
                    q16 = 8 * hq + q8
                    gc, j = q16 // 4, q16 % 4
                    pr = slice(32 * j, 32 * j + 32)
                    uv = uT[:, gc, :].rearrange("p (n l) -> p l n", l=L)
                    for ri in range(2):
                        bk = bank()
                        for sp_ in range(L):
                            mm(bk, ZT[pr, gc, sp_, ri, :], uv[pr, sp_, :], sp_ == 0, sp_ == L - 1, [ZT.b, uT.b],
                               tile_position=(32 * j, 0))
                        cp("act" if ri == 0 else "dve", SW[:, ri, q8, :], bk[:], [bk.b], [SW.b])
                Sv = SW[:, :, :, :].rearrange("p r q (s i) -> p r q s i", i=SC)
                sA = [128, 8, NSC]
                a8r = LRE[:, qs, 8:9].to_broadcast(sA); a8i = LIM[:, qs, 8:9].to_broadcast(sA)
                for i in range(1, SC):
                    pre_r = Sv[:, 0, :, :, i - 1]; pre_i = Sv[:, 1, :, :, i - 1]
                    tt("dve", TA[:], pre_r, a8r, ALU.mult, [SW.b, LRE.b], [TA.b])
                    tt("dve", TBt[:], pre_i, a8i, ALU.mult, [SW.b, LIM.b], [TBt.b])
                    tt("dve", TA[:], TA[:], TBt[:], ALU.subtract, [TA.b, TBt.b], [TA.b])
                    tt("dve", TBt[:], pre_i, a8r, ALU.mult, [SW.b, LRE.b], [TBt.b])
                    tt("dve", Sv[:, 0, :, :, i], Sv[:, 0, :, :, i], TA[:], ALU.add, [SW.b, TA.b], [SW.b])
                    tt("dve", TA[:], pre_r, a8i, ALU.mult, [SW.b, LIM.b], [TA.b])
                    tt("dve", TA[:], TA[:], TBt[:], ALU.add, [TA.b, TBt.b], [TA.b])
                    tt("dve", Sv[:, 1, :, :, i], Sv[:, 1, :, :, i], TA[:], ALU.add, [SW.b, TA.b], [SW.b])
                cp("dve", EE[:, :, qs, 0], Sv[:, :, :, 0, SC - 1], [SW.b], [EE.b])
                for sc in range(1, NSC):
                    er = EE[:, 0, qs, sc - 1]; ei = EE[:, 1, qs, sc - 1]
                    tt("dve", U1[:], er, LRE[:, qs, 23], ALU.mult, [EE.b, LRE.b], [U1.b])
                    tt("dve", U2[:], ei, LIM[:, qs, 23], ALU.mult, [EE.b, LIM.b], [U2.b])
                    tt("dve", U1[:], U1[:], U2[:], ALU.subtract, [U1.b, U2.b], [U1.b])
                    tt("dve", EE[:, 0, qs, sc], U1[:], Sv[:, 0, :, sc, SC - 1], ALU.add, [U1.b, SW.b], [EE.b])
                    tt("dve", U1[:], er, LIM[:, qs, 23], ALU.mult, [EE.b, LIM.b], [U1.b])
                    tt("dve", U2[:], ei, LRE[:, qs, 23], ALU.mult, [EE.b, LRE.b], [U2.b])
                    tt("dve", U1[:], U1[:], U2[:], ALU.add, [U1.b, U2.b], [U1.b])
                    tt("dve", EE[:, 1, qs, sc], U1[:], Sv[:, 1, :, sc, SC - 1], ALU.add, [U1.b, SW.b], [EE.b])
                cp("act", SP[:, 0, qs, 1:SC], SW[:, 0, :, 0:SC - 1], [SW.b], [SP.b])
                cp("act", SP[:, 1, qs, 1:SC], SW[:, 1, :, 0:SC - 1], [SW.b], [SP.b])
                sC = [128, 8, SC]
                PR = LRE[:, qs, 8:24]; PI = LIM[:, qs, 8:24]
                for sc in range(1, NSC):
                    er = EE[:, 0, qs, sc - 1:sc].to_broadcast(sC); ei = EE[:, 1, qs, sc - 1:sc].to_broadcast(sC)
                    tt("dve", V1[:], PR, er, ALU.mult, [LRE.b, EE.b], [V1.b])
                    tt("dve", V2[:], PI, ei, ALU.mult, [LIM.b, EE.b], [V2.b])
                    tt("dve", V1[:], V1[:], V2[:], ALU.subtract, [V1.b, V2.b], [V1.b])
                    tt("dve", V1[:], V1[:], Sv[:, 0, :, sc, :], ALU.add, [V1.b, SW.b], [V1.b])
                    tt("dve", V2[:], PR, ei, ALU.mult, [LRE.b, EE.b], [V2.b])
                    n_ = SC if sc < NSC - 1 else SC - 1
                    cp("act", SP[:, 0, qs, sc * SC + 1:sc * SC + 1 + n_], V1[:, :, 0:n_], [V1.b], [SP.b])
                    tt("dve", V1[:], PI, er, ALU.mult, [LIM.b, EE.b], [V1.b])
                    tt("dve", V2[:], V2[:], V1[:], ALU.add, [V1.b, V2.b], [V2.b])
                    tt("dve", V2[:], V2[:], Sv[:, 1, :, sc, :], ALU.add, [V2.b, SW.b], [V2.b])
                    cp("act", SP[:, 1, qs, sc * SC + 1:sc * SC + 1 + n_], V2[:, :, 0:n_], [V2.b], [SP.b])
                for sc in range(1, NSC):
                    cp("act", SP[:, :, qs, sc * SC], EE[:, :, qs, sc - 1], [EE.b], [SP.b])
            S.dma("sp", hre_p.rearrange("q x -> x q"), EE[:, 0, :, NSC - 1], reads=[EE.b], **SLOW)
            S.dma("sp", him_p.rearrange("q x -> x q"), EE[:, 1, :, NSC - 1], reads=[EE.b], **SLOW)
            GLUW = sb(pb, "GLUW", [128, 4, 512], BF16)
            S.dma("pool", GLUW[:], glu_w.rearrange("(c p) n -> p c n", p=128), writes=[GLUW.b])
            Y2 = sb(pb, "Y2", [128, NT], F32); Y3 = sb(pb, "Y3", [128, NT], F32)
            Gt = sb(pb, "Gt", [128, 4, NT], F32); Gb = sb(pb, "Gb", [128, 4, NT], BF16)
            MS = [sb(pb, "MS%d" % i, [128, 4, NT], BF16) for i in range(2)]
            CG = math.sqrt(2.0 / math.pi)
            NCT = NT // L
            for it in range(NTILES):
                t0 = it * NT
                c0 = it * NCT
                for gc in range(4):
                    bk = bank()
                    uv = uT[:, gc, t0:t0 + NT].rearrange("p (n l) -> p l n", l=L)
                    ov = bk[:].rearrange("p (n l) -> p l n", l=L)
                    for tp in range(L):
                        for sp_ in range(tp + 1):
                            mm(bk, BDT[:, gc, tp - sp_, :], uv[:, sp_, :], sp_ == 0, False, [BDT.b, uT.b],
                               out=ov[:, tp, :])
                        for j in range(4):
                            q16 = gc * 4 + j
                            for ri, CLt in enumerate((CLR, CLI)):
                                mm(bk, CLt[:, q16, tp, :], SP[:, ri, q16, c0:c0 + NCT], False,
                                   (j == 3 and ri == 1), [CLt.b, SP.b], out=ov[32 * j:32 * j + 32, tp, :],
                                   tile_position=(0, 32 * j))
                    act(Y2[:], bk[:], AF.Square, [bk.b], [Y2.b])
                    ts("dve", Y2[:], Y2[:], 0.044715, 1.0, ALU.mult, ALU.add, [Y2.b], [Y2.b])
                    tt("dve", Y3[:], Y2[:], bk[:], ALU.mult, [Y2.b, bk.b], [Y3.b])
                    act(Y3[:], Y3[:], AF.Sigmoid, [Y3.b], [Y3.b], scale=2.0 * CG)
                    tt("dve", Gt[:, gc, :], Y3[:], bk[:], ALU.mult, [Y3.b, bk.b], [Gt.b])
                    cp("act", Gb[:, gc, :], Gt[:, gc, :], [Gt.b], [Gb.b])
                M_ = MS[it % 2]
                for m in range(4):
                    bk = bank()
                    for kc in range(4):
                        mm(bk, GLUW[:, kc, m * 128:(m + 1) * 128], Gb[:, kc, :], kc == 0, kc == 3, [GLUW.b, Gb.b])
                    act(Y3[:], bk[:], AF.Sigmoid, [bk.b], [Y3.b])
                    tt("dve", M_[:, m, :], Y3[:], Gt[:, m, :], ALU.mult, [Y3.b, Gt.b], [M_.b])
                S.dma("sp", mix_s[it, :, 0:4, :], M_[:], reads=[M_.b])
            S.barrier()
            pb.close()
        s5.close()

        att = ExitStack()
        QT = sb(att, "QT", [128, 4, T], BF16)
        KT = sb(att, "KT", [128, 4, T], BF16)
        V = sb(att, "V", [128, T // 128, 512], BF16)
        with ExitStack() as ph:
          if 'A2' in PH:
            X = sb(ph, "X", [128, 4, 1024], F32)
            xnT = sb(ph, "xnT", [128, 8, NT], BF16)
            rs = sb(ph, "rs", [128, 4], F32); ss = sb(ph, "ss", [128, 4], F32)
            junk = sb(ph, "junk", [128, 1024], F32)
            W = sb(ph, "Wqkv", [128, 8, 1536], BF16)
            sq = sb(ph, "sq", [128, NT], BF16); rstd = sb(ph, "rstd", [128, NT], F32)
            knT = sb(ph, "knT", [128, 4, NT], F32)
            stg = sb(ph, "stg", [128, 4, 512], F32)
            S.dma("pool", W[:], w_in[:, 512:2048].rearrange("(c p) n -> p c n", p=128), writes=[W.b])
            for it in range(NTILES):
                t0 = it * NT
                load_norm_tile(X, xp[t0:t0 + NT, :].rearrange("(j p) f -> p j f", p=128), 4, G1, xnT, rs, ss, junk)
                for qk in range(2):
                    for m in range(4):
                        bk = bank()
                        for kc in range(8):
                            mm(bk, W[:, kc, qk * 512 + m * 128:qk * 512 + (m + 1) * 128], xnT[:, kc, :], kc == 0,
                               kc == 7, [W.b, xnT.b])
                        act(sq[:], bk[:], AF.Square, [bk.b], [sq.b])
                        b2 = bank()
                        mm(b2, BD64[:], sq[:], True, True, [BD64.b, sq.b])
                        act(rstd[:], b2[:], AF.Sqrt, [b2.b], [rstd.b], bias=EPS)
                        recip(rstd[:], rstd[:], [rstd.b], [rstd.b])
                        if qk == 0:
                            stt(QT[:, m, t0:t0 + NT], bk[:], QG[:, 0:1], rstd[:], ALU.mult, ALU.mult,
                                [bk.b, QG.b, rstd.b], [QT.b])
                        else:
                            stt(knT[:, m, :], bk[:], KG[:, 0:1], rstd[:], ALU.mult, ALU.mult,
                                [bk.b, KG.b, rstd.b], [knT.b])
                            cp("act", KT[:, m, t0:t0 + NT], knT[:, m, :], [knT.b], [KT.b])
                store_tm(knT, 4, NT, k_p[t0:t0 + NT, :], stg)
                for j in range(4):
                    bk = bank()
                    for kc in range(8):
                        mm(bk, xnT[:, kc, j * 128:(j + 1) * 128], W[:, kc, 1024:1536], kc == 0, kc == 7,
                           [W.b, xnT.b])
                    cp("act", stg[:, j, :], bk[:], [bk.b], [stg.b])
                    cp("dve", V[:, it * 4 + j, :], bk[:], [bk.b], [V.b])
                S.dma("sp", v_p[t0:t0 + NT, :].rearrange("(j p) f -> p j f", p=128), stg[:], reads=[stg.b])
            if 'S' in PH:
                for qk in range(2):
                    for m in range(4):
                        bk = bank()
                        for kc in range(8):
                            mm(bk, W[:, kc, qk * 512 + m * 128:qk * 512 + (m + 1) * 128], xnTs[:, kc, :], kc == 0,
                               kc == 7, [W.b, xnTs.b], out=bk[:, 0:TS])
                        act(sq[:, 0:TS], bk[:, 0:TS], AF.Square, [bk.b], [sq.b])
                        b2 = bank()
                        mm(b2, BD64[:], sq[:, 0:TS], True, True, [BD64.b, sq.b], out=b2[:, 0:TS])
                        act(rstd[:, 0:TS], b2[:, 0:TS], AF.Sqrt, [b2.b], [rstd.b], bias=EPS)
                        recip(rstd[:, 0:TS], rstd[:, 0:TS], [rstd.b], [rstd.b])
                        if qk == 0:
                            stt(QTs[:, m, :], bk[:, 0:TS], QG[:, 0:1], rstd[:, 0:TS], ALU.mult, ALU.mult,
                                [bk.b, QG.b, rstd.b], [QTs.b])
                        else:
                            stt(knT[:, m, 0:TS], bk[:, 0:TS], KG[:, 0:1], rstd[:, 0:TS], ALU.mult, ALU.mult,
                                [bk.b, KG.b, rstd.b], [knT.b])
                            cp("act", KTs[:, m, :], knT[:, m, 0:TS], [knT.b], [KTs.b])
                store_tm(knT, 4, TS, k_s, stg)
                VN = sb(ph, "VN", [4, NS, 4, 132], BF16)
                memset("dve", VN[:], 1.0, [VN.b])
                for sq_ in range(NS):
                    bk = bank()
                    for kc in range(8):
                        mm(bk, xnTs[:, kc, sq_ * 4:(sq_ + 1) * 4], W[:, kc, 1024:1536], kc == 0, kc == 7,
                           [W.b, xnTs.b], out=bk[0:4, :])
                    cp("act", stg[0:4, sq_ % 4, :], bk[0:4, :], [bk.b], [stg.b])
                    cp("dve", VN[:, sq_, :, 0:128], bk[0:4, :].rearrange("p (h d) -> p h d", d=128), [bk.b], [VN.b])
                    if sq_ % 4 == 3:
                        sg = sq_ // 4
                        S.dma("sp", v_s.rearrange("(s t) f -> t s f", t=4)[:, sg * 4:(sg + 1) * 4, :], stg[0:4, :, :],
                              reads=[stg.b])
                S.dma("sp", vn_s, VN[:, :, :, :].rearrange("p s h d -> p (s h d)"), reads=[VN.b])
            S.barrier()

        with ExitStack() as ph:
          if 'W' in PH and 'C' not in PH:
            convert_ffn_weights(ph)
            S.barrier()
          if 'C' in PH:
            if 'W' in PH:
                convert_ffn_weights(ph)
            PT = [sb(ph, "PT%d" % i, [128, NT], BF16) for i in range(6)]
            BIAS = sb(ph, "BIAS", [128, 4, 34], F32)
            for h in range(4):
                for bi in range(1, 34):
                    ts("dve", BIAS[:, h, bi:bi + 1], KPOS[:, 0:1], float(1 - 128 * bi), SLOPES[h], ALU.add, ALU.mult,
                       [KPOS.b], [BIAS.b])
            On = [sb(ph, "On%d" % i, [128, NT], F32) for i in range(2)]
            rden = [sb(ph, "rden%d" % i, [128, NT], F32) for i in range(2)]
            sq = sb(ph, "sqa", [128, NT], BF16); rstd = sb(ph, "rstda", [128, NT], F32)
            AO = [sb(ph, "AO%d" % i, [128, 4, NT], BF16) for i in range(2)]
            ptrr = [0]
            units = []
            for it in range(NTILES):
                for h in range(4):
                    nkb = (it * NT + NT) // 128
                    for kb in range(nkb):
                        units.append((it, h, kb, nkb))
            Ob = [PS[0], PS[1]]
            Db = [PS[2], PS[3]]

            def sbanks(ui):
                return [PS[4 + (ui % 2) * 2], PS[5 + (ui % 2) * 2]]

            def emit_qk(ui):
                it, h, kb, nkb = units[ui]
                q0 = it * NT
                k0 = kb * 128
                c0 = max(0, (k0 - q0) // 128) * 128
                Sb = sbanks(ui)
                for mp in range(2):
                    pr = slice(mp * 64, (mp + 1) * 64)
                    mm(Sb[mp], KT[pr, h, k0:k0 + 128], QT[pr, h, q0 + c0:q0 + NT], True, True,
                       [KT.b, QT.b], out=Sb[mp][:, c0:NT])

            def emit_rest(ui):
                it, h, kb, nkb = units[ui]
                q0 = it * NT
                k0 = kb * 128
                c0 = max(0, (k0 - q0) // 128) * 128
                slope = SLOPES[h]
                wq = 256 if slope * 511 > 64 else 512
                Sb = sbanks(ui)
                Ps = []
                for mp in range(2):
                    P = PT[ptrr[0] % len(PT)]; ptrr[0] += 1
                    Ps.append(P)
                    for g0 in range((c0 // wq) * wq, NT, wq):
                        lo = max(g0, c0)
                        hi = g0 + wq
                        bi = (q0 + hi - k0) // 128
                        act(P[:, lo:hi], Sb[mp][:, lo:hi], AF.Exp, [Sb[mp].b, BIAS.b], [P.b],
                            scale=0.125, bias=BIAS[:, h, bi:bi + 1])
                    if k0 >= q0:
                        tt("dve", P[:, c0:c0 + 128], P[:, c0:c0 + 128], CAUS[:], ALU.mult, [P.b, CAUS.b],
                           [P.b])
                for mp in range(2):
                    P = Ps[mp]
                    mm(Ob[mp], V[:, kb, h * 128:(h + 1) * 128], P[:, c0:NT], kb == 0, kb == nkb - 1,
                       [V.b, P.b], out=Ob[mp][:, c0:NT])
                    mm(Db[mp], ONE1[:], P[:, c0:NT], kb == 0, kb == nkb - 1, [ONE1.b, P.b],
                       out=Db[mp][:, c0:NT])

            def emit_epi(ui):
                it, h, kb, nkb = units[ui]
                for mp in range(2):
                    recip(rden[mp][:], Db[mp][:], [Db[mp].b], [rden[mp].b])
                    tt("dve", On[mp][:], Ob[mp][:], rden[mp][:], ALU.mult, [Ob[mp].b, rden[mp].b], [On[mp].b])
                stt(On[0][:], On[1][:], NLAM[:, 0:1], On[0][:], ALU.mult, ALU.add, [On[0].b, On[1].b, NLAM.b],
                    [On[0].b])
                act(sq[:], On[0][:], AF.Square, [On[0].b], [sq.b])
                b2 = sbanks(ui)[0]
                mm(b2, ON128[:], sq[:], True, True, [ON128.b, sq.b])
                act(rstd[:], b2[:], AF.Sqrt, [b2.b], [rstd.b], bias=EPS)
                recip(rstd[:], rstd[:], [rstd.b], [rstd.b])
                stt(AO[it % 2][:, h, :], On[0][:], SUBG[:, 0:1], rstd[:], ALU.mult, ALU.mult,
                    [On[0].b, SUBG.b, rstd.b], [AO[it % 2].b])
                if h == 3:
                    S.dma("sp", mix_s[it, :, 4:8, :], AO[it % 2][:], reads=[AO[it % 2].b])

            emit_qk(0)
            for ui in range(len(units)):
                if ui + 1 < len(units):
                    emit_qk(ui + 1)
                emit_rest(ui)
                if units[ui][2] == units[ui][3] - 1:
                    emit_epi(ui)
            S.barrier()
        att.close()


        with ExitStack() as ph:
          if 'S' in PH and 'CS' in PH:
            NKB = 4
            VN = sb(ph, "VNc", [4, NS, 4, 132], BF16)
            S.dma("sp", VN[:, :, :, :].rearrange("p s h d -> p (s h d)"), vn_s, writes=[VN.b])
            PTI = sb(ph, "PTI", [128, NS * 16], I32)
            IDX = sb(ph, "IDX", [128, NS * 16], U32)
            KPB = [sb(ph, "KPB%d" % i, [128, 512], F32) for i in range(NKB)]
            VPF = [sb(ph, "VPF%d" % i, [128, 512], F32) for i in range(NKB)]
            VPB = [sb(ph, "VPB%d" % i, [128, 4, 132], BF16) for i in range(32)]
            KpT = [sb(ph, "KpT%d" % i, [128, 4, 128], BF16) for i in range(2)]
            QB = sb(ph, "QB", [128, 4, NS, 2, 4], BF16)
            BIASS = sb(ph, "BIASS", [128, 16, 4, 8], F32)
            BIASN = sb(ph, "BIASN", [4, 4, 8], F32)
            TMPs = sb(ph, "TMPs", [128, 512], F32)
            PBs = [sb(ph, "PBs%d" % i, [128, 512], BF16) for i in range(2)]
            TNs = sb(ph, "TNs", [4, 32], F32); PNs = sb(ph, "PNs", [4, 32], BF16)
            RD = sb(ph, "RDs", [8, 4], F32)
            ONs = sb(ph, "ONs", [8, 4, 128], F32)
            CMB = sb(ph, "CMB", [8, 4], F32)
            OD4 = sb(ph, "OD4", [4, 512], F32)
            ODT = sb(ph, "ODT", [128, 4, TS], F32)
            sqs = sb(ph, "sqs", [128, TS], BF16); rstds = sb(ph, "rstds", [128, TS], F32)
            S.dma("sp", PTI[:], ptab.rearrange("s g -> (s g)").rearrange("(o n) -> o n", o=1).to_broadcast([128, NS * 16]),
                  writes=[PTI.b], **SLOW)
            ts("dve", IDX[:], PTI[:], 128.0, KPOS[:, 0:1], ALU.mult, ALU.add, [PTI.b, KPOS.b], [IDX.b])
            memset("dve", QB[:], 0.0, [QB.b])
            for mp in range(2):
                pr = slice(mp * 64, (mp + 1) * 64)
                cp("dve", QB[pr, :, :, mp, :], QTs[pr, :, :].rearrange("p h (s t) -> p h s t", t=4), [QTs.b], [QB.b])
            for pg in range(16):
                for h in range(4):
                    ts("dve", BIASS[:, pg, h, :], KPOS[:, 0:1].to_broadcast([128, 8]), float(pg * 128 - 2048), SLOPES[h],
                       ALU.add, ALU.mult, [KPOS.b], [BIASS.b])
            for h in range(4):
                ts("dve", BIASN[:, h, :], KPOS[0:4, 0:1].to_broadcast([4, 8]), SLOPES[h], None, ALU.mult, None, [KPOS.b],
                   [BIASN.b])
            for i in range(32):
                memset("dve", VPB[i][:], 1.0, [VPB[i].b])
            stt(CMB[:], IDF[0:8, 4:8], NLAM[0:8, 0:1], IDF[0:8, 0:4], ALU.mult, ALU.add, [IDF.b, NLAM.b], [CMB.b])
            ID4 = TB(IDF.t[0:4, 0:4], IDF.b)
            kcnt = 0
            ck_rows = cache_k
            cv_rows = cache_v
            for sq_ in range(NS):
                SBk = PS[sq_ % 2]
                PB_ = PBs[sq_ % 2]
                Ob = [PS[2], PS[3], PS[4], PS[5]]
                pend = []
                for pg in range(16):
                    Kp = KPB[kcnt % NKB]; Vf = VPF[kcnt % NKB]; Vp = VPB[kcnt % 32]; kcnt += 1
                    col = sq_ * 16 + pg
                    S.dmaf("pool", (lambda e, Kp=Kp, col=col: e.indirect_dma_start(
                        out=Kp[:, :], out_offset=None, in_=ck_rows,
                        in_offset=bass.IndirectOffsetOnAxis(ap=IDX[:, col:col + 1], axis=0))),
                        reads=[IDX.b], writes=[Kp.b])
                    S.dmaf("pool", (lambda e, Vf=Vf, col=col: e.indirect_dma_start(
                        out=Vf[:, :], out_offset=None, in_=cv_rows,
                        in_offset=bass.IndirectOffsetOnAxis(ap=IDX[:, col:col + 1], axis=0))),
                        reads=[IDX.b], writes=[Vf.b])
                    bkT = PS[6 + (pg % 2)]
                    KT_ = KpT[pg % 2]
                    for h in range(4):
                        tr(bkT, bkT[:, h * 128:(h + 1) * 128], Kp[:, h * 128:(h + 1) * 128], IDF, [Kp.b])
                    cp("act", KT_[:, :, :], bkT[:].rearrange("p (h n) -> p h n", n=128), [bkT.b], [KT_.b])
                    for h in range(4):
                        mm(SBk, KT_[:, h, :], QB[:, h, sq_, :, :].rearrange("p a b -> p (a b)"), True, True,
                           [KT_.b, QB.b], out=SBk[:, (pg * 4 + h) * 8:(pg * 4 + h + 1) * 8])
                    cp("dve", Vp[:, :, 0:128], Vf[:, :].rearrange("p (h d) -> p h d", d=128), [Vf.b], [Vp.b])
                    pend.append(Vp)
                NB = PS[6]
                for h in range(4):
                    mm(NB, KTs[:, h, sq_ * 4:(sq_ + 1) * 4], QB[:, h, sq_, :, :].rearrange("p a b -> p (a b)"), True, True,
                       [KTs.b, QB.b], out=NB[0:4, h * 8:(h + 1) * 8])
                stt(TMPs[:], SBk[:], 0.125, BIASS[:, :, :, :].rearrange("p a b c -> p (a b c)"), ALU.mult, ALU.add,
                    [SBk.b, BIASS.b], [TMPs.b])
                act(PB_[:], TMPs[:], AF.Exp, [TMPs.b], [PB_.b])
                stt(TNs[:], NB[0:4, 0:32], 0.125, BIASN[:, :, :].rearrange("p a b -> p (a b)"), ALU.mult, ALU.add,
                    [NB.b, BIASN.b], [TNs.b])
                act(TNs[:], TNs[:], AF.Exp, [TNs.b], [TNs.b])
                tt("dve", PNs[:, :].rearrange("p (a t) -> p a t", t=4), TNs[:, :].rearrange("p (a t) -> p a t", t=4),
                   CAUS[0:4, 0:4].unsqueeze(1).to_broadcast([4, 8, 4]), ALU.mult, [TNs.b, CAUS.b], [PNs.b])
                for pg in range(16):
                    Vp = pend[pg]
                    for h in range(4):
                        mm(Ob[h], PB_[:, (pg * 4 + h) * 8:(pg * 4 + h + 1) * 8], Vp[:, h, 0:129], pg == 0, False,
                           [PB_.b, Vp.b], out=Ob[h][0:8, 0:129])
                for h in range(4):
                    mm(Ob[h], PNs[0:4, h * 8:(h + 1) * 8], VN[0:4, sq_, h, 0:129], False, True, [PNs.b, VN.b],
                       out=Ob[h][0:8, 0:129])
                for h in range(4):
                    recip(RD[:, h:h + 1], Ob[h][0:8, 128:129], [Ob[h].b], [RD.b])
                    ts("dve", ONs[:, h, :], Ob[h][0:8, 0:128], RD[:, h:h + 1], None, ALU.mult, None, [Ob[h].b, RD.b],
                       [ONs.b])
                DBk = PS[7]
                mm(DBk, CMB[:, :], ONs[:, :, :].rearrange("p h d -> p (h d)"), True, True, [CMB.b, ONs.b],
                   out=DBk[0:4, :])
                cp("act", OD4[:], DBk[0:4, :], [DBk.b], [OD4.b])
                TBk = PS[6]
                for h in range(4):
                    tr(TBk, TBk[:, h * 4:(h + 1) * 4], OD4[0:4, h * 128:(h + 1) * 128], ID4, [OD4.b])
                cp("act", ODT[:, :, sq_ * 4:(sq_ + 1) * 4], TBk[:, 0:16].rearrange("p (h t) -> p h t", t=4), [TBk.b],
                   [ODT.b])
            for h in range(4):
                act(sqs[:], ODT[:, h, :], AF.Square, [ODT.b], [sqs.b])
                b2 = PS[7]
                mm(b2, ON128[:], sqs[:], True, True, [ON128.b, sqs.b], out=b2[:, 0:TS])
                act(rstds[:], b2[:, 0:TS], AF.Sqrt, [b2.b], [rstds.b], bias=EPS)
                recip(rstds[:], rstds[:], [rstds.b], [rstds.b])
                stt(MIXs[:, 4 + h, :], ODT[:, h, :], SUBG[:, 0:1], rstds[:], ALU.mult, ALU.mult,
                    [ODT.b, SUBG.b, rstds.b], [MIXs.b])
            S.barrier()

        with ExitStack() as ph:
          if 'D' in PH:
            XIN = sb(ph, "XIN", [128, 4, 1024], F32)
            xTs_ = [sb(ph, "xT%d" % i, [128, 8, NT], F32) for i in range(2)]
            MIX = sb(ph, "MIX", [128, 8, NT], BF16)
            A = sb(ph, "actA", [128, 8, NT], BF16)
            B3 = [sb(ph, "actB%d" % i, [128, 8, NT], BF16) for i in range(2)]
            Wo = sb(ph, "Wo", [128, 8, 1024], BF16); Wq = sb(ph, "Wq", [128, 8, 1024], BF16)
            Wo2 = sb(ph, "Wo2", [128, 8, 1024], BF16)
            sq2 = [sb(ph, "sqd%d" % i, [128, NT], BF16) for i in range(2)]; rstd = sb(ph, "rstdd", [128, NT], F32)
            Pm = [sb(ph, "Pm%d" % i, [128, NT], BF16) for i in range(2)]
            hT = sb(ph, "hT", [128, NFC, NT], BF16)
            HGs_ = [sb(ph, "HG%d" % i, [128, NT + 2], F32) for i in range(2)]; CARRY = sb(ph, "CARRY", [128, NFC, 2], F32)
            cvs_ = [sb(ph, "cv%d" % i, [128, NT], F32) for i in range(2)]
            rden = sb(ph, "rdend", [128, NT], F32)
            WG = [sb(ph, "WG%d" % i, [128, 8, 128], BF16) for i in range(2)]
            WV = [sb(ph, "WV%d" % i, [128, 8, 128], BF16) for i in range(2)]
            WD = [sb(ph, "WD%d" % i, [128, NFC, 128], BF16) for i in range(2)]
            YST = sb(ph, "YST", [128, 1024], F32)
            S.dma("pool", Wo[:], w_out.rearrange("(c p) n -> p c n", p=128), writes=[Wo.b])
            S.dma("pool", Wq[:], ca_wq.rearrange("(c p) n -> p c n", p=128), writes=[Wq.b])
            S.dma("pool", Wo2[:], ca_wo.rearrange("(c p) n -> p c n", p=128), writes=[Wo2.b])
            memset("dve", CARRY[:], 0.0, [CARRY.b])
            if 'B' not in PH or os.environ.get('KNOB'):
                memset("dve", hT[:, 0:4, :], 0.0, [hT.b])
                for it in range(NTILES):
                    S.dma("sp", mix_s[it, :, 0:4, :], hT[:, 0:4, :], reads=[hT.b])
                S.barrier()
            frr = [0]; grr = [0]

            def fbank():
                b = PS[frr[0] % 4]; frr[0] += 1
                return b

            def gbank():
                b = PS[4 + grr[0] % 4]; grr[0] += 1
                return b

            def d_loads(it):
                t0 = it * NT
                S.dma("sp", MIX[:], mix_s[it], writes=[MIX.b])
                S.dma("sp", XIN[:], xp[t0:t0 + NT, :].rearrange("(j p) f -> p j f", p=128), writes=[XIN.b])

            def ln_gen(xT, G, out):
                bk = fbank()
                for c in range(8):
                    act(sq2[c % 2][:], xT[:, c, :], AF.Square, [xT.b], [sq2[c % 2].b])
                    if c >= 1:
                        mm(bk, ON1024[:], sq2[(c - 1) % 2][:], c == 1, False, [ON1024.b, sq2[(c - 1) % 2].b])
                    yield
                mm(bk, ON1024[:], sq2[1][:], False, True, [ON1024.b, sq2[1].b])
                act(rstd[:], bk[:], AF.Sqrt, [bk.b], [rstd.b], bias=EPS)
                recip(rstd[:], rstd[:], [rstd.b], [rstd.b])
                yield
                for c in range(8):
                    stt(out[:, c, :], xT[:, c, :], G[:, c:c + 1], rstd[:], ALU.mult, ALU.mult,
                        [xT.b, G.b, rstd.b], [out.b])
                    if c % 4 == 3:
                        yield

            def front(it):
                xT = xTs_[it % 2]; Bb = B3[it % 2]
                for c in range(8):
                    bk = fbank()
                    for j in range(4):
                        tr(bk, bk[:, j * 128:(j + 1) * 128], XIN[:, j, c * 128:(c + 1) * 128], IDF, [XIN.b])
                    cp("act", xT[:, c, :], bk[:], [bk.b], [xT.b])
                    if c % 2 == 1:
                        yield
                for m in range(8):
                    bk = fbank()
                    for kc in range(8):
                        mm(bk, Wo[:, kc, m * 128:(m + 1) * 128], MIX[:, kc, :], kc == 0, kc == 7, [Wo.b, MIX.b])
                    tt("dve", xT[:, m, :], bk[:], xT[:, m, :], ALU.add, [bk.b, xT.b], [xT.b])
                    yield
                yield from ln_gen(xT, G2, A)
                for h in range(4):
                    bq = [fbank(), fbank()]
                    for dh in range(2):
                        for kc in range(8):
                            mm(bq[dh], Wq[:, kc, (2 * h + dh) * 128:(2 * h + dh + 1) * 128], A[:, kc, :], kc == 0,
                               kc == 7, [Wq.b, A.b])
                        act(sq2[dh][:], bq[dh][:], AF.Square, [bq[dh].b], [sq2[dh].b])
                    yield
                    b2 = fbank()
                    for dh in range(2):
                        mm(b2, ON256[:], sq2[dh][:], dh == 0, dh == 1, [ON256.b, sq2[dh].b])
                    act(rstd[:], b2[:], AF.Sqrt, [b2.b], [rstd.b], bias=EPS)
                    recip(rstd[:], rstd[:], [rstd.b], [rstd.b])
                    for dh in range(2):
                        stt(Bb[:, 2 * h + dh, :], bq[dh][:], CQG[:, dh:dh + 1], rstd[:], ALU.mult, ALU.mult,
                            [bq[dh].b, CQG.b, rstd.b], [Bb.b])
                    yield
                for h in range(4):
                    for mb in range(2):
                        bs = fbank()
                        for dh in range(2):
                            mm(bs, MKT[:, 2 * h + dh, mb * 128:(mb + 1) * 128], Bb[:, 2 * h + dh, :], dh == 0, dh == 1,
                               [MKT.b, Bb.b])
                        act(Pm[mb][:], bs[:], AF.Exp, [bs.b], [Pm[mb].b], scale=1.0 / 16)
                    yield
                    bd = fbank()
                    for mb in range(2):
                        mm(bd, ONE1[:], Pm[mb][:], mb == 0, mb == 1, [ONE1.b, Pm[mb].b])
                    recip(rden[:], bd[:], [bd.b], [rden.b])
                    for dvc in range(2):
                        bo = fbank()
                        for mb in range(2):
                            mm(bo, MV[:, mb, (2 * h + dvc) * 128:(2 * h + dvc + 1) * 128], Pm[mb][:], mb == 0, mb == 1,
                               [MV.b, Pm[mb].b])
                        tt("dve", A[:, 2 * h + dvc, :], bo[:], rden[:], ALU.mult, [bo.b, rden.b], [A.b])
                    yield
                for m in range(8):
                    bk = fbank()
                    for kc in range(8):
                        mm(bk, Wo2[:, kc, m * 128:(m + 1) * 128], A[:, kc, :], kc == 0, kc == 7, [Wo2.b, A.b])
                    tt("dve", xT[:, m, :], bk[:], xT[:, m, :], ALU.add, [bk.b, xT.b], [xT.b])
                    yield
                yield from ln_gen(xT, G3, Bb)

            def ffn(it):
                t0 = it * NT
                xT = xTs_[it % 2]; Bb = B3[it % 2]
                for fc in range(NFC):
                    wg = WG[fc % 2]; wv = WV[fc % 2]; HG = HGs_[fc % 2]; cv = cvs_[fc % 2]
                    S.dma("sp", wg[:], wg_s[:, fc], writes=[wg.b])
                    S.dma("sp", wv[:], wv_s[:, fc], writes=[wv.b])
                    if fc == NFC - 4:
                        for m in range(2):
                            S.dma("sp", WD[m][:], wd_s[:, m], writes=[WD[m].b])
                    bg = gbank(); bv = gbank()
                    for kc in range(8):
                        mm(bg, wg[:, kc, :], Bb[:, kc, :], kc == 0, kc == 7, [wg.b, Bb.b])
                    for kc in range(8):
                        mm(bv, wv[:, kc, :], Bb[:, kc, :], kc == 0, kc == 7, [wv.b, Bb.b])
                    cp("dve", HG[:, 0:2], CARRY[:, fc, :], [CARRY.b], [HG.b])
                    cp("act", HG[:, 2:NT + 2], bg[:], [bg.b], [HG.b])
                    cp("dve", CARRY[:, fc, :], HG[:, NT:NT + 2], [HG.b], [CARRY.b])
                    act(cv[:], HG[:, 2:NT + 2], AF.Identity, [HG.b, CW.b, CB.b], [cv.b], scale=CW[:, 2, fc:fc + 1],
                        bias=CB[:, fc:fc + 1])
                    stt(cv[:], HG[:, 1:NT + 1], CW[:, 1, fc:fc + 1], cv[:], ALU.mult, ALU.add, [HG.b, CW.b, cv.b],
                        [cv.b])
                    stt(cv[:], HG[:, 0:NT], CW[:, 0, fc:fc + 1], cv[:], ALU.mult, ALU.add, [HG.b, CW.b, cv.b], [cv.b])
                    act(cv[:], cv[:], AF.Silu, [cv.b], [cv.b])
                    tt("dve", hT[:, fc, :], cv[:], bv[:], ALU.mult, [cv.b, bv.b], [hT.b])
                    yield
                for m in range(8):
                    wd = WD[m % 2]
                    if m >= 2:
                        S.dma("sp", wd[:], wd_s[:, m], writes=[wd.b])
                    bk = gbank()
                    for fc in range(NFC):
                        mm(bk, wd[:, fc, :], hT[:, fc, :], fc == 0, fc == NFC - 1, [wd.b, hT.b])
                    tt("dve", xT[:, m, :], bk[:], xT[:, m, :], ALU.add, [bk.b, xT.b], [xT.b])
                    yield
                for j in range(4):
                    for c0 in (0, 4):
                        bk = gbank()
                        for c in range(c0, c0 + 4):
                            tr(bk, bk[:, (c - c0) * 128:(c - c0 + 1) * 128], xT[:, c, j * 128:(j + 1) * 128], IDF,
                               [xT.b])
                        cp("act", YST[:, c0 * 128:(c0 + 4) * 128], bk[:], [bk.b], [YST.b])
                    S.dma("sp", y_p[t0 + j * 128:t0 + (j + 1) * 128, :], YST[:], reads=[YST.b])
                    yield

            def drain(g, n):
                for _ in range(n):
                    try:
                        next(g)
                    except StopIteration:
                        return False
                return True

            d_loads(0)
            drain(front(0), 10 ** 6)
            for it in range(NTILES):
                nxt = None
                if it + 1 < NTILES:
                    d_loads(it + 1)
                    nxt = front(it + 1)
                for _ in ffn(it):
                    if nxt is not None:
                        drain(nxt, 2)
                if nxt is not None:
                    drain(nxt, 10 ** 6)
            for j in range(2):
                S.dma("sp", conv_p[j].rearrange("(c p) -> p c", p=128), CARRY[:, :, j], reads=[CARRY.b], **SLOW)
            S.barrier()

        with ExitStack() as ph:
          if 'S' in PH and 'DS' in PH:
            X = sb(ph, "Xds", [128, 1, 1024], F32)
            xT = sb(ph, "xTs", [128, 8, TS], F32)
            A = sb(ph, "actAs", [128, 8, TS], BF16); Bb = sb(ph, "actBs", [128, 8, TS], BF16)
            Wo = sb(ph, "Wos", [128, 8, 1024], BF16); Wq = sb(ph, "Wqs", [128, 8, 1024], BF16)
            Wo2 = sb(ph, "Wo2s", [128, 8, 1024], BF16)
            sq = sb(ph, "sqds", [128, 2, TS], BF16); rstd = sb(ph, "rstdds", [128, TS], F32)
            hT = sb(ph, "hTs", [128, NFC, TS], BF16)
            WG = [sb(ph, "WGs%d" % i, [128, 8, 128], BF16) for i in range(2)]
            WV = [sb(ph, "WVs%d" % i, [128, 8, 128], BF16) for i in range(2)]
            WD = [sb(ph, "WDs%d" % i, [128, NFC, 128], BF16) for i in range(2)]
            S.dma("pool", Wo[:], w_out.rearrange("(c p) n -> p c n", p=128), writes=[Wo.b])
            S.dma("pool", Wq[:], ca_wq.rearrange("(c p) n -> p c n", p=128), writes=[Wq.b])
            S.dma("pool", Wo2[:], ca_wo.rearrange("(c p) n -> p c n", p=128), writes=[Wo2.b])
            wrr = 0
            n = TS
            CMKt = sb(ph, "CMKt", [128, 2, 1024], F32)
            CMVb = [sb(ph, "CMVb%d" % i, [128, 2, 4, 260], BF16) for i in range(2)]
            MKs = sb(ph, "MKs", [128, 8, 256], BF16)
            PCs = sb(ph, "PCs", [128, 32], BF16)
            RDc = sb(ph, "RDc", [4, 4], F32)
            COn = sb(ph, "COn", [4, 4, 256], F32)
            SCt = sb(ph, "SCt", [32, F], F32)
            SCV = sb(ph, "SCV", [128, NFC, 32], F32)
            CVS = sb(ph, "CVS", [128, NFC, 32], F32)
            HGs = sb(ph, "HGs", [128, NS, 6], F32)
            cvs = sb(ph, "cvs", [128, NS, 4], F32)
            ID4 = TB(IDF.t[0:4, 0:4], IDF.b); ID32 = TB(IDF.t[0:32, 0:32], IDF.b)
            for i in range(2):
                memset("dve", CMVb[i][:], 1.0, [CMVb[i].b])
            S.dma("sp", SCt[:], st_conv, writes=[SCt.b])
            for f0 in range(0, NFC, 16):
                bk = bank()
                for fc in range(f0, min(f0 + 16, NFC)):
                    tr(bk, bk[:, (fc - f0) * 32:(fc - f0 + 1) * 32], SCt[0:32, fc * 128:(fc + 1) * 128], ID32, [SCt.b])
                nf = min(f0 + 16, NFC) - f0
                cp("act", SCV[:, f0:f0 + nf, :], bk[:, 0:nf * 32].rearrange("p (f x) -> p f x", x=32), [bk.b], [SCV.b])
            for c in range(8):
                cp("dve", xT[:, c, 0:n], xsT[:, c, :], [xsT.b], [xT.b])
            for m in range(8):
                bk = bank()
                for kc in range(8):
                    mm(bk, Wo[:, kc, m * 128:(m + 1) * 128], MIXs[:, kc, :], kc == 0, kc == 7, [Wo.b, MIXs.b],
                       out=bk[:, 0:n])
                tt("dve", xT[:, m, 0:n], bk[:, 0:n], xT[:, m, 0:n], ALU.add, [bk.b, xT.b], [xT.b])
            ln_fm(xT, G2, A, n, sq, rstd)
            for h in range(4):
                bq = [bank(), bank()]
                for dh in range(2):
                    for kc in range(8):
                        mm(bq[dh], Wq[:, kc, (2 * h + dh) * 128:(2 * h + dh + 1) * 128], A[:, kc, 0:n], kc == 0,
                           kc == 7, [Wq.b, A.b], out=bq[dh][:, 0:n])
                b2 = bank()
                for dh in range(2):
                    act(sq[:, dh, 0:n], bq[dh][:, 0:n], AF.Square, [bq[dh].b], [sq.b])
                    mm(b2, ON256[:], sq[:, dh, 0:n], dh == 0, dh == 1, [ON256.b, sq.b], out=b2[:, 0:n])
                act(rstd[:, 0:n], b2[:, 0:n], AF.Sqrt, [b2.b], [rstd.b], bias=EPS)
                recip(rstd[:, 0:n], rstd[:, 0:n], [rstd.b], [rstd.b])
                for dh in range(2):
                    stt(Bb[:, 2 * h + dh, 0:n], bq[dh][:, 0:n], CQG[:, dh:dh + 1], rstd[:, 0:n], ALU.mult, ALU.mult,
                        [bq[dh].b, CQG.b, rstd.b], [Bb.b])
            for sq_ in range(NS):
                CMV_ = CMVb[sq_ % 2]
                S.dma("sp", CMKt[:], cmk[sq_].rearrange("(mb p) f -> p mb f", p=128), writes=[CMKt.b])
                for mb in range(2):
                    S.dma("pool", CMV_[:, mb, :, 0:256], cmv[sq_, mb * 128:(mb + 1) * 128, :].rearrange(
                        "p (h d) -> p h d", d=256), writes=[CMV_.b])
                for mb in range(2):
                    for c0 in (0, 4):
                        bk = bank()
                        for c8 in range(c0, c0 + 4):
                            tr(bk, bk[:, (c8 - c0) * 128:(c8 - c0 + 1) * 128], CMKt[:, mb, c8 * 128:(c8 + 1) * 128], IDF,
                               [CMKt.b])
                        cp("act" if c0 == 0 else "dve", MKs[:, c0:c0 + 4, mb * 128:(mb + 1) * 128],
                           bk[:].rearrange("p (c n) -> p c n", n=128), [bk.b], [MKs.b])
                SBc = bank()
                for mb in range(2):
                    for h in range(4):
                        for dh in range(2):
                            mm(SBc, MKs[:, 2 * h + dh, mb * 128:(mb + 1) * 128], Bb[:, 2 * h + dh, sq_ * 4:(sq_ + 1) * 4],
                               dh == 0, dh == 1, [MKs.b, Bb.b], out=SBc[:, (mb * 4 + h) * 4:(mb * 4 + h + 1) * 4])
                act(PCs[:], SBc[:, 0:32], AF.Exp, [SBc.b], [PCs.b], scale=1.0 / 16)
                Oc = [bank(), bank(), bank(), bank()]
                for h in range(4):
                    for mb in range(2):
                        mm(Oc[h], PCs[:, (mb * 4 + h) * 4:(mb * 4 + h + 1) * 4], CMV_[:, mb, h, 0:257], mb == 0, mb == 1,
                           [PCs.b, CMV_.b], out=Oc[h][0:4, 0:257])
                for h in range(4):
                    recip(RDc[:, h:h + 1], Oc[h][0:4, 256:257], [Oc[h].b], [RDc.b])
                    ts("dve", COn[:, h, :], Oc[h][0:4, 0:256], RDc[:, h:h + 1], None, ALU.mult, None,
                       [Oc[h].b, RDc.b], [COn.b])
                TBk = bank()
                cof = COn[:, :, :].rearrange("p h d -> p (h d)")
                for c8 in range(8):
                    tr(TBk, TBk[:, c8 * 4:(c8 + 1) * 4], cof[0:4, c8 * 128:(c8 + 1) * 128], ID4, [COn.b])
                cp("act", A[:, :, sq_ * 4:(sq_ + 1) * 4], TBk[:, 0:32].rearrange("p (c t) -> p c t", t=4), [TBk.b],
                   [A.b])
            for m in range(8):
                bk = bank()
                for kc in range(8):
                    mm(bk, Wo2[:, kc, m * 128:(m + 1) * 128], A[:, kc, 0:n], kc == 0, kc == 7, [Wo2.b, A.b],
                       out=bk[:, 0:n])
                tt("dve", xT[:, m, 0:n], bk[:, 0:n], xT[:, m, 0:n], ALU.add, [bk.b, xT.b], [xT.b])
            ln_fm(xT, G3, Bb, n, sq, rstd)
            for fc in range(NFC):
                wg = WG[wrr % 2]; wv = WV[wrr % 2]; wrr += 1
                S.dma("sp", wg[:], wg_s[:, fc],
                      writes=[wg.b])
                S.dma("sp", wv[:], wv_s[:, fc],
                      writes=[wv.b])
                bg = bank(); bv = bank()
                for kc in range(8):
                    mm(bg, wg[:, kc, :], Bb[:, kc, 0:n], kc == 0, kc == 7, [wg.b, Bb.b], out=bg[:, 0:n])
                for kc in range(8):
                    mm(bv, wv[:, kc, :], Bb[:, kc, 0:n], kc == 0, kc == 7, [wv.b, Bb.b], out=bv[:, 0:n])
                cp("dve", HGs[:, :, 0:2], SCV[:, fc, :].rearrange("p (s j) -> p s j", j=2), [SCV.b], [HGs.b])
                cp("act", HGs[:, :, 2:6], bg[:, 0:n].rearrange("p (s t) -> p s t", t=4), [bg.b], [HGs.b])
                cp("dve", CVS[:, fc, :].rearrange("p (s j) -> p s j", j=2), HGs[:, :, 4:6], [HGs.b], [CVS.b])
                act(cvs[:], HGs[:, :, 2:6], AF.Identity, [HGs.b, CW.b, CB.b], [cvs.b], scale=CW[:, 2, fc:fc + 1],
                    bias=CB[:, fc:fc + 1])
                stt(cvs[:], HGs[:, :, 1:5], CW[:, 1, fc:fc + 1], cvs[:], ALU.mult, ALU.add, [HGs.b, CW.b, cvs.b],
                    [cvs.b])
                stt(cvs[:], HGs[:, :, 0:4], CW[:, 0, fc:fc + 1], cvs[:], ALU.mult, ALU.add, [HGs.b, CW.b, cvs.b],
                    [cvs.b])
                act(cvs[:], cvs[:], AF.Silu, [cvs.b], [cvs.b])
                tt("dve", hT[:, fc, 0:n], cvs[:, :, :].rearrange("p s t -> p (s t)"), bv[:, 0:n], ALU.mult,
                   [cvs.b, bv.b], [hT.b])
            for m in range(8):
                wd = WD[m % 2]
                S.dma("sp", wd[:], wd_s[:, m],
                      writes=[wd.b])
                bk = bank()
                for fc in range(NFC):
                    mm(bk, wd[:, fc, :], hT[:, fc, 0:n], fc == 0, fc == NFC - 1, [wd.b, hT.b], out=bk[:, 0:n])
                tt("dve", xT[:, m, 0:n], bk[:, 0:n], xT[:, m, 0:n], ALU.add, [bk.b, xT.b], [xT.b])
            store_tm(xT, 8, n, y_s, X)
            for f0 in range(0, NFC, 4):
                bk = bank()
                nf = min(f0 + 4, NFC) - f0
                for fc in range(f0, f0 + nf):
                    tr(bk, bk[0:32, (fc - f0) * 128:(fc - f0 + 1) * 128], CVS[:, fc, :], IDF, [CVS.b])
                cp("act", SCt[0:32, f0 * 128:(f0 + nf) * 128], bk[0:32, 0:nf * 128], [bk.b], [SCt.b])
            S.dma("sp", conv_s, SCt[:], reads=[SCt.b])

            S.barrier()

        S.barrier()
        S.emit()
    return nc


_NC_CACHE = {}


def _consts():
    ident = np.eye(128, dtype=np.float32)
    bd64 = np.zeros((128, 128), np.float32); bd64[:64, :64] = 1 / 64; bd64[64:, 64:] = 1 / 64
    m16 = np.kron(np.eye(8, dtype=np.float32), np.ones((16, 16), np.float32))
    caus = np.triu(np.ones((128, 128), np.float32))
    pwn = np.array(PW_N, np.float32)
    kpos = np.arange(128, dtype=np.float32)
    g2m = np.zeros((128, 2), np.float32)
    for p in range(128):
        g2m[p, (p // 16) % 2] = 1.0
    return dict(c_ident=ident, c_bd64=bd64, c_mask16=m16, c_caus=caus, c_pwn=pwn, c_kpos=kpos, c_g2m=g2m)


WNAMES = ["ln1_g", "ln2_g", "ln3_g", "mem_norm_g", "w_in", "ssm_a_re", "ssm_a_im", "ssm_b_re", "ssm_b_im",
          "ssm_c_re", "ssm_c_im", "ssm_d", "ssm_log_dt", "ssm_glu_w", "q_norm_g", "k_norm_g", "lam_q1", "lam_k1",
          "lam_q2", "lam_k2", "subln_g", "w_out", "ca_wq", "ca_wk", "ca_wv", "ca_q_norm_g", "ca_k_norm_g", "ca_wo",
          "ffn_wg", "ffn_wv", "ffn_wd", "ffn_conv_w", "ffn_conv_b"]


def make_in_maps(inp, cores):
    cst = _consts()
    shared = {n: np.ascontiguousarray(np.asarray(inp[n])[0]) for n in WNAMES}
    maps = []
    for c in cores:
        b = c % 4
        m = dict(shared)
        m.update(cst)
        m["xp"] = np.ascontiguousarray(np.asarray(inp["x_prompt"])[b])
        m["memp"] = np.ascontiguousarray(np.asarray(inp["mem_prompt"])[b])
        sl = slice(c * NS, (c + 1) * NS)
        m["xs"] = np.ascontiguousarray(np.asarray(inp["x_sample"])[sl]).reshape(TS, 1024)
        m["st_re"] = np.ascontiguousarray(np.asarray(inp["state_ssm_re"])[0, sl]).reshape(NS, 2048)
        m["st_im"] = np.ascontiguousarray(np.asarray(inp["state_ssm_im"])[0, sl]).reshape(NS, 2048)
        m["st_conv"] = np.ascontiguousarray(np.asarray(inp["state_conv"])[0, sl]).reshape(NS * 2, F)
        m["cmk"] = np.ascontiguousarray(np.asarray(inp["cache_mem_k"])[0, sl]).reshape(NS, 256, 1024)
        m["cmv"] = np.ascontiguousarray(np.asarray(inp["cache_mem_v"])[0, sl]).reshape(NS, 256, 1024)
        m["cache_k"] = np.asarray(inp["cache_k"]).reshape(2560 * 128, 512)
        m["cache_v"] = np.asarray(inp["cache_v"]).reshape(2560 * 128, 512)
        m["ptab"] = np.ascontiguousarray(np.asarray(inp["page_table"])[sl]).astype(np.int32)
        maps.append(m)
    return maps


def kernel(**inp):
    nc = build_nc()
    cores = list(range(NCORES))
    maps = make_in_maps(inp, cores)
    res = run_bass_kernel_spmd(nc, maps, core_ids=cores)
    R = res.results
    f32 = np.float32
    y_prompt = np.stack([R[b]["y_p"] for b in range(4)]).astype(f32)
    k_prompt = np.stack([R[b]["k_p"] for b in range(4)]).reshape(1, 4, T, 4, 2, 64).astype(f32)
    v_prompt = np.stack([R[b]["v_p"] for b in range(4)]).reshape(1, 4, T, 4, 128).astype(f32)
    hre = np.stack([R[b]["hre_p"] for b in range(4)]).reshape(1, 4, 32, 64).astype(f32)
    him = np.stack([R[b]["him_p"] for b in range(4)]).reshape(1, 4, 32, 64).astype(f32)
    conv_prompt = np.stack([R[b]["conv_p"] for b in range(4)]).reshape(1, 4, 2, F).astype(f32)
    mk = np.stack([R[b]["mk_p"] for b in range(4)]).reshape(1, 4, 256, 4, 256).astype(f32)
    mv = np.stack([R[b]["mv_p"] for b in range(4)]).reshape(1, 4, 256, 4, 256).astype(f32)
    y_sample = np.concatenate([R[c]["y_s"] for c in range(NCORES)]).reshape(128, 4, 1024).astype(f32)
    k_sample = np.concatenate([R[c]["k_s"] for c in range(NCORES)]).reshape(1, 128, 4, 4, 2, 64).astype(f32)
    v_sample = np.concatenate([R[c]["v_s"] for c in range(NCORES)]).reshape(1, 128, 4, 4, 128).astype(f32)
    hre_s = np.concatenate([R[c]["hre_s"] for c in range(NCORES)]).reshape(1, 128, 32, 64).astype(f32)
    him_s = np.concatenate([R[c]["him_s"] for c in range(NCORES)]).reshape(1, 128, 32, 64).astype(f32)
    conv_s = np.concatenate([R[c]["conv_s"] for c in range(NCORES)]).reshape(1, 128, 2, F).astype(f32)
    return (y_prompt, y_sample, k_prompt, v_prompt, k_sample, v_sample, hre, him, hre_s, him_s,
            conv_prompt, conv_s, mk, mv)
```

```python
import math
import os
PH = set(os.environ.get('KPH', 'W,M,A2,C,B,D,S,CS,DS').split(','))
import numpy as np
import ml_dtypes
import concourse.bass as bass
import concourse.mybir as mybir
from concourse.bass_utils import run_bass_kernel_spmd
from contextlib import ExitStack

F32 = mybir.dt.float32
BF16 = mybir.dt.bfloat16
I32 = mybir.dt.int32
U32 = mybir.dt.uint32
ALU = mybir.AluOpType
AF = mybir.ActivationFunctionType

ENGS = ("pe", "act", "dve", "pool", "sp")
N_DMA_SEMS = 24
EPS = 1e-6
NCORES = 8
T = 4096
NT = 512
NTILES = T // NT
L = 8
NCH = T // L
SC = 16
NSC = NCH // SC
F = 2816
NFC = F // 128
SLOPES = [2.0 ** (-8.0 * (h + 1) / 4) for h in range(4)]
LAM0 = 0.8 - 0.6 * math.exp(-0.3 * 0)
NS = 16
TS = 64
NPW = 25
PW_N = list(range(9)) + [8 * k for k in range(2, 17)] + [4]


class Buf:
    __slots__ = ("name", "last_w", "readers", "excl")

    def __init__(self, name):
        self.name = name
        self.excl = False
        self.last_w = None
        self.readers = []


class Sched:
    def __init__(self, nc, es):
        self.nc = nc
        self.ops = {e: [] for e in ENGS}
        self.count = {e: 0 for e in ENGS}
        self.sems = {e: es.enter_context(nc.semaphore("s_" + e)) for e in ENGS}
        self.dsems = [es.enter_context(nc.semaphore("d%d" % i)) for i in range(N_DMA_SEMS)]
        self.dcnt = [0] * N_DMA_SEMS
        self.drr = 0
        self.waited = {e: {} for e in ENGS}
        self.bufs = []

    def buf(self, name):
        b = Buf(name)
        self.bufs.append(b)
        return b

    def _collect(self, eng, reads, writes, is_dma):
        toks = []
        for b in reads:
            if b.last_w is not None:
                toks.append(b.last_w)
        for b in writes:
            if b.last_w is not None:
                toks.append(b.last_w)
            toks.extend(b.readers)
        need = {}
        for (k, v) in toks:
            if (not is_dma) and eng == "pe" and k == "pe":
                continue
            if need.get(k, -1) < v:
                need[k] = v
        waits = []
        w = self.waited[eng]
        for k, v in need.items():
            if w.get(k, -1) >= v:
                continue
            w[k] = v
            waits.append((k, v))
        return waits

    def _commit(self, tok, reads, writes):
        for b in reads:
            if b.excl:
                b.last_w = tok
                b.readers = []
            else:
                b.readers.append(tok)
        for b in writes:
            b.last_w = tok
            b.readers = []

    def op(self, eng, fn, reads=(), writes=()):
        waits = self._collect(eng, reads, writes, False)
        self.count[eng] += 1
        tok = (eng, self.count[eng])
        self.ops[eng].append((waits, fn, None))
        self._commit(tok, reads, writes)
        return tok

    def dmaf(self, eng, fn, reads=(), writes=()):
        waits = self._collect(eng, reads, writes, True)
        i = self.drr
        self.drr = (self.drr + 1) % N_DMA_SEMS
        k = ("d", i)
        prev = self.dcnt[i]
        w = self.waited[eng]
        if prev > 0 and w.get(k, -1) < prev:
            w[k] = prev
            waits.append((k, prev))
        self.dcnt[i] += 16
        tok = (k, self.dcnt[i])
        self.ops[eng].append((waits, fn, (i, 16)))
        self._commit(tok, reads, writes)
        return tok

    def dma(self, eng, out, in_, reads=(), writes=(), **kw):
        def fn(e, out=out, in_=in_, kw=kw):
            return e.dma_start(out=out, in_=in_, **kw)
        return self.dmaf(eng, fn, reads, writes)

    def barrier(self):
        targets = [(e, self.count[e]) for e in ENGS if self.count[e] > 0]
        targets += [(("d", i), c) for i, c in enumerate(self.dcnt) if c > 0]
        for e in ENGS:
            w = self.waited[e]
            waits = []
            for k, v in targets:
                if w.get(k, -1) < v:
                    w[k] = v
                    waits.append((k, v))
            if waits:
                self.ops[e].append((waits, None, None))
        for b in self.bufs:
            b.last_w = None
            b.readers = []

    def _sem(self, k):
        if isinstance(k, tuple):
            return self.dsems[k[1]]
        return self.sems[k]

    def emit(self):
        nc = self.nc
        with nc.Block() as block:
            def mk(ename):
                def body(e):
                    own = self.sems[ename]
                    for waits, fn, dinc in self.ops[ename]:
                        for (k, v) in waits:
                            e.wait_ge(self._sem(k), v)
                        if fn is None:
                            continue
                        ins = fn(e)
                        if dinc is not None:
                            ins.then_inc(self.dsems[dinc[0]], dinc[1])
                        else:
                            ins.then_inc(own, 1)
                return body
            block.tensor(mk("pe"))
            block.scalar(mk("act"))
            block.vector(mk("dve"))
            block.gpsimd(mk("pool"))
            block.sync(mk("sp"))


class TB:
    def __init__(self, t, b):
        self.t = t
        self.b = b

    def __getitem__(self, k):
        return self.t[k]


def build_nc():
    nc = bass.Bass("TRN2", target_bir_lowering=False)

    def din(name, shape, dt=F32):
        return nc.dram_tensor(name, list(shape), dt, kind="ExternalInput").ap()

    def dout(name, shape, dt=F32):
        return nc.dram_tensor(name, list(shape), dt, kind="ExternalOutput").ap()

    def dscr(name, shape, dt):
        return nc.dram_tensor(name, list(shape), dt, kind="Internal").ap()

    xp = din("xp", [T, 1024])
    memp = din("memp", [256, 1024])
    ln1_g = din("ln1_g", [1024]); ln2_g = din("ln2_g", [1024]); ln3_g = din("ln3_g", [1024])
    memn_g = din("mem_norm_g", [1024])
    w_in = din("w_in", [1024, 2048])
    a_re = din("ssm_a_re", [32, 64]); a_im = din("ssm_a_im", [32, 64])
    b_re = din("ssm_b_re", [32, 64, 16]); b_im = din("ssm_b_im", [32, 64, 16])
    c_re = din("ssm_c_re", [32, 16, 64]); c_im = din("ssm_c_im", [32, 16, 64])
    ssm_d = din("ssm_d", [32, 16]); log_dt = din("ssm_log_dt", [32])
    glu_w = din("ssm_glu_w", [512, 512])
    qn_g = din("q_norm_g", [64]); kn_g = din("k_norm_g", [64])
    lq1 = din("lam_q1", [64]); lk1 = din("lam_k1", [64]); lq2 = din("lam_q2", [64]); lk2 = din("lam_k2", [64])
    subln_g = din("subln_g", [128])
    w_out = din("w_out", [1024, 1024])
    ca_wq = din("ca_wq", [1024, 1024]); ca_wk = din("ca_wk", [1024, 1024]); ca_wv = din("ca_wv", [1024, 1024])
    caq_g = din("ca_q_norm_g", [256]); cak_g = din("ca_k_norm_g", [256])
    ca_wo = din("ca_wo", [1024, 1024])
    ffn_wg = din("ffn_wg", [1024, F]); ffn_wv = din("ffn_wv", [1024, F]); ffn_wd = din("ffn_wd", [F, 1024])
    conv_w = din("ffn_conv_w", [3, F]); conv_b = din("ffn_conv_b", [F])
    xs = din("xs", [TS, 1024])
    st_re = din("st_re", [NS, 2048]); st_im = din("st_im", [NS, 2048])
    st_conv = din("st_conv", [NS * 2, F])
    cmk = din("cmk", [NS, 256, 1024]); cmv = din("cmv", [NS, 256, 1024])
    cache_k = din("cache_k", [2560 * 128, 512]); cache_v = din("cache_v", [2560 * 128, 512])
    ptab = din("ptab", [NS, 16], I32)
    c_ident = din("c_ident", [128, 128])
    c_bd64 = din("c_bd64", [128, 128])
    c_mask16 = din("c_mask16", [128, 128])
    c_caus = din("c_caus", [128, 128])
    c_pwn = din("c_pwn", [NPW])
    c_kpos = din("c_kpos", [128])
    c_g2m = din("c_g2m", [128, 2])

    y_p = dout("y_p", [T, 1024]); k_p = dout("k_p", [T, 512]); v_p = dout("v_p", [T, 512])
    hre_p = dout("hre_p", [16, 128]); him_p = dout("him_p", [16, 128])
    conv_p = dout("conv_p", [2, F])
    mk_p = dout("mk_p", [256, 1024]); mv_p = dout("mv_p", [256, 1024])

    y_s = dout("y_s", [TS, 1024]); k_s = dout("k_s", [TS, 512]); v_s = dout("v_s", [TS, 512])
    hre_s = dout("hre_s", [NS, 2048]); him_s = dout("him_s", [NS, 2048])
    conv_s = dout("conv_s", [NS * 2, F])
    wg_s = dscr("wg_s", [128, NFC, 8, 128], BF16); wv_s = dscr("wv_s", [128, NFC, 8, 128], BF16)
    wd_s = dscr("wd_s", [128, 8, NFC, 128], BF16)
    mix_s = dscr("mix_s", [NTILES, 128, 8, NT], BF16)
    vn_s = dscr("vn_s", [4, NS * 4 * 132], BF16)

    es = ExitStack()
    with es:
        S = Sched(nc, es)

        def sb(stack, name, shape, dt):
            return TB(stack.enter_context(nc.sbuf_tensor(name, list(shape), dt)), S.buf(name))

        PS = [TB(es.enter_context(nc.psum_tensor("ps%d" % i, [128, 512], F32)), S.buf("ps%d" % i)) for i in range(8)]
        for p_ in PS:
            p_.b.excl = True
        psrr = [0]

        def bank():
            b = PS[psrr[0]]
            psrr[0] = (psrr[0] + 1) % 8
            return b

        SLOW = dict(allow_slow_non_contiguous=True)

        def mm(bk, lhsT, rhs, start, stop, reads, out=None, **kw):
            o = bk[:] if out is None else out
            S.op("pe", lambda e: e.matmul(o, lhsT=lhsT, rhs=rhs, start=start, stop=stop, **kw),
                 reads=reads, writes=[bk.b])

        def tr(bk, out, in_, ident, reads):
            S.op("pe", lambda e: e.transpose(out=out, in_=in_, identity=ident[:]), reads=reads + [ident.b],
                 writes=[bk.b])

        def act(out, in_, func, reads, writes, **kw):
            S.op("act", lambda e: e.activation(out=out, in_=in_, func=func, **kw), reads=reads, writes=writes)

        def tt(eng, out, in0, in1, op, reads, writes):
            S.op(eng, lambda e: e.tensor_tensor(out=out, in0=in0, in1=in1, op=op), reads=reads, writes=writes)

        def ts(eng, out, in0, s1, s2, op0, op1, reads, writes):
            if op1 is None:
                S.op(eng, lambda e: e.tensor_scalar(out=out, in0=in0, scalar1=s1, scalar2=None, op0=op0),
                     reads=reads, writes=writes)
            else:
                S.op(eng, lambda e: e.tensor_scalar(out=out, in0=in0, scalar1=s1, scalar2=s2, op0=op0, op1=op1),
                     reads=reads, writes=writes)

        def stt(out, in0, scalar, in1, op0, op1, reads, writes):
            S.op("dve", lambda e: e.scalar_tensor_tensor(out=out, in0=in0, scalar=scalar, in1=in1, op0=op0, op1=op1),
                 reads=reads, writes=writes)

        def cp(eng, out, in_, reads, writes):
            if eng == "act":
                S.op("act", lambda e: e.copy(out=out, in_=in_), reads=reads, writes=writes)
            else:
                S.op(eng, lambda e: e.tensor_copy(out=out, in_=in_), reads=reads, writes=writes)

        def memset(eng, ap, val, writes):
            S.op(eng, lambda e: e.memset(ap, val), writes=writes)

        def recip(out, in_, reads, writes):
            S.op("dve", lambda e: e.reciprocal(out=out, in_=in_), reads=reads, writes=writes)

        IDF = sb(es, "IDF", [128, 128], F32); IDB = sb(es, "IDB", [128, 128], BF16)
        ON1024 = sb(es, "ON1024", [128, 128], BF16); ON256 = sb(es, "ON256", [128, 128], BF16)
        ON128 = sb(es, "ON128", [128, 128], BF16); ONE1 = sb(es, "ONE1", [128, 128], BF16)
        BD64 = sb(es, "BD64", [128, 128], BF16)
        CAUS = sb(es, "CAUS", [128, 128], BF16)
        G1 = sb(es, "G1", [128, 8], F32); G2 = sb(es, "G2", [128, 8], F32); G3 = sb(es, "G3", [128, 8], F32)
        GM = sb(es, "GM", [128, 8], F32)
        QG = sb(es, "QG", [128, 1], F32); KG = sb(es, "KG", [128, 1], F32)
        SUBG = sb(es, "SUBG", [128, 1], F32)
        CQG = sb(es, "CQG", [128, 2], F32); CKG = sb(es, "CKG", [128, 2], F32)
        CW = sb(es, "CW", [128, 3, NFC], F32); CB = sb(es, "CB", [128, NFC], F32)
        KPOS = sb(es, "KPOS", [128, 1], F32)
        NLAM = sb(es, "NLAM", [128, 1], F32)
        LT = sb(es, "LT", [64, 4], F32)

        S.dma("sp", IDF[:], c_ident, writes=[IDF.b])
        S.dma("pool", IDB[:], c_ident, writes=[IDB.b])
        S.dma("pool", BD64[:], c_bd64, writes=[BD64.b])
        S.dma("pool", CAUS[:], c_caus, writes=[CAUS.b])
        memset("dve", ON1024[:], 1.0 / 1024, [ON1024.b]); memset("dve", ON256[:], 1.0 / 256, [ON256.b])
        memset("dve", ON128[:], 1.0 / 128, [ON128.b]); memset("dve", ONE1[:], 1.0, [ONE1.b])
        for (Gt, gsrc) in ((G1, ln1_g), (G2, ln2_g), (G3, ln3_g), (GM, memn_g)):
            S.dma("sp", Gt[:], gsrc.rearrange("(c p) -> p c", p=128), writes=[Gt.b], **SLOW)
        for (Gt, gsrc) in ((QG, qn_g), (KG, kn_g)):
            for hh in range(2):
                S.dma("sp", Gt[hh * 64:(hh + 1) * 64, :], gsrc.rearrange("(p o) -> p o", o=1), writes=[Gt.b], **SLOW)
        S.dma("sp", SUBG[:], subln_g.rearrange("(p o) -> p o", o=1), writes=[SUBG.b], **SLOW)
        S.dma("sp", CQG[:], caq_g.rearrange("(c p) -> p c", p=128), writes=[CQG.b], **SLOW)
        S.dma("sp", CKG[:], cak_g.rearrange("(c p) -> p c", p=128), writes=[CKG.b], **SLOW)
        for j in range(3):
            S.dma("sp", CW[:, j, :], conv_w[j].rearrange("(c p) -> p c", p=128), writes=[CW.b], **SLOW)
        S.dma("sp", CB[:], conv_b.rearrange("(c p) -> p c", p=128), writes=[CB.b], **SLOW)
        S.dma("sp", KPOS[:], c_kpos.rearrange("(p o) -> p o", o=1), writes=[KPOS.b], **SLOW)
        for i, src in enumerate((lq1, lk1, lq2, lk2)):
            S.dma("sp", LT[:, i:i + 1], src.rearrange("(p o) -> p o", o=1), writes=[LT.b], **SLOW)
        ts("dve", SUBG[:], SUBG[:], 1.0 - LAM0, None, ALU.mult, None, [SUBG.b], [SUBG.b])
        LP = sb(es, "LP", [64, 2], BF16)
        tt("dve", LP[:, 0:1], LT[:, 0:1], LT[:, 1:2], ALU.mult, [LT.b], [LP.b])
        tt("dve", LP[:, 1:2], LT[:, 2:3], LT[:, 3:4], ALU.mult, [LT.b], [LP.b])
        bk = bank()
        mm(bk, ONE1[0:64, :], LP[:, :], True, True, [ONE1.b, LP.b], out=bk[:, 0:2])
        LE = sb(es, "LE", [128, 2], F32)
        act(LE[:], bk[:, 0:2], AF.Exp, [bk.b], [LE.b])
        stt(NLAM[:], LE[:, 1:2], -LAM0, LE[:, 0:1], ALU.add, ALU.subtract, [LE.b], [NLAM.b])

        def convert_ffn_weights(stack):
            CIN = sb(stack, "CIN", [128, 4096], F32)
            COUT = sb(stack, "COUT", [128, 4096], BF16)
            for (dst, src) in ((wg_s, ffn_wg), (wv_s, ffn_wv)):
                for f0 in range(0, NFC, 4):
                    nf = min(4, NFC - f0)
                    cin = CIN[:, 0:8 * nf * 128].rearrange("p (kc x) -> p kc x", kc=8)
                    S.dma("pool", cin, src[:, f0 * 128:(f0 + nf) * 128].rearrange("(kc p) x -> p kc x", p=128),
                          writes=[CIN.b])
                    cout = COUT[:, 0:nf * 1024].rearrange("p (fc kc n) -> p fc kc n", kc=8, n=128)
                    cp("pool", cout, cin.rearrange("p kc (fc n) -> p fc kc n", n=128), [CIN.b], [COUT.b])
                    S.dma("pool", dst[:, f0:f0 + nf].rearrange("p fc kc n -> p (fc kc n)"), COUT[:, 0:nf * 1024],
                          reads=[COUT.b])
            for m in range(8):
                for f0 in range(0, NFC, 8):
                    f1 = min(f0 + 8, NFC)
                    S.dma("pool", CIN[:, f0 * 128:f1 * 128].rearrange("p (fc n) -> p fc n", n=128),
                          ffn_wd[f0 * 128:f1 * 128, m * 128:(m + 1) * 128].rearrange("(fc p) n -> p fc n", p=128),
                          writes=[CIN.b])
                cp("pool", COUT[:, 0:NFC * 128], CIN[:, 0:NFC * 128], [CIN.b], [COUT.b])
                S.dma("pool", wd_s[:, m].rearrange("p fc n -> p (fc n)"), COUT[:, 0:NFC * 128], reads=[COUT.b])

        def load_norm_tile(X, xsrc_ap, nblk, G, xnT, rs, ss, junk, raw_xT=None):
            S.dma("sp", X[:, 0:nblk, :], xsrc_ap, writes=[X.b])
            for j in range(nblk):
                act(junk[:], X[:, j, :], AF.Square, [X.b], [junk.b, ss.b], accum_out=ss[:, j:j + 1])
            act(rs[:, 0:nblk], ss[:, 0:nblk], AF.Sqrt, [ss.b], [rs.b], scale=1.0 / 1024, bias=EPS)
            recip(rs[:, 0:nblk], rs[:, 0:nblk], [rs.b], [rs.b])
            if raw_xT is not None:
                for c in range(8):
                    bk = bank()
                    for j in range(nblk):
                        tr(bk, bk[:, j * 128:(j + 1) * 128], X[:, j, c * 128:(c + 1) * 128], IDF, [X.b])
                    cp("act", raw_xT[:, c, 0:nblk * 128], bk[:, 0:nblk * 128], [bk.b], [raw_xT.b])
            for j in range(nblk):
                ts("dve", X[:, j, :], X[:, j, :], rs[:, j:j + 1], None, ALU.mult, None, [X.b, rs.b], [X.b])
            for c in range(8):
                bk = bank()
                for j in range(nblk):
                    tr(bk, bk[:, j * 128:(j + 1) * 128], X[:, j, c * 128:(c + 1) * 128], IDF, [X.b])
                ts("dve", xnT[:, c, 0:nblk * 128], bk[:, 0:nblk * 128], G[:, c:c + 1], None, ALU.mult, None,
                   [bk.b, G.b], [xnT.b])

        def ln_fm(xT, G, xnT, n, sq, rstd):
            bk = bank()
            for c in range(8):
                act(sq[:, c % 2, 0:n], xT[:, c, 0:n], AF.Square, [xT.b], [sq.b])
                mm(bk, ON1024[:], sq[:, c % 2, 0:n], c == 0, c == 7, [ON1024.b, sq.b], out=bk[:, 0:n])
            act(rstd[:, 0:n], bk[:, 0:n], AF.Sqrt, [bk.b], [rstd.b], bias=EPS)
            recip(rstd[:, 0:n], rstd[:, 0:n], [rstd.b], [rstd.b])
            for c in range(8):
                stt(xnT[:, c, 0:n], xT[:, c, 0:n], G[:, c:c + 1], rstd[:, 0:n], ALU.mult, ALU.mult,
                    [xT.b, G.b, rstd.b], [xnT.b])

        def store_tm(src_fm, nchunks, n, dst_ap, stage, ident=IDF):
            nblk = (n + 127) // 128
            for j in range(nblk):
                w = min(128, n - j * 128)
                for c0 in range(0, nchunks, 4):
                    bk = bank()
                    for c in range(c0, min(c0 + 4, nchunks)):
                        tr(bk, bk[0:w, (c - c0) * 128:(c - c0 + 1) * 128], src_fm[:, c, j * 128:j * 128 + w], ident,
                           [src_fm.b])
                    nn = (min(c0 + 4, nchunks) - c0) * 128
                    cp("act", stage[0:w, j, c0 * 128:c0 * 128 + nn], bk[0:w, 0:nn], [bk.b], [stage.b])
            if n % 128 == 0:
                S.dma("sp", dst_ap.rearrange("(j p) f -> p j f", p=128), stage[:, 0:nblk, 0:nchunks * 128],
                      reads=[stage.b])
            else:
                S.dma("sp", dst_ap, stage[0:n, 0, 0:nchunks * 128], reads=[stage.b])


        xnTs = sb(es, "xnTs", [128, 8, TS], BF16)
        xsT = sb(es, "xsT", [128, 8, TS], F32)
        MIXs = sb(es, "MIXs", [128, 8, TS], BF16)
        usT = sb(es, "usT", [128, 4, TS], BF16)
        QTs = sb(es, "QTs", [128, 4, TS], BF16); KTs = sb(es, "KTs", [128, 4, TS], BF16)
        if 'S' in PH:
            with ExitStack() as ph:
                Xs_ = sb(ph, "Xs_", [128, 1024], F32)
                rs = sb(ph, "rs_s", [128, 1], F32); ss = sb(ph, "ss_s", [128, 1], F32)
                junk = sb(ph, "junk_s", [128, 1024], F32)
                S.dma("sp", Xs_[0:TS, :], xs, writes=[Xs_.b])
                act(junk[0:TS, :], Xs_[0:TS, :], AF.Square, [Xs_.b], [junk.b, ss.b], accum_out=ss[0:TS, 0:1])
                act(rs[0:TS, :], ss[0:TS, :], AF.Sqrt, [ss.b], [rs.b], scale=1.0 / 1024, bias=EPS)
                recip(rs[0:TS, :], rs[0:TS, :], [rs.b], [rs.b])
                for c in range(8):
                    bk = bank()
                    tr(bk, bk[:, 0:TS], Xs_[0:TS, c * 128:(c + 1) * 128], TB(IDF.t[0:TS, 0:TS], IDF.b), [Xs_.b])
                    cp("act", xsT[:, c, :], bk[:, 0:TS], [bk.b], [xsT.b])
                ts("dve", Xs_[0:TS, :], Xs_[0:TS, :], rs[0:TS, 0:1], None, ALU.mult, None, [Xs_.b, rs.b], [Xs_.b])
                for c in range(8):
                    bk = bank()
                    tr(bk, bk[:, 0:TS], Xs_[0:TS, c * 128:(c + 1) * 128], TB(IDF.t[0:TS, 0:TS], IDF.b), [Xs_.b])
                    ts("dve", xnTs[:, c, :], bk[:, 0:TS], G1[:, c:c + 1], None, ALU.mult, None, [bk.b, G1.b], [xnTs.b])
                S.barrier()

        MKT = sb(es, "MKT", [128, 8, 256], BF16)
        MV = sb(es, "MV", [128, 2, 1024], BF16)
        with ExitStack() as ph:
          if 'M' in PH:
            Xm = sb(ph, "Xm", [128, 2, 1024], F32)
            xnTm = sb(ph, "xnTm", [128, 8, 256], BF16)
            rs = sb(ph, "rs_m", [128, 4], F32); ss = sb(ph, "ss_m", [128, 4], F32)
            junk = sb(ph, "junk_m", [128, 1024], F32)
            Wk = sb(ph, "Wk", [128, 8, 1024], BF16); Wv = sb(ph, "Wv", [128, 8, 1024], BF16)
            mkraw = sb(ph, "mkraw", [128, 8, 256], F32)
            sqm = sb(ph, "sqm", [128, 2, 256], BF16); rstdm = sb(ph, "rstdm", [128, 256], F32)
            stg = sb(ph, "stg_m", [128, 2, 1024], F32)
            S.dma("pool", Wk[:], ca_wk.rearrange("(c p) n -> p c n", p=128), writes=[Wk.b])
            S.dma("pool", Wv[:], ca_wv.rearrange("(c p) n -> p c n", p=128), writes=[Wv.b])
            KS = int(os.environ.get('KSTOP', '9'))
            load_norm_tile(Xm, memp.rearrange("(j p) f -> p j f", p=128), 2, GM, xnTm, rs, ss, junk)
            for m in (range(8) if KS >= 2 else ()):
                bk = bank()
                for kc in range(8):
                    mm(bk, Wk[:, kc, m * 128:(m + 1) * 128], xnTm[:, kc, :], kc == 0, kc == 7, [Wk.b, xnTm.b],
                       out=bk[:, 0:256])
                cp("act", mkraw[:, m, :], bk[:, 0:256], [bk.b], [mkraw.b])
            for h in (range(4) if KS >= 3 else ()):
                bk = bank()
                for dh in range(2):
                    act(sqm[:, dh, :], mkraw[:, 2 * h + dh, :], AF.Square, [mkraw.b], [sqm.b])
                    mm(bk, ON256[:], sqm[:, dh, :], dh == 0, dh == 1, [ON256.b, sqm.b], out=bk[:, 0:256])
                act(rstdm[:], bk[:, 0:256], AF.Sqrt, [bk.b], [rstdm.b], bias=EPS)
                recip(rstdm[:], rstdm[:], [rstdm.b], [rstdm.b])
                for dh in range(2):
                    stt(mkraw[:, 2 * h + dh, :], mkraw[:, 2 * h + dh, :], CKG[:, dh:dh + 1], rstdm[:], ALU.mult,
                        ALU.mult, [mkraw.b, CKG.b, rstdm.b], [mkraw.b])
                    cp("act", MKT[:, 2 * h + dh, :], mkraw[:, 2 * h + dh, :], [mkraw.b], [MKT.b])
            if KS >= 4:
                store_tm(mkraw, 8, 256, mk_p, stg)
            for j in (range(2) if KS >= 5 else ()):
                for half in range(2):
                    bk = bank()
                    for kc in range(8):
                        mm(bk, xnTm[:, kc, j * 128:(j + 1) * 128], Wv[:, kc, half * 512:(half + 1) * 512], kc == 0,
                           kc == 7, [Wv.b, xnTm.b])
                    cp("act", stg[:, j, half * 512:(half + 1) * 512], bk[:], [bk.b], [stg.b])
                    cp("dve", MV[:, j, half * 512:(half + 1) * 512], bk[:], [bk.b], [MV.b])
            S.dma("sp", mv_p.rearrange("(j p) f -> p j f", p=128), stg[:], reads=[stg.b])
            S.barrier()


        s5 = ExitStack()
        if 'B' in PH:
            TWO_PI = 2.0 * math.pi
            PI_S = 3.1415925
            ZT = sb(s5, "ZT", [128, 4, 8, 2, 128], BF16)
            CLR = sb(s5, "CLR", [128, 16, 8, 32], BF16)
            CLI = sb(s5, "CLI", [128, 16, 8, 32], BF16)
            BDT = sb(s5, "BDT", [128, 4, 8, 128], BF16)
            LRE = sb(s5, "LRE", [128, 16, NPW], F32); LIM = sb(s5, "LIM", [128, 16, NPW], F32)
            with ExitStack() as tb:
                AR = sb(tb, "AR", [128, 16], F32); AI = sb(tb, "AI", [128, 16], F32)
                DT = sb(tb, "DT", [128, 16], F32)
                ARD = sb(tb, "ARD", [128, 16], F32); TH = sb(tb, "TH", [128, 16], F32)
                PWN = sb(tb, "PWN", [128, NPW], F32)
                ANG = sb(tb, "ANG", [128, 16, NPW], F32); R = sb(tb, "Rr", [128, 16, NPW], F32)
                KF = sb(tb, "KF", [128, 16, NPW], F32); KI = sb(tb, "KI", [128, 16, NPW], I32)
                MG = sb(tb, "MG", [128, 16, NPW], F32)
                S.dma("sp", AR[:], a_re.rearrange("(q g2) p -> (g2 p) q", g2=2), writes=[AR.b], **SLOW)
                S.dma("sp", AI[:], a_im.rearrange("(q g2) p -> (g2 p) q", g2=2), writes=[AI.b], **SLOW)
                for g2 in range(2):
                    S.dma("sp", DT[g2 * 64:(g2 + 1) * 64, :],
                          log_dt.rearrange("(q g2) -> g2 q", g2=2)[g2:g2 + 1, :].to_broadcast([64, 16]),
                          writes=[DT.b], **SLOW)
                S.dma("sp", PWN[:], c_pwn.rearrange("(o n) -> o n", o=1).to_broadcast([128, NPW]), writes=[PWN.b],
                      **SLOW)
                act(DT[:], DT[:], AF.Exp, [DT.b], [DT.b])
                tt("dve", ARD[:], AR[:], DT[:], ALU.mult, [AR.b, DT.b], [ARD.b])
                tt("dve", TH[:], AI[:], DT[:], ALU.mult, [AI.b, DT.b], [TH.b])
                bshape = [128, 16, NPW]
                tt("dve", MG[:], ARD[:, :].unsqueeze(2).to_broadcast(bshape), PWN[:, :].unsqueeze(1).to_broadcast(bshape),
                   ALU.mult, [ARD.b, PWN.b], [MG.b])
                act(MG[:], MG[:], AF.Exp, [MG.b], [MG.b])
                tt("dve", ANG[:], TH[:, :].unsqueeze(2).to_broadcast(bshape), PWN[:, :].unsqueeze(1).to_broadcast(bshape),
                   ALU.mult, [TH.b, PWN.b], [ANG.b])
                ts("dve", KF[:], ANG[:], 1.0 / TWO_PI, 0.5, ALU.mult, ALU.add, [ANG.b], [KF.b])
                cp("dve", KI[:], KF[:], [KF.b], [KI.b])
                cp("dve", KF[:], KI[:], [KI.b], [KF.b])
                stt(R[:], KF[:], -TWO_PI, ANG[:], ALU.mult, ALU.add, [KF.b, ANG.b], [R.b])

                def wrap(Rt):
                    ts("dve", KF[:], Rt[:], -math.pi, None, ALU.is_lt, None, [Rt.b], [KF.b])
                    stt(Rt[:], KF[:], TWO_PI, Rt[:], ALU.mult, ALU.add, [KF.b, Rt.b], [Rt.b])
                    ts("dve", KF[:], Rt[:], math.pi, None, ALU.is_gt, None, [Rt.b], [KF.b])
                    stt(Rt[:], KF[:], -TWO_PI, Rt[:], ALU.mult, ALU.add, [KF.b, Rt.b], [Rt.b])
                    ts("dve", Rt[:], Rt[:], -PI_S, PI_S, ALU.max, ALU.min, [Rt.b], [Rt.b])
                wrap(R)
                act(LIM[:], R[:], AF.Sin, [R.b], [LIM.b])
                ts("dve", R[:], R[:], math.pi / 2, None, ALU.add, None, [R.b], [R.b])
                wrap(R)
                act(LRE[:], R[:], AF.Sin, [R.b], [LRE.b])
                tt("dve", LRE[:], LRE[:], MG[:], ALU.mult, [LRE.b, MG.b], [LRE.b])
                tt("dve", LIM[:], LIM[:], MG[:], ALU.mult, [LIM.b, MG.b], [LIM.b])
                NRE = sb(tb, "NRE", [128, 16], F32); DEN = sb(tb, "DEN", [128, 16], F32)
                FRE = sb(tb, "FRE", [128, 16], F32); FIM = sb(tb, "FIM", [128, 16], F32)
                T1 = sb(tb, "T1", [128, 16], F32)
                ts("dve", NRE[:], LRE[:, :, 1], -1.0, None, ALU.add, None, [LRE.b], [NRE.b])
                tt("dve", DEN[:], AR[:], AR[:], ALU.mult, [AR.b], [DEN.b])
                tt("dve", T1[:], AI[:], AI[:], ALU.mult, [AI.b], [T1.b])
                tt("dve", DEN[:], DEN[:], T1[:], ALU.add, [DEN.b, T1.b], [DEN.b])
                recip(DEN[:], DEN[:], [DEN.b], [DEN.b])
                tt("dve", FRE[:], NRE[:], AR[:], ALU.mult, [NRE.b, AR.b], [FRE.b])
                tt("dve", T1[:], LIM[:, :, 1], AI[:], ALU.mult, [LIM.b, AI.b], [T1.b])
                tt("dve", FRE[:], FRE[:], T1[:], ALU.add, [FRE.b, T1.b], [FRE.b])
                tt("dve", FRE[:], FRE[:], DEN[:], ALU.mult, [FRE.b, DEN.b], [FRE.b])
                tt("dve", FIM[:], LIM[:, :, 1], AR[:], ALU.mult, [LIM.b, AR.b], [FIM.b])
                tt("dve", T1[:], NRE[:], AI[:], ALU.mult, [NRE.b, AI.b], [T1.b])
                tt("dve", FIM[:], FIM[:], T1[:], ALU.subtract, [FIM.b, T1.b], [FIM.b])
                tt("dve", FIM[:], FIM[:], DEN[:], ALU.mult, [FIM.b, DEN.b], [FIM.b])
                BR = sb(tb, "BR", [128, 16, 16], F32); BI = sb(tb, "BI", [128, 16, 16], F32)
                BBR = sb(tb, "BBR", [128, 16, 16], F32); BBI = sb(tb, "BBI", [128, 16, 16], F32)
                T2 = sb(tb, "T2", [128, 16, 16], F32)
                S.dma("sp", BR[:], b_re.rearrange("(q g2) p c -> (g2 p) q c", g2=2), writes=[BR.b], **SLOW)
                S.dma("sp", BI[:], b_im.rearrange("(q g2) p c -> (g2 p) q c", g2=2), writes=[BI.b], **SLOW)
                s3 = [128, 16, 16]
                fr = FRE[:, :].unsqueeze(2).to_broadcast(s3); fi = FIM[:, :].unsqueeze(2).to_broadcast(s3)
                tt("dve", BBR[:], BR[:], fr, ALU.mult, [BR.b, FRE.b], [BBR.b])
                tt("dve", T2[:], BI[:], fi, ALU.mult, [BI.b, FIM.b], [T2.b])
                tt("dve", BBR[:], BBR[:], T2[:], ALU.subtract, [BBR.b, T2.b], [BBR.b])
                tt("dve", BBI[:], BI[:], fr, ALU.mult, [BI.b, FRE.b], [BBI.b])
                tt("dve", T2[:], BR[:], fi, ALU.mult, [BR.b, FIM.b], [T2.b])
                tt("dve", BBI[:], BBI[:], T2[:], ALU.add, [BBI.b, T2.b], [BBI.b])
                ZR = sb(tb, "ZR", [128, 16, 8, 16], F32); ZI = sb(tb, "ZI", [128, 16, 8, 16], F32)
                T3 = sb(tb, "T3", [128, 16, 8, 16], F32)
                s4 = [128, 16, 8, 16]
                lr = LRE[:, :, 0:8].unsqueeze(3).to_broadcast(s4); li = LIM[:, :, 0:8].unsqueeze(3).to_broadcast(s4)
                br_ = BBR[:, :, :].unsqueeze(2).to_broadcast(s4); bi_ = BBI[:, :, :].unsqueeze(2).to_broadcast(s4)
                tt("dve", ZR[:], lr, br_, ALU.mult, [LRE.b, BBR.b], [ZR.b])
                tt("dve", T3[:], li, bi_, ALU.mult, [LIM.b, BBI.b], [T3.b])
                tt("dve", ZR[:], ZR[:], T3[:], ALU.subtract, [ZR.b, T3.b], [ZR.b])
                tt("dve", ZI[:], lr, bi_, ALU.mult, [LRE.b, BBI.b], [ZI.b])
                tt("dve", T3[:], li, br_, ALU.mult, [LIM.b, BBR.b], [T3.b])
                tt("dve", ZI[:], ZI[:], T3[:], ALU.add, [ZI.b, T3.b], [ZI.b])
                E4 = sb(tb, "E4", [128, 4, 8, 2, 128], F32)
                memset("pool", E4[:], 0.0, [E4.b])
                for ri, Zt in enumerate((ZR, ZI)):
                    for gc in range(4):
                        for g2 in range(2):
                            pr = slice(g2 * 64, (g2 + 1) * 64)
                            dst = E4[pr, gc, :, ri, :].rearrange("p t (j x) -> p t j x", x=32)[:, :, :, g2 * 16:(g2 + 1) * 16]
                            src = Zt[pr, gc * 4:(gc + 1) * 4, :, :].rearrange("p j t c -> p t j c")
                            cp("pool", dst, src, [Zt.b], [E4.b])
                for gc in range(4):
                    for ri in range(2):
                        for s0 in (0, 4):
                            bk = bank()
                            for sp_ in range(s0, s0 + 4):
                                tr(bk, bk[:, (sp_ - s0) * 128:(sp_ - s0 + 1) * 128], E4[:, gc, 7 - sp_, ri, :], IDF, [E4.b])
                            cp("act", ZT[:, gc, s0:s0 + 4, ri, :], bk[:].rearrange("p (s n) -> p s n", n=128), [bk.b],
                               [ZT.b])
                CN = sb(tb, "CN", [128, 2, 4, 64], F32)
                CE = sb(tb, "CE", [128, 2, 4, 2, 64], F32)
                CTR = sb(tb, "CTR", [128, 4, 128], F32); CTI = sb(tb, "CTI", [128, 4, 128], F32)
                CTIN = sb(tb, "CTIN", [128, 4, 128], F32)
                G2M = sb(tb, "G2M", [128, 2], F32)
                S.dma("sp", G2M[:], c_g2m, writes=[G2M.b])
                S.dma("sp", CN[:, 0, :, :], c_re.rearrange("(gc r) c p -> (r c) gc p", gc=4), writes=[CN.b], **SLOW)
                S.dma("sp", CN[:, 1, :, :], c_im.rearrange("(gc r) c p -> (r c) gc p", gc=4), writes=[CN.b], **SLOW)
                for ri in range(2):
                    for g2 in range(2):
                        ts("dve", CE[:, ri, :, g2, :], CN[:, ri, :, :], G2M[:, g2:g2 + 1], None, ALU.mult, None,
                           [CN.b, G2M.b], [CE.b])
                for ri, CTt in enumerate((CTR, CTI)):
                    bk = bank()
                    for gc in range(4):
                        tr(bk, bk[:, gc * 128:(gc + 1) * 128], CE[:, ri, gc, :, :].rearrange("p a b -> p (a b)"), IDF,
                           [CE.b])
                    cp("act", CTt[:], bk[:].rearrange("p (g n) -> p g n", n=128), [bk.b], [CTt.b])
                ts("dve", CTIN[:], CTI[:], -1.0, None, ALU.mult, None, [CTI.b], [CTIN.b])
                s5s = [128, 16, 8, 32]
                T4 = sb(tb, "T4", [128, 16, 8, 32], F32); T5 = sb(tb, "T5", [128, 16, 8, 32], F32)
                ctr = CTR[:, :, :].rearrange("p g (j x) -> p (g j) x", x=32).unsqueeze(2).to_broadcast(s5s)
                cti = CTI[:, :, :].rearrange("p g (j x) -> p (g j) x", x=32).unsqueeze(2).to_broadcast(s5s)
                l1r = LRE[:, :, 1:9].unsqueeze(3).to_broadcast(s5s); l1i = LIM[:, :, 1:9].unsqueeze(3).to_broadcast(s5s)
                tt("dve", T4[:], ctr, l1r, ALU.mult, [CTR.b, LRE.b], [T4.b])
                tt("dve", T5[:], cti, l1i, ALU.mult, [CTI.b, LIM.b], [T5.b])
                tt("dve", CLR[:], T4[:], T5[:], ALU.subtract, [T4.b, T5.b], [CLR.b])
                tt("dve", T4[:], ctr, l1i, ALU.mult, [CTR.b, LIM.b], [T4.b])
                tt("dve", T5[:], cti, l1r, ALU.mult, [CTI.b, LRE.b], [T5.b])
                tt("dve", T4[:], T4[:], T5[:], ALU.add, [T4.b, T5.b], [T4.b])
                ts("dve", CLI[:], T4[:], -1.0, None, ALU.mult, None, [T4.b], [CLI.b])
                MASK16 = sb(tb, "MASK16", [128, 128], F32); DCOL = sb(tb, "DCOL", [128, 4], F32)
                T6 = sb(tb, "T6", [128, 128], F32)
                S.dma("sp", MASK16[:], c_mask16, writes=[MASK16.b])
                S.dma("sp", DCOL[:], ssm_d.rearrange("(gc g8) c -> (g8 c) gc", gc=4), writes=[DCOL.b], **SLOW)
                for gc in range(4):
                    for tau in range(8):
                        bk = bank()
                        mm(bk, E4[:, gc, tau, 0, :], CTR[:, gc, :], True, False, [E4.b, CTR.b], out=bk[:, 0:128])
                        mm(bk, E4[:, gc, tau, 1, :], CTIN[:, gc, :], False, True, [E4.b, CTIN.b], out=bk[:, 0:128])
                        if tau == 0:
                            tt("dve", T6[:], bk[:, 0:128], MASK16[:], ALU.mult, [bk.b, MASK16.b], [T6.b])
                            stt(BDT[:, gc, 0, :], IDF[:], DCOL[:, gc:gc + 1], T6[:], ALU.mult, ALU.add,
                                [IDF.b, DCOL.b, T6.b], [BDT.b])
                        else:
                            tt("dve", BDT[:, gc, tau, :], bk[:, 0:128], MASK16[:], ALU.mult, [bk.b, MASK16.b], [BDT.b])
                S.barrier()

            pb = ExitStack()
            uT = sb(pb, "uT", [128, 4, T], BF16)
            with ExitStack() as ph:
                X = sb(ph, "Xu", [128, 4, 1024], F32)
                xnT = sb(ph, "xnTu", [128, 8, NT], BF16)
                rs = sb(ph, "rsu", [128, 4], F32); ss = sb(ph, "ssu", [128, 4], F32)
                junk = sb(ph, "junku", [128, 1024], F32)
                Wu = sb(ph, "Wu", [128, 8, 512], BF16)
                S.dma("pool", Wu[:], w_in[:, 0:512].rearrange("(c p) n -> p c n", p=128), writes=[Wu.b])
                for it in range(NTILES):
                    t0 = it * NT
                    load_norm_tile(X, xp[t0:t0 + NT, :].rearrange("(j p) f -> p j f", p=128), 4, G1, xnT, rs, ss, junk)
                    for m in range(4):
                        bk = bank()
                        for kc in range(8):
                            mm(bk, Wu[:, kc, m * 128:(m + 1) * 128], xnT[:, kc, :], kc == 0, kc == 7, [Wu.b, xnT.b])
                        cp("act", uT[:, m, t0:t0 + NT], bk[:], [bk.b], [uT.b])
                if 'S' in PH:
                    for m in range(4):
                        bk = bank()
                        for kc in range(8):
                            mm(bk, Wu[:, kc, m * 128:(m + 1) * 128], xnTs[:, kc, :], kc == 0, kc == 7, [Wu.b, xnTs.b],
                               out=bk[:, 0:TS])
                        cp("act", usT[:, m, :], bk[:, 0:TS], [bk.b], [usT.b])
                S.barrier()
            if 'S' in PH:
              with ExitStack() as ph:
                STt = sb(ph, "STt", [NS, 2, 2048], F32)
                S0 = sb(ph, "S0", [128, 2, 16, NS], F32); S0b = sb(ph, "S0b", [128, 2, 16, NS], BF16)
                SN = sb(ph, "SN", [128, 2, 16, NS], F32)
                W1 = sb(ph, "W1", [128, 16, NS], F32); W2 = sb(ph, "W2", [128, 16, NS], F32)
                hst = sb(ph, "hst", [NS, 2, 2048], F32)
                Y2s = sb(ph, "Y2s", [128, TS], F32); Y3s = sb(ph, "Y3s", [128, TS], F32)
                Gts = sb(ph, "Gts", [128, 4, TS], F32); Gbs = sb(ph, "Gbs", [128, 4, TS], BF16)
                GLUWs = sb(ph, "GLUWs", [128, 4, 512], BF16)
                S.dma("pool", GLUWs[:], glu_w.rearrange("(c p) n -> p c n", p=128), writes=[GLUWs.b])
                S.dma("sp", STt[:, 0, :], st_re, writes=[STt.b])
                S.dma("sp", STt[:, 1, :], st_im, writes=[STt.b])
                ID16 = TB(IDF.t[0:NS, 0:NS], IDF.b)
                for ri in range(2):
                    bk = bank()
                    for q16 in range(16):
                        tr(bk, bk[:, q16 * NS:(q16 + 1) * NS], STt[:, ri, q16 * 128:(q16 + 1) * 128], ID16, [STt.b])
                    cp("act", S0[:, ri, :, :], bk[:, 0:16 * NS].rearrange("p (q s) -> p q s", s=NS), [bk.b], [S0.b])
                    cp("dve", S0b[:, ri, :, :], S0[:, ri, :, :], [S0.b], [S0b.b])
                for q16 in range(16):
                    gc, j = q16 // 4, q16 % 4
                    pr = slice(32 * j, 32 * j + 32)
                    uv = usT[:, gc, :].rearrange("p (n l) -> p l n", l=4)
                    bk = bank()
                    for ri in range(2):
                        for sp_ in range(4):
                            mm(bk, ZT[pr, gc, 4 + sp_, ri, :], uv[pr, sp_, :], sp_ == 0, sp_ == 3, [ZT.b, usT.b],
                               out=bk[:, ri * NS:(ri + 1) * NS], tile_position=(32 * j, 0))
                    cp("act", SN[:, :, q16, :], bk[:, 0:2 * NS].rearrange("p (r s) -> p r s", s=NS), [bk.b], [SN.b])
                s3 = [128, 16, NS]
                l4r = LRE[:, :, 4:5].to_broadcast(s3); l4i = LIM[:, :, 4:5].to_broadcast(s3)
                tt("dve", W1[:], S0[:, 0, :, :], l4r, ALU.mult, [S0.b, LRE.b], [W1.b])
                tt("dve", W2[:], S0[:, 1, :, :], l4i, ALU.mult, [S0.b, LIM.b], [W2.b])
                tt("dve", W1[:], W1[:], W2[:], ALU.subtract, [W1.b, W2.b], [W1.b])
                tt("dve", SN[:, 0, :, :], SN[:, 0, :, :], W1[:], ALU.add, [SN.b, W1.b], [SN.b])
                tt("dve", W1[:], S0[:, 0, :, :], l4i, ALU.mult, [S0.b, LIM.b], [W1.b])
                tt("dve", W2[:], S0[:, 1, :, :], l4r, ALU.mult, [S0.b, LRE.b], [W2.b])
                tt("dve", W1[:], W1[:], W2[:], ALU.add, [W1.b, W2.b], [W1.b])
                tt("dve", SN[:, 1, :, :], SN[:, 1, :, :], W1[:], ALU.add, [SN.b, W1.b], [SN.b])
                for ri in range(2):
                    for q0 in range(0, 16, 4):
                        bk = bank()
                        for q16 in range(q0, q0 + 4):
                            tr(bk, bk[0:NS, (q16 - q0) * 128:(q16 - q0 + 1) * 128], SN[:, ri, q16, :], IDF, [SN.b])
                        cp("act", hst[:, ri, q0 * 128:(q0 + 4) * 128], bk[0:NS, :], [bk.b], [hst.b])
                S.dma("sp", hre_s, hst[:, 0, :], reads=[hst.b])
                S.dma("sp", him_s, hst[:, 1, :], reads=[hst.b])
                for gc in range(4):
                    bk = bank()
                    uv = usT[:, gc, :].rearrange("p (n l) -> p l n", l=4)
                    ov = bk[:, 0:TS].rearrange("p (n l) -> p l n", l=4)
                    for tp in range(4):
                        for sp_ in range(tp + 1):
                            mm(bk, BDT[:, gc, tp - sp_, :], uv[:, sp_, :], sp_ == 0, False, [BDT.b, usT.b],
                               out=ov[:, tp, :])
                        for j in range(4):
                            q16 = gc * 4 + j
                            for ri, CLt in enumerate((CLR, CLI)):
                                mm(bk, CLt[:, q16, tp, :], S0b[:, ri, q16, :], False, (j == 3 and ri == 1),
                                   [CLt.b, S0b.b], out=ov[32 * j:32 * j + 32, tp, :], tile_position=(0, 32 * j))
                    act(Y2s[:], bk[:, 0:TS], AF.Square, [bk.b], [Y2s.b])
                    ts("dve", Y2s[:], Y2s[:], 0.044715, 1.0, ALU.mult, ALU.add, [Y2s.b], [Y2s.b])
                    tt("dve", Y3s[:], Y2s[:], bk[:, 0:TS], ALU.mult, [Y2s.b, bk.b], [Y3s.b])
                    act(Y3s[:], Y3s[:], AF.Sigmoid, [Y3s.b], [Y3s.b], scale=2.0 * math.sqrt(2.0 / math.pi))
                    tt("dve", Gts[:, gc, :], Y3s[:], bk[:, 0:TS], ALU.mult, [Y3s.b, bk.b], [Gts.b])
                    cp("act", Gbs[:, gc, :], Gts[:, gc, :], [Gts.b], [Gbs.b])
                for m in range(4):
                    bk = bank()
                    for kc in range(4):
                        mm(bk, GLUWs[:, kc, m * 128:(m + 1) * 128], Gbs[:, kc, :], kc == 0, kc == 3, [GLUWs.b, Gbs.b],
                           out=bk[:, 0:TS])
                    act(Y3s[:], bk[:, 0:TS], AF.Sigmoid, [bk.b], [Y3s.b])
                    tt("dve", MIXs[:, m, :], Y3s[:], Gts[:, m, :], ALU.mult, [Y3s.b, Gts.b], [MIXs.b])
                S.barrier()
            SW = sb(pb, "SW", [128, 2, 8, NCH], F32)
            SP = sb(pb, "SPv", [128, 2, 16, NCH], BF16)
            EE = sb(pb, "EE", [128, 2, 16, NSC], F32)
            TA = sb(pb, "TA", [128, 8, NSC], F32); TBt = sb(pb, "TBt", [128, 8, NSC], F32)
            U1 = sb(pb, "U1", [128, 8], F32); U2 = sb(pb, "U2", [128, 8], F32)
            V1 = sb(pb, "V1", [128, 8, SC], F32); V2 = sb(pb, "V2", [128, 8, SC], F32)
            memset("dve", SP[:, :, :, 0:1], 0.0, [SP.b])
            for hq in range(2):
                qs = slice(8 * hq, 8 * hq + 8)
                for q8 in range(8):
                    q16 = 8 * hq + q8
                    gc, j = q16 // 4, q16 % 4
                    pr = slice(32 * j, 32 * j + 32)
                    uv = uT[:, gc, :].rearrange("p (n l) -> p l n", l=L)
                    for ri in range(2):
                        bk = bank()
                        for sp_ in range(L):
                            mm(bk, ZT[pr, gc, sp_, ri, :], uv[pr, sp_, :], sp_ == 0, sp_ == L - 1, [ZT.b, uT.b],
                               tile_position=(32 * j, 0))
                        cp("act" if ri == 0 else "dve", SW[:, ri, q8, :], bk[:], [bk.b], [SW.b])
                Sv = SW[:, :, :, :].rearrange("p r q (s i) -> p r q s i", i=SC)
                sA = [128, 8, NSC]
                a8r = LRE[:, qs, 8:9].to_broadcast(sA); a8i = LIM[:, qs, 8:9].to_broadcast(sA)
                for i in range(1, SC):
                    pre_r = Sv[:, 0, :, :, i - 1]; pre_i = Sv[:, 1, :, :, i - 1]
                    tt("dve", TA[:], pre_r, a8r, ALU.mult, [SW.b, LRE.b], [TA.b])
                    tt("dve", TBt[:], pre_i, a8i, ALU.mult, [SW.b, LIM.b], [TBt.b])
                    tt("dve", TA[:], TA[:], TBt[:], ALU.subtract, [TA.b, TBt.b], [TA.b])
                    tt("dve", TBt[:], pre_i, a8r, ALU.mult, [SW.b, LRE.b], [TBt.b])
                    tt("dve", Sv[:, 0, :, :, i], Sv[:, 0, :, :, i], TA[:], ALU.add, [SW.b, TA.b], [SW.b])
                    tt("dve", TA[:], pre_r, a8i, ALU.mult, [SW.b, LIM.b], [TA.b])
                    tt("dve", TA[:], TA[:], TBt[:], ALU.add, [TA.b, TBt.b], [TA.b])
                    tt("dve", Sv[:, 1, :, :, i], Sv[:, 1, :, :, i], TA[:], ALU.add, [SW.b, TA.b], [SW.b])
                cp("dve", EE[:, :, qs, 0], Sv[:, :, :, 0, SC - 1], [SW.b], [EE.b])
                for sc in range(1, NSC):
                    er = EE[:, 0, qs, sc - 1]; ei = EE[:, 1, qs, sc - 1]
                    tt("dve", U1[:], er, LRE[:, qs, 23], ALU.mult, [EE.b, LRE.b], [U1.b])
                    tt("dve", U2[:], ei, LIM[:, qs, 23], ALU.mult, [EE.b, LIM.b], [U2.b])
                    tt("dve", U1[:], U1[:], U2[:], ALU.subtract, [U1.b, U2.b], [U1.b])
                    tt("dve", EE[:, 0, qs, sc], U1[:], Sv[:, 0, :, sc, SC - 1], ALU.add, [U1.b, SW.b], [EE.b])
                    tt("dve", U1[:], er, LIM[:, qs, 23], ALU.mult, [EE.b, LIM.b], [U1.b])
                    tt("dve", U2[:], ei, LRE[:, qs, 23], ALU.mult, [EE.b, LRE.b], [U2.b])
                    tt("dve", U1[:], U1[:], U2[:], ALU.add, [U1.b, U2.b], [U1.b])
                    tt("dve", EE[:, 1, qs, sc], U1[:], Sv[:, 1, :, sc, SC - 1], ALU.add, [U1.b, SW.b], [EE.b])
                cp("act", SP[:, 0, qs, 1:SC], SW[:, 0, :, 0:SC - 1], [SW.b], [SP.b])
                cp("act", SP[:, 1, qs, 1:SC], SW[:, 1, :, 0:SC - 1], [SW.b], [SP.b])
                sC = [128, 8, SC]
                PR = LRE[:, qs, 8:24]; PI = LIM[:, qs, 8:24]
                for sc in range(1, NSC):
                    er = EE[:, 0, qs, sc - 1:sc].to_broadcast(sC); ei = EE[:, 1, qs, sc - 1:sc].to_broadcast(sC)
                    tt("dve", V1[:], PR, er, ALU.mult, [LRE.b, EE.b], [V1.b])
                    tt("dve", V2[:], PI, ei, ALU.mult, [LIM.b, EE.b], [V2.b])
                    tt("dve", V1[:], V1[:], V2[:], ALU.subtract, [V1.b, V2.b], [V1.b])
                    tt("dve", V1[:], V1[:], Sv[:, 0, :, sc, :], ALU.add, [V1.b, SW.b], [V1.b])
                    tt("dve", V2[:], PR, ei, ALU.mult, [LRE.b, EE.b], [V2.b])
                    n_ = SC if sc < NSC - 1 else SC - 1
                    cp("act", SP[:, 0, qs, sc * SC + 1:sc * SC + 1 + n_], V1[:, :, 0:n_], [V1.b], [SP.b])
                    tt("dve", V1[:], PI, er, ALU.mult, [LIM.b, EE.b], [V1.b])
                    tt("dve", V2[:], V2[:], V1[:], ALU.add, [V1.b, V2.b], [V2.b])
                    tt("dve", V2[:], V2[:], Sv[:, 1, :, sc, :], ALU.add, [V2.b, SW.b], [V2.b])
                    cp("act", SP[:, 1, qs, sc * SC + 1:sc * SC + 1 + n_], V2[:, :, 0:n_], [V2.b], [SP.b])
                for sc in range(1, NSC):
                    cp("act", SP[:, :, qs, sc * SC], EE[:, :, qs, sc - 1], [EE.b], [SP.b])
            S.dma("sp", hre_p.rearrange("q x -> x q"), EE[:, 0, :, NSC - 1], reads=[EE.b], **SLOW)
            S.dma("sp", him_p.rearrange("q x -> x q"), EE[:, 1, :, NSC - 1], reads=[EE.b], **SLOW)
            GLUW = sb(pb, "GLUW", [128, 4, 512], BF16)
            S.dma("pool", GLUW[:], glu_w.rearrange("(c p) n -> p c n", p=128), writes=[GLUW.b])
            Y2 = sb(pb, "Y2", [128, NT], F32); Y3 = sb(pb, "Y3", [128, NT], F32)
            Gt = sb(pb, "Gt", [128, 4, NT], F32); Gb = sb(pb, "Gb", [128, 4, NT], BF16)
            MS = [sb(pb, "MS%d" % i, [128, 4, NT], BF16) for i in range(2)]
            CG = math.sqrt(2.0 / math.pi)
            NCT = NT // L
            for it in range(NTILES):
                t0 = it * NT
                c0 = it * NCT
                for gc in range(4):
                    bk = bank()
                    uv = uT[:, gc, t0:t0 + NT].rearrange("p (n l) -> p l n", l=L)
                    ov = bk[:].rearrange("p (n l) -> p l n", l=L)
                    for tp in range(L):
                        for sp_ in range(tp + 1):
                            mm(bk, BDT[:, gc, tp - sp_, :], uv[:, sp_, :], sp_ == 0, False, [BDT.b, uT.b],
                               out=ov[:, tp, :])
                        for j in range(4):
                            q16 = gc * 4 + j
                            for ri, CLt in enumerate((CLR, CLI)):
                                mm(bk, CLt[:, q16, tp, :], SP[:, ri, q16, c0:c0 + NCT], False,
                                   (j == 3 and ri == 1), [CLt.b, SP.b], out=ov[32 * j:32 * j + 32, tp, :],
                                   tile_position=(0, 32 * j))
                    act(Y2[:], bk[:], AF.Square, [bk.b], [Y2.b])
                    ts("dve", Y2[:], Y2[:], 0.044715, 1.0, ALU.mult, ALU.add, [Y2.b], [Y2.b])
                    tt("dve", Y3[:], Y2[:], bk[:], ALU.mult, [Y2.b, bk.b], [Y3.b])
                    act(Y3[:], Y3[:], AF.Sigmoid, [Y3.b], [Y3.b], scale=2.0 * CG)
                    tt("dve", Gt[:, gc, :], Y3[:], bk[:], ALU.mult, [Y3.b, bk.b], [Gt.b])
                    cp("act", Gb[:, gc, :], Gt[:, gc, :], [Gt.b], [Gb.b])
                M_ = MS[it % 2]
                for m in range(4):
                    bk = bank()
                    for kc in range(4):
                        mm(bk, GLUW[:, kc, m * 128:(m + 1) * 128], Gb[:, kc, :], kc == 0, kc == 3, [GLUW.b, Gb.b])
                    act(Y3[:], bk[:], AF.Sigmoid, [bk.b], [Y3.b])
                    tt("dve", M_[:, m, :], Y3[:], Gt[:, m, :], ALU.mult, [Y3.b, Gt.b], [M_.b])
                S.dma("sp", mix_s[it, :, 0:4, :], M_[:], reads=[M_.b])
            S.barrier()
            pb.close()
        s5.close()

        att = ExitStack()
        QT = sb(att, "QT", [128, 4, T], BF16)
        KT = sb(att, "KT", [128, 4, T], BF16)
        V = sb(att, "V", [128, T // 128, 512], BF16)
        with ExitStack() as ph:
          if 'A2' in PH:
            X = sb(ph, "X", [128, 4, 1024], F32)
            xnT = sb(ph, "xnT", [128, 8, NT], BF16)
            rs = sb(ph, "rs", [128, 4], F32); ss = sb(ph, "ss", [128, 4], F32)
            junk = sb(ph, "junk", [128, 1024], F32)
            W = sb(ph, "Wqkv", [128, 8, 1536], BF16)
            sq = sb(ph, "sq", [128, NT], BF16); rstd = sb(ph, "rstd", [128, NT], F32)
            knT = sb(ph, "knT", [128, 4, NT], F32)
            stg = sb(ph, "stg", [128, 4, 512], F32)
            S.dma("pool", W[:], w_in[:, 512:2048].rearrange("(c p) n -> p c n", p=128), writes=[W.b])
            for it in range(NTILES):
                t0 = it * NT
                load_norm_tile(X, xp[t0:t0 + NT, :].rearrange("(j p) f -> p j f", p=128), 4, G1, xnT, rs, ss, junk)
                for qk in range(2):
                    for m in range(4):
                        bk = bank()
                        for kc in range(8):
                            mm(bk, W[:, kc, qk * 512 + m * 128:qk * 512 + (m + 1) * 128], xnT[:, kc, :], kc == 0,
                               kc == 7, [W.b, xnT.b])
                        act(sq[:], bk[:], AF.Square, [bk.b], [sq.b])
                        b2 = bank()
                        mm(b2, BD64[:], sq[:], True, True, [BD64.b, sq.b])
                        act(rstd[:], b2[:], AF.Sqrt, [b2.b], [rstd.b], bias=EPS)
                        recip(rstd[:], rstd[:], [rstd.b], [rstd.b])
                        if qk == 0:
                            stt(QT[:, m, t0:t0 + NT], bk[:], QG[:, 0:1], rstd[:], ALU.mult, ALU.mult,
                                [bk.b, QG.b, rstd.b], [QT.b])
                        else:
                            stt(knT[:, m, :], bk[:], KG[:, 0:1], rstd[:], ALU.mult, ALU.mult,
                                [bk.b, KG.b, rstd.b], [knT.b])
                            cp("act", KT[:, m, t0:t0 + NT], knT[:, m, :], [knT.b], [KT.b])
                store_tm(knT, 4, NT, k_p[t0:t0 + NT, :], stg)
                for j in range(4):
                    bk = bank()
                    for kc in range(8):
                        mm(bk, xnT[:, kc, j * 128:(j + 1) * 128], W[:, kc, 1024:1536], kc == 0, kc == 7,
                           [W.b, xnT.b])
                    cp("act", stg[:, j, :], bk[:], [bk.b], [stg.b])
                    cp("dve", V[:, it * 4 + j, :], bk[:], [bk.b], [V.b])
                S.dma("sp", v_p[t0:t0 + NT, :].rearrange("(j p) f -> p j f", p=128), stg[:], reads=[stg.b])
            if 'S' in PH:
                for qk in range(2):
                    for m in range(4):
                        bk = bank()
                        for kc in range(8):
                            mm(bk, W[:, kc, qk * 512 + m * 128:qk * 512 + (m + 1) * 128], xnTs[:, kc, :], kc == 0,
                               kc == 7, [W.b, xnTs.b], out=bk[:, 0:TS])
                        act(sq[:, 0:TS], bk[:, 0:TS], AF.Square, [bk.b], [sq.b])
                        b2 = bank()
                        mm(b2, BD64[:], sq[:, 0:TS], True, True, [BD64.b, sq.b], out=b2[:, 0:TS])
                        act(rstd[:, 0:TS], b2[:, 0:TS], AF.Sqrt, [b2.b], [rstd.b], bias=EPS)
                        recip(rstd[:, 0:TS], rstd[:, 0:TS], [rstd.b], [rstd.b])
                        if qk == 0:
                            stt(QTs[:, m, :], bk[:, 0:TS], QG[:, 0:1], rstd[:, 0:TS], ALU.mult, ALU.mult,
                                [bk.b, QG.b, rstd.b], [QTs.b])
                        else:
                            stt(knT[:, m, 0:TS], bk[:, 0:TS], KG[:, 0:1], rstd[:, 0:TS], ALU.mult, ALU.mult,
                                [bk.b, KG.b, rstd.b], [knT.b])
                            cp("act", KTs[:, m, :], knT[:, m, 0:TS], [knT.b], [KTs.b])
                store_tm(knT, 4, TS, k_s, stg)
                VN = sb(ph, "VN", [4, NS, 4, 132], BF16)
                memset("dve", VN[:], 1.0, [VN.b])
                for sq_ in range(NS):
                    bk = bank()
                    for kc in range(8):
                        mm(bk, xnTs[:, kc, sq_ * 4:(sq_ + 1) * 4], W[:, kc, 1024:1536], kc == 0, kc == 7,
                           [W.b, xnTs.b], out=bk[0:4, :])
                    cp("act", stg[0:4, sq_ % 4, :], bk[0:4, :], [bk.b], [stg.b])
                    cp("dve", VN[:, sq_, :, 0:128], bk[0:4, :].rearrange("p (h d) -> p h d", d=128), [bk.b], [VN.b])
                    if sq_ % 4 == 3:
                        sg = sq_ // 4
                        S.dma("sp", v_s.rearrange("(s t) f -> t s f", t=4)[:, sg * 4:(sg + 1) * 4, :], stg[0:4, :, :],
                              reads=[stg.b])
                S.dma("sp", vn_s, VN[:, :, :, :].rearrange("p s h d -> p (s h d)"), reads=[VN.b])
            S.barrier()

        with ExitStack() as ph:
          if 'W' in PH and 'C' not in PH:
            convert_ffn_weights(ph)
            S.barrier()
          if 'C' in PH:
            if 'W' in PH:
                convert_ffn_weights(ph)
            PT = [sb(ph, "PT%d" % i, [128, NT], BF16) for i in range(6)]
            BIAS = sb(ph, "BIAS", [128, 4, 34], F32)
            for h in range(4):
                for bi in range(1, 34):
                    ts("dve", BIAS[:, h, bi:bi + 1], KPOS[:, 0:1], float(1 - 128 * bi), SLOPES[h], ALU.add, ALU.mult,
                       [KPOS.b], [BIAS.b])
            On = [sb(ph, "On%d" % i, [128, NT], F32) for i in range(2)]
            rden = [sb(ph, "rden%d" % i, [128, NT], F32) for i in range(2)]
            sq = sb(ph, "sqa", [128, NT], BF16); rstd = sb(ph, "rstda", [128, NT], F32)
            AO = [sb(ph, "AO%d" % i, [128, 4, NT], BF16) for i in range(2)]
            ptrr = [0]
            units = []
            for it in range(NTILES):
                for h in range(4):
                    nkb = (it * NT + NT) // 128
                    for kb in range(nkb):
                        units.append((it, h, kb, nkb))
            Ob = [PS[0], PS[1]]
            Db = [PS[2], PS[3]]

            def sbanks(ui):
                return [PS[4 + (ui % 2) * 2], PS[5 + (ui % 2) * 2]]

            def emit_qk(ui):
                it, h, kb, nkb = units[ui]
                q0 = it * NT
                k0 = kb * 128
                c0 = max(0, (k0 - q0) // 128) * 128
                Sb = sbanks(ui)
                for mp in range(2):
                    pr = slice(mp * 64, (mp + 1) * 64)
                    mm(Sb[mp], KT[pr, h, k0:k0 + 128], QT[pr, h, q0 + c0:q0 + NT], True, True,
                       [KT.b, QT.b], out=Sb[mp][:, c0:NT])

            def emit_rest(ui):
                it, h, kb, nkb = units[ui]
                q0 = it * NT
                k0 = kb * 128
                c0 = max(0, (k0 - q0) // 128) * 128
                slope = SLOPES[h]
                wq = 256 if slope * 511 > 64 else 512
                Sb = sbanks(ui)
                Ps = []
                for mp in range(2):
                    P = PT[ptrr[0] % len(PT)]; ptrr[0] += 1
                    Ps.append(P)
                    for g0 in range((c0 // wq) * wq, NT, wq):
                        lo = max(g0, c0)
                        hi = g0 + wq
                        bi = (q0 + hi - k0) // 128
                        act(P[:, lo:hi], Sb[mp][:, lo:hi], AF.Exp, [Sb[mp].b, BIAS.b], [P.b],
                            scale=0.125, bias=BIAS[:, h, bi:bi + 1])
                    if k0 >= q0:
                        tt("dve", P[:, c0:c0 + 128], P[:, c0:c0 + 128], CAUS[:], ALU.mult, [P.b, CAUS.b],
                           [P.b])
                for mp in range(2):
                    P = Ps[mp]
                    mm(Ob[mp], V[:, kb, h * 128:(h + 1) * 128], P[:, c0:NT], kb == 0, kb == nkb - 1,
                       [V.b, P.b], out=Ob[mp][:, c0:NT])
                    mm(Db[mp], ONE1[:], P[:, c0:NT], kb == 0, kb == nkb - 1, [ONE1.b, P.b],
                       out=Db[mp][:, c0:NT])

            def emit_epi(ui):
                it, h, kb, nkb = units[ui]
                for mp in range(2):
                    recip(rden[mp][:], Db[mp][:], [Db[mp].b], [rden[mp].b])
                    tt("dve", On[mp][:], Ob[mp][:], rden[mp][:], ALU.mult, [Ob[mp].b, rden[mp].b], [On[mp].b])
                stt(On[0][:], On[1][:], NLAM[:, 0:1], On[0][:], ALU.mult, ALU.add, [On[0].b, On[1].b, NLAM.b],
                    [On[0].b])
                act(sq[:], On[0][:], AF.Square, [On[0].b], [sq.b])
                b2 = sbanks(ui)[0]
                mm(b2, ON128[:], sq[:], True, True, [ON128.b, sq.b])
                act(rstd[:], b2[:], AF.Sqrt, [b2.b], [rstd.b], bias=EPS)
                recip(rstd[:], rstd[:], [rstd.b], [rstd.b])
                stt(AO[it % 2][:, h, :], On[0][:], SUBG[:, 0:1], rstd[:], ALU.mult, ALU.mult,
                    [On[0].b, SUBG.b, rstd.b], [AO[it % 2].b])
                if h == 3:
                    S.dma("sp", mix_s[it, :, 4:8, :], AO[it % 2][:], reads=[AO[it % 2].b])

            emit_qk(0)
            for ui in range(len(units)):
                if ui + 1 < len(units):
                    emit_qk(ui + 1)
                emit_rest(ui)
                if units[ui][2] == units[ui][3] - 1:
                    emit_epi(ui)
            S.barrier()
        att.close()


        with ExitStack() as ph:
          if 'S' in PH and 'CS' in PH:
            NKB = 4
            VN = sb(ph, "VNc", [4, NS, 4, 132], BF16)
            S.dma("sp", VN[:, :, :, :].rearrange("p s h d -> p (s h d)"), vn_s, writes=[VN.b])
            PTI = sb(ph, "PTI", [128, NS * 16], I32)
            IDX = sb(ph, "IDX", [128, NS * 16], U32)
            KPB = [sb(ph, "KPB%d" % i, [128, 512], F32) for i in range(NKB)]
            VPF = [sb(ph, "VPF%d" % i, [128, 512], F32) for i in range(NKB)]
            VPB = [sb(ph, "VPB%d" % i, [128, 4, 132], BF16) for i in range(32)]
            KpT = [sb(ph, "KpT%d" % i, [128, 4, 128], BF16) for i in range(2)]
            QB = sb(ph, "QB", [128, 4, NS, 2, 4], BF16)
            BIASS = sb(ph, "BIASS", [128, 16, 4, 8], F32)
            BIASN = sb(ph, "BIASN", [4, 4, 8], F32)
            TMPs = sb(ph, "TMPs", [128, 512], F32)
            PBs = [sb(ph, "PBs%d" % i, [128, 512], BF16) for i in range(2)]
            TNs = sb(ph, "TNs", [4, 32], F32); PNs = sb(ph, "PNs", [4, 32], BF16)
            RD = sb(ph, "RDs", [8, 4], F32)
            ONs = sb(ph, "ONs", [8, 4, 128], F32)
            CMB = sb(ph, "CMB", [8, 4], F32)
            OD4 = sb(ph, "OD4", [4, 512], F32)
            ODT = sb(ph, "ODT", [128, 4, TS], F32)
            sqs = sb(ph, "sqs", [128, TS], BF16); rstds = sb(ph, "rstds", [128, TS], F32)
            S.dma("sp", PTI[:], ptab.rearrange("s g -> (s g)").rearrange("(o n) -> o n", o=1).to_broadcast([128, NS * 16]),
                  writes=[PTI.b], **SLOW)
            ts("dve", IDX[:], PTI[:], 128.0, KPOS[:, 0:1], ALU.mult, ALU.add, [PTI.b, KPOS.b], [IDX.b])
            memset("dve", QB[:], 0.0, [QB.b])
            for mp in range(2):
                pr = slice(mp * 64, (mp + 1) * 64)
                cp("dve", QB[pr, :, :, mp, :], QTs[pr, :, :].rearrange("p h (s t) -> p h s t", t=4), [QTs.b], [QB.b])
            for pg in range(16):
                for h in range(4):
                    ts("dve", BIASS[:, pg, h, :], KPOS[:, 0:1].to_broadcast([128, 8]), float(pg * 128 - 2048), SLOPES[h],
                       ALU.add, ALU.mult, [KPOS.b], [BIASS.b])
            for h in range(4):
                ts("dve", BIASN[:, h, :], KPOS[0:4, 0:1].to_broadcast([4, 8]), SLOPES[h], None, ALU.mult, None, [KPOS.b],
                   [BIASN.b])
            for i in range(32):
                memset("dve", VPB[i][:], 1.0, [VPB[i].b])
            stt(CMB[:], IDF[0:8, 4:8], NLAM[0:8, 0:1], IDF[0:8, 0:4], ALU.mult, ALU.add, [IDF.b, NLAM.b], [CMB.b])
            ID4 = TB(IDF.t[0:4, 0:4], IDF.b)
            kcnt = 0
            ck_rows = cache_k
            cv_rows = cache_v
            for sq_ in range(NS):
                SBk = PS[sq_ % 2]
                PB_ = PBs[sq_ % 2]
                Ob = [PS[2], PS[3], PS[4], PS[5]]
                pend = []
                for pg in range(16):
                    Kp = KPB[kcnt % NKB]; Vf = VPF[kcnt % NKB]; Vp = VPB[kcnt % 32]; kcnt += 1
                    col = sq_ * 16 + pg
                    S.dmaf("pool", (lambda e, Kp=Kp, col=col: e.indirect_dma_start(
                        out=Kp[:, :], out_offset=None, in_=ck_rows,
                        in_offset=bass.IndirectOffsetOnAxis(ap=IDX[:, col:col + 1], axis=0))),
                        reads=[IDX.b], writes=[Kp.b])
                    S.dmaf("pool", (lambda e, Vf=Vf, col=col: e.indirect_dma_start(
                        out=Vf[:, :], out_offset=None, in_=cv_rows,
                        in_offset=bass.IndirectOffsetOnAxis(ap=IDX[:, col:col + 1], axis=0))),
                        reads=[IDX.b], writes=[Vf.b])
                    bkT = PS[6 + (pg % 2)]
                    KT_ = KpT[pg % 2]
                    for h in range(4):
                        tr(bkT, bkT[:, h * 128:(h + 1) * 128], Kp[:, h * 128:(h + 1) * 128], IDF, [Kp.b])
                    cp("act", KT_[:, :, :], bkT[:].rearrange("p (h n) -> p h n", n=128), [bkT.b], [KT_.b])
                    for h in range(4):
                        mm(SBk, KT_[:, h, :], QB[:, h, sq_, :, :].rearrange("p a b -> p (a b)"), True, True,
                           [KT_.b, QB.b], out=SBk[:, (pg * 4 + h) * 8:(pg * 4 + h + 1) * 8])
                    cp("dve", Vp[:, :, 0:128], Vf[:, :].rearrange("p (h d) -> p h d", d=128), [Vf.b], [Vp.b])
                    pend.append(Vp)
                NB = PS[6]
                for h in range(4):
                    mm(NB, KTs[:, h, sq_ * 4:(sq_ + 1) * 4], QB[:, h, sq_, :, :].rearrange("p a b -> p (a b)"), True, True,
                       [KTs.b, QB.b], out=NB[0:4, h * 8:(h + 1) * 8])
                stt(TMPs[:], SBk[:], 0.125, BIASS[:, :, :, :].rearrange("p a b c -> p (a b c)"), ALU.mult, ALU.add,
                    [SBk.b, BIASS.b], [TMPs.b])
                act(PB_[:], TMPs[:], AF.Exp, [TMPs.b], [PB_.b])
                stt(TNs[:], NB[0:4, 0:32], 0.125, BIASN[:, :, :].rearrange("p a b -> p (a b)"), ALU.mult, ALU.add,
                    [NB.b, BIASN.b], [TNs.b])
                act(TNs[:], TNs[:], AF.Exp, [TNs.b], [TNs.b])
                tt("dve", PNs[:, :].rearrange("p (a t) -> p a t", t=4), TNs[:, :].rearrange("p (a t) -> p a t", t=4),
                   CAUS[0:4, 0:4].unsqueeze(1).to_broadcast([4, 8, 4]), ALU.mult, [TNs.b, CAUS.b], [PNs.b])
                for pg in range(16):
                    Vp = pend[pg]
                    for h in range(4):
                        mm(Ob[h], PB_[:, (pg * 4 + h) * 8:(pg * 4 + h + 1) * 8], Vp[:, h, 0:129], pg == 0, False,
                           [PB_.b, Vp.b], out=Ob[h][0:8, 0:129])
                for h in range(4):
                    mm(Ob[h], PNs[0:4, h * 8:(h + 1) * 8], VN[0:4, sq_, h, 0:129], False, True, [PNs.b, VN.b],
                       out=Ob[h][0:8, 0:129])
                for h in range(4):
                    recip(RD[:, h:h + 1], Ob[h][0:8, 128:129], [Ob[h].b], [RD.b])
                    ts("dve", ONs[:, h, :], Ob[h][0:8, 0:128], RD[:, h:h + 1], None, ALU.mult, None, [Ob[h].b, RD.b],
                       [ONs.b])
                DBk = PS[7]
                mm(DBk, CMB[:, :], ONs[:, :, :].rearrange("p h d -> p (h d)"), True, True, [CMB.b, ONs.b],
                   out=DBk[0:4, :])
                cp("act", OD4[:], DBk[0:4, :], [DBk.b], [OD4.b])
                TBk = PS[6]
                for h in range(4):
                    tr(TBk, TBk[:, h * 4:(h + 1) * 4], OD4[0:4, h * 128:(h + 1) * 128], ID4, [OD4.b])
                cp("act", ODT[:, :, sq_ * 4:(sq_ + 1) * 4], TBk[:, 0:16].rearrange("p (h t) -> p h t", t=4), [TBk.b],
                   [ODT.b])
            for h in range(4):
                act(sqs[:], ODT[:, h, :], AF.Square, [ODT.b], [sqs.b])
                b2 = PS[7]
                mm(b2, ON128[:], sqs[:], True, True, [ON128.b, sqs.b], out=b2[:, 0:TS])
                act(rstds[:], b2[:, 0:TS], AF.Sqrt, [b2.b], [rstds.b], bias=EPS)
                recip(rstds[:], rstds[:], [rstds.b], [rstds.b])
                stt(MIXs[:, 4 + h, :], ODT[:, h, :], SUBG[:, 0:1], rstds[:], ALU.mult, ALU.mult,
                    [ODT.b, SUBG.b, rstds.b], [MIXs.b])
            S.barrier()

        with ExitStack() as ph:
          if 'D' in PH:
            XIN = sb(ph, "XIN", [128, 4, 1024], F32)
            xTs_ = [sb(ph, "xT%d" % i, [128, 8, NT], F32) for i in range(2)]
            MIX = sb(ph, "MIX", [128, 8, NT], BF16)
            A = sb(ph, "actA", [128, 8, NT], BF16)
            B3 = [sb(ph, "actB%d" % i, [128, 8, NT], BF16) for i in range(2)]
            Wo = sb(ph, "Wo", [128, 8, 1024], BF16); Wq = sb(ph, "Wq", [128, 8, 1024], BF16)
            Wo2 = sb(ph, "Wo2", [128, 8, 1024], BF16)
            sq2 = [sb(ph, "sqd%d" % i, [128, NT], BF16) for i in range(2)]; rstd = sb(ph, "rstdd", [128, NT], F32)
            Pm = [sb(ph, "Pm%d" % i, [128, NT], BF16) for i in range(2)]
            hT = sb(ph, "hT", [128, NFC, NT], BF16)
            HGs_ = [sb(ph, "HG%d" % i, [128, NT + 2], F32) for i in range(2)]; CARRY = sb(ph, "CARRY", [128, NFC, 2], F32)
            cvs_ = [sb(ph, "cv%d" % i, [128, NT], F32) for i in range(2)]
            rden = sb(ph, "rdend", [128, NT], F32)
            WG = [sb(ph, "WG%d" % i, [128, 8, 128], BF16) for i in range(2)]
            WV = [sb(ph, "WV%d" % i, [128, 8, 128], BF16) for i in range(2)]
            WD = [sb(ph, "WD%d" % i, [128, NFC, 128], BF16) for i in range(2)]
            YST = sb(ph, "YST", [128, 1024], F32)
            S.dma("pool", Wo[:], w_out.rearrange("(c p) n -> p c n", p=128), writes=[Wo.b])
            S.dma("pool", Wq[:], ca_wq.rearrange("(c p) n -> p c n", p=128), writes=[Wq.b])
            S.dma("pool", Wo2[:], ca_wo.rearrange("(c p) n -> p c n", p=128), writes=[Wo2.b])
            memset("dve", CARRY[:], 0.0, [CARRY.b])
            if 'B' not in PH or os.environ.get('KNOB'):
                memset("dve", hT[:, 0:4, :], 0.0, [hT.b])
                for it in range(NTILES):
                    S.dma("sp", mix_s[it, :, 0:4, :], hT[:, 0:4, :], reads=[hT.b])
                S.barrier()
            frr = [0]; grr = [0]

            def fbank():
                b = PS[frr[0] % 4]; frr[0] += 1
                return b

            def gbank():
                b = PS[4 + grr[0] % 4]; grr[0] += 1
                return b

            def d_loads(it):
                t0 = it * NT
                S.dma("pool", MIX[:], mix_s[it], writes=[MIX.b])
                S.dma("pool", XIN[:], xp[t0:t0 + NT, :].rearrange("(j p) f -> p j f", p=128), writes=[XIN.b])

            def ln_gen(xT, G, out):
                bk = fbank()
                for c in range(8):
                    act(sq2[c % 2][:], xT[:, c, :], AF.Square, [xT.b], [sq2[c % 2].b])
                    if c >= 1:
                        mm(bk, ON1024[:], sq2[(c - 1) % 2][:], c == 1, False, [ON1024.b, sq2[(c - 1) % 2].b])
                    yield
                mm(bk, ON1024[:], sq2[1][:], False, True, [ON1024.b, sq2[1].b])
                act(rstd[:], bk[:], AF.Sqrt, [bk.b], [rstd.b], bias=EPS)
                recip(rstd[:], rstd[:], [rstd.b], [rstd.b])
                yield
                for c in range(8):
                    stt(out[:, c, :], xT[:, c, :], G[:, c:c + 1], rstd[:], ALU.mult, ALU.mult,
                        [xT.b, G.b, rstd.b], [out.b])
                    if c % 4 == 3:
                        yield

            def front(it):
                xT = xTs_[it % 2]; Bb = B3[it % 2]
                for c in range(8):
                    bk = fbank()
                    for j in range(4):
                        tr(bk, bk[:, j * 128:(j + 1) * 128], XIN[:, j, c * 128:(c + 1) * 128], IDF, [XIN.b])
                    cp("act", xT[:, c, :], bk[:], [bk.b], [xT.b])
                    if c % 4 == 3:
                        yield
                for m in range(8):
                    bk = fbank()
                    for kc in range(8):
                        mm(bk, Wo[:, kc, m * 128:(m + 1) * 128], MIX[:, kc, :], kc == 0, kc == 7, [Wo.b, MIX.b])
                    tt("dve", xT[:, m, :], bk[:], xT[:, m, :], ALU.add, [bk.b, xT.b], [xT.b])
                    if m % 2 == 1:
                        yield
                yield from ln_gen(xT, G2, A)
                for h in range(4):
                    bq = [fbank(), fbank()]
                    for dh in range(2):
                        for kc in range(8):
                            mm(bq[dh], Wq[:, kc, (2 * h + dh) * 128:(2 * h + dh + 1) * 128], A[:, kc, :], kc == 0,
                               kc == 7, [Wq.b, A.b])
                        act(sq2[dh][:], bq[dh][:], AF.Square, [bq[dh].b], [sq2[dh].b])
                    yield
                    b2 = fbank()
                    for dh in range(2):
                        mm(b2, ON256[:], sq2[dh][:], dh == 0, dh == 1, [ON256.b, sq2[dh].b])
                    act(rstd[:], b2[:], AF.Sqrt, [b2.b], [rstd.b], bias=EPS)
                    recip(rstd[:], rstd[:], [rstd.b], [rstd.b])
                    for dh in range(2):
                        stt(Bb[:, 2 * h + dh, :], bq[dh][:], CQG[:, dh:dh + 1], rstd[:], ALU.mult, ALU.mult,
                            [bq[dh].b, CQG.b, rstd.b], [Bb.b])
                    yield
                for h in range(4):
                    for mb in range(2):
                        bs = fbank()
                        for dh in range(2):
                            mm(bs, MKT[:, 2 * h + dh, mb * 128:(mb + 1) * 128], Bb[:, 2 * h + dh, :], dh == 0, dh == 1,
                               [MKT.b, Bb.b])
                        act(Pm[mb][:], bs[:], AF.Exp, [bs.b], [Pm[mb].b], scale=1.0 / 16)
                    yield
                    bd = fbank()
                    for mb in range(2):
                        mm(bd, ONE1[:], Pm[mb][:], mb == 0, mb == 1, [ONE1.b, Pm[mb].b])
                    recip(rden[:], bd[:], [bd.b], [rden.b])
                    for dvc in range(2):
                        bo = fbank()
                        for mb in range(2):
                            mm(bo, MV[:, mb, (2 * h + dvc) * 128:(2 * h + dvc + 1) * 128], Pm[mb][:], mb == 0, mb == 1,
                               [MV.b, Pm[mb].b])
                        tt("dve", A[:, 2 * h + dvc, :], bo[:], rden[:], ALU.mult, [bo.b, rden.b], [A.b])
                    yield
                for m in range(8):
                    bk = fbank()
                    for kc in range(8):
                        mm(bk, Wo2[:, kc, m * 128:(m + 1) * 128], A[:, kc, :], kc == 0, kc == 7, [Wo2.b, A.b])
                    tt("dve", xT[:, m, :], bk[:], xT[:, m, :], ALU.add, [bk.b, xT.b], [xT.b])
                    if m % 2 == 1:
                        yield
                yield from ln_gen(xT, G3, Bb)

            def ffn(it):
                t0 = it * NT
                xT = xTs_[it % 2]; Bb = B3[it % 2]
                for fc in range(NFC):
                    wg = WG[fc % 2]; wv = WV[fc % 2]; HG = HGs_[fc % 2]; cv = cvs_[fc % 2]
                    S.dma("sp", wg[:], wg_s[:, fc], writes=[wg.b])
                    S.dma("sp", wv[:], wv_s[:, fc], writes=[wv.b])
                    if fc == NFC - 4:
                        for m in range(2):
                            S.dma("sp", WD[m][:], wd_s[:, m], writes=[WD[m].b])
                    bg = gbank(); bv = gbank()
                    for kc in range(8):
                        mm(bg, wg[:, kc, :], Bb[:, kc, :], kc == 0, kc == 7, [wg.b, Bb.b])
                    for kc in range(8):
                        mm(bv, wv[:, kc, :], Bb[:, kc, :], kc == 0, kc == 7, [wv.b, Bb.b])
                    cp("dve", HG[:, 0:2], CARRY[:, fc, :], [CARRY.b], [HG.b])
                    cp("act", HG[:, 2:NT + 2], bg[:], [bg.b], [HG.b])
                    cp("dve", CARRY[:, fc, :], HG[:, NT:NT + 2], [HG.b], [CARRY.b])
                    act(cv[:], HG[:, 2:NT + 2], AF.Identity, [HG.b, CW.b, CB.b], [cv.b], scale=CW[:, 2, fc:fc + 1],
                        bias=CB[:, fc:fc + 1])
                    stt(cv[:], HG[:, 1:NT + 1], CW[:, 1, fc:fc + 1], cv[:], ALU.mult, ALU.add, [HG.b, CW.b, cv.b],
                        [cv.b])
                    stt(cv[:], HG[:, 0:NT], CW[:, 0, fc:fc + 1], cv[:], ALU.mult, ALU.add, [HG.b, CW.b, cv.b], [cv.b])
                    act(cv[:], cv[:], AF.Silu, [cv.b], [cv.b])
                    tt("dve", hT[:, fc, :], cv[:], bv[:], ALU.mult, [cv.b, bv.b], [hT.b])
                    yield
                for m in range(8):
                    wd = WD[m % 2]
                    if m >= 2:
                        S.dma("sp", wd[:], wd_s[:, m], writes=[wd.b])
                    bk = gbank()
                    for fc in range(NFC):
                        mm(bk, wd[:, fc, :], hT[:, fc, :], fc == 0, fc == NFC - 1, [wd.b, hT.b])
                    tt("dve", xT[:, m, :], bk[:], xT[:, m, :], ALU.add, [bk.b, xT.b], [xT.b])
                    yield
                for j in range(4):
                    for c0 in (0, 4):
                        bk = gbank()
                        for c in range(c0, c0 + 4):
                            tr(bk, bk[:, (c - c0) * 128:(c - c0 + 1) * 128], xT[:, c, j * 128:(j + 1) * 128], IDF,
                               [xT.b])
                        cp("act", YST[:, c0 * 128:(c0 + 4) * 128], bk[:], [bk.b], [YST.b])
                    S.dma("pool", y_p[t0 + j * 128:t0 + (j + 1) * 128, :], YST[:], reads=[YST.b])
                    yield

            def drain(g, n):
                for _ in range(n):
                    try:
                        next(g)
                    except StopIteration:
                        return False
                return True

            d_loads(0)
            drain(front(0), 10 ** 6)
            for it in range(NTILES):
                nxt = None
                if it + 1 < NTILES:
                    d_loads(it + 1)
                    nxt = front(it + 1)
                for _ in ffn(it):
                    if nxt is not None:
                        drain(nxt, 1)
                if nxt is not None:
                    drain(nxt, 10 ** 6)
            for j in range(2):
                S.dma("sp", conv_p[j].rearrange("(c p) -> p c", p=128), CARRY[:, :, j], reads=[CARRY.b], **SLOW)
            S.barrier()

        with ExitStack() as ph:
          if 'S' in PH and 'DS' in PH:
            X = sb(ph, "Xds", [128, 1, 1024], F32)
            xT = sb(ph, "xTs", [128, 8, TS], F32)
            A = sb(ph, "actAs", [128, 8, TS], BF16); Bb = sb(ph, "actBs", [128, 8, TS], BF16)
            Wo = sb(ph, "Wos", [128, 8, 1024], BF16); Wq = sb(ph, "Wqs", [128, 8, 1024], BF16)
            Wo2 = sb(ph, "Wo2s", [128, 8, 1024], BF16)
            sq = sb(ph, "sqds", [128, 2, TS], BF16); rstd = sb(ph, "rstdds", [128, TS], F32)
            hT = sb(ph, "hTs", [128, NFC, TS], BF16)
            WG = [sb(ph, "WGs%d" % i, [128, 8, 128], BF16) for i in range(2)]
            WV = [sb(ph, "WVs%d" % i, [128, 8, 128], BF16) for i in range(2)]
            WD = [sb(ph, "WDs%d" % i, [128, NFC, 128], BF16) for i in range(2)]
            S.dma("pool", Wo[:], w_out.rearrange("(c p) n -> p c n", p=128), writes=[Wo.b])
            S.dma("pool", Wq[:], ca_wq.rearrange("(c p) n -> p c n", p=128), writes=[Wq.b])
            S.dma("pool", Wo2[:], ca_wo.rearrange("(c p) n -> p c n", p=128), writes=[Wo2.b])
            wrr = 0
            n = TS
            CMKt = sb(ph, "CMKt", [128, 2, 1024], F32)
            CMVb = [sb(ph, "CMVb%d" % i, [128, 2, 4, 260], BF16) for i in range(2)]
            MKs = sb(ph, "MKs", [128, 8, 256], BF16)
            PCs = sb(ph, "PCs", [128, 32], BF16)
            RDc = sb(ph, "RDc", [4, 4], F32)
            COn = sb(ph, "COn", [4, 4, 256], F32)
            SCt = sb(ph, "SCt", [32, F], F32)
            SCV = sb(ph, "SCV", [128, NFC, 32], F32)
            CVS = sb(ph, "CVS", [128, NFC, 32], F32)
            HGs = sb(ph, "HGs", [128, NS, 6], F32)
            cvs = sb(ph, "cvs", [128, NS, 4], F32)
            ID4 = TB(IDF.t[0:4, 0:4], IDF.b); ID32 = TB(IDF.t[0:32, 0:32], IDF.b)
            for i in range(2):
                memset("dve", CMVb[i][:], 1.0, [CMVb[i].b])
            S.dma("sp", SCt[:], st_conv, writes=[SCt.b])
            for f0 in range(0, NFC, 16):
                bk = bank()
                for fc in range(f0, min(f0 + 16, NFC)):
                    tr(bk, bk[:, (fc - f0) * 32:(fc - f0 + 1) * 32], SCt[0:32, fc * 128:(fc + 1) * 128], ID32, [SCt.b])
                nf = min(f0 + 16, NFC) - f0
                cp("act", SCV[:, f0:f0 + nf, :], bk[:, 0:nf * 32].rearrange("p (f x) -> p f x", x=32), [bk.b], [SCV.b])
            for c in range(8):
                cp("dve", xT[:, c, 0:n], xsT[:, c, :], [xsT.b], [xT.b])
            for m in range(8):
                bk = bank()
                for kc in range(8):
                    mm(bk, Wo[:, kc, m * 128:(m + 1) * 128], MIXs[:, kc, :], kc == 0, kc == 7, [Wo.b, MIXs.b],
                       out=bk[:, 0:n])
                tt("dve", xT[:, m, 0:n], bk[:, 0:n], xT[:, m, 0:n], ALU.add, [bk.b, xT.b], [xT.b])
            ln_fm(xT, G2, A, n, sq, rstd)
            for h in range(4):
                bq = [bank(), bank()]
                for dh in range(2):
                    for kc in range(8):
                        mm(bq[dh], Wq[:, kc, (2 * h + dh) * 128:(2 * h + dh + 1) * 128], A[:, kc, 0:n], kc == 0,
                           kc == 7, [Wq.b, A.b], out=bq[dh][:, 0:n])
                b2 = bank()
                for dh in range(2):
                    act(sq[:, dh, 0:n], bq[dh][:, 0:n], AF.Square, [bq[dh].b], [sq.b])
                    mm(b2, ON256[:], sq[:, dh, 0:n], dh == 0, dh == 1, [ON256.b, sq.b], out=b2[:, 0:n])
                act(rstd[:, 0:n], b2[:, 0:n], AF.Sqrt, [b2.b], [rstd.b], bias=EPS)
                recip(rstd[:, 0:n], rstd[:, 0:n], [rstd.b], [rstd.b])
                for dh in range(2):
                    stt(Bb[:, 2 * h + dh, 0:n], bq[dh][:, 0:n], CQG[:, dh:dh + 1], rstd[:, 0:n], ALU.mult, ALU.mult,
                        [bq[dh].b, CQG.b, rstd.b], [Bb.b])
            for sq_ in range(NS):
                CMV_ = CMVb[sq_ % 2]
                S.dma("sp", CMKt[:], cmk[sq_].rearrange("(mb p) f -> p mb f", p=128), writes=[CMKt.b])
                for mb in range(2):
                    S.dma("pool", CMV_[:, mb, :, 0:256], cmv[sq_, mb * 128:(mb + 1) * 128, :].rearrange(
                        "p (h d) -> p h d", d=256), writes=[CMV_.b])
                for mb in range(2):
                    for c0 in (0, 4):
                        bk = bank()
                        for c8 in range(c0, c0 + 4):
                            tr(bk, bk[:, (c8 - c0) * 128:(c8 - c0 + 1) * 128], CMKt[:, mb, c8 * 128:(c8 + 1) * 128], IDF,
                               [CMKt.b])
                        cp("act" if c0 == 0 else "dve", MKs[:, c0:c0 + 4, mb * 128:(mb + 1) * 128],
                           bk[:].rearrange("p (c n) -> p c n", n=128), [bk.b], [MKs.b])
                SBc = bank()
                for mb in range(2):
                    for h in range(4):
                        for dh in range(2):
                            mm(SBc, MKs[:, 2 * h + dh, mb * 128:(mb + 1) * 128], Bb[:, 2 * h + dh, sq_ * 4:(sq_ + 1) * 4],
                               dh == 0, dh == 1, [MKs.b, Bb.b], out=SBc[:, (mb * 4 + h) * 4:(mb * 4 + h + 1) * 4])
                act(PCs[:], SBc[:, 0:32], AF.Exp, [SBc.b], [PCs.b], scale=1.0 / 16)
                Oc = [bank(), bank(), bank(), bank()]
                for h in range(4):
                    for mb in range(2):
                        mm(Oc[h], PCs[:, (mb * 4 + h) * 4:(mb * 4 + h + 1) * 4], CMV_[:, mb, h, 0:257], mb == 0, mb == 1,
                           [PCs.b, CMV_.b], out=Oc[h][0:4, 0:257])
                for h in range(4):
                    recip(RDc[:, h:h + 1], Oc[h][0:4, 256:257], [Oc[h].b], [RDc.b])
                    ts("dve", COn[:, h, :], Oc[h][0:4, 0:256], RDc[:, h:h + 1], None, ALU.mult, None,
                       [Oc[h].b, RDc.b], [COn.b])
                TBk = bank()
                cof = COn[:, :, :].rearrange("p h d -> p (h d)")
                for c8 in range(8):
                    tr(TBk, TBk[:, c8 * 4:(c8 + 1) * 4], cof[0:4, c8 * 128:(c8 + 1) * 128], ID4, [COn.b])
                cp("act", A[:, :, sq_ * 4:(sq_ + 1) * 4], TBk[:, 0:32].rearrange("p (c t) -> p c t", t=4), [TBk.b],
                   [A.b])
            for m in range(8):
                bk = bank()
                for kc in range(8):
                    mm(bk, Wo2[:, kc, m * 128:(m + 1) * 128], A[:, kc, 0:n], kc == 0, kc == 7, [Wo2.b, A.b],
                       out=bk[:, 0:n])
                tt("dve", xT[:, m, 0:n], bk[:, 0:n], xT[:, m, 0:n], ALU.add, [bk.b, xT.b], [xT.b])
            ln_fm(xT, G3, Bb, n, sq, rstd)
            for fc in range(NFC):
                wg = WG[wrr % 2]; wv = WV[wrr % 2]; wrr += 1
                S.dma("sp", wg[:], wg_s[:, fc],
                      writes=[wg.b])
                S.dma("sp", wv[:], wv_s[:, fc],
                      writes=[wv.b])
                bg = bank(); bv = bank()
                for kc in range(8):
                    mm(bg, wg[:, kc, :], Bb[:, kc, 0:n], kc == 0, kc == 7, [wg.b, Bb.b], out=bg[:, 0:n])
                for kc in range(8):
                    mm(bv, wv[:, kc, :], Bb[:, kc, 0:n], kc == 0, kc == 7, [wv.b, Bb.b], out=bv[:, 0:n])
                cp("dve", HGs[:, :, 0:2], SCV[:, fc, :].rearrange("p (s j) -> p s j", j=2), [SCV.b], [HGs.b])
                cp("act", HGs[:, :, 2:6], bg[:, 0:n].rearrange("p (s t) -> p s t", t=4), [bg.b], [HGs.b])
                cp("dve", CVS[:, fc, :].rearrange("p (s j) -> p s j", j=2), HGs[:, :, 4:6], [HGs.b], [CVS.b])
                act(cvs[:], HGs[:, :, 2:6], AF.Identity, [HGs.b, CW.b, CB.b], [cvs.b], scale=CW[:, 2, fc:fc + 1],
                    bias=CB[:, fc:fc + 1])
                stt(cvs[:], HGs[:, :, 1:5], CW[:, 1, fc:fc + 1], cvs[:], ALU.mult, ALU.add, [HGs.b, CW.b, cvs.b],
                    [cvs.b])
                stt(cvs[:], HGs[:, :, 0:4], CW[:, 0, fc:fc + 1], cvs[:], ALU.mult, ALU.add, [HGs.b, CW.b, cvs.b],
                    [cvs.b])
                act(cvs[:], cvs[:], AF.Silu, [cvs.b], [cvs.b])
                tt("dve", hT[:, fc, 0:n], cvs[:, :, :].rearrange("p s t -> p (s t)"), bv[:, 0:n], ALU.mult,
                   [cvs.b, bv.b], [hT.b])
            for m in range(8):
                wd = WD[m % 2]
                S.dma("sp", wd[:], wd_s[:, m],
                      writes=[wd.b])
                bk = bank()
                for fc in range(NFC):
                    mm(bk, wd[:, fc, :], hT[:, fc, 0:n], fc == 0, fc == NFC - 1, [wd.b, hT.b], out=bk[:, 0:n])
                tt("dve", xT[:, m, 0:n], bk[:, 0:n], xT[:, m, 0:n], ALU.add, [bk.b, xT.b], [xT.b])
            store_tm(xT, 8, n, y_s, X)
            for f0 in range(0, NFC, 4):
                bk = bank()
                nf = min(f0 + 4, NFC) - f0
                for fc in range(f0, f0 + nf):
                    tr(bk, bk[0:32, (fc - f0) * 128:(fc - f0 + 1) * 128], CVS[:, fc, :], IDF, [CVS.b])
                cp("act", SCt[0:32, f0 * 128:(f0 + nf) * 128], bk[0:32, 0:nf * 128], [bk.b], [SCt.b])
            S.dma("sp", conv_s, SCt[:], reads=[SCt.b])

            S.barrier()

        S.barrier()
        S.emit()
    return nc


_NC_CACHE = {}


def _consts():
    ident = np.eye(128, dtype=np.float32)
    bd64 = np.zeros((128, 128), np.float32); bd64[:64, :64] = 1 / 64; bd64[64:, 64:] = 1 / 64
    m16 = np.kron(np.eye(8, dtype=np.float32), np.ones((16, 16), np.float32))
    caus = np.triu(np.ones((128, 128), np.float32))
    pwn = np.array(PW_N, np.float32)
    kpos = np.arange(128, dtype=np.float32)
    g2m = np.zeros((128, 2), np.float32)
    for p in range(128):
        g2m[p, (p // 16) % 2] = 1.0
    return dict(c_ident=ident, c_bd64=bd64, c_mask16=m16, c_caus=caus, c_pwn=pwn, c_kpos=kpos, c_g2m=g2m)


WNAMES = ["ln1_g", "ln2_g", "ln3_g", "mem_norm_g", "w_in", "ssm_a_re", "ssm_a_im", "ssm_b_re", "ssm_b_im",
          "ssm_c_re", "ssm_c_im", "ssm_d", "ssm_log_dt", "ssm_glu_w", "q_norm_g", "k_norm_g", "lam_q1", "lam_k1",
          "lam_q2", "lam_k2", "subln_g", "w_out", "ca_wq", "ca_wk", "ca_wv", "ca_q_norm_g", "ca_k_norm_g", "ca_wo",
          "ffn_wg", "ffn_wv", "ffn_wd", "ffn_conv_w", "ffn_conv_b"]


def make_in_maps(inp, cores):
    cst = _consts()
    shared = {n: np.ascontiguousarray(np.asarray(inp[n])[0]) for n in WNAMES}
    maps = []
    for c in cores:
        b = c % 4
        m = dict(shared)
        m.update(cst)
        m["xp"] = np.ascontiguousarray(np.asarray(inp["x_prompt"])[b])
        m["memp"] = np.ascontiguousarray(np.asarray(inp["mem_prompt"])[b])
        sl = slice(c * NS, (c + 1) * NS)
        m["xs"] = np.ascontiguousarray(np.asarray(inp["x_sample"])[sl]).reshape(TS, 1024)
        m["st_re"] = np.ascontiguousarray(np.asarray(inp["state_ssm_re"])[0, sl]).reshape(NS, 2048)
        m["st_im"] = np.ascontiguousarray(np.asarray(inp["state_ssm_im"])[0, sl]).reshape(NS, 2048)
        m["st_conv"] = np.ascontiguousarray(np.asarray(inp["state_conv"])[0, sl]).reshape(NS * 2, F)
        m["cmk"] = np.ascontiguousarray(np.asarray(inp["cache_mem_k"])[0, sl]).reshape(NS, 256, 1024)
        m["cmv"] = np.ascontiguousarray(np.asarray(inp["cache_mem_v"])[0, sl]).reshape(NS, 256, 1024)
        m["cache_k"] = np.asarray(inp["cache_k"]).reshape(2560 * 128, 512)
        m["cache_v"] = np.asarray(inp["cache_v"]).reshape(2560 * 128, 512)
        m["ptab"] = np.ascontiguousarray(np.asarray(inp["page_table"])[sl]).astype(np.int32)
        maps.append(m)
    return maps


def kernel(**inp):
    nc = build_nc()
    cores = list(range(NCORES))
    maps = make_in_maps(inp, cores)
    res = run_bass_kernel_spmd(nc, maps, core_ids=cores)
    R = res.results
    f32 = np.float32
    y_prompt = np.stack([R[b]["y_p"] for b in range(4)]).astype(f32)
    k_prompt = np.stack([R[b]["k_p"] for b in range(4)]).reshape(1, 4, T, 4, 2, 64).astype(f32)
    v_prompt = np.stack([R[b]["v_p"] for b in range(4)]).reshape(1, 4, T, 4, 128).astype(f32)
    hre = np.stack([R[b]["hre_p"] for b in range(4)]).reshape(1, 4, 32, 64).astype(f32)
    him = np.stack([R[b]["him_p"] for b in range(4)]).reshape(1, 4, 32, 64).astype(f32)
    conv_prompt = np.stack([R[b]["conv_p"] for b in range(4)]).reshape(1, 4, 2, F).astype(f32)
    mk = np.stack([R[b]["mk_p"] for b in range(4)]).reshape(1, 4, 256, 4, 256).astype(f32)
    mv = np.stack([R[b]["mv_p"] for b in range(4)]).reshape(1, 4, 256, 4, 256).astype(f32)
    y_sample = np.concatenate([R[c]["y_s"] for c in range(NCORES)]).reshape(128, 4, 1024).astype(f32)
    k_sample = np.concatenate([R[c]["k_s"] for c in range(NCORES)]).reshape(1, 128, 4, 4, 2, 64).astype(f32)
    v_sample = np.concatenate([R[c]["v_s"] for c in range(NCORES)]).reshape(1, 128, 4, 4, 128).astype(f32)
    hre_s = np.concatenate([R[c]["hre_s"] for c in range(NCORES)]).reshape(1, 128, 32, 64).astype(f32)
    him_s = np.concatenate([R[c]["him_s"] for c in range(NCORES)]).reshape(1, 128, 32, 64).astype(f32)
    conv_s = np.concatenate([R[c]["conv_s"] for c in range(NCORES)]).reshape(1, 128, 2, F).astype(f32)
    return (y_prompt, y_sample, k_prompt, v_prompt, k_sample, v_sample, hre, him, hre_s, him_s,
            conv_prompt, conv_s, mk, mv)
```

```python
import math
import os
PH = set(os.environ.get('KPH', 'W,M,A2,C,B,D,S,CS,DS').split(','))
import numpy as np
import ml_dtypes
import concourse.bass as bass
import concourse.mybir as mybir
from concourse.bass_utils import run_bass_kernel_spmd
from contextlib import ExitStack

F32 = mybir.dt.float32
BF16 = mybir.dt.bfloat16
I32 = mybir.dt.int32
U32 = mybir.dt.uint32
ALU = mybir.AluOpType
AF = mybir.ActivationFunctionType

ENGS = ("pe", "act", "dve", "pool", "sp")
N_DMA_SEMS = 24
EPS = 1e-6
NCORES = 8
T = 4096
NT = 512
NTILES = T // NT
L = 8
NCH = T // L
SC = 16
NSC = NCH // SC
F = 2816
NFC = F // 128
SLOPES = [2.0 ** (-8.0 * (h + 1) / 4) for h in range(4)]
LAM0 = 0.8 - 0.6 * math.exp(-0.3 * 0)
NS = 16
TS = 64
NPW = 25
PW_N = list(range(9)) + [8 * k for k in range(2, 17)] + [4]


class Buf:
    __slots__ = ("name", "last_w", "readers", "excl")

    def __init__(self, name):
        self.name = name
        self.excl = False
        self.last_w = None
        self.readers = []


class Sched:
    def __init__(self, nc, es):
        self.nc = nc
        self.ops = {e: [] for e in ENGS}
        self.count = {e: 0 for e in ENGS}
        self.sems = {e: es.enter_context(nc.semaphore("s_" + e)) for e in ENGS}
        self.dsems = [es.enter_context(nc.semaphore("d%d" % i)) for i in range(N_DMA_SEMS)]
        self.dcnt = [0] * N_DMA_SEMS
        self.drr = 0
        self.waited = {e: {} for e in ENGS}
        self.bufs = []

    def buf(self, name):
        b = Buf(name)
        self.bufs.append(b)
        return b

    def _collect(self, eng, reads, writes, is_dma):
        toks = []
        for b in reads:
            if b.last_w is not None:
                toks.append(b.last_w)
        for b in writes:
            if b.last_w is not None:
                toks.append(b.last_w)
            toks.extend(b.readers)
        need = {}
        for (k, v) in toks:
            if (not is_dma) and eng == "pe" and k == "pe":
                continue
            if need.get(k, -1) < v:
                need[k] = v
        waits = []
        w = self.waited[eng]
        for k, v in need.items():
            if w.get(k, -1) >= v:
                continue
            w[k] = v
            waits.append((k, v))
        return waits

    def _commit(self, tok, reads, writes):
        for b in reads:
            if b.excl:
                b.last_w = tok
                b.readers = []
            else:
                b.readers.append(tok)
        for b in writes:
            b.last_w = tok
            b.readers = []

    def op(self, eng, fn, reads=(), writes=()):
        waits = self._collect(eng, reads, writes, False)
        self.count[eng] += 1
        tok = (eng, self.count[eng])
        self.ops[eng].append((waits, fn, None))
        self._commit(tok, reads, writes)
        return tok

    def dmaf(self, eng, fn, reads=(), writes=()):
        waits = self._collect(eng, reads, writes, True)
        i = self.drr
        self.drr = (self.drr + 1) % N_DMA_SEMS
        k = ("d", i)
        prev = self.dcnt[i]
        w = self.waited[eng]
        if prev > 0 and w.get(k, -1) < prev:
            w[k] = prev
            waits.append((k, prev))
        self.dcnt[i] += 16
        tok = (k, self.dcnt[i])
        self.ops[eng].append((waits, fn, (i, 16)))
        self._commit(tok, reads, writes)
        return tok

    def dma(self, eng, out, in_, reads=(), writes=(), **kw):
        def fn(e, out=out, in_=in_, kw=kw):
            return e.dma_start(out=out, in_=in_, **kw)
        return self.dmaf(eng, fn, reads, writes)

    def barrier(self):
        targets = [(e, self.count[e]) for e in ENGS if self.count[e] > 0]
        targets += [(("d", i), c) for i, c in enumerate(self.dcnt) if c > 0]
        for e in ENGS:
            w = self.waited[e]
            waits = []
            for k, v in targets:
                if w.get(k, -1) < v:
                    w[k] = v
                    waits.append((k, v))
            if waits:
                self.ops[e].append((waits, None, None))
        for b in self.bufs:
            b.last_w = None
            b.readers = []

    def _sem(self, k):
        if isinstance(k, tuple):
            return self.dsems[k[1]]
        return self.sems[k]

    def emit(self):
        nc = self.nc
        with nc.Block() as block:
            def mk(ename):
                def body(e):
                    own = self.sems[ename]
                    for waits, fn, dinc in self.ops[ename]:
                        for (k, v) in waits:
                            e.wait_ge(self._sem(k), v)
                        if fn is None:
                            continue
                        ins = fn(e)
                        if dinc is not None:
                            ins.then_inc(self.dsems[dinc[0]], dinc[1])
                        else:
                            ins.then_inc(own, 1)
                return body
            block.tensor(mk("pe"))
            block.scalar(mk("act"))
            block.vector(mk("dve"))
            block.gpsimd(mk("pool"))
            block.sync(mk("sp"))


class TB:
    def __init__(self, t, b):
        self.t = t
        self.b = b

    def __getitem__(self, k):
        return self.t[k]


def build_nc():
    nc = bass.Bass("TRN2", target_bir_lowering=False)

    def din(name, shape, dt=F32):
        return nc.dram_tensor(name, list(shape), dt, kind="ExternalInput").ap()

    def dout(name, shape, dt=F32):
        return nc.dram_tensor(name, list(shape), dt, kind="ExternalOutput").ap()

    def dscr(name, shape, dt):
        return nc.dram_tensor(name, list(shape), dt, kind="Internal").ap()

    xp = din("xp", [T, 1024])
    memp = din("memp", [256, 1024])
    ln1_g = din("ln1_g", [1024]); ln2_g = din("ln2_g", [1024]); ln3_g = din("ln3_g", [1024])
    memn_g = din("mem_norm_g", [1024])
    w_in = din("w_in", [1024, 2048])
    a_re = din("ssm_a_re", [32, 64]); a_im = din("ssm_a_im", [32, 64])
    b_re = din("ssm_b_re", [32, 64, 16]); b_im = din("ssm_b_im", [32, 64, 16])
    c_re = din("ssm_c_re", [32, 16, 64]); c_im = din("ssm_c_im", [32, 16, 64])
    ssm_d = din("ssm_d", [32, 16]); log_dt = din("ssm_log_dt", [32])
    glu_w = din("ssm_glu_w", [512, 512])
    qn_g = din("q_norm_g", [64]); kn_g = din("k_norm_g", [64])
    lq1 = din("lam_q1", [64]); lk1 = din("lam_k1", [64]); lq2 = din("lam_q2", [64]); lk2 = din("lam_k2", [64])
    subln_g = din("subln_g", [128])
    w_out = din("w_out", [1024, 1024])
    ca_wq = din("ca_wq", [1024, 1024]); ca_wk = din("ca_wk", [1024, 1024]); ca_wv = din("ca_wv", [1024, 1024])
    caq_g = din("ca_q_norm_g", [256]); cak_g = din("ca_k_norm_g", [256])
    ca_wo = din("ca_wo", [1024, 1024])
    ffn_wg = din("ffn_wg", [1024, F]); ffn_wv = din("ffn_wv", [1024, F]); ffn_wd = din("ffn_wd", [F, 1024])
    conv_w = din("ffn_conv_w", [3, F]); conv_b = din("ffn_conv_b", [F])
    xs = din("xs", [TS, 1024])
    st_re = din("st_re", [NS, 2048]); st_im = din("st_im", [NS, 2048])
    st_conv = din("st_conv", [NS * 2, F])
    cmk = din("cmk", [NS, 256, 1024]); cmv = din("cmv", [NS, 256, 1024])
    cache_k = din("cache_k", [2560 * 128, 512]); cache_v = din("cache_v", [2560 * 128, 512])
    ptab = din("ptab", [NS, 16], I32)
    c_ident = din("c_ident", [128, 128])
    c_bd64 = din("c_bd64", [128, 128])
    c_mask16 = din("c_mask16", [128, 128])
    c_caus = din("c_caus", [128, 128])
    c_pwn = din("c_pwn", [NPW])
    c_kpos = din("c_kpos", [128])
    c_g2m = din("c_g2m", [128, 2])

    y_p = dout("y_p", [T, 1024]); k_p = dout("k_p", [T, 512]); v_p = dout("v_p", [T, 512])
    hre_p = dout("hre_p", [16, 128]); him_p = dout("him_p", [16, 128])
    conv_p = dout("conv_p", [2, F])
    mk_p = dout("mk_p", [256, 1024]); mv_p = dout("mv_p", [256, 1024])

    y_s = dout("y_s", [TS, 1024]); k_s = dout("k_s", [TS, 512]); v_s = dout("v_s", [TS, 512])
    hre_s = dout("hre_s", [NS, 2048]); him_s = dout("him_s", [NS, 2048])
    conv_s = dout("conv_s", [NS * 2, F])
    wg_s = dscr("wg_s", [128, NFC, 8, 128], BF16); wv_s = dscr("wv_s", [128, NFC, 8, 128], BF16)
    wd_s = dscr("wd_s", [128, 8, NFC, 128], BF16)
    mix_s = dscr("mix_s", [NTILES, 128, 8, NT], BF16)
    vn_s = dscr("vn_s", [4, NS * 4 * 132], BF16)
    xn_s = dscr("xn_s", [NTILES, 128, 8, NT], BF16)

    es = ExitStack()
    with es:
        S = Sched(nc, es)

        def sb(stack, name, shape, dt):
            return TB(stack.enter_context(nc.sbuf_tensor(name, list(shape), dt)), S.buf(name))

        PS = [TB(es.enter_context(nc.psum_tensor("ps%d" % i, [128, 512], F32)), S.buf("ps%d" % i)) for i in range(8)]
        for p_ in PS:
            p_.b.excl = True
        psrr = [0]

        def bank():
            b = PS[psrr[0]]
            psrr[0] = (psrr[0] + 1) % 8
            return b

        SLOW = dict(allow_slow_non_contiguous=True)

        def mm(bk, lhsT, rhs, start, stop, reads, out=None, **kw):
            o = bk[:] if out is None else out
            S.op("pe", lambda e: e.matmul(o, lhsT=lhsT, rhs=rhs, start=start, stop=stop, **kw),
                 reads=reads, writes=[bk.b])

        def tr(bk, out, in_, ident, reads):
            S.op("pe", lambda e: e.transpose(out=out, in_=in_, identity=ident[:]), reads=reads + [ident.b],
                 writes=[bk.b])

        def act(out, in_, func, reads, writes, **kw):
            S.op("act", lambda e: e.activation(out=out, in_=in_, func=func, **kw), reads=reads, writes=writes)

        def tt(eng, out, in0, in1, op, reads, writes):
            S.op(eng, lambda e: e.tensor_tensor(out=out, in0=in0, in1=in1, op=op), reads=reads, writes=writes)

        def ts(eng, out, in0, s1, s2, op0, op1, reads, writes):
            if op1 is None:
                S.op(eng, lambda e: e.tensor_scalar(out=out, in0=in0, scalar1=s1, scalar2=None, op0=op0),
                     reads=reads, writes=writes)
            else:
                S.op(eng, lambda e: e.tensor_scalar(out=out, in0=in0, scalar1=s1, scalar2=s2, op0=op0, op1=op1),
                     reads=reads, writes=writes)

        def stt(out, in0, scalar, in1, op0, op1, reads, writes):
            S.op("dve", lambda e: e.scalar_tensor_tensor(out=out, in0=in0, scalar=scalar, in1=in1, op0=op0, op1=op1),
                 reads=reads, writes=writes)

        def cp(eng, out, in_, reads, writes):
            if eng == "act":
                S.op("act", lambda e: e.copy(out=out, in_=in_), reads=reads, writes=writes)
            else:
                S.op(eng, lambda e: e.tensor_copy(out=out, in_=in_), reads=reads, writes=writes)

        def memset(eng, ap, val, writes):
            S.op(eng, lambda e: e.memset(ap, val), writes=writes)

        def recip(out, in_, reads, writes):
            S.op("dve", lambda e: e.reciprocal(out=out, in_=in_), reads=reads, writes=writes)

        IDF = sb(es, "IDF", [128, 128], F32); IDB = sb(es, "IDB", [128, 128], BF16)
        ON1024 = sb(es, "ON1024", [128, 128], BF16); ON256 = sb(es, "ON256", [128, 128], BF16)
        ON128 = sb(es, "ON128", [128, 128], BF16); ONE1 = sb(es, "ONE1", [128, 128], BF16)
        BD64 = sb(es, "BD64", [128, 128], BF16)
        CAUS = sb(es, "CAUS", [128, 128], BF16)
        G1 = sb(es, "G1", [128, 8], F32); G2 = sb(es, "G2", [128, 8], F32); G3 = sb(es, "G3", [128, 8], F32)
        GM = sb(es, "GM", [128, 8], F32)
        QG = sb(es, "QG", [128, 1], F32); KG = sb(es, "KG", [128, 1], F32)
        SUBG = sb(es, "SUBG", [128, 1], F32)
        CQG = sb(es, "CQG", [128, 2], F32); CKG = sb(es, "CKG", [128, 2], F32)
        CW = sb(es, "CW", [128, 3, NFC], F32); CB = sb(es, "CB", [128, NFC], F32)
        KPOS = sb(es, "KPOS", [128, 1], F32)
        NLAM = sb(es, "NLAM", [128, 1], F32)
        LT = sb(es, "LT", [64, 4], F32)

        S.dma("sp", IDF[:], c_ident, writes=[IDF.b])
        S.dma("pool", IDB[:], c_ident, writes=[IDB.b])
        S.dma("pool", BD64[:], c_bd64, writes=[BD64.b])
        S.dma("pool", CAUS[:], c_caus, writes=[CAUS.b])
        memset("dve", ON1024[:], 1.0 / 1024, [ON1024.b]); memset("dve", ON256[:], 1.0 / 256, [ON256.b])
        memset("dve", ON128[:], 1.0 / 128, [ON128.b]); memset("dve", ONE1[:], 1.0, [ONE1.b])
        for (Gt, gsrc) in ((G1, ln1_g), (G2, ln2_g), (G3, ln3_g), (GM, memn_g)):
            S.dma("sp", Gt[:], gsrc.rearrange("(c p) -> p c", p=128), writes=[Gt.b], **SLOW)
        for (Gt, gsrc) in ((QG, qn_g), (KG, kn_g)):
            for hh in range(2):
                S.dma("sp", Gt[hh * 64:(hh + 1) * 64, :], gsrc.rearrange("(p o) -> p o", o=1), writes=[Gt.b], **SLOW)
        S.dma("sp", SUBG[:], subln_g.rearrange("(p o) -> p o", o=1), writes=[SUBG.b], **SLOW)
        S.dma("sp", CQG[:], caq_g.rearrange("(c p) -> p c", p=128), writes=[CQG.b], **SLOW)
        S.dma("sp", CKG[:], cak_g.rearrange("(c p) -> p c", p=128), writes=[CKG.b], **SLOW)
        for j in range(3):
            S.dma("sp", CW[:, j, :], conv_w[j].rearrange("(c p) -> p c", p=128), writes=[CW.b], **SLOW)
        S.dma("sp", CB[:], conv_b.rearrange("(c p) -> p c", p=128), writes=[CB.b], **SLOW)
        S.dma("sp", KPOS[:], c_kpos.rearrange("(p o) -> p o", o=1), writes=[KPOS.b], **SLOW)
        for i, src in enumerate((lq1, lk1, lq2, lk2)):
            S.dma("sp", LT[:, i:i + 1], src.rearrange("(p o) -> p o", o=1), writes=[LT.b], **SLOW)
        ts("dve", SUBG[:], SUBG[:], 1.0 - LAM0, None, ALU.mult, None, [SUBG.b], [SUBG.b])
        LP = sb(es, "LP", [64, 2], BF16)
        tt("dve", LP[:, 0:1], LT[:, 0:1], LT[:, 1:2], ALU.mult, [LT.b], [LP.b])
        tt("dve", LP[:, 1:2], LT[:, 2:3], LT[:, 3:4], ALU.mult, [LT.b], [LP.b])
        bk = bank()
        mm(bk, ONE1[0:64, :], LP[:, :], True, True, [ONE1.b, LP.b], out=bk[:, 0:2])
        LE = sb(es, "LE", [128, 2], F32)
        act(LE[:], bk[:, 0:2], AF.Exp, [bk.b], [LE.b])
        stt(NLAM[:], LE[:, 1:2], -LAM0, LE[:, 0:1], ALU.add, ALU.subtract, [LE.b], [NLAM.b])

        def convert_ffn_weights(stack):
            CIN = sb(stack, "CIN", [128, 4096], F32)
            COUT = sb(stack, "COUT", [128, 4096], BF16)
            for (dst, src) in ((wg_s, ffn_wg), (wv_s, ffn_wv)):
                for f0 in range(0, NFC, 4):
                    nf = min(4, NFC - f0)
                    cin = CIN[:, 0:8 * nf * 128].rearrange("p (kc x) -> p kc x", kc=8)
                    S.dma("pool", cin, src[:, f0 * 128:(f0 + nf) * 128].rearrange("(kc p) x -> p kc x", p=128),
                          writes=[CIN.b])
                    cout = COUT[:, 0:nf * 1024].rearrange("p (fc kc n) -> p fc kc n", kc=8, n=128)
                    cp("pool", cout, cin.rearrange("p kc (fc n) -> p fc kc n", n=128), [CIN.b], [COUT.b])
                    S.dma("pool", dst[:, f0:f0 + nf].rearrange("p fc kc n -> p (fc kc n)"), COUT[:, 0:nf * 1024],
                          reads=[COUT.b])
            for m in range(8):
                for f0 in range(0, NFC, 8):
                    f1 = min(f0 + 8, NFC)
                    S.dma("pool", CIN[:, f0 * 128:f1 * 128].rearrange("p (fc n) -> p fc n", n=128),
                          ffn_wd[f0 * 128:f1 * 128, m * 128:(m + 1) * 128].rearrange("(fc p) n -> p fc n", p=128),
                          writes=[CIN.b])
                cp("pool", COUT[:, 0:NFC * 128], CIN[:, 0:NFC * 128], [CIN.b], [COUT.b])
                S.dma("pool", wd_s[:, m].rearrange("p fc n -> p (fc n)"), COUT[:, 0:NFC * 128], reads=[COUT.b])

        def load_norm_tile(X, xsrc_ap, nblk, G, xnT, rs, ss, junk, raw_xT=None):
            S.dma("sp", X[:, 0:nblk, :], xsrc_ap, writes=[X.b])
            for j in range(nblk):
                act(junk[:], X[:, j, :], AF.Square, [X.b], [junk.b, ss.b], accum_out=ss[:, j:j + 1])
            act(rs[:, 0:nblk], ss[:, 0:nblk], AF.Sqrt, [ss.b], [rs.b], scale=1.0 / 1024, bias=EPS)
            recip(rs[:, 0:nblk], rs[:, 0:nblk], [rs.b], [rs.b])
            if raw_xT is not None:
                for c in range(8):
                    bk = bank()
                    for j in range(nblk):
                        tr(bk, bk[:, j * 128:(j + 1) * 128], X[:, j, c * 128:(c + 1) * 128], IDF, [X.b])
                    cp("act", raw_xT[:, c, 0:nblk * 128], bk[:, 0:nblk * 128], [bk.b], [raw_xT.b])
            for j in range(nblk):
                ts("dve", X[:, j, :], X[:, j, :], rs[:, j:j + 1], None, ALU.mult, None, [X.b, rs.b], [X.b])
            for c in range(8):
                bk = bank()
                for j in range(nblk):
                    tr(bk, bk[:, j * 128:(j + 1) * 128], X[:, j, c * 128:(c + 1) * 128], IDF, [X.b])
                ts("dve", xnT[:, c, 0:nblk * 128], bk[:, 0:nblk * 128], G[:, c:c + 1], None, ALU.mult, None,
                   [bk.b, G.b], [xnT.b])

        def ln_fm(xT, G, xnT, n, sq, rstd):
            bk = bank()
            for c in range(8):
                act(sq[:, c % 2, 0:n], xT[:, c, 0:n], AF.Square, [xT.b], [sq.b])
                mm(bk, ON1024[:], sq[:, c % 2, 0:n], c == 0, c == 7, [ON1024.b, sq.b], out=bk[:, 0:n])
            act(rstd[:, 0:n], bk[:, 0:n], AF.Sqrt, [bk.b], [rstd.b], bias=EPS)
            recip(rstd[:, 0:n], rstd[:, 0:n], [rstd.b], [rstd.b])
            for c in range(8):
                stt(xnT[:, c, 0:n], xT[:, c, 0:n], G[:, c:c + 1], rstd[:, 0:n], ALU.mult, ALU.mult,
                    [xT.b, G.b, rstd.b], [xnT.b])

        def store_tm(src_fm, nchunks, n, dst_ap, stage, ident=IDF):
            nblk = (n + 127) // 128
            for j in range(nblk):
                w = min(128, n - j * 128)
                for c0 in range(0, nchunks, 4):
                    bk = bank()
                    for c in range(c0, min(c0 + 4, nchunks)):
                        tr(bk, bk[0:w, (c - c0) * 128:(c - c0 + 1) * 128], src_fm[:, c, j * 128:j * 128 + w], ident,
                           [src_fm.b])
                    nn = (min(c0 + 4, nchunks) - c0) * 128
                    cp("act", stage[0:w, j, c0 * 128:c0 * 128 + nn], bk[0:w, 0:nn], [bk.b], [stage.b])
            if n % 128 == 0:
                S.dma("sp", dst_ap.rearrange("(j p) f -> p j f", p=128), stage[:, 0:nblk, 0:nchunks * 128],
                      reads=[stage.b])
            else:
                S.dma("sp", dst_ap, stage[0:n, 0, 0:nchunks * 128], reads=[stage.b])


        xnTs = sb(es, "xnTs", [128, 8, TS], BF16)
        xsT = sb(es, "xsT", [128, 8, TS], F32)
        MIXs = sb(es, "MIXs", [128, 8, TS], BF16)
        usT = sb(es, "usT", [128, 4, TS], BF16)
        QTs = sb(es, "QTs", [128, 4, TS], BF16); KTs = sb(es, "KTs", [128, 4, TS], BF16)
        if 'S' in PH:
            with ExitStack() as ph:
                Xs_ = sb(ph, "Xs_", [128, 1024], F32)
                rs = sb(ph, "rs_s", [128, 1], F32); ss = sb(ph, "ss_s", [128, 1], F32)
                junk = sb(ph, "junk_s", [128, 1024], F32)
                S.dma("sp", Xs_[0:TS, :], xs, writes=[Xs_.b])
                act(junk[0:TS, :], Xs_[0:TS, :], AF.Square, [Xs_.b], [junk.b, ss.b], accum_out=ss[0:TS, 0:1])
                act(rs[0:TS, :], ss[0:TS, :], AF.Sqrt, [ss.b], [rs.b], scale=1.0 / 1024, bias=EPS)
                recip(rs[0:TS, :], rs[0:TS, :], [rs.b], [rs.b])
                for c in range(8):
                    bk = bank()
                    tr(bk, bk[:, 0:TS], Xs_[0:TS, c * 128:(c + 1) * 128], TB(IDF.t[0:TS, 0:TS], IDF.b), [Xs_.b])
                    cp("act", xsT[:, c, :], bk[:, 0:TS], [bk.b], [xsT.b])
                ts("dve", Xs_[0:TS, :], Xs_[0:TS, :], rs[0:TS, 0:1], None, ALU.mult, None, [Xs_.b, rs.b], [Xs_.b])
                for c in range(8):
                    bk = bank()
                    tr(bk, bk[:, 0:TS], Xs_[0:TS, c * 128:(c + 1) * 128], TB(IDF.t[0:TS, 0:TS], IDF.b), [Xs_.b])
                    ts("dve", xnTs[:, c, :], bk[:, 0:TS], G1[:, c:c + 1], None, ALU.mult, None, [bk.b, G1.b], [xnTs.b])
                S.barrier()

        MKT = sb(es, "MKT", [128, 8, 256], BF16)
        MV = sb(es, "MV", [128, 2, 1024], BF16)
        with ExitStack() as ph:
          if 'M' in PH:
            Xm = sb(ph, "Xm", [128, 2, 1024], F32)
            xnTm = sb(ph, "xnTm", [128, 8, 256], BF16)
            rs = sb(ph, "rs_m", [128, 4], F32); ss = sb(ph, "ss_m", [128, 4], F32)
            junk = sb(ph, "junk_m", [128, 1024], F32)
            Wk = sb(ph, "Wk", [128, 8, 1024], BF16); Wv = sb(ph, "Wv", [128, 8, 1024], BF16)
            mkraw = sb(ph, "mkraw", [128, 8, 256], F32)
            sqm = sb(ph, "sqm", [128, 2, 256], BF16); rstdm = sb(ph, "rstdm", [128, 256], F32)
            stg = sb(ph, "stg_m", [128, 2, 1024], F32)
            S.dma("pool", Wk[:], ca_wk.rearrange("(c p) n -> p c n", p=128), writes=[Wk.b])
            S.dma("pool", Wv[:], ca_wv.rearrange("(c p) n -> p c n", p=128), writes=[Wv.b])
            KS = int(os.environ.get('KSTOP', '9'))
            load_norm_tile(Xm, memp.rearrange("(j p) f -> p j f", p=128), 2, GM, xnTm, rs, ss, junk)
            for m in (range(8) if KS >= 2 else ()):
                bk = bank()
                for kc in range(8):
                    mm(bk, Wk[:, kc, m * 128:(m + 1) * 128], xnTm[:, kc, :], kc == 0, kc == 7, [Wk.b, xnTm.b],
                       out=bk[:, 0:256])
                cp("act", mkraw[:, m, :], bk[:, 0:256], [bk.b], [mkraw.b])
            for h in (range(4) if KS >= 3 else ()):
                bk = bank()
                for dh in range(2):
                    act(sqm[:, dh, :], mkraw[:, 2 * h + dh, :], AF.Square, [mkraw.b], [sqm.b])
                    mm(bk, ON256[:], sqm[:, dh, :], dh == 0, dh == 1, [ON256.b, sqm.b], out=bk[:, 0:256])
                act(rstdm[:], bk[:, 0:256], AF.Sqrt, [bk.b], [rstdm.b], bias=EPS)
                recip(rstdm[:], rstdm[:], [rstdm.b], [rstdm.b])
                for dh in range(2):
                    stt(mkraw[:, 2 * h + dh, :], mkraw[:, 2 * h + dh, :], CKG[:, dh:dh + 1], rstdm[:], ALU.mult,
                        ALU.mult, [mkraw.b, CKG.b, rstdm.b], [mkraw.b])
                    cp("act", MKT[:, 2 * h + dh, :], mkraw[:, 2 * h + dh, :], [mkraw.b], [MKT.b])
            if KS >= 4:
                store_tm(mkraw, 8, 256, mk_p, stg)
            for j in (range(2) if KS >= 5 else ()):
                for half in range(2):
                    bk = bank()
                    for kc in range(8):
                        mm(bk, xnTm[:, kc, j * 128:(j + 1) * 128], Wv[:, kc, half * 512:(half + 1) * 512], kc == 0,
                           kc == 7, [Wv.b, xnTm.b])
                    cp("act", stg[:, j, half * 512:(half + 1) * 512], bk[:], [bk.b], [stg.b])
                    cp("dve", MV[:, j, half * 512:(half + 1) * 512], bk[:], [bk.b], [MV.b])
            S.dma("sp", mv_p.rearrange("(j p) f -> p j f", p=128), stg[:], reads=[stg.b])
            S.barrier()


        s5 = ExitStack()
        if 'B' in PH:
            TWO_PI = 2.0 * math.pi
            PI_S = 3.1415925
            ZT = sb(s5, "ZT", [128, 4, 8, 2, 128], BF16)
            CLR = sb(s5, "CLR", [128, 16, 8, 32], BF16)
            CLI = sb(s5, "CLI", [128, 16, 8, 32], BF16)
            BDT = sb(s5, "BDT", [128, 4, 8, 128], BF16)
            LRE = sb(s5, "LRE", [128, 16, NPW], F32); LIM = sb(s5, "LIM", [128, 16, NPW], F32)
            with ExitStack() as tb:
                AR = sb(tb, "AR", [128, 16], F32); AI = sb(tb, "AI", [128, 16], F32)
                DT = sb(tb, "DT", [128, 16], F32)
                ARD = sb(tb, "ARD", [128, 16], F32); TH = sb(tb, "TH", [128, 16], F32)
                PWN = sb(tb, "PWN", [128, NPW], F32)
                ANG = sb(tb, "ANG", [128, 16, NPW], F32); R = sb(tb, "Rr", [128, 16, NPW], F32)
                KF = sb(tb, "KF", [128, 16, NPW], F32); KI = sb(tb, "KI", [128, 16, NPW], I32)
                MG = sb(tb, "MG", [128, 16, NPW], F32)
                S.dma("sp", AR[:], a_re.rearrange("(q g2) p -> (g2 p) q", g2=2), writes=[AR.b], **SLOW)
                S.dma("sp", AI[:], a_im.rearrange("(q g2) p -> (g2 p) q", g2=2), writes=[AI.b], **SLOW)
                for g2 in range(2):
                    S.dma("sp", DT[g2 * 64:(g2 + 1) * 64, :],
                          log_dt.rearrange("(q g2) -> g2 q", g2=2)[g2:g2 + 1, :].to_broadcast([64, 16]),
                          writes=[DT.b], **SLOW)
                S.dma("sp", PWN[:], c_pwn.rearrange("(o n) -> o n", o=1).to_broadcast([128, NPW]), writes=[PWN.b],
                      **SLOW)
                act(DT[:], DT[:], AF.Exp, [DT.b], [DT.b])
                tt("dve", ARD[:], AR[:], DT[:], ALU.mult, [AR.b, DT.b], [ARD.b])
                tt("dve", TH[:], AI[:], DT[:], ALU.mult, [AI.b, DT.b], [TH.b])
                bshape = [128, 16, NPW]
                tt("dve", MG[:], ARD[:, :].unsqueeze(2).to_broadcast(bshape), PWN[:, :].unsqueeze(1).to_broadcast(bshape),
                   ALU.mult, [ARD.b, PWN.b], [MG.b])
                act(MG[:], MG[:], AF.Exp, [MG.b], [MG.b])
                tt("dve", ANG[:], TH[:, :].unsqueeze(2).to_broadcast(bshape), PWN[:, :].unsqueeze(1).to_broadcast(bshape),
                   ALU.mult, [TH.b, PWN.b], [ANG.b])
                ts("dve", KF[:], ANG[:], 1.0 / TWO_PI, 0.5, ALU.mult, ALU.add, [ANG.b], [KF.b])
                cp("dve", KI[:], KF[:], [KF.b], [KI.b])
                cp("dve", KF[:], KI[:], [KI.b], [KF.b])
                stt(R[:], KF[:], -TWO_PI, ANG[:], ALU.mult, ALU.add, [KF.b, ANG.b], [R.b])

                def wrap(Rt):
                    ts("dve", KF[:], Rt[:], -math.pi, None, ALU.is_lt, None, [Rt.b], [KF.b])
                    stt(Rt[:], KF[:], TWO_PI, Rt[:], ALU.mult, ALU.add, [KF.b, Rt.b], [Rt.b])
                    ts("dve", KF[:], Rt[:], math.pi, None, ALU.is_gt, None, [Rt.b], [KF.b])
                    stt(Rt[:], KF[:], -TWO_PI, Rt[:], ALU.mult, ALU.add, [KF.b, Rt.b], [Rt.b])
                    ts("dve", Rt[:], Rt[:], -PI_S, PI_S, ALU.max, ALU.min, [Rt.b], [Rt.b])
                wrap(R)
                act(LIM[:], R[:], AF.Sin, [R.b], [LIM.b])
                ts("dve", R[:], R[:], math.pi / 2, None, ALU.add, None, [R.b], [R.b])
                wrap(R)
                act(LRE[:], R[:], AF.Sin, [R.b], [LRE.b])
                tt("dve", LRE[:], LRE[:], MG[:], ALU.mult, [LRE.b, MG.b], [LRE.b])
                tt("dve", LIM[:], LIM[:], MG[:], ALU.mult, [LIM.b, MG.b], [LIM.b])
                NRE = sb(tb, "NRE", [128, 16], F32); DEN = sb(tb, "DEN", [128, 16], F32)
                FRE = sb(tb, "FRE", [128, 16], F32); FIM = sb(tb, "FIM", [128, 16], F32)
                T1 = sb(tb, "T1", [128, 16], F32)
                ts("dve", NRE[:], LRE[:, :, 1], -1.0, None, ALU.add, None, [LRE.b], [NRE.b])
                tt("dve", DEN[:], AR[:], AR[:], ALU.mult, [AR.b], [DEN.b])
                tt("dve", T1[:], AI[:], AI[:], ALU.mult, [AI.b], [T1.b])
                tt("dve", DEN[:], DEN[:], T1[:], ALU.add, [DEN.b, T1.b], [DEN.b])
                recip(DEN[:], DEN[:], [DEN.b], [DEN.b])
                tt("dve", FRE[:], NRE[:], AR[:], ALU.mult, [NRE.b, AR.b], [FRE.b])
                tt("dve", T1[:], LIM[:, :, 1], AI[:], ALU.mult, [LIM.b, AI.b], [T1.b])
                tt("dve", FRE[:], FRE[:], T1[:], ALU.add, [FRE.b, T1.b], [FRE.b])
                tt("dve", FRE[:], FRE[:], DEN[:], ALU.mult, [FRE.b, DEN.b], [FRE.b])
                tt("dve", FIM[:], LIM[:, :, 1], AR[:], ALU.mult, [LIM.b, AR.b], [FIM.b])
                tt("dve", T1[:], NRE[:], AI[:], ALU.mult, [NRE.b, AI.b], [T1.b])
                tt("dve", FIM[:], FIM[:], T1[:], ALU.subtract, [FIM.b, T1.b], [FIM.b])
                tt("dve", FIM[:], FIM[:], DEN[:], ALU.mult, [FIM.b, DEN.b], [FIM.b])
                BR = sb(tb, "BR", [128, 16, 16], F32); BI = sb(tb, "BI", [128, 16, 16], F32)
                BBR = sb(tb, "BBR", [128, 16, 16], F32); BBI = sb(tb, "BBI", [128, 16, 16], F32)
                T2 = sb(tb, "T2", [128, 16, 16], F32)
                S.dma("sp", BR[:], b_re.rearrange("(q g2) p c -> (g2 p) q c", g2=2), writes=[BR.b], **SLOW)
                S.dma("sp", BI[:], b_im.rearrange("(q g2) p c -> (g2 p) q c", g2=2), writes=[BI.b], **SLOW)
                s3 = [128, 16, 16]
                fr = FRE[:, :].unsqueeze(2).to_broadcast(s3); fi = FIM[:, :].unsqueeze(2).to_broadcast(s3)
                tt("dve", BBR[:], BR[:], fr, ALU.mult, [BR.b, FRE.b], [BBR.b])
                tt("dve", T2[:], BI[:], fi, ALU.mult, [BI.b, FIM.b], [T2.b])
                tt("dve", BBR[:], BBR[:], T2[:], ALU.subtract, [BBR.b, T2.b], [BBR.b])
                tt("dve", BBI[:], BI[:], fr, ALU.mult, [BI.b, FRE.b], [BBI.b])
                tt("dve", T2[:], BR[:], fi, ALU.mult, [BR.b, FIM.b], [T2.b])
                tt("dve", BBI[:], BBI[:], T2[:], ALU.add, [BBI.b, T2.b], [BBI.b])
                ZR = sb(tb, "ZR", [128, 16, 8, 16], F32); ZI = sb(tb, "ZI", [128, 16, 8, 16], F32)
                T3 = sb(tb, "T3", [128, 16, 8, 16], F32)
                s4 = [128, 16, 8, 16]
                lr = LRE[:, :, 0:8].unsqueeze(3).to_broadcast(s4); li = LIM[:, :, 0:8].unsqueeze(3).to_broadcast(s4)
                br_ = BBR[:, :, :].unsqueeze(2).to_broadcast(s4); bi_ = BBI[:, :, :].unsqueeze(2).to_broadcast(s4)
                tt("dve", ZR[:], lr, br_, ALU.mult, [LRE.b, BBR.b], [ZR.b])
                tt("dve", T3[:], li, bi_, ALU.mult, [LIM.b, BBI.b], [T3.b])
                tt("dve", ZR[:], ZR[:], T3[:], ALU.subtract, [ZR.b, T3.b], [ZR.b])
                tt("dve", ZI[:], lr, bi_, ALU.mult, [LRE.b, BBI.b], [ZI.b])
                tt("dve", T3[:], li, br_, ALU.mult, [LIM.b, BBR.b], [T3.b])
                tt("dve", ZI[:], ZI[:], T3[:], ALU.add, [ZI.b, T3.b], [ZI.b])
                E4 = sb(tb, "E4", [128, 4, 8, 2, 128], F32)
                memset("pool", E4[:], 0.0, [E4.b])
                for ri, Zt in enumerate((ZR, ZI)):
                    for gc in range(4):
                        for g2 in range(2):
                            pr = slice(g2 * 64, (g2 + 1) * 64)
                            dst = E4[pr, gc, :, ri, :].rearrange("p t (j x) -> p t j x", x=32)[:, :, :, g2 * 16:(g2 + 1) * 16]
                            src = Zt[pr, gc * 4:(gc + 1) * 4, :, :].rearrange("p j t c -> p t j c")
                            cp("pool", dst, src, [Zt.b], [E4.b])
                for gc in range(4):
                    for ri in range(2):
                        for s0 in (0, 4):
                            bk = bank()
                            for sp_ in range(s0, s0 + 4):
                                tr(bk, bk[:, (sp_ - s0) * 128:(sp_ - s0 + 1) * 128], E4[:, gc, 7 - sp_, ri, :], IDF, [E4.b])
                            cp("act", ZT[:, gc, s0:s0 + 4, ri, :], bk[:].rearrange("p (s n) -> p s n", n=128), [bk.b],
                               [ZT.b])
                CN = sb(tb, "CN", [128, 2, 4, 64], F32)
                CE = sb(tb, "CE", [128, 2, 4, 2, 64], F32)
                CTR = sb(tb, "CTR", [128, 4, 128], F32); CTI = sb(tb, "CTI", [128, 4, 128], F32)
                CTIN = sb(tb, "CTIN", [128, 4, 128], F32)
                G2M = sb(tb, "G2M", [128, 2], F32)
                S.dma("sp", G2M[:], c_g2m, writes=[G2M.b])
                S.dma("sp", CN[:, 0, :, :], c_re.rearrange("(gc r) c p -> (r c) gc p", gc=4), writes=[CN.b], **SLOW)
                S.dma("sp", CN[:, 1, :, :], c_im.rearrange("(gc r) c p -> (r c) gc p", gc=4), writes=[CN.b], **SLOW)
                for ri in range(2):
                    for g2 in range(2):
                        ts("dve", CE[:, ri, :, g2, :], CN[:, ri, :, :], G2M[:, g2:g2 + 1], None, ALU.mult, None,
                           [CN.b, G2M.b], [CE.b])
                for ri, CTt in enumerate((CTR, CTI)):
                    bk = bank()
                    for gc in range(4):
                        tr(bk, bk[:, gc * 128:(gc + 1) * 128], CE[:, ri, gc, :, :].rearrange("p a b -> p (a b)"), IDF,
                           [CE.b])
                    cp("act", CTt[:], bk[:].rearrange("p (g n) -> p g n", n=128), [bk.b], [CTt.b])
                ts("dve", CTIN[:], CTI[:], -1.0, None, ALU.mult, None, [CTI.b], [CTIN.b])
                s5s = [128, 16, 8, 32]
                T4 = sb(tb, "T4", [128, 16, 8, 32], F32); T5 = sb(tb, "T5", [128, 16, 8, 32], F32)
                ctr = CTR[:, :, :].rearrange("p g (j x) -> p (g j) x", x=32).unsqueeze(2).to_broadcast(s5s)
                cti = CTI[:, :, :].rearrange("p g (j x) -> p (g j) x", x=32).unsqueeze(2).to_broadcast(s5s)
                l1r = LRE[:, :, 1:9].unsqueeze(3).to_broadcast(s5s); l1i = LIM[:, :, 1:9].unsqueeze(3).to_broadcast(s5s)
                tt("dve", T4[:], ctr, l1r, ALU.mult, [CTR.b, LRE.b], [T4.b])
                tt("dve", T5[:], cti, l1i, ALU.mult, [CTI.b, LIM.b], [T5.b])
                tt("dve", CLR[:], T4[:], T5[:], ALU.subtract, [T4.b, T5.b], [CLR.b])
                tt("dve", T4[:], ctr, l1i, ALU.mult, [CTR.b, LIM.b], [T4.b])
                tt("dve", T5[:], cti, l1r, ALU.mult, [CTI.b, LRE.b], [T5.b])
                tt("dve", T4[:], T4[:], T5[:], ALU.add, [T4.b, T5.b], [T4.b])
                ts("dve", CLI[:], T4[:], -1.0, None, ALU.mult, None, [T4.b], [CLI.b])
                MASK16 = sb(tb, "MASK16", [128, 128], F32); DCOL = sb(tb, "DCOL", [128, 4], F32)
                T6 = sb(tb, "T6", [128, 128], F32)
                S.dma("sp", MASK16[:], c_mask16, writes=[MASK16.b])
                S.dma("sp", DCOL[:], ssm_d.rearrange("(gc g8) c -> (g8 c) gc", gc=4), writes=[DCOL.b], **SLOW)
                for gc in range(4):
                    for tau in range(8):
                        bk = bank()
                        mm(bk, E4[:, gc, tau, 0, :], CTR[:, gc, :], True, False, [E4.b, CTR.b], out=bk[:, 0:128])
                        mm(bk, E4[:, gc, tau, 1, :], CTIN[:, gc, :], False, True, [E4.b, CTIN.b], out=bk[:, 0:128])
                        if tau == 0:
                            tt("dve", T6[:], bk[:, 0:128], MASK16[:], ALU.mult, [bk.b, MASK16.b], [T6.b])
                            stt(BDT[:, gc, 0, :], IDF[:], DCOL[:, gc:gc + 1], T6[:], ALU.mult, ALU.add,
                                [IDF.b, DCOL.b, T6.b], [BDT.b])
                        else:
                            tt("dve", BDT[:, gc, tau, :], bk[:, 0:128], MASK16[:], ALU.mult, [bk.b, MASK16.b], [BDT.b])
                S.barrier()

            pb = ExitStack()
            uT = sb(pb, "uT", [128, 4, T], BF16)
            with ExitStack() as ph:
                X = sb(ph, "Xu", [128, 4, 1024], F32)
                xnTu = [sb(ph, "xnTu%d" % i, [128, 8, NT], BF16) for i in range(2)]
                rs = sb(ph, "rsu", [128, 4], F32); ss = sb(ph, "ssu", [128, 4], F32)
                junk = sb(ph, "junku", [128, 1024], F32)
                Wu = sb(ph, "Wu", [128, 8, 512], BF16)
                S.dma("pool", Wu[:], w_in[:, 0:512].rearrange("(c p) n -> p c n", p=128), writes=[Wu.b])
                for it in range(NTILES):
                    t0 = it * NT
                    xnT = xnTu[it % 2]
                    load_norm_tile(X, xp[t0:t0 + NT, :].rearrange("(j p) f -> p j f", p=128), 4, G1, xnT, rs, ss, junk)
                    S.dma("sp", xn_s[it], xnT[:], reads=[xnT.b])
                    for m in range(4):
                        bk = bank()
                        for kc in range(8):
                            mm(bk, Wu[:, kc, m * 128:(m + 1) * 128], xnT[:, kc, :], kc == 0, kc == 7, [Wu.b, xnT.b])
                        cp("act", uT[:, m, t0:t0 + NT], bk[:], [bk.b], [uT.b])
                if 'S' in PH:
                    for m in range(4):
                        bk = bank()
                        for kc in range(8):
                            mm(bk, Wu[:, kc, m * 128:(m + 1) * 128], xnTs[:, kc, :], kc == 0, kc == 7, [Wu.b, xnTs.b],
                               out=bk[:, 0:TS])
                        cp("act", usT[:, m, :], bk[:, 0:TS], [bk.b], [usT.b])
                S.barrier()
            if 'S' in PH:
              with ExitStack() as ph:
                STt = sb(ph, "STt", [NS, 2, 2048], F32)
                S0 = sb(ph, "S0", [128, 2, 16, NS], F32); S0b = sb(ph, "S0b", [128, 2, 16, NS], BF16)
                SN = sb(ph, "SN", [128, 2, 16, NS], F32)
                W1 = sb(ph, "W1", [128, 16, NS], F32); W2 = sb(ph, "W2", [128, 16, NS], F32)
                hst = sb(ph, "hst", [NS, 2, 2048], F32)
                Y2s = sb(ph, "Y2s", [128, TS], F32); Y3s = sb(ph, "Y3s", [128, TS], F32)
                Gts = sb(ph, "Gts", [128, 4, TS], F32); Gbs = sb(ph, "Gbs", [128, 4, TS], BF16)
                GLUWs = sb(ph, "GLUWs", [128, 4, 512], BF16)
                S.dma("pool", GLUWs[:], glu_w.rearrange("(c p) n -> p c n", p=128), writes=[GLUWs.b])
                S.dma("sp", STt[:, 0, :], st_re, writes=[STt.b])
                S.dma("sp", STt[:, 1, :], st_im, writes=[STt.b])
                ID16 = TB(IDF.t[0:NS, 0:NS], IDF.b)
                for ri in range(2):
                    bk = bank()
                    for q16 in range(16):
                        tr(bk, bk[:, q16 * NS:(q16 + 1) * NS], STt[:, ri, q16 * 128:(q16 + 1) * 128], ID16, [STt.b])
                    cp("act", S0[:, ri, :, :], bk[:, 0:16 * NS].rearrange("p (q s) -> p q s", s=NS), [bk.b], [S0.b])
                    cp("dve", S0b[:, ri, :, :], S0[:, ri, :, :], [S0.b], [S0b.b])
                for q16 in range(16):
                    gc, j = q16 // 4, q16 % 4
                    pr = slice(32 * j, 32 * j + 32)
                    uv = usT[:, gc, :].rearrange("p (n l) -> p l n", l=4)
                    bk = bank()
                    for ri in range(2):
                        for sp_ in range(4):
                            mm(bk, ZT[pr, gc, 4 + sp_, ri, :], uv[pr, sp_, :], sp_ == 0, sp_ == 3, [ZT.b, usT.b],
                               out=bk[:, ri * NS:(ri + 1) * NS], tile_position=(32 * j, 0))
                    cp("act", SN[:, :, q16, :], bk[:, 0:2 * NS].rearrange("p (r s) -> p r s", s=NS), [bk.b], [SN.b])
                s3 = [128, 16, NS]
                l4r = LRE[:, :, 4:5].to_broadcast(s3); l4i = LIM[:, :, 4:5].to_broadcast(s3)
                tt("dve", W1[:], S0[:, 0, :, :], l4r, ALU.mult, [S0.b, LRE.b], [W1.b])
                tt("dve", W2[:], S0[:, 1, :, :], l4i, ALU.mult, [S0.b, LIM.b], [W2.b])
                tt("dve", W1[:], W1[:], W2[:], ALU.subtract, [W1.b, W2.b], [W1.b])
                tt("dve", SN[:, 0, :, :], SN[:, 0, :, :], W1[:], ALU.add, [SN.b, W1.b], [SN.b])
                tt("dve", W1[:], S0[:, 0, :, :], l4i, ALU.mult, [S0.b, LIM.b], [W1.b])
                tt("dve", W2[:], S0[:, 1, :, :], l4r, ALU.mult, [S0.b, LRE.b], [W2.b])
                tt("dve", W1[:], W1[:], W2[:], ALU.add, [W1.b, W2.b], [W1.b])
                tt("dve", SN[:, 1, :, :], SN[:, 1, :, :], W1[:], ALU.add, [SN.b, W1.b], [SN.b])
                for ri in range(2):
                    for q0 in range(0, 16, 4):
                        bk = bank()
                        for q16 in range(q0, q0 + 4):
                            tr(bk, bk[0:NS, (q16 - q0) * 128:(q16 - q0 + 1) * 128], SN[:, ri, q16, :], IDF, [SN.b])
                        cp("act", hst[:, ri, q0 * 128:(q0 + 4) * 128], bk[0:NS, :], [bk.b], [hst.b])
                S.dma("sp", hre_s, hst[:, 0, :], reads=[hst.b])
                S.dma("sp", him_s, hst[:, 1, :], reads=[hst.b])
                for gc in range(4):
                    bk = bank()
                    uv = usT[:, gc, :].rearrange("p (n l) -> p l n", l=4)
                    ov = bk[:, 0:TS].rearrange("p (n l) -> p l n", l=4)
                    for tp in range(4):
                        for sp_ in range(tp + 1):
                            mm(bk, BDT[:, gc, tp - sp_, :], uv[:, sp_, :], sp_ == 0, False, [BDT.b, usT.b],
                               out=ov[:, tp, :])
                        for j in range(4):
                            q16 = gc * 4 + j
                            for ri, CLt in enumerate((CLR, CLI)):
                                mm(bk, CLt[:, q16, tp, :], S0b[:, ri, q16, :], False, (j == 3 and ri == 1),
                                   [CLt.b, S0b.b], out=ov[32 * j:32 * j + 32, tp, :], tile_position=(0, 32 * j))
                    act(Y2s[:], bk[:, 0:TS], AF.Square, [bk.b], [Y2s.b])
                    ts("dve", Y2s[:], Y2s[:], 0.044715, 1.0, ALU.mult, ALU.add, [Y2s.b], [Y2s.b])
                    tt("dve", Y3s[:], Y2s[:], bk[:, 0:TS], ALU.mult, [Y2s.b, bk.b], [Y3s.b])
                    act(Y3s[:], Y3s[:], AF.Sigmoid, [Y3s.b], [Y3s.b], scale=2.0 * math.sqrt(2.0 / math.pi))
                    tt("dve", Gts[:, gc, :], Y3s[:], bk[:, 0:TS], ALU.mult, [Y3s.b, bk.b], [Gts.b])
                    cp("act", Gbs[:, gc, :], Gts[:, gc, :], [Gts.b], [Gbs.b])
                for m in range(4):
                    bk = bank()
                    for kc in range(4):
                        mm(bk, GLUWs[:, kc, m * 128:(m + 1) * 128], Gbs[:, kc, :], kc == 0, kc == 3, [GLUWs.b, Gbs.b],
                           out=bk[:, 0:TS])
                    act(Y3s[:], bk[:, 0:TS], AF.Sigmoid, [bk.b], [Y3s.b])
                    tt("dve", MIXs[:, m, :], Y3s[:], Gts[:, m, :], ALU.mult, [Y3s.b, Gts.b], [MIXs.b])
                S.barrier()
            SW = sb(pb, "SW", [128, 2, 8, NCH], F32)
            SP = sb(pb, "SPv", [128, 2, 16, NCH], BF16)
            EE = sb(pb, "EE", [128, 2, 16, NSC], F32)
            TA = sb(pb, "TA", [128, 8, NSC], F32); TBt = sb(pb, "TBt", [128, 8, NSC], F32)
            U1 = sb(pb, "U1", [128, 8], F32); U2 = sb(pb, "U2", [128, 8], F32)
            V1 = sb(pb, "V1", [128, 8, SC], F32); V2 = sb(pb, "V2", [128, 8, SC], F32)
            memset("dve", SP[:, :, :, 0:1], 0.0, [SP.b])
            for hq in range(2):
                qs = slice(8 * hq, 8 * hq + 8)
                for q8 in range(8):
                    q16 = 8 * hq + q8
                    gc, j = q16 // 4, q16 % 4
                    pr = slice(32 * j, 32 * j + 32)
                    uv = uT[:, gc, :].rearrange("p (n l) -> p l n", l=L)
                    for ri in range(2):
                        bk = bank()
                        for sp_ in range(L):
                            mm(bk, ZT[pr, gc, sp_, ri, :], uv[pr, sp_, :], sp_ == 0, sp_ == L - 1, [ZT.b, uT.b],
                               tile_position=(32 * j, 0))
                        cp("act" if ri == 0 else "dve", SW[:, ri, q8, :], bk[:], [bk.b], [SW.b])
                Sv = SW[:, :, :, :].rearrange("p r q (s i) -> p r q s i", i=SC)
                sA = [128, 8, NSC]
                a8r = LRE[:, qs, 8:9].to_broadcast(sA); a8i = LIM[:, qs, 8:9].to_broadcast(sA)
                for i in range(1, SC):
                    pre_r = Sv[:, 0, :, :, i - 1]; pre_i = Sv[:, 1, :, :, i - 1]
                    tt("dve", TA[:], pre_r, a8r, ALU.mult, [SW.b, LRE.b], [TA.b])
                    tt("dve", TBt[:], pre_i, a8i, ALU.mult, [SW.b, LIM.b], [TBt.b])
                    tt("dve", TA[:], TA[:], TBt[:], ALU.subtract, [TA.b, TBt.b], [TA.b])
                    tt("dve", TBt[:], pre_i, a8r, ALU.mult, [SW.b, LRE.b], [TBt.b])
                    tt("dve", Sv[:, 0, :, :, i], Sv[:, 0, :, :, i], TA[:], ALU.add, [SW.b, TA.b], [SW.b])
                    tt("dve", TA[:], pre_r, a8i, ALU.mult, [SW.b, LIM.b], [TA.b])
                    tt("dve", TA[:], TA[:], TBt[:], ALU.add, [TA.b, TBt.b], [TA.b])
                    tt("dve", Sv[:, 1, :, :, i], Sv[:, 1, :, :, i], TA[:], ALU.add, [SW.b, TA.b], [SW.b])
                cp("dve", EE[:, :, qs, 0], Sv[:, :, :, 0, SC - 1], [SW.b], [EE.b])
                for sc in range(1, NSC):
                    er = EE[:, 0, qs, sc - 1]; ei = EE[:, 1, qs, sc - 1]
                    tt("dve", U1[:], er, LRE[:, qs, 23], ALU.mult, [EE.b, LRE.b], [U1.b])
                    tt("dve", U2[:], ei, LIM[:, qs, 23], ALU.mult, [EE.b, LIM.b], [U2.b])
                    tt("dve", U1[:], U1[:], U2[:], ALU.subtract, [U1.b, U2.b], [U1.b])
                    tt("dve", EE[:, 0, qs, sc], U1[:], Sv[:, 0, :, sc, SC - 1], ALU.add, [U1.b, SW.b], [EE.b])
                    tt("dve", U1[:], er, LIM[:, qs, 23], ALU.mult, [EE.b, LIM.b], [U1.b])
                    tt("dve", U2[:], ei, LRE[:, qs, 23], ALU.mult, [EE.b, LRE.b], [U2.b])
                    tt("dve", U1[:], U1[:], U2[:], ALU.add, [U1.b, U2.b], [U1.b])
                    tt("dve", EE[:, 1, qs, sc], U1[:], Sv[:, 1, :, sc, SC - 1], ALU.add, [U1.b, SW.b], [EE.b])
                cp("act", SP[:, 0, qs, 1:SC], SW[:, 0, :, 0:SC - 1], [SW.b], [SP.b])
                cp("act", SP[:, 1, qs, 1:SC], SW[:, 1, :, 0:SC - 1], [SW.b], [SP.b])
                sC = [128, 8, SC]
                PR = LRE[:, qs, 8:24]; PI = LIM[:, qs, 8:24]
                for sc in range(1, NSC):
                    er = EE[:, 0, qs, sc - 1:sc].to_broadcast(sC); ei = EE[:, 1, qs, sc - 1:sc].to_broadcast(sC)
                    tt("dve", V1[:], PR, er, ALU.mult, [LRE.b, EE.b], [V1.b])
                    tt("dve", V2[:], PI, ei, ALU.mult, [LIM.b, EE.b], [V2.b])
                    tt("dve", V1[:], V1[:], V2[:], ALU.subtract, [V1.b, V2.b], [V1.b])
                    tt("dve", V1[:], V1[:], Sv[:, 0, :, sc, :], ALU.add, [V1.b, SW.b], [V1.b])
                    tt("dve", V2[:], PR, ei, ALU.mult, [LRE.b, EE.b], [V2.b])
                    n_ = SC if sc < NSC - 1 else SC - 1
                    cp("act", SP[:, 0, qs, sc * SC + 1:sc * SC + 1 + n_], V1[:, :, 0:n_], [V1.b], [SP.b])
                    tt("dve", V1[:], PI, er, ALU.mult, [LIM.b, EE.b], [V1.b])
                    tt("dve", V2[:], V2[:], V1[:], ALU.add, [V1.b, V2.b], [V2.b])
                    tt("dve", V2[:], V2[:], Sv[:, 1, :, sc, :], ALU.add, [V2.b, SW.b], [V2.b])
                    cp("act", SP[:, 1, qs, sc * SC + 1:sc * SC + 1 + n_], V2[:, :, 0:n_], [V2.b], [SP.b])
                for sc in range(1, NSC):
                    cp("act", SP[:, :, qs, sc * SC], EE[:, :, qs, sc - 1], [EE.b], [SP.b])
            S.dma("sp", hre_p.rearrange("q x -> x q"), EE[:, 0, :, NSC - 1], reads=[EE.b], **SLOW)
            S.dma("sp", him_p.rearrange("q x -> x q"), EE[:, 1, :, NSC - 1], reads=[EE.b], **SLOW)
            GLUW = sb(pb, "GLUW", [128, 4, 512], BF16)
            S.dma("pool", GLUW[:], glu_w.rearrange("(c p) n -> p c n", p=128), writes=[GLUW.b])
            Y2 = sb(pb, "Y2", [128, NT], F32); Y3 = sb(pb, "Y3", [128, NT], F32)
            Gt = sb(pb, "Gt", [128, 4, NT], F32); Gb = sb(pb, "Gb", [128, 4, NT], BF16)
            MS = [sb(pb, "MS%d" % i, [128, 4, NT], BF16) for i in range(2)]
            CG = math.sqrt(2.0 / math.pi)
            NCT = NT // L
            for it in range(NTILES):
                t0 = it * NT
                c0 = it * NCT
                for gc in range(4):
                    bk = bank()
                    uv = uT[:, gc, t0:t0 + NT].rearrange("p (n l) -> p l n", l=L)
                    ov = bk[:].rearrange("p (n l) -> p l n", l=L)
                    for tp in range(L):
                        for sp_ in range(tp + 1):
                            mm(bk, BDT[:, gc, tp - sp_, :], uv[:, sp_, :], sp_ == 0, False, [BDT.b, uT.b],
                               out=ov[:, tp, :])
                        for j in range(4):
                            q16 = gc * 4 + j
                            for ri, CLt in enumerate((CLR, CLI)):
                                mm(bk, CLt[:, q16, tp, :], SP[:, ri, q16, c0:c0 + NCT], False,
                                   (j == 3 and ri == 1), [CLt.b, SP.b], out=ov[32 * j:32 * j + 32, tp, :],
                                   tile_position=(0, 32 * j))
                    act(Y2[:], bk[:], AF.Square, [bk.b], [Y2.b])
                    ts("dve", Y2[:], Y2[:], 0.044715, 1.0, ALU.mult, ALU.add, [Y2.b], [Y2.b])
                    tt("dve", Y3[:], Y2[:], bk[:], ALU.mult, [Y2.b, bk.b], [Y3.b])
                    act(Y3[:], Y3[:], AF.Sigmoid, [Y3.b], [Y3.b], scale=2.0 * CG)
                    tt("dve", Gt[:, gc, :], Y3[:], bk[:], ALU.mult, [Y3.b, bk.b], [Gt.b])
                    cp("act", Gb[:, gc, :], Gt[:, gc, :], [Gt.b], [Gb.b])
                M_ = MS[it % 2]
                for m in range(4):
                    bk = bank()
                    for kc in range(4):
                        mm(bk, GLUW[:, kc, m * 128:(m + 1) * 128], Gb[:, kc, :], kc == 0, kc == 3, [GLUW.b, Gb.b])
                    act(Y3[:], bk[:], AF.Sigmoid, [bk.b], [Y3.b])
                    tt("dve", M_[:, m, :], Y3[:], Gt[:, m, :], ALU.mult, [Y3.b, Gt.b], [M_.b])
                S.dma("sp", mix_s[it, :, 0:4, :], M_[:], reads=[M_.b])
            S.barrier()
            pb.close()
        s5.close()

        att = ExitStack()
        QT = sb(att, "QT", [128, 4, T], BF16)
        KT = sb(att, "KT", [128, 4, T], BF16)
        V = sb(att, "V", [128, T // 128, 512], BF16)
        with ExitStack() as ph:
          if 'A2' in PH:
            xnT2 = [sb(ph, "xnT%d" % i, [128, 8, NT], BF16) for i in range(2)]
            W = sb(ph, "Wqkv", [128, 8, 1536], BF16)
            sqA = [sb(ph, "sq%d" % i, [128, NT], BF16) for i in range(2)]
            rstdA = [sb(ph, "rstd%d" % i, [128, NT], F32) for i in range(2)]
            sq = sqA[0]; rstd = rstdA[0]
            knT = sb(ph, "knT", [128, 4, NT], F32)
            stg = sb(ph, "stg", [128, 4, 512], F32)
            S.dma("pool", W[:], w_in[:, 512:2048].rearrange("(c p) n -> p c n", p=128), writes=[W.b])
            S.dma("sp", xnT2[0][:], xn_s[0], writes=[xnT2[0].b])
            for it in range(NTILES):
                t0 = it * NT
                xnT = xnT2[it % 2]
                if it + 1 < NTILES:
                    S.dma("sp", xnT2[(it + 1) % 2][:], xn_s[it + 1], writes=[xnT2[(it + 1) % 2].b])
                for qk in range(2):
                    for m in range(4):
                        bk = bank()
                        for kc in range(8):
                            mm(bk, W[:, kc, qk * 512 + m * 128:qk * 512 + (m + 1) * 128], xnT[:, kc, :], kc == 0,
                               kc == 7, [W.b, xnT.b])
                        sq_t = sqA[m % 2]; rstd_t = rstdA[m % 2]
                        act(sq_t[:], bk[:], AF.Square, [bk.b], [sq_t.b])
                        b2 = bank()
                        mm(b2, BD64[:], sq_t[:], True, True, [BD64.b, sq_t.b])
                        act(rstd_t[:], b2[:], AF.Sqrt, [b2.b], [rstd_t.b], bias=EPS)
                        recip(rstd_t[:], rstd_t[:], [rstd_t.b], [rstd_t.b])
                        if qk == 0:
                            stt(QT[:, m, t0:t0 + NT], bk[:], QG[:, 0:1], rstd_t[:], ALU.mult, ALU.mult,
                                [bk.b, QG.b, rstd_t.b], [QT.b])
                        else:
                            stt(knT[:, m, :], bk[:], KG[:, 0:1], rstd_t[:], ALU.mult, ALU.mult,
                                [bk.b, KG.b, rstd_t.b], [knT.b])
                            cp("act", KT[:, m, t0:t0 + NT], knT[:, m, :], [knT.b], [KT.b])
                store_tm(knT, 4, NT, k_p[t0:t0 + NT, :], stg)
                for j in range(4):
                    bk = bank()
                    for kc in range(8):
                        mm(bk, xnT[:, kc, j * 128:(j + 1) * 128], W[:, kc, 1024:1536], kc == 0, kc == 7,
                           [W.b, xnT.b])
                    cp("act", stg[:, j, :], bk[:], [bk.b], [stg.b])
                    cp("dve", V[:, it * 4 + j, :], bk[:], [bk.b], [V.b])
                S.dma("sp", v_p[t0:t0 + NT, :].rearrange("(j p) f -> p j f", p=128), stg[:], reads=[stg.b])
            if 'S' in PH:
                for qk in range(2):
                    for m in range(4):
                        bk = bank()
                        for kc in range(8):
                            mm(bk, W[:, kc, qk * 512 + m * 128:qk * 512 + (m + 1) * 128], xnTs[:, kc, :], kc == 0,
                               kc == 7, [W.b, xnTs.b], out=bk[:, 0:TS])
                        act(sq[:, 0:TS], bk[:, 0:TS], AF.Square, [bk.b], [sq.b])
                        b2 = bank()
                        mm(b2, BD64[:], sq[:, 0:TS], True, True, [BD64.b, sq.b], out=b2[:, 0:TS])
                        act(rstd[:, 0:TS], b2[:, 0:TS], AF.Sqrt, [b2.b], [rstd.b], bias=EPS)
                        recip(rstd[:, 0:TS], rstd[:, 0:TS], [rstd.b], [rstd.b])
                        if qk == 0:
                            stt(QTs[:, m, :], bk[:, 0:TS], QG[:, 0:1], rstd[:, 0:TS], ALU.mult, ALU.mult,
                                [bk.b, QG.b, rstd.b], [QTs.b])
                        else:
                            stt(knT[:, m, 0:TS], bk[:, 0:TS], KG[:, 0:1], rstd[:, 0:TS], ALU.mult, ALU.mult,
                                [bk.b, KG.b, rstd.b], [knT.b])
                            cp("act", KTs[:, m, :], knT[:, m, 0:TS], [knT.b], [KTs.b])
                store_tm(knT, 4, TS, k_s, stg)
                VN = sb(ph, "VN", [4, NS, 4, 132], BF16)
                memset("dve", VN[:], 1.0, [VN.b])
                for sq_ in range(NS):
                    bk = bank()
                    for kc in range(8):
                        mm(bk, xnTs[:, kc, sq_ * 4:(sq_ + 1) * 4], W[:, kc, 1024:1536], kc == 0, kc == 7,
                           [W.b, xnTs.b], out=bk[0:4, :])
                    cp("act", stg[0:4, sq_ % 4, :], bk[0:4, :], [bk.b], [stg.b])
                    cp("dve", VN[:, sq_, :, 0:128], bk[0:4, :].rearrange("p (h d) -> p h d", d=128), [bk.b], [VN.b])
                    if sq_ % 4 == 3:
                        sg = sq_ // 4
                        S.dma("sp", v_s.rearrange("(s t) f -> t s f", t=4)[:, sg * 4:(sg + 1) * 4, :], stg[0:4, :, :],
                              reads=[stg.b])
                S.dma("sp", vn_s, VN[:, :, :, :].rearrange("p s h d -> p (s h d)"), reads=[VN.b])
            S.barrier()

        with ExitStack() as ph:
          if 'W' in PH and 'C' not in PH:
            convert_ffn_weights(ph)
            S.barrier()
          if 'C' in PH:
            if 'W' in PH:
                convert_ffn_weights(ph)
            PT = [sb(ph, "PT%d" % i, [128, NT], BF16) for i in range(6)]
            BIAS = sb(ph, "BIAS", [128, 4, 34], F32)
            for h in range(4):
                for bi in range(1, 34):
                    ts("dve", BIAS[:, h, bi:bi + 1], KPOS[:, 0:1], float(1 - 128 * bi), SLOPES[h], ALU.add, ALU.mult,
                       [KPOS.b], [BIAS.b])
            On = [sb(ph, "On%d" % i, [128, NT], F32) for i in range(2)]
            rden = [sb(ph, "rden%d" % i, [128, NT], F32) for i in range(2)]
            sq = sb(ph, "sqa", [128, NT], BF16); rstd = sb(ph, "rstda", [128, NT], F32)
            AO = [sb(ph, "AO%d" % i, [128, 4, NT], BF16) for i in range(2)]
            ptrr = [0]
            units = []
            for it in range(NTILES):
                for h in range(4):
                    nkb = (it * NT + NT) // 128
                    for kb in range(nkb):
                        units.append((it, h, kb, nkb))
            Ob = [PS[0], PS[1]]
            Db = [PS[2], PS[3]]

            def sbanks(ui):
                return [PS[4 + (ui % 2) * 2], PS[5 + (ui % 2) * 2]]

            def emit_qk(ui):
                it, h, kb, nkb = units[ui]
                q0 = it * NT
                k0 = kb * 128
                c0 = max(0, (k0 - q0) // 128) * 128
                Sb = sbanks(ui)
                for mp in range(2):
                    pr = slice(mp * 64, (mp + 1) * 64)
                    mm(Sb[mp], KT[pr, h, k0:k0 + 128], QT[pr, h, q0 + c0:q0 + NT], True, True,
                       [KT.b, QT.b], out=Sb[mp][:, c0:NT])

            def emit_rest(ui):
                it, h, kb, nkb = units[ui]
                q0 = it * NT
                k0 = kb * 128
                c0 = max(0, (k0 - q0) // 128) * 128
                slope = SLOPES[h]
                wq = 256 if slope * 511 > 64 else 512
                Sb = sbanks(ui)
                Ps = []
                for mp in range(2):
                    P = PT[ptrr[0] % len(PT)]; ptrr[0] += 1
                    Ps.append(P)
                    for g0 in range((c0 // wq) * wq, NT, wq):
                        lo = max(g0, c0)
                        hi = g0 + wq
                        bi = (q0 + hi - k0) // 128
                        act(P[:, lo:hi], Sb[mp][:, lo:hi], AF.Exp, [Sb[mp].b, BIAS.b], [P.b],
                            scale=0.125, bias=BIAS[:, h, bi:bi + 1])
                    if k0 >= q0:
                        tt("dve", P[:, c0:c0 + 128], P[:, c0:c0 + 128], CAUS[:], ALU.mult, [P.b, CAUS.b],
                           [P.b])
                for mp in range(2):
                    P = Ps[mp]
                    mm(Ob[mp], V[:, kb, h * 128:(h + 1) * 128], P[:, c0:NT], kb == 0, kb == nkb - 1,
                       [V.b, P.b], out=Ob[mp][:, c0:NT])
                    mm(Db[mp], ONE1[:], P[:, c0:NT], kb == 0, kb == nkb - 1, [ONE1.b, P.b],
                       out=Db[mp][:, c0:NT])

            def emit_epi(ui):
                it, h, kb, nkb = units[ui]
                for mp in range(2):
                    recip(rden[mp][:], Db[mp][:], [Db[mp].b], [rden[mp].b])
                    tt("dve", On[mp][:], Ob[mp][:], rden[mp][:], ALU.mult, [Ob[mp].b, rden[mp].b], [On[mp].b])
                stt(On[0][:], On[1][:], NLAM[:, 0:1], On[0][:], ALU.mult, ALU.add, [On[0].b, On[1].b, NLAM.b],
                    [On[0].b])
                act(sq[:], On[0][:], AF.Square, [On[0].b], [sq.b])
                b2 = sbanks(ui)[0]
                mm(b2, ON128[:], sq[:], True, True, [ON128.b, sq.b])
                act(rstd[:], b2[:], AF.Sqrt, [b2.b], [rstd.b], bias=EPS)
                recip(rstd[:], rstd[:], [rstd.b], [rstd.b])
                stt(AO[it % 2][:, h, :], On[0][:], SUBG[:, 0:1], rstd[:], ALU.mult, ALU.mult,
                    [On[0].b, SUBG.b, rstd.b], [AO[it % 2].b])
                if h == 3:
                    S.dma("sp", mix_s[it, :, 4:8, :], AO[it % 2][:], reads=[AO[it % 2].b])

            emit_qk(0)
            for ui in range(len(units)):
                if ui + 1 < len(units):
                    emit_qk(ui + 1)
                emit_rest(ui)
                if units[ui][2] == units[ui][3] - 1:
                    emit_epi(ui)
            S.barrier()
        att.close()


        with ExitStack() as ph:
          if 'S' in PH and 'CS' in PH:
            NKB = 8
            VN = sb(ph, "VNc", [4, NS, 4, 132], BF16)
            S.dma("sp", VN[:, :, :, :].rearrange("p s h d -> p (s h d)"), vn_s, writes=[VN.b])
            PTI = sb(ph, "PTI", [128, NS * 16], I32)
            IDX = sb(ph, "IDX", [128, NS * 16], U32)
            KPB = [sb(ph, "KPB%d" % i, [128, 512], F32) for i in range(NKB)]
            VPF = [sb(ph, "VPF%d" % i, [128, 512], F32) for i in range(NKB)]
            VPB = [sb(ph, "VPB%d" % i, [128, 4, 132], BF16) for i in range(32)]
            KpT = [sb(ph, "KpT%d" % i, [128, 4, 128], BF16) for i in range(2)]
            QB = sb(ph, "QB", [128, 4, NS, 2, 4], BF16)
            BIASS = sb(ph, "BIASS", [128, 16, 4, 8], F32)
            BIASN = sb(ph, "BIASN", [4, 4, 8], F32)
            TMPs = sb(ph, "TMPs", [128, 512], F32)
            PBs = [sb(ph, "PBs%d" % i, [128, 512], BF16) for i in range(2)]
            TNs = sb(ph, "TNs", [4, 32], F32); PNs = sb(ph, "PNs", [4, 32], BF16)
            RD = sb(ph, "RDs", [8, 4], F32)
            ONs = sb(ph, "ONs", [8, 4, 128], F32)
            CMB = sb(ph, "CMB", [8, 4], F32)
            OD4 = sb(ph, "OD4", [4, 512], F32)
            ODT = sb(ph, "ODT", [128, 4, TS], F32)
            sqs = sb(ph, "sqs", [128, TS], BF16); rstds = sb(ph, "rstds", [128, TS], F32)
            S.dma("sp", PTI[:], ptab.rearrange("s g -> (s g)").rearrange("(o n) -> o n", o=1).to_broadcast([128, NS * 16]),
                  writes=[PTI.b], **SLOW)
            ts("dve", IDX[:], PTI[:], 128.0, KPOS[:, 0:1], ALU.mult, ALU.add, [PTI.b, KPOS.b], [IDX.b])
            memset("dve", QB[:], 0.0, [QB.b])
            for mp in range(2):
                pr = slice(mp * 64, (mp + 1) * 64)
                cp("dve", QB[pr, :, :, mp, :], QTs[pr, :, :].rearrange("p h (s t) -> p h s t", t=4), [QTs.b], [QB.b])
            for pg in range(16):
                for h in range(4):
                    ts("dve", BIASS[:, pg, h, :], KPOS[:, 0:1].to_broadcast([128, 8]), float(pg * 128 - 2048), SLOPES[h],
                       ALU.add, ALU.mult, [KPOS.b], [BIASS.b])
            for h in range(4):
                ts("dve", BIASN[:, h, :], KPOS[0:4, 0:1].to_broadcast([4, 8]), SLOPES[h], None, ALU.mult, None, [KPOS.b],
                   [BIASN.b])
            for i in range(32):
                memset("dve", VPB[i][:], 1.0, [VPB[i].b])
            stt(CMB[:], IDF[0:8, 4:8], NLAM[0:8, 0:1], IDF[0:8, 0:4], ALU.mult, ALU.add, [IDF.b, NLAM.b], [CMB.b])
            ID4 = TB(IDF.t[0:4, 0:4], IDF.b)
            kcnt = 0
            ck_rows = cache_k
            cv_rows = cache_v
            for sq_ in range(NS):
                SBk = PS[sq_ % 2]
                PB_ = PBs[sq_ % 2]
                Ob = [PS[2], PS[3], PS[4], PS[5]]
                pend = []
                for pg in range(16):
                    Kp = KPB[kcnt % NKB]; Vf = VPF[kcnt % NKB]; Vp = VPB[kcnt % 32]; kcnt += 1
                    col = sq_ * 16 + pg
                    S.dmaf("pool", (lambda e, Kp=Kp, col=col: e.indirect_dma_start(
                        out=Kp[:, :], out_offset=None, in_=ck_rows,
                        in_offset=bass.IndirectOffsetOnAxis(ap=IDX[:, col:col + 1], axis=0))),
                        reads=[IDX.b], writes=[Kp.b])
                    S.dmaf("pool", (lambda e, Vf=Vf, col=col: e.indirect_dma_start(
                        out=Vf[:, :], out_offset=None, in_=cv_rows,
                        in_offset=bass.IndirectOffsetOnAxis(ap=IDX[:, col:col + 1], axis=0))),
                        reads=[IDX.b], writes=[Vf.b])
                    bkT = PS[6 + (pg % 2)]
                    KT_ = KpT[pg % 2]
                    for h in range(4):
                        tr(bkT, bkT[:, h * 128:(h + 1) * 128], Kp[:, h * 128:(h + 1) * 128], IDF, [Kp.b])
                    cp("act", KT_[:, :, :], bkT[:].rearrange("p (h n) -> p h n", n=128), [bkT.b], [KT_.b])
                    for h in range(4):
                        mm(SBk, KT_[:, h, :], QB[:, h, sq_, :, :].rearrange("p a b -> p (a b)"), True, True,
                           [KT_.b, QB.b], out=SBk[:, (pg * 4 + h) * 8:(pg * 4 + h + 1) * 8])
                    cp("dve", Vp[:, :, 0:128], Vf[:, :].rearrange("p (h d) -> p h d", d=128), [Vf.b], [Vp.b])
                    pend.append(Vp)
                NB = PS[6]
                for h in range(4):
                    mm(NB, KTs[:, h, sq_ * 4:(sq_ + 1) * 4], QB[:, h, sq_, :, :].rearrange("p a b -> p (a b)"), True, True,
                       [KTs.b, QB.b], out=NB[0:4, h * 8:(h + 1) * 8])
                stt(TMPs[:], SBk[:], 0.125, BIASS[:, :, :, :].rearrange("p a b c -> p (a b c)"), ALU.mult, ALU.add,
                    [SBk.b, BIASS.b], [TMPs.b])
                act(PB_[:], TMPs[:], AF.Exp, [TMPs.b], [PB_.b])
                stt(TNs[:], NB[0:4, 0:32], 0.125, BIASN[:, :, :].rearrange("p a b -> p (a b)"), ALU.mult, ALU.add,
                    [NB.b, BIASN.b], [TNs.b])
                act(TNs[:], TNs[:], AF.Exp, [TNs.b], [TNs.b])
                tt("dve", PNs[:, :].rearrange("p (a t) -> p a t", t=4), TNs[:, :].rearrange("p (a t) -> p a t", t=4),
                   CAUS[0:4, 0:4].unsqueeze(1).to_broadcast([4, 8, 4]), ALU.mult, [TNs.b, CAUS.b], [PNs.b])
                for pg in range(16):
                    Vp = pend[pg]
                    for h in range(4):
                        mm(Ob[h], PB_[:, (pg * 4 + h) * 8:(pg * 4 + h + 1) * 8], Vp[:, h, 0:129], pg == 0, False,
                           [PB_.b, Vp.b], out=Ob[h][0:8, 0:129])
                for h in range(4):
                    mm(Ob[h], PNs[0:4, h * 8:(h + 1) * 8], VN[0:4, sq_, h, 0:129], False, True, [PNs.b, VN.b],
                       out=Ob[h][0:8, 0:129])
                for h in range(4):
                    recip(RD[:, h:h + 1], Ob[h][0:8, 128:129], [Ob[h].b], [RD.b])
                    ts("dve", ONs[:, h, :], Ob[h][0:8, 0:128], RD[:, h:h + 1], None, ALU.mult, None, [Ob[h].b, RD.b],
                       [ONs.b])
                DBk = PS[7]
                mm(DBk, CMB[:, :], ONs[:, :, :].rearrange("p h d -> p (h d)"), True, True, [CMB.b, ONs.b],
                   out=DBk[0:4, :])
                cp("act", OD4[:], DBk[0:4, :], [DBk.b], [OD4.b])
                TBk = PS[6]
                for h in range(4):
                    tr(TBk, TBk[:, h * 4:(h + 1) * 4], OD4[0:4, h * 128:(h + 1) * 128], ID4, [OD4.b])
                cp("act", ODT[:, :, sq_ * 4:(sq_ + 1) * 4], TBk[:, 0:16].rearrange("p (h t) -> p h t", t=4), [TBk.b],
                   [ODT.b])
            for h in range(4):
                act(sqs[:], ODT[:, h, :], AF.Square, [ODT.b], [sqs.b])
                b2 = PS[7]
                mm(b2, ON128[:], sqs[:], True, True, [ON128.b, sqs.b], out=b2[:, 0:TS])
                act(rstds[:], b2[:, 0:TS], AF.Sqrt, [b2.b], [rstds.b], bias=EPS)
                recip(rstds[:], rstds[:], [rstds.b], [rstds.b])
                stt(MIXs[:, 4 + h, :], ODT[:, h, :], SUBG[:, 0:1], rstds[:], ALU.mult, ALU.mult,
                    [ODT.b, SUBG.b, rstds.b], [MIXs.b])
            S.barrier()

        with ExitStack() as ph:
          if 'D' in PH:
            XIN = sb(ph, "XIN", [128, 4, 1024], F32)
            xTs_ = [sb(ph, "xT%d" % i, [128, 8, NT], F32) for i in range(2)]
            MIX = sb(ph, "MIX", [128, 8, NT], BF16)
            A = sb(ph, "actA", [128, 8, NT], BF16)
            B3 = [sb(ph, "actB%d" % i, [128, 8, NT], BF16) for i in range(2)]
            Wo = sb(ph, "Wo", [128, 8, 1024], BF16); Wq = sb(ph, "Wq", [128, 8, 1024], BF16)
            Wo2 = sb(ph, "Wo2", [128, 8, 1024], BF16)
            sq2 = [sb(ph, "sqd%d" % i, [128, NT], BF16) for i in range(2)]; rstd = sb(ph, "rstdd", [128, NT], F32)
            Pm = [sb(ph, "Pm%d" % i, [128, NT], BF16) for i in range(2)]
            hT = sb(ph, "hT", [128, NFC, NT], BF16)
            HGs_ = [sb(ph, "HG%d" % i, [128, NT + 2], F32) for i in range(2)]; CARRY = sb(ph, "CARRY", [128, NFC, 2], F32)
            cvs_ = [sb(ph, "cv%d" % i, [128, NT], F32) for i in range(2)]
            rden = sb(ph, "rdend", [128, NT], F32)
            WG = [sb(ph, "WG%d" % i, [128, 8, 128], BF16) for i in range(2)]
            WV = [sb(ph, "WV%d" % i, [128, 8, 128], BF16) for i in range(2)]
            WD = [sb(ph, "WD%d" % i, [128, NFC, 128], BF16) for i in range(2)]
            YST = sb(ph, "YST", [128, 1024], F32)
            S.dma("pool", Wo[:], w_out.rearrange("(c p) n -> p c n", p=128), writes=[Wo.b])
            S.dma("pool", Wq[:], ca_wq.rearrange("(c p) n -> p c n", p=128), writes=[Wq.b])
            S.dma("pool", Wo2[:], ca_wo.rearrange("(c p) n -> p c n", p=128), writes=[Wo2.b])
            memset("dve", CARRY[:], 0.0, [CARRY.b])
            if 'B' not in PH or os.environ.get('KNOB'):
                memset("dve", hT[:, 0:4, :], 0.0, [hT.b])
                for it in range(NTILES):
                    S.dma("sp", mix_s[it, :, 0:4, :], hT[:, 0:4, :], reads=[hT.b])
                S.barrier()
            frr = [0]; grr = [0]

            def fbank():
                b = PS[frr[0] % 4]; frr[0] += 1
                return b

            def gbank():
                b = PS[4 + grr[0] % 4]; grr[0] += 1
                return b

            def d_loads(it):
                t0 = it * NT
                S.dma("pool", MIX[:], mix_s[it], writes=[MIX.b])
                S.dma("pool", XIN[:], xp[t0:t0 + NT, :].rearrange("(j p) f -> p j f", p=128), writes=[XIN.b])

            def ln_gen(xT, G, out):
                bk = fbank()
                for c in range(8):
                    act(sq2[c % 2][:], xT[:, c, :], AF.Square, [xT.b], [sq2[c % 2].b])
                    if c >= 1:
                        mm(bk, ON1024[:], sq2[(c - 1) % 2][:], c == 1, False, [ON1024.b, sq2[(c - 1) % 2].b])
                    yield
                mm(bk, ON1024[:], sq2[1][:], False, True, [ON1024.b, sq2[1].b])
                act(rstd[:], bk[:], AF.Sqrt, [bk.b], [rstd.b], bias=EPS)
                recip(rstd[:], rstd[:], [rstd.b], [rstd.b])
                yield
                for c in range(8):
                    stt(out[:, c, :], xT[:, c, :], G[:, c:c + 1], rstd[:], ALU.mult, ALU.mult,
                        [xT.b, G.b, rstd.b], [out.b])
                    if c % 4 == 3:
                        yield

            def front(it):
                xT = xTs_[it % 2]; Bb = B3[it % 2]
                for c in range(8):
                    bk = fbank()
                    for j in range(4):
                        tr(bk, bk[:, j * 128:(j + 1) * 128], XIN[:, j, c * 128:(c + 1) * 128], IDF, [XIN.b])
                    cp("act", xT[:, c, :], bk[:], [bk.b], [xT.b])
                    if c % 4 == 3:
                        yield
                for m in range(8):
                    bk = fbank()
                    for kc in range(8):
                        mm(bk, Wo[:, kc, m * 128:(m + 1) * 128], MIX[:, kc, :], kc == 0, kc == 7, [Wo.b, MIX.b])
                    tt("dve", xT[:, m, :], bk[:], xT[:, m, :], ALU.add, [bk.b, xT.b], [xT.b])
                    if m % 2 == 1:
                        yield
                yield from ln_gen(xT, G2, A)
                for h in range(4):
                    bq = [fbank(), fbank()]
                    for dh in range(2):
                        for kc in range(8):
                            mm(bq[dh], Wq[:, kc, (2 * h + dh) * 128:(2 * h + dh + 1) * 128], A[:, kc, :], kc == 0,
                               kc == 7, [Wq.b, A.b])
                        act(sq2[dh][:], bq[dh][:], AF.Square, [bq[dh].b], [sq2[dh].b])
                    yield
                    b2 = fbank()
                    for dh in range(2):
                        mm(b2, ON256[:], sq2[dh][:], dh == 0, dh == 1, [ON256.b, sq2[dh].b])
                    act(rstd[:], b2[:], AF.Sqrt, [b2.b], [rstd.b], bias=EPS)
                    recip(rstd[:], rstd[:], [rstd.b], [rstd.b])
                    for dh in range(2):
                        stt(Bb[:, 2 * h + dh, :], bq[dh][:], CQG[:, dh:dh + 1], rstd[:], ALU.mult, ALU.mult,
                            [bq[dh].b, CQG.b, rstd.b], [Bb.b])
                    yield
                for h in range(4):
                    for mb in range(2):
                        bs = fbank()
                        for dh in range(2):
                            mm(bs, MKT[:, 2 * h + dh, mb * 128:(mb + 1) * 128], Bb[:, 2 * h + dh, :], dh == 0, dh == 1,
                               [MKT.b, Bb.b])
                        act(Pm[mb][:], bs[:], AF.Exp, [bs.b], [Pm[mb].b], scale=1.0 / 16)
                    yield
                    bd = fbank()
                    for mb in range(2):
                        mm(bd, ONE1[:], Pm[mb][:], mb == 0, mb == 1, [ONE1.b, Pm[mb].b])
                    recip(rden[:], bd[:], [bd.b], [rden.b])
                    for dvc in range(2):
                        bo = fbank()
                        for mb in range(2):
                            mm(bo, MV[:, mb, (2 * h + dvc) * 128:(2 * h + dvc + 1) * 128], Pm[mb][:], mb == 0, mb == 1,
                               [MV.b, Pm[mb].b])
                        tt("dve", A[:, 2 * h + dvc, :], bo[:], rden[:], ALU.mult, [bo.b, rden.b], [A.b])
                    yield
                for m in range(8):
                    bk = fbank()
                    for kc in range(8):
                        mm(bk, Wo2[:, kc, m * 128:(m + 1) * 128], A[:, kc, :], kc == 0, kc == 7, [Wo2.b, A.b])
                    tt("dve", xT[:, m, :], bk[:], xT[:, m, :], ALU.add, [bk.b, xT.b], [xT.b])
                    if m % 2 == 1:
                        yield
                yield from ln_gen(xT, G3, Bb)

            def ffn(it):
                t0 = it * NT
                xT = xTs_[it % 2]; Bb = B3[it % 2]
                for fc in range(NFC):
                    wg = WG[fc % 2]; wv = WV[fc % 2]; HG = HGs_[fc % 2]; cv = cvs_[fc % 2]
                    S.dma("sp", wg[:], wg_s[:, fc], writes=[wg.b])
                    S.dma("sp", wv[:], wv_s[:, fc], writes=[wv.b])
                    if fc == NFC - 4:
                        for m in range(2):
                            S.dma("sp", WD[m][:], wd_s[:, m], writes=[WD[m].b])
                    bg = gbank(); bv = gbank()
                    for kc in range(8):
                        mm(bg, wg[:, kc, :], Bb[:, kc, :], kc == 0, kc == 7, [wg.b, Bb.b])
                    for kc in range(8):
                        mm(bv, wv[:, kc, :], Bb[:, kc, :], kc == 0, kc == 7, [wv.b, Bb.b])
                    cp("dve", HG[:, 0:2], CARRY[:, fc, :], [CARRY.b], [HG.b])
                    cp("act", HG[:, 2:NT + 2], bg[:], [bg.b], [HG.b])
                    cp("dve", CARRY[:, fc, :], HG[:, NT:NT + 2], [HG.b], [CARRY.b])
                    act(cv[:], HG[:, 2:NT + 2], AF.Identity, [HG.b, CW.b, CB.b], [cv.b], scale=CW[:, 2, fc:fc + 1],
                        bias=CB[:, fc:fc + 1])
                    stt(cv[:], HG[:, 1:NT + 1], CW[:, 1, fc:fc + 1], cv[:], ALU.mult, ALU.add, [HG.b, CW.b, cv.b],
                        [cv.b])
                    stt(cv[:], HG[:, 0:NT], CW[:, 0, fc:fc + 1], cv[:], ALU.mult, ALU.add, [HG.b, CW.b, cv.b], [cv.b])
                    act(cv[:], cv[:], AF.Silu, [cv.b], [cv.b])
                    tt("dve", hT[:, fc, :], cv[:], bv[:], ALU.mult, [cv.b, bv.b], [hT.b])
                    yield
                for m in range(8):
                    wd = WD[m % 2]
                    if m >= 2:
                        S.dma("sp", wd[:], wd_s[:, m], writes=[wd.b])
                    bk = gbank()
                    for fc in range(NFC):
                        mm(bk, wd[:, fc, :], hT[:, fc, :], fc == 0, fc == NFC - 1, [wd.b, hT.b])
                    tt("dve", xT[:, m, :], bk[:], xT[:, m, :], ALU.add, [bk.b, xT.b], [xT.b])
                    yield
                for j in range(4):
                    for c0 in (0, 4):
                        bk = gbank()
                        for c in range(c0, c0 + 4):
                            tr(bk, bk[:, (c - c0) * 128:(c - c0 + 1) * 128], xT[:, c, j * 128:(j + 1) * 128], IDF,
                               [xT.b])
                        cp("act", YST[:, c0 * 128:(c0 + 4) * 128], bk[:], [bk.b], [YST.b])
                    S.dma("pool", y_p[t0 + j * 128:t0 + (j + 1) * 128, :], YST[:], reads=[YST.b])
                    yield

            def drain(g, n):
                for _ in range(n):
                    try:
                        next(g)
                    except StopIteration:
                        return False
                return True

            d_loads(0)
            drain(front(0), 10 ** 6)
            for it in range(NTILES):
                nxt = None
                if it + 1 < NTILES:
                    d_loads(it + 1)
                    nxt = front(it + 1)
                for _ in ffn(it):
                    if nxt is not None:
                        drain(nxt, 1)
                if nxt is not None:
                    drain(nxt, 10 ** 6)
            for j in range(2):
                S.dma("sp", conv_p[j].rearrange("(c p) -> p c", p=128), CARRY[:, :, j], reads=[CARRY.b], **SLOW)
            S.barrier()

        with ExitStack() as ph:
          if 'S' in PH and 'DS' in PH:
            X = sb(ph, "Xds", [128, 1, 1024], F32)
            xT = sb(ph, "xTs", [128, 8, TS], F32)
            A = sb(ph, "actAs", [128, 8, TS], BF16); Bb = sb(ph, "actBs", [128, 8, TS], BF16)
            Wo = sb(ph, "Wos", [128, 8, 1024], BF16); Wq = sb(ph, "Wqs", [128, 8, 1024], BF16)
            Wo2 = sb(ph, "Wo2s", [128, 8, 1024], BF16)
            sq = sb(ph, "sqds", [128, 2, TS], BF16); rstd = sb(ph, "rstdds", [128, TS], F32)
            hT = sb(ph, "hTs", [128, NFC, TS], BF16)
            WG = [sb(ph, "WGs%d" % i, [128, 8, 128], BF16) for i in range(2)]
            WV = [sb(ph, "WVs%d" % i, [128, 8, 128], BF16) for i in range(2)]
            WD = [sb(ph, "WDs%d" % i, [128, NFC, 128], BF16) for i in range(2)]
            S.dma("pool", Wo[:], w_out.rearrange("(c p) n -> p c n", p=128), writes=[Wo.b])
            S.dma("pool", Wq[:], ca_wq.rearrange("(c p) n -> p c n", p=128), writes=[Wq.b])
            S.dma("pool", Wo2[:], ca_wo.rearrange("(c p) n -> p c n", p=128), writes=[Wo2.b])
            wrr = 0
            n = TS
            CMKts = [sb(ph, "CMKt%d" % i, [128, 2, 1024], F32) for i in range(2)]
            CMVb = [sb(ph, "CMVb%d" % i, [128, 2, 4, 260], BF16) for i in range(2)]
            MKs = sb(ph, "MKs", [128, 8, 256], BF16)
            PCs = sb(ph, "PCs", [128, 32], BF16)
            RDc = sb(ph, "RDc", [4, 4], F32)
            COn = sb(ph, "COn", [4, 4, 256], F32)
            SCt = sb(ph, "SCt", [32, F], F32)
            SCV = sb(ph, "SCV", [128, NFC, 32], F32)
            CVS = sb(ph, "CVS", [128, NFC, 32], F32)
            HGs = sb(ph, "HGs", [128, NS, 6], F32)
            cvs = sb(ph, "cvs", [128, NS, 4], F32)
            ID4 = TB(IDF.t[0:4, 0:4], IDF.b); ID32 = TB(IDF.t[0:32, 0:32], IDF.b)
            for i in range(2):
                memset("dve", CMVb[i][:], 1.0, [CMVb[i].b])
            S.dma("sp", SCt[:], st_conv, writes=[SCt.b])
            for f0 in range(0, NFC, 16):
                bk = bank()
                for fc in range(f0, min(f0 + 16, NFC)):
                    tr(bk, bk[:, (fc - f0) * 32:(fc - f0 + 1) * 32], SCt[0:32, fc * 128:(fc + 1) * 128], ID32, [SCt.b])
                nf = min(f0 + 16, NFC) - f0
                cp("act", SCV[:, f0:f0 + nf, :], bk[:, 0:nf * 32].rearrange("p (f x) -> p f x", x=32), [bk.b], [SCV.b])
            for c in range(8):
                cp("dve", xT[:, c, 0:n], xsT[:, c, :], [xsT.b], [xT.b])
            for m in range(8):
                bk = bank()
                for kc in range(8):
                    mm(bk, Wo[:, kc, m * 128:(m + 1) * 128], MIXs[:, kc, :], kc == 0, kc == 7, [Wo.b, MIXs.b],
                       out=bk[:, 0:n])
                tt("dve", xT[:, m, 0:n], bk[:, 0:n], xT[:, m, 0:n], ALU.add, [bk.b, xT.b], [xT.b])
            ln_fm(xT, G2, A, n, sq, rstd)
            for h in range(4):
                bq = [bank(), bank()]
                for dh in range(2):
                    for kc in range(8):
                        mm(bq[dh], Wq[:, kc, (2 * h + dh) * 128:(2 * h + dh + 1) * 128], A[:, kc, 0:n], kc == 0,
                           kc == 7, [Wq.b, A.b], out=bq[dh][:, 0:n])
                b2 = bank()
                for dh in range(2):
                    act(sq[:, dh, 0:n], bq[dh][:, 0:n], AF.Square, [bq[dh].b], [sq.b])
                    mm(b2, ON256[:], sq[:, dh, 0:n], dh == 0, dh == 1, [ON256.b, sq.b], out=b2[:, 0:n])
                act(rstd[:, 0:n], b2[:, 0:n], AF.Sqrt, [b2.b], [rstd.b], bias=EPS)
                recip(rstd[:, 0:n], rstd[:, 0:n], [rstd.b], [rstd.b])
                for dh in range(2):
                    stt(Bb[:, 2 * h + dh, 0:n], bq[dh][:, 0:n], CQG[:, dh:dh + 1], rstd[:, 0:n], ALU.mult, ALU.mult,
                        [bq[dh].b, CQG.b, rstd.b], [Bb.b])
            for sq_ in range(NS):
                CMV_ = CMVb[sq_ % 2]; CMKt = CMKts[sq_ % 2]
                S.dma("sp", CMKt[:], cmk[sq_].rearrange("(mb p) f -> p mb f", p=128), writes=[CMKt.b])
                for mb in range(2):
                    S.dma("pool", CMV_[:, mb, :, 0:256], cmv[sq_, mb * 128:(mb + 1) * 128, :].rearrange(
                        "p (h d) -> p h d", d=256), writes=[CMV_.b])
                for mb in range(2):
                    for c0 in (0, 4):
                        bk = bank()
                        for c8 in range(c0, c0 + 4):
                            tr(bk, bk[:, (c8 - c0) * 128:(c8 - c0 + 1) * 128], CMKt[:, mb, c8 * 128:(c8 + 1) * 128], IDF,
                               [CMKt.b])
                        cp("act" if c0 == 0 else "dve", MKs[:, c0:c0 + 4, mb * 128:(mb + 1) * 128],
                           bk[:].rearrange("p (c n) -> p c n", n=128), [bk.b], [MKs.b])
                SBc = bank()
                for mb in range(2):
                    for h in range(4):
                        for dh in range(2):
                            mm(SBc, MKs[:, 2 * h + dh, mb * 128:(mb + 1) * 128], Bb[:, 2 * h + dh, sq_ * 4:(sq_ + 1) * 4],
                               dh == 0, dh == 1, [MKs.b, Bb.b], out=SBc[:, (mb * 4 + h) * 4:(mb * 4 + h + 1) * 4])
                act(PCs[:], SBc[:, 0:32], AF.Exp, [SBc.b], [PCs.b], scale=1.0 / 16)
                Oc = [bank(), bank(), bank(), bank()]
                for h in range(4):
                    for mb in range(2):
                        mm(Oc[h], PCs[:, (mb * 4 + h) * 4:(mb * 4 + h + 1) * 4], CMV_[:, mb, h, 0:257], mb == 0, mb == 1,
                           [PCs.b, CMV_.b], out=Oc[h][0:4, 0:257])
                for h in range(4):
                    recip(RDc[:, h:h + 1], Oc[h][0:4, 256:257], [Oc[h].b], [RDc.b])
                    ts("dve", COn[:, h, :], Oc[h][0:4, 0:256], RDc[:, h:h + 1], None, ALU.mult, None,
                       [Oc[h].b, RDc.b], [COn.b])
                TBk = bank()
                cof = COn[:, :, :].rearrange("p h d -> p (h d)")
                for c8 in range(8):
                    tr(TBk, TBk[:, c8 * 4:(c8 + 1) * 4], cof[0:4, c8 * 128:(c8 + 1) * 128], ID4, [COn.b])
                cp("act", A[:, :, sq_ * 4:(sq_ + 1) * 4], TBk[:, 0:32].rearrange("p (c t) -> p c t", t=4), [TBk.b],
                   [A.b])
            for m in range(8):
                bk = bank()
                for kc in range(8):
                    mm(bk, Wo2[:, kc, m * 128:(m + 1) * 128], A[:, kc, 0:n], kc == 0, kc == 7, [Wo2.b, A.b],
                       out=bk[:, 0:n])
                tt("dve", xT[:, m, 0:n], bk[:, 0:n], xT[:, m, 0:n], ALU.add, [bk.b, xT.b], [xT.b])
            ln_fm(xT, G3, Bb, n, sq, rstd)
            for fc in range(NFC):
                wg = WG[wrr % 2]; wv = WV[wrr % 2]; wrr += 1
                S.dma("sp", wg[:], wg_s[:, fc],
                      writes=[wg.b])
                S.dma("sp", wv[:], wv_s[:, fc],
                      writes=[wv.b])
                bg = bank(); bv = bank()
                for kc in range(8):
                    mm(bg, wg[:, kc, :], Bb[:, kc, 0:n], kc == 0, kc == 7, [wg.b, Bb.b], out=bg[:, 0:n])
                for kc in range(8):
                    mm(bv, wv[:, kc, :], Bb[:, kc, 0:n], kc == 0, kc == 7, [wv.b, Bb.b], out=bv[:, 0:n])
                cp("dve", HGs[:, :, 0:2], SCV[:, fc, :].rearrange("p (s j) -> p s j", j=2), [SCV.b], [HGs.b])
                cp("act", HGs[:, :, 2:6], bg[:, 0:n].rearrange("p (s t) -> p s t", t=4), [bg.b], [HGs.b])
                cp("dve", CVS[:, fc, :].rearrange("p (s j) -> p s j", j=2), HGs[:, :, 4:6], [HGs.b], [CVS.b])
                act(cvs[:], HGs[:, :, 2:6], AF.Identity, [HGs.b, CW.b, CB.b], [cvs.b], scale=CW[:, 2, fc:fc + 1],
                    bias=CB[:, fc:fc + 1])
                stt(cvs[:], HGs[:, :, 1:5], CW[:, 1, fc:fc + 1], cvs[:], ALU.mult, ALU.add, [HGs.b, CW.b, cvs.b],
                    [cvs.b])
                stt(cvs[:], HGs[:, :, 0:4], CW[:, 0, fc:fc + 1], cvs[:], ALU.mult, ALU.add, [HGs.b, CW.b, cvs.b],
                    [cvs.b])
                act(cvs[:], cvs[:], AF.Silu, [cvs.b], [cvs.b])
                tt("dve", hT[:, fc, 0:n], cvs[:, :, :].rearrange("p s t -> p (s t)"), bv[:, 0:n], ALU.mult,
                   [cvs.b, bv.b], [hT.b])
            for m in range(8):
                wd = WD[m % 2]
                S.dma("sp", wd[:], wd_s[:, m],
                      writes=[wd.b])
                bk = bank()
                for fc in range(NFC):
                    mm(bk, wd[:, fc, :], hT[:, fc, 0:n], fc == 0, fc == NFC - 1, [wd.b, hT.b], out=bk[:, 0:n])
                tt("dve", xT[:, m, 0:n], bk[:, 0:n], xT[:, m, 0:n], ALU.add, [bk.b, xT.b], [xT.b])
            store_tm(xT, 8, n, y_s, X)
            for f0 in range(0, NFC, 4):
                bk = bank()
                nf = min(f0 + 4, NFC) - f0
                for fc in range(f0, f0 + nf):
                    tr(bk, bk[0:32, (fc - f0) * 128:(fc - f0 + 1) * 128], CVS[:, fc, :], IDF, [CVS.b])
                cp("act", SCt[0:32, f0 * 128:(f0 + nf) * 128], bk[0:32, 0:nf * 128], [bk.b], [SCt.b])
            S.dma("sp", conv_s, SCt[:], reads=[SCt.b])

            S.barrier()

        S.barrier()
        S.emit()
    return nc


_NC_CACHE = {}


def _consts():
    ident = np.eye(128, dtype=np.float32)
    bd64 = np.zeros((128, 128), np.float32); bd64[:64, :64] = 1 / 64; bd64[64:, 64:] = 1 / 64
    m16 = np.kron(np.eye(8, dtype=np.float32), np.ones((16, 16), np.float32))
    caus = np.triu(np.ones((128, 128), np.float32))
    pwn = np.array(PW_N, np.float32)
    kpos = np.arange(128, dtype=np.float32)
    g2m = np.zeros((128, 2), np.float32)
    for p in range(128):
        g2m[p, (p // 16) % 2] = 1.0
    return dict(c_ident=ident, c_bd64=bd64, c_mask16=m16, c_caus=caus, c_pwn=pwn, c_kpos=kpos, c_g2m=g2m)


WNAMES = ["ln1_g", "ln2_g", "ln3_g", "mem_norm_g", "w_in", "ssm_a_re", "ssm_a_im", "ssm_b_re", "ssm_b_im",
          "ssm_c_re", "ssm_c_im", "ssm_d", "ssm_log_dt", "ssm_glu_w", "q_norm_g", "k_norm_g", "lam_q1", "lam_k1",
          "lam_q2", "lam_k2", "subln_g", "w_out", "ca_wq", "ca_wk", "ca_wv", "ca_q_norm_g", "ca_k_norm_g", "ca_wo",
          "ffn_wg", "ffn_wv", "ffn_wd", "ffn_conv_w", "ffn_conv_b"]


def make_in_maps(inp, cores):
    cst = _consts()
    shared = {n: np.ascontiguousarray(np.asarray(inp[n])[0]) for n in WNAMES}
    maps = []
    for c in cores:
        b = c % 4
        m = dict(shared)
        m.update(cst)
        m["xp"] = np.ascontiguousarray(np.asarray(inp["x_prompt"])[b])
        m["memp"] = np.ascontiguousarray(np.asarray(inp["mem_prompt"])[b])
        sl = slice(c * NS, (c + 1) * NS)
        m["xs"] = np.ascontiguousarray(np.asarray(inp["x_sample"])[sl]).reshape(TS, 1024)
        m["st_re"] = np.ascontiguousarray(np.asarray(inp["state_ssm_re"])[0, sl]).reshape(NS, 2048)
        m["st_im"] = np.ascontiguousarray(np.asarray(inp["state_ssm_im"])[0, sl]).reshape(NS, 2048)
        m["st_conv"] = np.ascontiguousarray(np.asarray(inp["state_conv"])[0, sl]).reshape(NS * 2, F)
        m["cmk"] = np.ascontiguousarray(np.asarray(inp["cache_mem_k"])[0, sl]).reshape(NS, 256, 1024)
        m["cmv"] = np.ascontiguousarray(np.asarray(inp["cache_mem_v"])[0, sl]).reshape(NS, 256, 1024)
        m["cache_k"] = np.asarray(inp["cache_k"]).reshape(2560 * 128, 512)
        m["cache_v"] = np.asarray(inp["cache_v"]).reshape(2560 * 128, 512)
        m["ptab"] = np.ascontiguousarray(np.asarray(inp["page_table"])[sl]).astype(np.int32)
        maps.append(m)
    return maps


def kernel(**inp):
    nc = build_nc()
    cores = list(range(NCORES))
    maps = make_in_maps(inp, cores)
    res = run_bass_kernel_spmd(nc, maps, core_ids=cores)
    R = res.results
    f32 = np.float32
    y_prompt = np.stack([R[b]["y_p"] for b in range(4)]).astype(f32)
    k_prompt = np.stack([R[b]["k_p"] for b in range(4)]).reshape(1, 4, T, 4, 2, 64).astype(f32)
    v_prompt = np.stack([R[b]["v_p"] for b in range(4)]).reshape(1, 4, T, 4, 128).astype(f32)
    hre = np.stack([R[b]["hre_p"] for b in range(4)]).reshape(1, 4, 32, 64).astype(f32)
    him = np.stack([R[b]["him_p"] for b in range(4)]).reshape(1, 4, 32, 64).astype(f32)
    conv_prompt = np.stack([R[b]["conv_p"] for b in range(4)]).reshape(1, 4, 2, F).astype(f32)
    mk = np.stack([R[b]["mk_p"] for b in range(4)]).reshape(1, 4, 256, 4, 256).astype(f32)
    mv = np.stack([R[b]["mv_p"] for b in range(4)]).reshape(1, 4, 256, 4, 256).astype(f32)
    y_sample = np.concatenate([R[c]["y_s"] for c in range(NCORES)]).reshape(128, 4, 1024).astype(f32)
    k_sample = np.concatenate([R[c]["k_s"] for c in range(NCORES)]).reshape(1, 128, 4, 4, 2, 64).astype(f32)
    v_sample = np.concatenate([R[c]["v_s"] for c in range(NCORES)]).reshape(1, 128, 4, 4, 128).astype(f32)
    hre_s = np.concatenate([R[c]["hre_s"] for c in range(NCORES)]).reshape(1, 128, 32, 64).astype(f32)
    him_s = np.concatenate([R[c]["him_s"] for c in range(NCORES)]).reshape(1, 128, 32, 64).astype(f32)
    conv_s = np.concatenate([R[c]["conv_s"] for c in range(NCORES)]).reshape(1, 128, 2, F).astype(f32)
    return (y_prompt, y_sample, k_prompt, v_prompt, k_sample, v_sample, hre, him, hre_s, him_s,
            conv_prompt, conv_s, mk, mv)
```

```python
import math
import os
PH = set(os.environ.get('KPH', 'W,M,A2,C,B,D,S,CS,DS').split(','))
import numpy as np
import ml_dtypes
import concourse.bass as bass
import concourse.mybir as mybir
from concourse.bass_utils import run_bass_kernel_spmd
from contextlib import ExitStack

F32 = mybir.dt.float32
BF16 = mybir.dt.bfloat16
I32 = mybir.dt.int32
U32 = mybir.dt.uint32
ALU = mybir.AluOpType
AF = mybir.ActivationFunctionType

ENGS = ("pe", "act", "dve", "pool", "sp")
N_DMA_SEMS = 24
EPS = 1e-6
NCORES = 8
T = 4096
NT = 512
NTILES = T // NT
L = 8
NCH = T // L
SC = 16
NSC = NCH // SC
F = 2816
NFC = F // 128
SLOPES = [2.0 ** (-8.0 * (h + 1) / 4) for h in range(4)]
LAM0 = 0.8 - 0.6 * math.exp(-0.3 * 0)
NS = 16
TS = 64
NPW = 25
PW_N = list(range(9)) + [8 * k for k in range(2, 17)] + [4]


class Buf:
    __slots__ = ("name", "last_w", "readers", "excl")

    def __init__(self, name):
        self.name = name
        self.excl = False
        self.last_w = None
        self.readers = []


class Sched:
    def __init__(self, nc, es):
        self.nc = nc
        self.ops = {e: [] for e in ENGS}
        self.count = {e: 0 for e in ENGS}
        self.sems = {e: es.enter_context(nc.semaphore("s_" + e)) for e in ENGS}
        self.dsems = [es.enter_context(nc.semaphore("d%d" % i)) for i in range(N_DMA_SEMS)]
        self.dcnt = [0] * N_DMA_SEMS
        self.drr = 0
        self.waited = {e: {} for e in ENGS}
        self.bufs = []

    def buf(self, name):
        b = Buf(name)
        self.bufs.append(b)
        return b

    def _collect(self, eng, reads, writes, is_dma):
        toks = []
        for b in reads:
            if b.last_w is not None:
                toks.append(b.last_w)
        for b in writes:
            if b.last_w is not None:
                toks.append(b.last_w)
            toks.extend(b.readers)
        need = {}
        for (k, v) in toks:
            if (not is_dma) and eng == "pe" and k == "pe":
                continue
            if need.get(k, -1) < v:
                need[k] = v
        waits = []
        w = self.waited[eng]
        for k, v in need.items():
            if w.get(k, -1) >= v:
                continue
            w[k] = v
            waits.append((k, v))
        return waits

    def _commit(self, tok, reads, writes):
        for b in reads:
            if b.excl:
                b.last_w = tok
                b.readers = []
            else:
                b.readers.append(tok)
        for b in writes:
            b.last_w = tok
            b.readers = []

    def op(self, eng, fn, reads=(), writes=()):
        waits = self._collect(eng, reads, writes, False)
        self.count[eng] += 1
        tok = (eng, self.count[eng])
        self.ops[eng].append((waits, fn, None))
        self._commit(tok, reads, writes)
        return tok

    def dmaf(self, eng, fn, reads=(), writes=()):
        waits = self._collect(eng, reads, writes, True)
        i = self.drr
        self.drr = (self.drr + 1) % N_DMA_SEMS
        k = ("d", i)
        prev = self.dcnt[i]
        w = self.waited[eng]
        if prev > 0 and w.get(k, -1) < prev:
            w[k] = prev
            waits.append((k, prev))
        self.dcnt[i] += 16
        tok = (k, self.dcnt[i])
        self.ops[eng].append((waits, fn, (i, 16)))
        self._commit(tok, reads, writes)
        return tok

    def dma(self, eng, out, in_, reads=(), writes=(), **kw):
        def fn(e, out=out, in_=in_, kw=kw):
            return e.dma_start(out=out, in_=in_, **kw)
        return self.dmaf(eng, fn, reads, writes)

    def barrier(self):
        targets = [(e, self.count[e]) for e in ENGS if self.count[e] > 0]
        targets += [(("d", i), c) for i, c in enumerate(self.dcnt) if c > 0]
        for e in ENGS:
            w = self.waited[e]
            waits = []
            for k, v in targets:
                if w.get(k, -1) < v:
                    w[k] = v
                    waits.append((k, v))
            if waits:
                self.ops[e].append((waits, None, None))
        for b in self.bufs:
            b.last_w = None
            b.readers = []

    def _sem(self, k):
        if isinstance(k, tuple):
            return self.dsems[k[1]]
        return self.sems[k]

    def emit(self):
        nc = self.nc
        with nc.Block() as block:
            def mk(ename):
                def body(e):
                    own = self.sems[ename]
                    for waits, fn, dinc in self.ops[ename]:
                        for (k, v) in waits:
                            e.wait_ge(self._sem(k), v)
                        if fn is None:
                            continue
                        ins = fn(e)
                        if dinc is not None:
                            ins.then_inc(self.dsems[dinc[0]], dinc[1])
                        else:
                            ins.then_inc(own, 1)
                return body
            block.tensor(mk("pe"))
            block.scalar(mk("act"))
            block.vector(mk("dve"))
            block.gpsimd(mk("pool"))
            block.sync(mk("sp"))


class TB:
    def __init__(self, t, b):
        self.t = t
        self.b = b

    def __getitem__(self, k):
        return self.t[k]


def build_nc():
    nc = bass.Bass("TRN2", target_bir_lowering=False)

    def din(name, shape, dt=F32):
        return nc.dram_tensor(name, list(shape), dt, kind="ExternalInput").ap()

    def dout(name, shape, dt=F32):
        return nc.dram_tensor(name, list(shape), dt, kind="ExternalOutput").ap()

    def dscr(name, shape, dt):
        return nc.dram_tensor(name, list(shape), dt, kind="Internal").ap()

    xp = din("xp", [T, 1024])
    memp = din("memp", [256, 1024])
    ln1_g = din("ln1_g", [1024]); ln2_g = din("ln2_g", [1024]); ln3_g = din("ln3_g", [1024])
    memn_g = din("mem_norm_g", [1024])
    w_in = din("w_in", [1024, 2048])
    a_re = din("ssm_a_re", [32, 64]); a_im = din("ssm_a_im", [32, 64])
    b_re = din("ssm_b_re", [32, 64, 16]); b_im = din("ssm_b_im", [32, 64, 16])
    c_re = din("ssm_c_re", [32, 16, 64]); c_im = din("ssm_c_im", [32, 16, 64])
    ssm_d = din("ssm_d", [32, 16]); log_dt = din("ssm_log_dt", [32])
    glu_w = din("ssm_glu_w", [512, 512])
    qn_g = din("q_norm_g", [64]); kn_g = din("k_norm_g", [64])
    lq1 = din("lam_q1", [64]); lk1 = din("lam_k1", [64]); lq2 = din("lam_q2", [64]); lk2 = din("lam_k2", [64])
    subln_g = din("subln_g", [128])
    w_out = din("w_out", [1024, 1024])
    ca_wq = din("ca_wq", [1024, 1024]); ca_wk = din("ca_wk", [1024, 1024]); ca_wv = din("ca_wv", [1024, 1024])
    caq_g = din("ca_q_norm_g", [256]); cak_g = din("ca_k_norm_g", [256])
    ca_wo = din("ca_wo", [1024, 1024])
    ffn_wg = din("ffn_wg", [1024, F]); ffn_wv = din("ffn_wv", [1024, F]); ffn_wd = din("ffn_wd", [F, 1024])
    conv_w = din("ffn_conv_w", [3, F]); conv_b = din("ffn_conv_b", [F])
    xs = din("xs", [TS, 1024])
    st_re = din("st_re", [NS, 2048]); st_im = din("st_im", [NS, 2048])
    st_conv = din("st_conv", [NS * 2, F])
    cmk = din("cmk", [NS, 256, 1024]); cmv = din("cmv", [NS, 256, 1024])
    cache_k = din("cache_k", [2560 * 128, 512]); cache_v = din("cache_v", [2560 * 128, 512])
    ptab = din("ptab", [NS, 16], I32)
    c_ident = din("c_ident", [128, 128])
    c_bd64 = din("c_bd64", [128, 128])
    c_mask16 = din("c_mask16", [128, 128])
    c_caus = din("c_caus", [128, 128])
    c_pwn = din("c_pwn", [NPW])
    c_kpos = din("c_kpos", [128])
    c_g2m = din("c_g2m", [128, 2])

    y_p = dout("y_p", [T, 1024]); k_p = dout("k_p", [T, 512]); v_p = dout("v_p", [T, 512])
    hre_p = dout("hre_p", [16, 128]); him_p = dout("him_p", [16, 128])
    conv_p = dout("conv_p", [2, F])
    mk_p = dout("mk_p", [256, 1024]); mv_p = dout("mv_p", [256, 1024])

    y_s = dout("y_s", [TS, 1024]); k_s = dout("k_s", [TS, 512]); v_s = dout("v_s", [TS, 512])
    hre_s = dout("hre_s", [NS, 2048]); him_s = dout("him_s", [NS, 2048])
    conv_s = dout("conv_s", [NS * 2, F])
    wg_s = dscr("wg_s", [128, NFC, 8, 128], BF16); wv_s = dscr("wv_s", [128, NFC, 8, 128], BF16)
    wd_s = dscr("wd_s", [128, 8, NFC, 128], BF16)
    mix_s = dscr("mix_s", [NTILES, 128, 8, NT], BF16)
    vn_s = dscr("vn_s", [4, NS * 4 * 132], BF16)
    xn_s = dscr("xn_s", [NTILES, 128, 8, NT], BF16)

    es = ExitStack()
    with es:
        S = Sched(nc, es)

        def sb(stack, name, shape, dt):
            return TB(stack.enter_context(nc.sbuf_tensor(name, list(shape), dt)), S.buf(name))

        PS = [TB(es.enter_context(nc.psum_tensor("ps%d" % i, [128, 512], F32)), S.buf("ps%d" % i)) for i in range(8)]
        for p_ in PS:
            p_.b.excl = True
        psrr = [0]

        def bank():
            b = PS[psrr[0]]
            psrr[0] = (psrr[0] + 1) % 8
            return b

        SLOW = dict(allow_slow_non_contiguous=True)

        def mm(bk, lhsT, rhs, start, stop, reads, out=None, **kw):
            o = bk[:] if out is None else out
            S.op("pe", lambda e: e.matmul(o, lhsT=lhsT, rhs=rhs, start=start, stop=stop, **kw),
                 reads=reads, writes=[bk.b])

        def tr(bk, out, in_, ident, reads):
            S.op("pe", lambda e: e.transpose(out=out, in_=in_, identity=ident[:]), reads=reads + [ident.b],
                 writes=[bk.b])

        def act(out, in_, func, reads, writes, **kw):
            S.op("act", lambda e: e.activation(out=out, in_=in_, func=func, **kw), reads=reads, writes=writes)

        def tt(eng, out, in0, in1, op, reads, writes):
            S.op(eng, lambda e: e.tensor_tensor(out=out, in0=in0, in1=in1, op=op), reads=reads, writes=writes)

        def ts(eng, out, in0, s1, s2, op0, op1, reads, writes):
            if op1 is None:
                S.op(eng, lambda e: e.tensor_scalar(out=out, in0=in0, scalar1=s1, scalar2=None, op0=op0),
                     reads=reads, writes=writes)
            else:
                S.op(eng, lambda e: e.tensor_scalar(out=out, in0=in0, scalar1=s1, scalar2=s2, op0=op0, op1=op1),
                     reads=reads, writes=writes)

        def stt(out, in0, scalar, in1, op0, op1, reads, writes):
            S.op("dve", lambda e: e.scalar_tensor_tensor(out=out, in0=in0, scalar=scalar, in1=in1, op0=op0, op1=op1),
                 reads=reads, writes=writes)

        def cp(eng, out, in_, reads, writes):
            if eng == "act":
                S.op("act", lambda e: e.copy(out=out, in_=in_), reads=reads, writes=writes)
            else:
                S.op(eng, lambda e: e.tensor_copy(out=out, in_=in_), reads=reads, writes=writes)

        def memset(eng, ap, val, writes):
            S.op(eng, lambda e: e.memset(ap, val), writes=writes)

        def recip(out, in_, reads, writes):
            S.op("dve", lambda e: e.reciprocal(out=out, in_=in_), reads=reads, writes=writes)

        IDF = sb(es, "IDF", [128, 128], F32); IDB = sb(es, "IDB", [128, 128], BF16)
        ON1024 = sb(es, "ON1024", [128, 128], BF16); ON256 = sb(es, "ON256", [128, 128], BF16)
        ON128 = sb(es, "ON128", [128, 128], BF16); ONE1 = sb(es, "ONE1", [128, 128], BF16)
        BD64 = sb(es, "BD64", [128, 128], BF16)
        CAUS = sb(es, "CAUS", [128, 128], BF16)
        G1 = sb(es, "G1", [128, 8], F32); G2 = sb(es, "G2", [128, 8], F32); G3 = sb(es, "G3", [128, 8], F32)
        GM = sb(es, "GM", [128, 8], F32)
        QG = sb(es, "QG", [128, 1], F32); KG = sb(es, "KG", [128, 1], F32)
        SUBG = sb(es, "SUBG", [128, 1], F32)
        CQG = sb(es, "CQG", [128, 2], F32); CKG = sb(es, "CKG", [128, 2], F32)
        CW = sb(es, "CW", [128, 3, NFC], F32); CB = sb(es, "CB", [128, NFC], F32)
        KPOS = sb(es, "KPOS", [128, 1], F32)
        NLAM = sb(es, "NLAM", [128, 1], F32)
        LT = sb(es, "LT", [64, 4], F32)

        S.dma("sp", IDF[:], c_ident, writes=[IDF.b])
        S.dma("pool", IDB[:], c_ident, writes=[IDB.b])
        S.dma("pool", BD64[:], c_bd64, writes=[BD64.b])
        S.dma("pool", CAUS[:], c_caus, writes=[CAUS.b])
        memset("dve", ON1024[:], 1.0 / 1024, [ON1024.b]); memset("dve", ON256[:], 1.0 / 256, [ON256.b])
        memset("dve", ON128[:], 1.0 / 128, [ON128.b]); memset("dve", ONE1[:], 1.0, [ONE1.b])
        for (Gt, gsrc) in ((G1, ln1_g), (G2, ln2_g), (G3, ln3_g), (GM, memn_g)):
            S.dma("sp", Gt[:], gsrc.rearrange("(c p) -> p c", p=128), writes=[Gt.b], **SLOW)
        for (Gt, gsrc) in ((QG, qn_g), (KG, kn_g)):
            for hh in range(2):
                S.dma("sp", Gt[hh * 64:(hh + 1) * 64, :], gsrc.rearrange("(p o) -> p o", o=1), writes=[Gt.b], **SLOW)
        S.dma("sp", SUBG[:], subln_g.rearrange("(p o) -> p o", o=1), writes=[SUBG.b], **SLOW)
        S.dma("sp", CQG[:], caq_g.rearrange("(c p) -> p c", p=128), writes=[CQG.b], **SLOW)
        S.dma("sp", CKG[:], cak_g.rearrange("(c p) -> p c", p=128), writes=[CKG.b], **SLOW)
        for j in range(3):
            S.dma("sp", CW[:, j, :], conv_w[j].rearrange("(c p) -> p c", p=128), writes=[CW.b], **SLOW)
        S.dma("sp", CB[:], conv_b.rearrange("(c p) -> p c", p=128), writes=[CB.b], **SLOW)
        S.dma("sp", KPOS[:], c_kpos.rearrange("(p o) -> p o", o=1), writes=[KPOS.b], **SLOW)
        for i, src in enumerate((lq1, lk1, lq2, lk2)):
            S.dma("sp", LT[:, i:i + 1], src.rearrange("(p o) -> p o", o=1), writes=[LT.b], **SLOW)
        ts("dve", SUBG[:], SUBG[:], 1.0 - LAM0, None, ALU.mult, None, [SUBG.b], [SUBG.b])
        LP = sb(es, "LP", [64, 2], BF16)
        tt("dve", LP[:, 0:1], LT[:, 0:1], LT[:, 1:2], ALU.mult, [LT.b], [LP.b])
        tt("dve", LP[:, 1:2], LT[:, 2:3], LT[:, 3:4], ALU.mult, [LT.b], [LP.b])
        bk = bank()
        mm(bk, ONE1[0:64, :], LP[:, :], True, True, [ONE1.b, LP.b], out=bk[:, 0:2])
        LE = sb(es, "LE", [128, 2], F32)
        act(LE[:], bk[:, 0:2], AF.Exp, [bk.b], [LE.b])
        stt(NLAM[:], LE[:, 1:2], -LAM0, LE[:, 0:1], ALU.add, ALU.subtract, [LE.b], [NLAM.b])

        def convert_ffn_weights(stack):
            CIN = [sb(stack, "CIN%d" % i, [128, 4096], F32) for i in range(2)]
            COUT = [sb(stack, "COUT%d" % i, [128, 4096], BF16) for i in range(2)]
            groups = []
            for (dst, src) in ((wg_s, ffn_wg), (wv_s, ffn_wv)):
                for f0 in range(0, NFC, 4):
                    groups.append((dst, src, f0, min(4, NFC - f0)))
            for m in range(8):
                groups.append((None, None, m, 0))

            def load(g, i):
                dst, src, f0, nf = g
                if dst is not None:
                    cin = CIN[i][:, 0:8 * nf * 128].rearrange("p (kc x) -> p kc x", kc=8)
                    S.dma("pool", cin, src[:, f0 * 128:(f0 + nf) * 128].rearrange("(kc p) x -> p kc x", p=128),
                          writes=[CIN[i].b])
                else:
                    m = f0
                    for a0 in range(0, NFC, 8):
                        a1 = min(a0 + 8, NFC)
                        S.dma("pool", CIN[i][:, a0 * 128:a1 * 128].rearrange("p (fc n) -> p fc n", n=128),
                              ffn_wd[a0 * 128:a1 * 128, m * 128:(m + 1) * 128].rearrange("(fc p) n -> p fc n", p=128),
                              writes=[CIN[i].b])

            def conv_store(g, i):
                dst, src, f0, nf = g
                if dst is not None:
                    cin = CIN[i][:, 0:8 * nf * 128].rearrange("p (kc x) -> p kc x", kc=8)
                    cout = COUT[i][:, 0:nf * 1024].rearrange("p (fc kc n) -> p fc kc n", kc=8, n=128)
                    cp("dve", cout, cin.rearrange("p kc (fc n) -> p fc kc n", n=128), [CIN[i].b], [COUT[i].b])
                    S.dma("pool", dst[:, f0:f0 + nf].rearrange("p fc kc n -> p (fc kc n)"), COUT[i][:, 0:nf * 1024],
                          reads=[COUT[i].b])
                else:
                    m = f0
                    cp("dve", COUT[i][:, 0:NFC * 128], CIN[i][:, 0:NFC * 128], [CIN[i].b], [COUT[i].b])
                    S.dma("pool", wd_s[:, m].rearrange("p fc n -> p (fc n)"), COUT[i][:, 0:NFC * 128],
                          reads=[COUT[i].b])

            load(groups[0], 0)
            yield
            for gi, g in enumerate(groups):
                conv_store(g, gi % 2)
                if gi + 1 < len(groups):
                    load(groups[gi + 1], (gi + 1) % 2)
                yield

        def load_norm_tile(X, xsrc_ap, nblk, G, xnT, rs, ss, junk, raw_xT=None):
            S.dma("sp", X[:, 0:nblk, :], xsrc_ap, writes=[X.b])
            for j in range(nblk):
                act(junk[:], X[:, j, :], AF.Square, [X.b], [junk.b, ss.b], accum_out=ss[:, j:j + 1])
            act(rs[:, 0:nblk], ss[:, 0:nblk], AF.Sqrt, [ss.b], [rs.b], scale=1.0 / 1024, bias=EPS)
            recip(rs[:, 0:nblk], rs[:, 0:nblk], [rs.b], [rs.b])
            if raw_xT is not None:
                for c in range(8):
                    bk = bank()
                    for j in range(nblk):
                        tr(bk, bk[:, j * 128:(j + 1) * 128], X[:, j, c * 128:(c + 1) * 128], IDF, [X.b])
                    cp("act", raw_xT[:, c, 0:nblk * 128], bk[:, 0:nblk * 128], [bk.b], [raw_xT.b])
            for j in range(nblk):
                ts("dve", X[:, j, :], X[:, j, :], rs[:, j:j + 1], None, ALU.mult, None, [X.b, rs.b], [X.b])
            for c in range(8):
                bk = bank()
                for j in range(nblk):
                    tr(bk, bk[:, j * 128:(j + 1) * 128], X[:, j, c * 128:(c + 1) * 128], IDF, [X.b])
                ts("dve", xnT[:, c, 0:nblk * 128], bk[:, 0:nblk * 128], G[:, c:c + 1], None, ALU.mult, None,
                   [bk.b, G.b], [xnT.b])

        def ln_fm(xT, G, xnT, n, sq, rstd):
            bk = bank()
            for c in range(8):
                act(sq[:, c % 2, 0:n], xT[:, c, 0:n], AF.Square, [xT.b], [sq.b])
                mm(bk, ON1024[:], sq[:, c % 2, 0:n], c == 0, c == 7, [ON1024.b, sq.b], out=bk[:, 0:n])
            act(rstd[:, 0:n], bk[:, 0:n], AF.Sqrt, [bk.b], [rstd.b], bias=EPS)
            recip(rstd[:, 0:n], rstd[:, 0:n], [rstd.b], [rstd.b])
            for c in range(8):
                stt(xnT[:, c, 0:n], xT[:, c, 0:n], G[:, c:c + 1], rstd[:, 0:n], ALU.mult, ALU.mult,
                    [xT.b, G.b, rstd.b], [xnT.b])

        def store_tm(src_fm, nchunks, n, dst_ap, stage, ident=IDF):
            nblk = (n + 127) // 128
            for j in range(nblk):
                w = min(128, n - j * 128)
                for c0 in range(0, nchunks, 4):
                    bk = bank()
                    for c in range(c0, min(c0 + 4, nchunks)):
                        tr(bk, bk[0:w, (c - c0) * 128:(c - c0 + 1) * 128], src_fm[:, c, j * 128:j * 128 + w], ident,
                           [src_fm.b])
                    nn = (min(c0 + 4, nchunks) - c0) * 128
                    cp("act", stage[0:w, j, c0 * 128:c0 * 128 + nn], bk[0:w, 0:nn], [bk.b], [stage.b])
            if n % 128 == 0:
                S.dma("sp", dst_ap.rearrange("(j p) f -> p j f", p=128), stage[:, 0:nblk, 0:nchunks * 128],
                      reads=[stage.b])
            else:
                S.dma("sp", dst_ap, stage[0:n, 0, 0:nchunks * 128], reads=[stage.b])


        xnTs = sb(es, "xnTs", [128, 8, TS], BF16)
        xsT = sb(es, "xsT", [128, 8, TS], F32)
        MIXs = sb(es, "MIXs", [128, 8, TS], BF16)
        usT = sb(es, "usT", [128, 4, TS], BF16)
        QTs = sb(es, "QTs", [128, 4, TS], BF16); KTs = sb(es, "KTs", [128, 4, TS], BF16)
        if 'S' in PH:
            with ExitStack() as ph:
                Xs_ = sb(ph, "Xs_", [128, 1024], F32)
                rs = sb(ph, "rs_s", [128, 1], F32); ss = sb(ph, "ss_s", [128, 1], F32)
                junk = sb(ph, "junk_s", [128, 1024], F32)
                S.dma("sp", Xs_[0:TS, :], xs, writes=[Xs_.b])
                act(junk[0:TS, :], Xs_[0:TS, :], AF.Square, [Xs_.b], [junk.b, ss.b], accum_out=ss[0:TS, 0:1])
                act(rs[0:TS, :], ss[0:TS, :], AF.Sqrt, [ss.b], [rs.b], scale=1.0 / 1024, bias=EPS)
                recip(rs[0:TS, :], rs[0:TS, :], [rs.b], [rs.b])
                for c in range(8):
                    bk = bank()
                    tr(bk, bk[:, 0:TS], Xs_[0:TS, c * 128:(c + 1) * 128], TB(IDF.t[0:TS, 0:TS], IDF.b), [Xs_.b])
                    cp("act", xsT[:, c, :], bk[:, 0:TS], [bk.b], [xsT.b])
                ts("dve", Xs_[0:TS, :], Xs_[0:TS, :], rs[0:TS, 0:1], None, ALU.mult, None, [Xs_.b, rs.b], [Xs_.b])
                for c in range(8):
                    bk = bank()
                    tr(bk, bk[:, 0:TS], Xs_[0:TS, c * 128:(c + 1) * 128], TB(IDF.t[0:TS, 0:TS], IDF.b), [Xs_.b])
                    ts("dve", xnTs[:, c, :], bk[:, 0:TS], G1[:, c:c + 1], None, ALU.mult, None, [bk.b, G1.b], [xnTs.b])
                S.barrier()

        MKT = sb(es, "MKT", [128, 8, 256], BF16)
        MV = sb(es, "MV", [128, 2, 1024], BF16)
        with ExitStack() as ph:
          if 'M' in PH:
            Xm = sb(ph, "Xm", [128, 2, 1024], F32)
            xnTm = sb(ph, "xnTm", [128, 8, 256], BF16)
            rs = sb(ph, "rs_m", [128, 4], F32); ss = sb(ph, "ss_m", [128, 4], F32)
            junk = sb(ph, "junk_m", [128, 1024], F32)
            Wk = sb(ph, "Wk", [128, 8, 1024], BF16); Wv = sb(ph, "Wv", [128, 8, 1024], BF16)
            mkraw = sb(ph, "mkraw", [128, 8, 256], F32)
            sqm = sb(ph, "sqm", [128, 2, 256], BF16); rstdm = sb(ph, "rstdm", [128, 256], F32)
            stg = sb(ph, "stg_m", [128, 2, 1024], F32)
            S.dma("pool", Wk[:], ca_wk.rearrange("(c p) n -> p c n", p=128), writes=[Wk.b])
            S.dma("pool", Wv[:], ca_wv.rearrange("(c p) n -> p c n", p=128), writes=[Wv.b])
            KS = int(os.environ.get('KSTOP', '9'))
            load_norm_tile(Xm, memp.rearrange("(j p) f -> p j f", p=128), 2, GM, xnTm, rs, ss, junk)
            for m in (range(8) if KS >= 2 else ()):
                bk = bank()
                for kc in range(8):
                    mm(bk, Wk[:, kc, m * 128:(m + 1) * 128], xnTm[:, kc, :], kc == 0, kc == 7, [Wk.b, xnTm.b],
                       out=bk[:, 0:256])
                cp("act", mkraw[:, m, :], bk[:, 0:256], [bk.b], [mkraw.b])
            for h in (range(4) if KS >= 3 else ()):
                bk = bank()
                for dh in range(2):
                    act(sqm[:, dh, :], mkraw[:, 2 * h + dh, :], AF.Square, [mkraw.b], [sqm.b])
                    mm(bk, ON256[:], sqm[:, dh, :], dh == 0, dh == 1, [ON256.b, sqm.b], out=bk[:, 0:256])
                act(rstdm[:], bk[:, 0:256], AF.Sqrt, [bk.b], [rstdm.b], bias=EPS)
                recip(rstdm[:], rstdm[:], [rstdm.b], [rstdm.b])
                for dh in range(2):
                    stt(mkraw[:, 2 * h + dh, :], mkraw[:, 2 * h + dh, :], CKG[:, dh:dh + 1], rstdm[:], ALU.mult,
                        ALU.mult, [mkraw.b, CKG.b, rstdm.b], [mkraw.b])
                    cp("act", MKT[:, 2 * h + dh, :], mkraw[:, 2 * h + dh, :], [mkraw.b], [MKT.b])
            if KS >= 4:
                store_tm(mkraw, 8, 256, mk_p, stg)
            for j in (range(2) if KS >= 5 else ()):
                for half in range(2):
                    bk = bank()
                    for kc in range(8):
                        mm(bk, xnTm[:, kc, j * 128:(j + 1) * 128], Wv[:, kc, half * 512:(half + 1) * 512], kc == 0,
                           kc == 7, [Wv.b, xnTm.b])
                    cp("act", stg[:, j, half * 512:(half + 1) * 512], bk[:], [bk.b], [stg.b])
                    cp("dve", MV[:, j, half * 512:(half + 1) * 512], bk[:], [bk.b], [MV.b])
            S.dma("sp", mv_p.rearrange("(j p) f -> p j f", p=128), stg[:], reads=[stg.b])
            S.barrier()


        s5 = ExitStack()
        if 'B' in PH:
            TWO_PI = 2.0 * math.pi
            PI_S = 3.1415925
            ZT = sb(s5, "ZT", [128, 4, 8, 2, 128], BF16)
            CLR = sb(s5, "CLR", [128, 16, 8, 32], BF16)
            CLI = sb(s5, "CLI", [128, 16, 8, 32], BF16)
            BDT = sb(s5, "BDT", [128, 4, 8, 128], BF16)
            LRE = sb(s5, "LRE", [128, 16, NPW], F32); LIM = sb(s5, "LIM", [128, 16, NPW], F32)
            with ExitStack() as tb:
                AR = sb(tb, "AR", [128, 16], F32); AI = sb(tb, "AI", [128, 16], F32)
                DT = sb(tb, "DT", [128, 16], F32)
                ARD = sb(tb, "ARD", [128, 16], F32); TH = sb(tb, "TH", [128, 16], F32)
                PWN = sb(tb, "PWN", [128, NPW], F32)
                ANG = sb(tb, "ANG", [128, 16, NPW], F32); R = sb(tb, "Rr", [128, 16, NPW], F32)
                KF = sb(tb, "KF", [128, 16, NPW], F32); KI = sb(tb, "KI", [128, 16, NPW], I32)
                MG = sb(tb, "MG", [128, 16, NPW], F32)
                S.dma("sp", AR[:], a_re.rearrange("(q g2) p -> (g2 p) q", g2=2), writes=[AR.b], **SLOW)
                S.dma("sp", AI[:], a_im.rearrange("(q g2) p -> (g2 p) q", g2=2), writes=[AI.b], **SLOW)
                for g2 in range(2):
                    S.dma("sp", DT[g2 * 64:(g2 + 1) * 64, :],
                          log_dt.rearrange("(q g2) -> g2 q", g2=2)[g2:g2 + 1, :].to_broadcast([64, 16]),
                          writes=[DT.b], **SLOW)
                S.dma("sp", PWN[:], c_pwn.rearrange("(o n) -> o n", o=1).to_broadcast([128, NPW]), writes=[PWN.b],
                      **SLOW)
                act(DT[:], DT[:], AF.Exp, [DT.b], [DT.b])
                tt("dve", ARD[:], AR[:], DT[:], ALU.mult, [AR.b, DT.b], [ARD.b])
                tt("dve", TH[:], AI[:], DT[:], ALU.mult, [AI.b, DT.b], [TH.b])
                bshape = [128, 16, NPW]
                tt("dve", MG[:], ARD[:, :].unsqueeze(2).to_broadcast(bshape), PWN[:, :].unsqueeze(1).to_broadcast(bshape),
                   ALU.mult, [ARD.b, PWN.b], [MG.b])
                act(MG[:], MG[:], AF.Exp, [MG.b], [MG.b])
                tt("dve", ANG[:], TH[:, :].unsqueeze(2).to_broadcast(bshape), PWN[:, :].unsqueeze(1).to_broadcast(bshape),
                   ALU.mult, [TH.b, PWN.b], [ANG.b])
                ts("dve", KF[:], ANG[:], 1.0 / TWO_PI, 0.5, ALU.mult, ALU.add, [ANG.b], [KF.b])
                cp("dve", KI[:], KF[:], [KF.b], [KI.b])
                cp("dve", KF[:], KI[:], [KI.b], [KF.b])
                stt(R[:], KF[:], -TWO_PI, ANG[:], ALU.mult, ALU.add, [KF.b, ANG.b], [R.b])

                def wrap(Rt):
                    ts("dve", KF[:], Rt[:], -math.pi, None, ALU.is_lt, None, [Rt.b], [KF.b])
                    stt(Rt[:], KF[:], TWO_PI, Rt[:], ALU.mult, ALU.add, [KF.b, Rt.b], [Rt.b])
                    ts("dve", KF[:], Rt[:], math.pi, None, ALU.is_gt, None, [Rt.b], [KF.b])
                    stt(Rt[:], KF[:], -TWO_PI, Rt[:], ALU.mult, ALU.add, [KF.b, Rt.b], [Rt.b])
                    ts("dve", Rt[:], Rt[:], -PI_S, PI_S, ALU.max, ALU.min, [Rt.b], [Rt.b])
                wrap(R)
                act(LIM[:], R[:], AF.Sin, [R.b], [LIM.b])
                ts("dve", R[:], R[:], math.pi / 2, None, ALU.add, None, [R.b], [R.b])
                wrap(R)
                act(LRE[:], R[:], AF.Sin, [R.b], [LRE.b])
                tt("dve", LRE[:], LRE[:], MG[:], ALU.mult, [LRE.b, MG.b], [LRE.b])
                tt("dve", LIM[:], LIM[:], MG[:], ALU.mult, [LIM.b, MG.b], [LIM.b])
                NRE = sb(tb, "NRE", [128, 16], F32); DEN = sb(tb, "DEN", [128, 16], F32)
                FRE = sb(tb, "FRE", [128, 16], F32); FIM = sb(tb, "FIM", [128, 16], F32)
                T1 = sb(tb, "T1", [128, 16], F32)
                ts("dve", NRE[:], LRE[:, :, 1], -1.0, None, ALU.add, None, [LRE.b], [NRE.b])
                tt("dve", DEN[:], AR[:], AR[:], ALU.mult, [AR.b], [DEN.b])
                tt("dve", T1[:], AI[:], AI[:], ALU.mult, [AI.b], [T1.b])
                tt("dve", DEN[:], DEN[:], T1[:], ALU.add, [DEN.b, T1.b], [DEN.b])
                recip(DEN[:], DEN[:], [DEN.b], [DEN.b])
                tt("dve", FRE[:], NRE[:], AR[:], ALU.mult, [NRE.b, AR.b], [FRE.b])
                tt("dve", T1[:], LIM[:, :, 1], AI[:], ALU.mult, [LIM.b, AI.b], [T1.b])
                tt("dve", FRE[:], FRE[:], T1[:], ALU.add, [FRE.b, T1.b], [FRE.b])
                tt("dve", FRE[:], FRE[:], DEN[:], ALU.mult, [FRE.b, DEN.b], [FRE.b])
                tt("dve", FIM[:], LIM[:, :, 1], AR[:], ALU.mult, [LIM.b, AR.b], [FIM.b])
                tt("dve", T1[:], NRE[:], AI[:], ALU.mult, [NRE.b, AI.b], [T1.b])
                tt("dve", FIM[:], FIM[:], T1[:], ALU.subtract, [FIM.b, T1.b], [FIM.b])
                tt("dve", FIM[:], FIM[:], DEN[:], ALU.mult, [FIM.b, DEN.b], [FIM.b])
                BR = sb(tb, "BR", [128, 16, 16], F32); BI = sb(tb, "BI", [128, 16, 16], F32)
                BBR = sb(tb, "BBR", [128, 16, 16], F32); BBI = sb(tb, "BBI", [128, 16, 16], F32)
                T2 = sb(tb, "T2", [128, 16, 16], F32)
                S.dma("sp", BR[:], b_re.rearrange("(q g2) p c -> (g2 p) q c", g2=2), writes=[BR.b], **SLOW)
                S.dma("sp", BI[:], b_im.rearrange("(q g2) p c -> (g2 p) q c", g2=2), writes=[BI.b], **SLOW)
                s3 = [128, 16, 16]
                fr = FRE[:, :].unsqueeze(2).to_broadcast(s3); fi = FIM[:, :].unsqueeze(2).to_broadcast(s3)
                tt("dve", BBR[:], BR[:], fr, ALU.mult, [BR.b, FRE.b], [BBR.b])
                tt("dve", T2[:], BI[:], fi, ALU.mult, [BI.b, FIM.b], [T2.b])
                tt("dve", BBR[:], BBR[:], T2[:], ALU.subtract, [BBR.b, T2.b], [BBR.b])
                tt("dve", BBI[:], BI[:], fr, ALU.mult, [BI.b, FRE.b], [BBI.b])
                tt("dve", T2[:], BR[:], fi, ALU.mult, [BR.b, FIM.b], [T2.b])
                tt("dve", BBI[:], BBI[:], T2[:], ALU.add, [BBI.b, T2.b], [BBI.b])
                ZR = sb(tb, "ZR", [128, 16, 8, 16], F32); ZI = sb(tb, "ZI", [128, 16, 8, 16], F32)
                T3 = sb(tb, "T3", [128, 16, 8, 16], F32)
                s4 = [128, 16, 8, 16]
                lr = LRE[:, :, 0:8].unsqueeze(3).to_broadcast(s4); li = LIM[:, :, 0:8].unsqueeze(3).to_broadcast(s4)
                br_ = BBR[:, :, :].unsqueeze(2).to_broadcast(s4); bi_ = BBI[:, :, :].unsqueeze(2).to_broadcast(s4)
                tt("dve", ZR[:], lr, br_, ALU.mult, [LRE.b, BBR.b], [ZR.b])
                tt("dve", T3[:], li, bi_, ALU.mult, [LIM.b, BBI.b], [T3.b])
                tt("dve", ZR[:], ZR[:], T3[:], ALU.subtract, [ZR.b, T3.b], [ZR.b])
                tt("dve", ZI[:], lr, bi_, ALU.mult, [LRE.b, BBI.b], [ZI.b])
                tt("dve", T3[:], li, br_, ALU.mult, [LIM.b, BBR.b], [T3.b])
                tt("dve", ZI[:], ZI[:], T3[:], ALU.add, [ZI.b, T3.b], [ZI.b])
                E4 = sb(tb, "E4", [128, 4, 8, 2, 128], F32)
                memset("pool", E4[:], 0.0, [E4.b])
                for ri, Zt in enumerate((ZR, ZI)):
                    for gc in range(4):
                        for g2 in range(2):
                            pr = slice(g2 * 64, (g2 + 1) * 64)
                            dst = E4[pr, gc, :, ri, :].rearrange("p t (j x) -> p t j x", x=32)[:, :, :, g2 * 16:(g2 + 1) * 16]
                            src = Zt[pr, gc * 4:(gc + 1) * 4, :, :].rearrange("p j t c -> p t j c")
                            cp("pool", dst, src, [Zt.b], [E4.b])
                for gc in range(4):
                    for ri in range(2):
                        for s0 in (0, 4):
                            bk = bank()
                            for sp_ in range(s0, s0 + 4):
                                tr(bk, bk[:, (sp_ - s0) * 128:(sp_ - s0 + 1) * 128], E4[:, gc, 7 - sp_, ri, :], IDF, [E4.b])
                            cp("act", ZT[:, gc, s0:s0 + 4, ri, :], bk[:].rearrange("p (s n) -> p s n", n=128), [bk.b],
                               [ZT.b])
                CN = sb(tb, "CN", [128, 2, 4, 64], F32)
                CE = sb(tb, "CE", [128, 2, 4, 2, 64], F32)
                CTR = sb(tb, "CTR", [128, 4, 128], F32); CTI = sb(tb, "CTI", [128, 4, 128], F32)
                CTIN = sb(tb, "CTIN", [128, 4, 128], F32)
                G2M = sb(tb, "G2M", [128, 2], F32)
                S.dma("sp", G2M[:], c_g2m, writes=[G2M.b])
                S.dma("sp", CN[:, 0, :, :], c_re.rearrange("(gc r) c p -> (r c) gc p", gc=4), writes=[CN.b], **SLOW)
                S.dma("sp", CN[:, 1, :, :], c_im.rearrange("(gc r) c p -> (r c) gc p", gc=4), writes=[CN.b], **SLOW)
                for ri in range(2):
                    for g2 in range(2):
                        ts("dve", CE[:, ri, :, g2, :], CN[:, ri, :, :], G2M[:, g2:g2 + 1], None, ALU.mult, None,
                           [CN.b, G2M.b], [CE.b])
                for ri, CTt in enumerate((CTR, CTI)):
                    bk = bank()
                    for gc in range(4):
                        tr(bk, bk[:, gc * 128:(gc + 1) * 128], CE[:, ri, gc, :, :].rearrange("p a b -> p (a b)"), IDF,
                           [CE.b])
                    cp("act", CTt[:], bk[:].rearrange("p (g n) -> p g n", n=128), [bk.b], [CTt.b])
                ts("dve", CTIN[:], CTI[:], -1.0, None, ALU.mult, None, [CTI.b], [CTIN.b])
                s5s = [128, 16, 8, 32]
                T4 = sb(tb, "T4", [128, 16, 8, 32], F32); T5 = sb(tb, "T5", [128, 16, 8, 32], F32)
                ctr = CTR[:, :, :].rearrange("p g (j x) -> p (g j) x", x=32).unsqueeze(2).to_broadcast(s5s)
                cti = CTI[:, :, :].rearrange("p g (j x) -> p (g j) x", x=32).unsqueeze(2).to_broadcast(s5s)
                l1r = LRE[:, :, 1:9].unsqueeze(3).to_broadcast(s5s); l1i = LIM[:, :, 1:9].unsqueeze(3).to_broadcast(s5s)
                tt("dve", T4[:], ctr, l1r, ALU.mult, [CTR.b, LRE.b], [T4.b])
                tt("dve", T5[:], cti, l1i, ALU.mult, [CTI.b, LIM.b], [T5.b])
                tt("dve", CLR[:], T4[:], T5[:], ALU.subtract, [T4.b, T5.b], [CLR.b])
                tt("dve", T4[:], ctr, l1i, ALU.mult, [CTR.b, LIM.b], [T4.b])
                tt("dve", T5[:], cti, l1r, ALU.mult, [CTI.b, LRE.b], [T5.b])
                tt("dve", T4[:], T4[:], T5[:], ALU.add, [T4.b, T5.b], [T4.b])
                ts("dve", CLI[:], T4[:], -1.0, None, ALU.mult, None, [T4.b], [CLI.b])
                MASK16 = sb(tb, "MASK16", [128, 128], F32); DCOL = sb(tb, "DCOL", [128, 4], F32)
                T6 = sb(tb, "T6", [128, 128], F32)
                S.dma("sp", MASK16[:], c_mask16, writes=[MASK16.b])
                S.dma("sp", DCOL[:], ssm_d.rearrange("(gc g8) c -> (g8 c) gc", gc=4), writes=[DCOL.b], **SLOW)
                for gc in range(4):
                    for tau in range(8):
                        bk = bank()
                        mm(bk, E4[:, gc, tau, 0, :], CTR[:, gc, :], True, False, [E4.b, CTR.b], out=bk[:, 0:128])
                        mm(bk, E4[:, gc, tau, 1, :], CTIN[:, gc, :], False, True, [E4.b, CTIN.b], out=bk[:, 0:128])
                        if tau == 0:
                            tt("dve", T6[:], bk[:, 0:128], MASK16[:], ALU.mult, [bk.b, MASK16.b], [T6.b])
                            stt(BDT[:, gc, 0, :], IDF[:], DCOL[:, gc:gc + 1], T6[:], ALU.mult, ALU.add,
                                [IDF.b, DCOL.b, T6.b], [BDT.b])
                        else:
                            tt("dve", BDT[:, gc, tau, :], bk[:, 0:128], MASK16[:], ALU.mult, [bk.b, MASK16.b], [BDT.b])
                S.barrier()

            pb = ExitStack()
            uT = sb(pb, "uT", [128, 4, T], BF16)
            with ExitStack() as ph:
                Xu2 = [sb(ph, "Xu%d" % i, [128, 4, 1024], F32) for i in range(2)]
                xnTu = [sb(ph, "xnTu%d" % i, [128, 8, NT], BF16) for i in range(2)]
                rs = sb(ph, "rsu", [128, 4], F32); ss = sb(ph, "ssu", [128, 4], F32)
                junk = sb(ph, "junku", [128, 1024], F32)
                Wu = sb(ph, "Wu", [128, 8, 512], BF16)
                S.dma("pool", Wu[:], w_in[:, 0:512].rearrange("(c p) n -> p c n", p=128), writes=[Wu.b])
                for it in range(NTILES):
                    t0 = it * NT
                    xnT = xnTu[it % 2]; X = Xu2[it % 2]
                    load_norm_tile(X, xp[t0:t0 + NT, :].rearrange("(j p) f -> p j f", p=128), 4, G1, xnT, rs, ss, junk)
                    S.dma("sp", xn_s[it], xnT[:], reads=[xnT.b])
                    for m in range(4):
                        bk = bank()
                        for kc in range(8):
                            mm(bk, Wu[:, kc, m * 128:(m + 1) * 128], xnT[:, kc, :], kc == 0, kc == 7, [Wu.b, xnT.b])
                        cp("act", uT[:, m, t0:t0 + NT], bk[:], [bk.b], [uT.b])
                if 'S' in PH:
                    for m in range(4):
                        bk = bank()
                        for kc in range(8):
                            mm(bk, Wu[:, kc, m * 128:(m + 1) * 128], xnTs[:, kc, :], kc == 0, kc == 7, [Wu.b, xnTs.b],
                               out=bk[:, 0:TS])
                        cp("act", usT[:, m, :], bk[:, 0:TS], [bk.b], [usT.b])
                S.barrier()
            if 'S' in PH:
              with ExitStack() as ph:
                STt = sb(ph, "STt", [NS, 2, 2048], F32)
                S0 = sb(ph, "S0", [128, 2, 16, NS], F32); S0b = sb(ph, "S0b", [128, 2, 16, NS], BF16)
                SN = sb(ph, "SN", [128, 2, 16, NS], F32)
                W1 = sb(ph, "W1", [128, 16, NS], F32); W2 = sb(ph, "W2", [128, 16, NS], F32)
                hst = sb(ph, "hst", [NS, 2, 2048], F32)
                Y2s = sb(ph, "Y2s", [128, TS], F32); Y3s = sb(ph, "Y3s", [128, TS], F32)
                Gts = sb(ph, "Gts", [128, 4, TS], F32); Gbs = sb(ph, "Gbs", [128, 4, TS], BF16)
                GLUWs = sb(ph, "GLUWs", [128, 4, 512], BF16)
                S.dma("pool", GLUWs[:], glu_w.rearrange("(c p) n -> p c n", p=128), writes=[GLUWs.b])
                S.dma("sp", STt[:, 0, :], st_re, writes=[STt.b])
                S.dma("sp", STt[:, 1, :], st_im, writes=[STt.b])
                ID16 = TB(IDF.t[0:NS, 0:NS], IDF.b)
                for ri in range(2):
                    bk = bank()
                    for q16 in range(16):
                        tr(bk, bk[:, q16 * NS:(q16 + 1) * NS], STt[:, ri, q16 * 128:(q16 + 1) * 128], ID16, [STt.b])
                    cp("act", S0[:, ri, :, :], bk[:, 0:16 * NS].rearrange("p (q s) -> p q s", s=NS), [bk.b], [S0.b])
                    cp("dve", S0b[:, ri, :, :], S0[:, ri, :, :], [S0.b], [S0b.b])
                for q16 in range(16):
                    gc, j = q16 // 4, q16 % 4
                    pr = slice(32 * j, 32 * j + 32)
                    uv = usT[:, gc, :].rearrange("p (n l) -> p l n", l=4)
                    bk = bank()
                    for ri in range(2):
                        for sp_ in range(4):
                            mm(bk, ZT[pr, gc, 4 + sp_, ri, :], uv[pr, sp_, :], sp_ == 0, sp_ == 3, [ZT.b, usT.b],
                               out=bk[:, ri * NS:(ri + 1) * NS], tile_position=(32 * j, 0))
                    cp("act", SN[:, :, q16, :], bk[:, 0:2 * NS].rearrange("p (r s) -> p r s", s=NS), [bk.b], [SN.b])
                s3 = [128, 16, NS]
                l4r = LRE[:, :, 4:5].to_broadcast(s3); l4i = LIM[:, :, 4:5].to_broadcast(s3)
                tt("dve", W1[:], S0[:, 0, :, :], l4r, ALU.mult, [S0.b, LRE.b], [W1.b])
                tt("dve", W2[:], S0[:, 1, :, :], l4i, ALU.mult, [S0.b, LIM.b], [W2.b])
                tt("dve", W1[:], W1[:], W2[:], ALU.subtract, [W1.b, W2.b], [W1.b])
                tt("dve", SN[:, 0, :, :], SN[:, 0, :, :], W1[:], ALU.add, [SN.b, W1.b], [SN.b])
                tt("dve", W1[:], S0[:, 0, :, :], l4i, ALU.mult, [S0.b, LIM.b], [W1.b])
                tt("dve", W2[:], S0[:, 1, :, :], l4r, ALU.mult, [S0.b, LRE.b], [W2.b])
                tt("dve", W1[:], W1[:], W2[:], ALU.add, [W1.b, W2.b], [W1.b])
                tt("dve", SN[:, 1, :, :], SN[:, 1, :, :], W1[:], ALU.add, [SN.b, W1.b], [SN.b])
                for ri in range(2):
                    for q0 in range(0, 16, 4):
                        bk = bank()
                        for q16 in range(q0, q0 + 4):
                            tr(bk, bk[0:NS, (q16 - q0) * 128:(q16 - q0 + 1) * 128], SN[:, ri, q16, :], IDF, [SN.b])
                        cp("act", hst[:, ri, q0 * 128:(q0 + 4) * 128], bk[0:NS, :], [bk.b], [hst.b])
                S.dma("sp", hre_s, hst[:, 0, :], reads=[hst.b])
                S.dma("sp", him_s, hst[:, 1, :], reads=[hst.b])
                for gc in range(4):
                    bk = bank()
                    uv = usT[:, gc, :].rearrange("p (n l) -> p l n", l=4)
                    ov = bk[:, 0:TS].rearrange("p (n l) -> p l n", l=4)
                    for tp in range(4):
                        for sp_ in range(tp + 1):
                            mm(bk, BDT[:, gc, tp - sp_, :], uv[:, sp_, :], sp_ == 0, False, [BDT.b, usT.b],
                               out=ov[:, tp, :])
                        for j in range(4):
                            q16 = gc * 4 + j
                            for ri, CLt in enumerate((CLR, CLI)):
                                mm(bk, CLt[:, q16, tp, :], S0b[:, ri, q16, :], False, (j == 3 and ri == 1),
                                   [CLt.b, S0b.b], out=ov[32 * j:32 * j + 32, tp, :], tile_position=(0, 32 * j))
                    act(Y2s[:], bk[:, 0:TS], AF.Square, [bk.b], [Y2s.b])
                    ts("dve", Y2s[:], Y2s[:], 0.044715, 1.0, ALU.mult, ALU.add, [Y2s.b], [Y2s.b])
                    tt("dve", Y3s[:], Y2s[:], bk[:, 0:TS], ALU.mult, [Y2s.b, bk.b], [Y3s.b])
                    act(Y3s[:], Y3s[:], AF.Sigmoid, [Y3s.b], [Y3s.b], scale=2.0 * math.sqrt(2.0 / math.pi))
                    tt("dve", Gts[:, gc, :], Y3s[:], bk[:, 0:TS], ALU.mult, [Y3s.b, bk.b], [Gts.b])
                    cp("act", Gbs[:, gc, :], Gts[:, gc, :], [Gts.b], [Gbs.b])
                for m in range(4):
                    bk = bank()
                    for kc in range(4):
                        mm(bk, GLUWs[:, kc, m * 128:(m + 1) * 128], Gbs[:, kc, :], kc == 0, kc == 3, [GLUWs.b, Gbs.b],
                           out=bk[:, 0:TS])
                    act(Y3s[:], bk[:, 0:TS], AF.Sigmoid, [bk.b], [Y3s.b])
                    tt("dve", MIXs[:, m, :], Y3s[:], Gts[:, m, :], ALU.mult, [Y3s.b, Gts.b], [MIXs.b])
                S.barrier()
            SW = sb(pb, "SW", [128, 2, 8, NCH], F32)
            SP = sb(pb, "SPv", [128, 2, 16, NCH], BF16)
            EE = sb(pb, "EE", [128, 2, 16, NSC], F32)
            TA = sb(pb, "TA", [128, 8, NSC], F32); TBt = sb(pb, "TBt", [128, 8, NSC], F32)
            U1 = sb(pb, "U1", [128, 8], F32); U2 = sb(pb, "U2", [128, 8], F32)
            V1 = sb(pb, "V1", [128, 8, SC], F32); V2 = sb(pb, "V2", [128, 8, SC], F32)
            memset("dve", SP[:, :, :, 0:1], 0.0, [SP.b])
            for hq in range(2):
                qs = slice(8 * hq, 8 * hq + 8)
                for q8 in range(8):
                    q16 = 8 * hq + q8
                    gc, j = q16 // 4, q16 % 4
                    pr = slice(32 * j, 32 * j + 32)
                    uv = uT[:, gc, :].rearrange("p (n l) -> p l n", l=L)
                    for ri in range(2):
                        bk = bank()
                        for sp_ in range(L):
                            mm(bk, ZT[pr, gc, sp_, ri, :], uv[pr, sp_, :], sp_ == 0, sp_ == L - 1, [ZT.b, uT.b],
                               tile_position=(32 * j, 0))
                        cp("act" if ri == 0 else "dve", SW[:, ri, q8, :], bk[:], [bk.b], [SW.b])
                Sv = SW[:, :, :, :].rearrange("p r q (s i) -> p r q s i", i=SC)
                sA = [128, 8, NSC]
                a8r = LRE[:, qs, 8:9].to_broadcast(sA); a8i = LIM[:, qs, 8:9].to_broadcast(sA)
                for i in range(1, SC):
                    pre_r = Sv[:, 0, :, :, i - 1]; pre_i = Sv[:, 1, :, :, i - 1]
                    tt("dve", TA[:], pre_r, a8r, ALU.mult, [SW.b, LRE.b], [TA.b])
                    tt("dve", TBt[:], pre_i, a8i, ALU.mult, [SW.b, LIM.b], [TBt.b])
                    tt("dve", TA[:], TA[:], TBt[:], ALU.subtract, [TA.b, TBt.b], [TA.b])
                    tt("dve", TBt[:], pre_i, a8r, ALU.mult, [SW.b, LRE.b], [TBt.b])
                    tt("dve", Sv[:, 0, :, :, i], Sv[:, 0, :, :, i], TA[:], ALU.add, [SW.b, TA.b], [SW.b])
                    tt("dve", TA[:], pre_r, a8i, ALU.mult, [SW.b, LIM.b], [TA.b])
                    tt("dve", TA[:], TA[:], TBt[:], ALU.add, [TA.b, TBt.b], [TA.b])
                    tt("dve", Sv[:, 1, :, :, i], Sv[:, 1, :, :, i], TA[:], ALU.add, [SW.b, TA.b], [SW.b])
                cp("dve", EE[:, :, qs, 0], Sv[:, :, :, 0, SC - 1], [SW.b], [EE.b])
                for sc in range(1, NSC):
                    er = EE[:, 0, qs, sc - 1]; ei = EE[:, 1, qs, sc - 1]
                    tt("dve", U1[:], er, LRE[:, qs, 23], ALU.mult, [EE.b, LRE.b], [U1.b])
                    tt("dve", U2[:], ei, LIM[:, qs, 23], ALU.mult, [EE.b, LIM.b], [U2.b])
                    tt("dve", U1[:], U1[:], U2[:], ALU.subtract, [U1.b, U2.b], [U1.b])
                    tt("dve", EE[:, 0, qs, sc], U1[:], Sv[:, 0, :, sc, SC - 1], ALU.add, [U1.b, SW.b], [EE.b])
                    tt("dve", U1[:], er, LIM[:, qs, 23], ALU.mult, [EE.b, LIM.b], [U1.b])
                    tt("dve", U2[:], ei, LRE[:, qs, 23], ALU.mult, [EE.b, LRE.b], [U2.b])
                    tt("dve", U1[:], U1[:], U2[:], ALU.add, [U1.b, U2.b], [U1.b])
                    tt("dve", EE[:, 1, qs, sc], U1[:], Sv[:, 1, :, sc, SC - 1], ALU.add, [U1.b, SW.b], [EE.b])
                cp("act", SP[:, 0, qs, 1:SC], SW[:, 0, :, 0:SC - 1], [SW.b], [SP.b])
                cp("act", SP[:, 1, qs, 1:SC], SW[:, 1, :, 0:SC - 1], [SW.b], [SP.b])
                sC = [128, 8, SC]
                PR = LRE[:, qs, 8:24]; PI = LIM[:, qs, 8:24]
                for sc in range(1, NSC):
                    er = EE[:, 0, qs, sc - 1:sc].to_broadcast(sC); ei = EE[:, 1, qs, sc - 1:sc].to_broadcast(sC)
                    tt("dve", V1[:], PR, er, ALU.mult, [LRE.b, EE.b], [V1.b])
                    tt("dve", V2[:], PI, ei, ALU.mult, [LIM.b, EE.b], [V2.b])
                    tt("dve", V1[:], V1[:], V2[:], ALU.subtract, [V1.b, V2.b], [V1.b])
                    tt("dve", V1[:], V1[:], Sv[:, 0, :, sc, :], ALU.add, [V1.b, SW.b], [V1.b])
                    tt("dve", V2[:], PR, ei, ALU.mult, [LRE.b, EE.b], [V2.b])
                    n_ = SC if sc < NSC - 1 else SC - 1
                    cp("act", SP[:, 0, qs, sc * SC + 1:sc * SC + 1 + n_], V1[:, :, 0:n_], [V1.b], [SP.b])
                    tt("dve", V1[:], PI, er, ALU.mult, [LIM.b, EE.b], [V1.b])
                    tt("dve", V2[:], V2[:], V1[:], ALU.add, [V1.b, V2.b], [V2.b])
                    tt("dve", V2[:], V2[:], Sv[:, 1, :, sc, :], ALU.add, [V2.b, SW.b], [V2.b])
                    cp("act", SP[:, 1, qs, sc * SC + 1:sc * SC + 1 + n_], V2[:, :, 0:n_], [V2.b], [SP.b])
                for sc in range(1, NSC):
                    cp("act", SP[:, :, qs, sc * SC], EE[:, :, qs, sc - 1], [EE.b], [SP.b])
            S.dma("sp", hre_p.rearrange("q x -> x q"), EE[:, 0, :, NSC - 1], reads=[EE.b], **SLOW)
            S.dma("sp", him_p.rearrange("q x -> x q"), EE[:, 1, :, NSC - 1], reads=[EE.b], **SLOW)
            GLUW = sb(pb, "GLUW", [128, 4, 512], BF16)
            S.dma("pool", GLUW[:], glu_w.rearrange("(c p) n -> p c n", p=128), writes=[GLUW.b])
            Y2 = sb(pb, "Y2", [128, NT], F32); Y3 = sb(pb, "Y3", [128, NT], F32)
            Gt = sb(pb, "Gt", [128, 4, NT], F32); Gb = sb(pb, "Gb", [128, 4, NT], BF16)
            MS = [sb(pb, "MS%d" % i, [128, 4, NT], BF16) for i in range(2)]
            CG = math.sqrt(2.0 / math.pi)
            NCT = NT // L
            for it in range(NTILES):
                t0 = it * NT
                c0 = it * NCT
                for gc in range(4):
                    bk = bank()
                    uv = uT[:, gc, t0:t0 + NT].rearrange("p (n l) -> p l n", l=L)
                    ov = bk[:].rearrange("p (n l) -> p l n", l=L)
                    for tp in range(L):
                        for sp_ in range(tp + 1):
                            mm(bk, BDT[:, gc, tp - sp_, :], uv[:, sp_, :], sp_ == 0, False, [BDT.b, uT.b],
                               out=ov[:, tp, :])
                        for j in range(4):
                            q16 = gc * 4 + j
                            for ri, CLt in enumerate((CLR, CLI)):
                                mm(bk, CLt[:, q16, tp, :], SP[:, ri, q16, c0:c0 + NCT], False,
                                   (j == 3 and ri == 1), [CLt.b, SP.b], out=ov[32 * j:32 * j + 32, tp, :],
                                   tile_position=(0, 32 * j))
                    act(Y2[:], bk[:], AF.Square, [bk.b], [Y2.b])
                    ts("dve", Y2[:], Y2[:], 0.044715, 1.0, ALU.mult, ALU.add, [Y2.b], [Y2.b])
                    tt("dve", Y3[:], Y2[:], bk[:], ALU.mult, [Y2.b, bk.b], [Y3.b])
                    act(Y3[:], Y3[:], AF.Sigmoid, [Y3.b], [Y3.b], scale=2.0 * CG)
                    tt("dve", Gt[:, gc, :], Y3[:], bk[:], ALU.mult, [Y3.b, bk.b], [Gt.b])
                    cp("act", Gb[:, gc, :], Gt[:, gc, :], [Gt.b], [Gb.b])
                M_ = MS[it % 2]
                for m in range(4):
                    bk = bank()
                    for kc in range(4):
                        mm(bk, GLUW[:, kc, m * 128:(m + 1) * 128], Gb[:, kc, :], kc == 0, kc == 3, [GLUW.b, Gb.b])
                    act(Y3[:], bk[:], AF.Sigmoid, [bk.b], [Y3.b])
                    tt("dve", M_[:, m, :], Y3[:], Gt[:, m, :], ALU.mult, [Y3.b, Gt.b], [M_.b])
                S.dma("sp", mix_s[it, :, 0:4, :], M_[:], reads=[M_.b])
            S.barrier()
            pb.close()
        s5.close()

        att = ExitStack()
        QT = sb(att, "QT", [128, 4, T], BF16)
        KT = sb(att, "KT", [128, 4, T], BF16)
        V = sb(att, "V", [128, T // 128, 512], BF16)
        with ExitStack() as ph:
          if 'A2' in PH:
            xnT2 = [sb(ph, "xnT%d" % i, [128, 8, NT], BF16) for i in range(2)]
            W = sb(ph, "Wqkv", [128, 8, 1536], BF16)
            sqA = [sb(ph, "sq%d" % i, [128, NT], BF16) for i in range(2)]
            rstdA = [sb(ph, "rstd%d" % i, [128, NT], F32) for i in range(2)]
            sq = sqA[0]; rstd = rstdA[0]
            knT = sb(ph, "knT", [128, 4, NT], F32)
            stg = sb(ph, "stg", [128, 4, 512], F32)
            S.dma("pool", W[:], w_in[:, 512:2048].rearrange("(c p) n -> p c n", p=128), writes=[W.b])
            S.dma("sp", xnT2[0][:], xn_s[0], writes=[xnT2[0].b])
            for it in range(NTILES):
                t0 = it * NT
                xnT = xnT2[it % 2]
                if it + 1 < NTILES:
                    S.dma("sp", xnT2[(it + 1) % 2][:], xn_s[it + 1], writes=[xnT2[(it + 1) % 2].b])
                for qk in range(2):
                    for m in range(4):
                        bk = bank()
                        for kc in range(8):
                            mm(bk, W[:, kc, qk * 512 + m * 128:qk * 512 + (m + 1) * 128], xnT[:, kc, :], kc == 0,
                               kc == 7, [W.b, xnT.b])
                        sq_t = sqA[m % 2]; rstd_t = rstdA[m % 2]
                        act(sq_t[:], bk[:], AF.Square, [bk.b], [sq_t.b])
                        b2 = bank()
                        mm(b2, BD64[:], sq_t[:], True, True, [BD64.b, sq_t.b])
                        act(rstd_t[:], b2[:], AF.Sqrt, [b2.b], [rstd_t.b], bias=EPS)
                        recip(rstd_t[:], rstd_t[:], [rstd_t.b], [rstd_t.b])
                        if qk == 0:
                            stt(QT[:, m, t0:t0 + NT], bk[:], QG[:, 0:1], rstd_t[:], ALU.mult, ALU.mult,
                                [bk.b, QG.b, rstd_t.b], [QT.b])
                        else:
                            stt(knT[:, m, :], bk[:], KG[:, 0:1], rstd_t[:], ALU.mult, ALU.mult,
                                [bk.b, KG.b, rstd_t.b], [knT.b])
                            cp("act", KT[:, m, t0:t0 + NT], knT[:, m, :], [knT.b], [KT.b])
                store_tm(knT, 4, NT, k_p[t0:t0 + NT, :], stg)
                for j in range(4):
                    bk = bank()
                    for kc in range(8):
                        mm(bk, xnT[:, kc, j * 128:(j + 1) * 128], W[:, kc, 1024:1536], kc == 0, kc == 7,
                           [W.b, xnT.b])
                    cp("act", stg[:, j, :], bk[:], [bk.b], [stg.b])
                    cp("dve", V[:, it * 4 + j, :], bk[:], [bk.b], [V.b])
                S.dma("sp", v_p[t0:t0 + NT, :].rearrange("(j p) f -> p j f", p=128), stg[:], reads=[stg.b])
            if 'S' in PH:
                for qk in range(2):
                    for m in range(4):
                        bk = bank()
                        for kc in range(8):
                            mm(bk, W[:, kc, qk * 512 + m * 128:qk * 512 + (m + 1) * 128], xnTs[:, kc, :], kc == 0,
                               kc == 7, [W.b, xnTs.b], out=bk[:, 0:TS])
                        act(sq[:, 0:TS], bk[:, 0:TS], AF.Square, [bk.b], [sq.b])
                        b2 = bank()
                        mm(b2, BD64[:], sq[:, 0:TS], True, True, [BD64.b, sq.b], out=b2[:, 0:TS])
                        act(rstd[:, 0:TS], b2[:, 0:TS], AF.Sqrt, [b2.b], [rstd.b], bias=EPS)
                        recip(rstd[:, 0:TS], rstd[:, 0:TS], [rstd.b], [rstd.b])
                        if qk == 0:
                            stt(QTs[:, m, :], bk[:, 0:TS], QG[:, 0:1], rstd[:, 0:TS], ALU.mult, ALU.mult,
                                [bk.b, QG.b, rstd.b], [QTs.b])
                        else:
                            stt(knT[:, m, 0:TS], bk[:, 0:TS], KG[:, 0:1], rstd[:, 0:TS], ALU.mult, ALU.mult,
                                [bk.b, KG.b, rstd.b], [knT.b])
                            cp("act", KTs[:, m, :], knT[:, m, 0:TS], [knT.b], [KTs.b])
                store_tm(knT, 4, TS, k_s, stg)
                VN = sb(ph, "VN", [4, NS, 4, 132], BF16)
                memset("dve", VN[:], 1.0, [VN.b])
                for sq_ in range(NS):
                    bk = bank()
                    for kc in range(8):
                        mm(bk, xnTs[:, kc, sq_ * 4:(sq_ + 1) * 4], W[:, kc, 1024:1536], kc == 0, kc == 7,
                           [W.b, xnTs.b], out=bk[0:4, :])
                    cp("act", stg[0:4, sq_ % 4, :], bk[0:4, :], [bk.b], [stg.b])
                    cp("dve", VN[:, sq_, :, 0:128], bk[0:4, :].rearrange("p (h d) -> p h d", d=128), [bk.b], [VN.b])
                    if sq_ % 4 == 3:
                        sg = sq_ // 4
                        S.dma("sp", v_s.rearrange("(s t) f -> t s f", t=4)[:, sg * 4:(sg + 1) * 4, :], stg[0:4, :, :],
                              reads=[stg.b])
                S.dma("sp", vn_s, VN[:, :, :, :].rearrange("p s h d -> p (s h d)"), reads=[VN.b])
            S.barrier()

        with ExitStack() as ph:
          if 'W' in PH and 'C' not in PH:
            for _ in convert_ffn_weights(ph):
                pass
            S.barrier()
          if 'C' in PH:
            cgen = convert_ffn_weights(ph) if 'W' in PH else iter(())
            PT = [sb(ph, "PT%d" % i, [128, NT], BF16) for i in range(6)]
            BIAS = sb(ph, "BIAS", [128, 4, 34], F32)
            for h in range(4):
                for bi in range(1, 34):
                    ts("dve", BIAS[:, h, bi:bi + 1], KPOS[:, 0:1], float(1 - 128 * bi), SLOPES[h], ALU.add, ALU.mult,
                       [KPOS.b], [BIAS.b])
            On = [sb(ph, "On%d" % i, [128, NT], F32) for i in range(2)]
            rden = [sb(ph, "rden%d" % i, [128, NT], F32) for i in range(2)]
            sq = sb(ph, "sqa", [128, NT], BF16); rstd = sb(ph, "rstda", [128, NT], F32)
            AO = [sb(ph, "AO%d" % i, [128, 4, NT], BF16) for i in range(2)]
            ptrr = [0]
            units = []
            for it in range(NTILES):
                for h in range(4):
                    nkb = (it * NT + NT) // 128
                    for kb in range(nkb):
                        units.append((it, h, kb, nkb))
            Ob = [PS[0], PS[1]]
            Db = [PS[2], PS[3]]

            def sbanks(ui):
                return [PS[4 + (ui % 2) * 2], PS[5 + (ui % 2) * 2]]

            def emit_qk(ui):
                it, h, kb, nkb = units[ui]
                q0 = it * NT
                k0 = kb * 128
                c0 = max(0, (k0 - q0) // 128) * 128
                Sb = sbanks(ui)
                for mp in range(2):
                    pr = slice(mp * 64, (mp + 1) * 64)
                    mm(Sb[mp], KT[pr, h, k0:k0 + 128], QT[pr, h, q0 + c0:q0 + NT], True, True,
                       [KT.b, QT.b], out=Sb[mp][:, c0:NT])

            def emit_rest(ui):
                it, h, kb, nkb = units[ui]
                q0 = it * NT
                k0 = kb * 128
                c0 = max(0, (k0 - q0) // 128) * 128
                slope = SLOPES[h]
                wq = 256 if slope * 511 > 64 else 512
                Sb = sbanks(ui)
                Ps = []
                for mp in range(2):
                    P = PT[ptrr[0] % len(PT)]; ptrr[0] += 1
                    Ps.append(P)
                    for g0 in range((c0 // wq) * wq, NT, wq):
                        lo = max(g0, c0)
                        hi = g0 + wq
                        bi = (q0 + hi - k0) // 128
                        act(P[:, lo:hi], Sb[mp][:, lo:hi], AF.Exp, [Sb[mp].b, BIAS.b], [P.b],
                            scale=0.125, bias=BIAS[:, h, bi:bi + 1])
                    if k0 >= q0:
                        tt("dve", P[:, c0:c0 + 128], P[:, c0:c0 + 128], CAUS[:], ALU.mult, [P.b, CAUS.b],
                           [P.b])
                for mp in range(2):
                    P = Ps[mp]
                    mm(Ob[mp], V[:, kb, h * 128:(h + 1) * 128], P[:, c0:NT], kb == 0, kb == nkb - 1,
                       [V.b, P.b], out=Ob[mp][:, c0:NT])
                    mm(Db[mp], ONE1[:], P[:, c0:NT], kb == 0, kb == nkb - 1, [ONE1.b, P.b],
                       out=Db[mp][:, c0:NT])

            def emit_epi(ui):
                it, h, kb, nkb = units[ui]
                for mp in range(2):
                    recip(rden[mp][:], Db[mp][:], [Db[mp].b], [rden[mp].b])
                    tt("dve", On[mp][:], Ob[mp][:], rden[mp][:], ALU.mult, [Ob[mp].b, rden[mp].b], [On[mp].b])
                stt(On[0][:], On[1][:], NLAM[:, 0:1], On[0][:], ALU.mult, ALU.add, [On[0].b, On[1].b, NLAM.b],
                    [On[0].b])
                act(sq[:], On[0][:], AF.Square, [On[0].b], [sq.b])
                b2 = sbanks(ui)[0]
                mm(b2, ON128[:], sq[:], True, True, [ON128.b, sq.b])
                act(rstd[:], b2[:], AF.Sqrt, [b2.b], [rstd.b], bias=EPS)
                recip(rstd[:], rstd[:], [rstd.b], [rstd.b])
                stt(AO[it % 2][:, h, :], On[0][:], SUBG[:, 0:1], rstd[:], ALU.mult, ALU.mult,
                    [On[0].b, SUBG.b, rstd.b], [AO[it % 2].b])
                if h == 3:
                    S.dma("sp", mix_s[it, :, 4:8, :], AO[it % 2][:], reads=[AO[it % 2].b])

            emit_qk(0)
            for ui in range(len(units)):
                if ui + 1 < len(units):
                    emit_qk(ui + 1)
                emit_rest(ui)
                if units[ui][2] == units[ui][3] - 1:
                    emit_epi(ui)
                if ui % 16 == 8:
                    next(cgen, None)
            for _ in cgen:
                pass
            S.barrier()
        att.close()


        with ExitStack() as ph:
          if 'S' in PH and 'CS' in PH:
            NKB = 8
            VN = sb(ph, "VNc", [4, NS, 4, 132], BF16)
            S.dma("sp", VN[:, :, :, :].rearrange("p s h d -> p (s h d)"), vn_s, writes=[VN.b])
            PTI = sb(ph, "PTI", [128, NS * 16], I32)
            IDX = sb(ph, "IDX", [128, NS * 16], U32)
            KPB = [sb(ph, "KPB%d" % i, [128, 512], F32) for i in range(NKB)]
            VPF = [sb(ph, "VPF%d" % i, [128, 512], F32) for i in range(NKB)]
            VPB = [sb(ph, "VPB%d" % i, [128, 4, 132], BF16) for i in range(32)]
            KpT = [sb(ph, "KpT%d" % i, [128, 4, 128], BF16) for i in range(2)]
            QB = sb(ph, "QB", [128, 4, NS, 2, 4], BF16)
            BIASS = sb(ph, "BIASS", [128, 16, 4, 8], F32)
            BIASN = sb(ph, "BIASN", [4, 4, 8], F32)
            TMPs = sb(ph, "TMPs", [128, 512], F32)
            PBs = [sb(ph, "PBs%d" % i, [128, 512], BF16) for i in range(2)]
            TNs = sb(ph, "TNs", [4, 32], F32); PNs = sb(ph, "PNs", [4, 32], BF16)
            RD = sb(ph, "RDs", [8, 4], F32)
            ONs = sb(ph, "ONs", [8, 4, 128], F32)
            CMB = sb(ph, "CMB", [8, 4], F32)
            OD4 = sb(ph, "OD4", [4, 512], F32)
            ODT = sb(ph, "ODT", [128, 4, TS], F32)
            sqs = sb(ph, "sqs", [128, TS], BF16); rstds = sb(ph, "rstds", [128, TS], F32)
            S.dma("sp", PTI[:], ptab.rearrange("s g -> (s g)").rearrange("(o n) -> o n", o=1).to_broadcast([128, NS * 16]),
                  writes=[PTI.b], **SLOW)
            ts("dve", IDX[:], PTI[:], 128.0, KPOS[:, 0:1], ALU.mult, ALU.add, [PTI.b, KPOS.b], [IDX.b])
            memset("dve", QB[:], 0.0, [QB.b])
            for mp in range(2):
                pr = slice(mp * 64, (mp + 1) * 64)
                cp("dve", QB[pr, :, :, mp, :], QTs[pr, :, :].rearrange("p h (s t) -> p h s t", t=4), [QTs.b], [QB.b])
            for pg in range(16):
                for h in range(4):
                    ts("dve", BIASS[:, pg, h, :], KPOS[:, 0:1].to_broadcast([128, 8]), float(pg * 128 - 2048), SLOPES[h],
                       ALU.add, ALU.mult, [KPOS.b], [BIASS.b])
            for h in range(4):
                ts("dve", BIASN[:, h, :], KPOS[0:4, 0:1].to_broadcast([4, 8]), SLOPES[h], None, ALU.mult, None, [KPOS.b],
                   [BIASN.b])
            for i in range(32):
                memset("dve", VPB[i][:], 1.0, [VPB[i].b])
            stt(CMB[:], IDF[0:8, 4:8], NLAM[0:8, 0:1], IDF[0:8, 0:4], ALU.mult, ALU.add, [IDF.b, NLAM.b], [CMB.b])
            ID4 = TB(IDF.t[0:4, 0:4], IDF.b)
            kcnt = 0
            ck_rows = cache_k
            cv_rows = cache_v
            for sq_ in range(NS):
                SBk = PS[sq_ % 2]
                PB_ = PBs[sq_ % 2]
                Ob = [PS[2], PS[3], PS[4], PS[5]]
                pend = []
                for pg in range(16):
                    Kp = KPB[kcnt % NKB]; Vf = VPF[kcnt % NKB]; Vp = VPB[kcnt % 32]; kcnt += 1
                    col = sq_ * 16 + pg
                    S.dmaf("pool", (lambda e, Kp=Kp, col=col: e.indirect_dma_start(
                        out=Kp[:, :], out_offset=None, in_=ck_rows,
                        in_offset=bass.IndirectOffsetOnAxis(ap=IDX[:, col:col + 1], axis=0))),
                        reads=[IDX.b], writes=[Kp.b])
                    S.dmaf("pool", (lambda e, Vf=Vf, col=col: e.indirect_dma_start(
                        out=Vf[:, :], out_offset=None, in_=cv_rows,
                        in_offset=bass.IndirectOffsetOnAxis(ap=IDX[:, col:col + 1], axis=0))),
                        reads=[IDX.b], writes=[Vf.b])
                    bkT = PS[6 + (pg % 2)]
                    KT_ = KpT[pg % 2]
                    for h in range(4):
                        tr(bkT, bkT[:, h * 128:(h + 1) * 128], Kp[:, h * 128:(h + 1) * 128], IDF, [Kp.b])
                    cp("act", KT_[:, :, :], bkT[:].rearrange("p (h n) -> p h n", n=128), [bkT.b], [KT_.b])
                    for h in range(4):
                        mm(SBk, KT_[:, h, :], QB[:, h, sq_, :, :].rearrange("p a b -> p (a b)"), True, True,
                           [KT_.b, QB.b], out=SBk[:, (pg * 4 + h) * 8:(pg * 4 + h + 1) * 8])
                    cp("dve", Vp[:, :, 0:128], Vf[:, :].rearrange("p (h d) -> p h d", d=128), [Vf.b], [Vp.b])
                    pend.append(Vp)
                NB = PS[6]
                for h in range(4):
                    mm(NB, KTs[:, h, sq_ * 4:(sq_ + 1) * 4], QB[:, h, sq_, :, :].rearrange("p a b -> p (a b)"), True, True,
                       [KTs.b, QB.b], out=NB[0:4, h * 8:(h + 1) * 8])
                stt(TMPs[:], SBk[:], 0.125, BIASS[:, :, :, :].rearrange("p a b c -> p (a b c)"), ALU.mult, ALU.add,
                    [SBk.b, BIASS.b], [TMPs.b])
                act(PB_[:], TMPs[:], AF.Exp, [TMPs.b], [PB_.b])
                stt(TNs[:], NB[0:4, 0:32], 0.125, BIASN[:, :, :].rearrange("p a b -> p (a b)"), ALU.mult, ALU.add,
                    [NB.b, BIASN.b], [TNs.b])
                act(TNs[:], TNs[:], AF.Exp, [TNs.b], [TNs.b])
                tt("dve", PNs[:, :].rearrange("p (a t) -> p a t", t=4), TNs[:, :].rearrange("p (a t) -> p a t", t=4),
                   CAUS[0:4, 0:4].unsqueeze(1).to_broadcast([4, 8, 4]), ALU.mult, [TNs.b, CAUS.b], [PNs.b])
                for pg in range(16):
                    Vp = pend[pg]
                    for h in range(4):
                        mm(Ob[h], PB_[:, (pg * 4 + h) * 8:(pg * 4 + h + 1) * 8], Vp[:, h, 0:129], pg == 0, False,
                           [PB_.b, Vp.b], out=Ob[h][0:8, 0:129])
                for h in range(4):
                    mm(Ob[h], PNs[0:4, h * 8:(h + 1) * 8], VN[0:4, sq_, h, 0:129], False, True, [PNs.b, VN.b],
                       out=Ob[h][0:8, 0:129])
                for h in range(4):
                    recip(RD[:, h:h + 1], Ob[h][0:8, 128:129], [Ob[h].b], [RD.b])
                    ts("dve", ONs[:, h, :], Ob[h][0:8, 0:128], RD[:, h:h + 1], None, ALU.mult, None, [Ob[h].b, RD.b],
                       [ONs.b])
                DBk = PS[7]
                mm(DBk, CMB[:, :], ONs[:, :, :].rearrange("p h d -> p (h d)"), True, True, [CMB.b, ONs.b],
                   out=DBk[0:4, :])
                cp("act", OD4[:], DBk[0:4, :], [DBk.b], [OD4.b])
                TBk = PS[6]
                for h in range(4):
                    tr(TBk, TBk[:, h * 4:(h + 1) * 4], OD4[0:4, h * 128:(h + 1) * 128], ID4, [OD4.b])
                cp("act", ODT[:, :, sq_ * 4:(sq_ + 1) * 4], TBk[:, 0:16].rearrange("p (h t) -> p h t", t=4), [TBk.b],
                   [ODT.b])
            for h in range(4):
                act(sqs[:], ODT[:, h, :], AF.Square, [ODT.b], [sqs.b])
                b2 = PS[7]
                mm(b2, ON128[:], sqs[:], True, True, [ON128.b, sqs.b], out=b2[:, 0:TS])
                act(rstds[:], b2[:, 0:TS], AF.Sqrt, [b2.b], [rstds.b], bias=EPS)
                recip(rstds[:], rstds[:], [rstds.b], [rstds.b])
                stt(MIXs[:, 4 + h, :], ODT[:, h, :], SUBG[:, 0:1], rstds[:], ALU.mult, ALU.mult,
                    [ODT.b, SUBG.b, rstds.b], [MIXs.b])
            S.barrier()

        with ExitStack() as ph:
          if 'D' in PH:
            XIN = sb(ph, "XIN", [128, 4, 1024], F32)
            xTs_ = [sb(ph, "xT%d" % i, [128, 8, NT], F32) for i in range(2)]
            MIX = sb(ph, "MIX", [128, 8, NT], BF16)
            A = sb(ph, "actA", [128, 8, NT], BF16)
            B3 = [sb(ph, "actB%d" % i, [128, 8, NT], BF16) for i in range(2)]
            Wo = sb(ph, "Wo", [128, 8, 1024], BF16); Wq = sb(ph, "Wq", [128, 8, 1024], BF16)
            Wo2 = sb(ph, "Wo2", [128, 8, 1024], BF16)
            sq2 = [sb(ph, "sqd%d" % i, [128, NT], BF16) for i in range(2)]; rstd = sb(ph, "rstdd", [128, NT], F32)
            Pm = [sb(ph, "Pm%d" % i, [128, NT], BF16) for i in range(2)]
            hT = sb(ph, "hT", [128, NFC, NT], BF16)
            HGs_ = [sb(ph, "HG%d" % i, [128, NT + 2], F32) for i in range(2)]; CARRY = sb(ph, "CARRY", [128, NFC, 2], F32)
            cvs_ = [sb(ph, "cv%d" % i, [128, NT], F32) for i in range(2)]
            rden = sb(ph, "rdend", [128, NT], F32)
            WG = [sb(ph, "WG%d" % i, [128, 8, 128], BF16) for i in range(2)]
            WV = [sb(ph, "WV%d" % i, [128, 8, 128], BF16) for i in range(2)]
            WD = [sb(ph, "WD%d" % i, [128, NFC, 128], BF16) for i in range(2)]
            YST = sb(ph, "YST", [128, 1024], F32)
            S.dma("pool", Wo[:], w_out.rearrange("(c p) n -> p c n", p=128), writes=[Wo.b])
            S.dma("pool", Wq[:], ca_wq.rearrange("(c p) n -> p c n", p=128), writes=[Wq.b])
            S.dma("pool", Wo2[:], ca_wo.rearrange("(c p) n -> p c n", p=128), writes=[Wo2.b])
            memset("dve", CARRY[:], 0.0, [CARRY.b])
            if 'B' not in PH or os.environ.get('KNOB'):
                memset("dve", hT[:, 0:4, :], 0.0, [hT.b])
                for it in range(NTILES):
                    S.dma("sp", mix_s[it, :, 0:4, :], hT[:, 0:4, :], reads=[hT.b])
                S.barrier()
            frr = [0]; grr = [0]

            def fbank():
                b = PS[frr[0] % 4]; frr[0] += 1
                return b

            def gbank():
                b = PS[4 + grr[0] % 4]; grr[0] += 1
                return b

            def d_loads(it):
                t0 = it * NT
                S.dma("pool", MIX[:], mix_s[it], writes=[MIX.b])
                S.dma("pool", XIN[:], xp[t0:t0 + NT, :].rearrange("(j p) f -> p j f", p=128), writes=[XIN.b])

            def ln_gen(xT, G, out):
                bk = fbank()
                for c in range(8):
                    act(sq2[c % 2][:], xT[:, c, :], AF.Square, [xT.b], [sq2[c % 2].b])
                    if c >= 1:
                        mm(bk, ON1024[:], sq2[(c - 1) % 2][:], c == 1, False, [ON1024.b, sq2[(c - 1) % 2].b])
                    yield
                mm(bk, ON1024[:], sq2[1][:], False, True, [ON1024.b, sq2[1].b])
                act(rstd[:], bk[:], AF.Sqrt, [bk.b], [rstd.b], bias=EPS)
                recip(rstd[:], rstd[:], [rstd.b], [rstd.b])
                yield
                for c in range(8):
                    stt(out[:, c, :], xT[:, c, :], G[:, c:c + 1], rstd[:], ALU.mult, ALU.mult,
                        [xT.b, G.b, rstd.b], [out.b])
                    if c % 4 == 3:
                        yield

            def front(it):
                xT = xTs_[it % 2]; Bb = B3[it % 2]
                for c in range(8):
                    bk = fbank()
                    for j in range(4):
                        tr(bk, bk[:, j * 128:(j + 1) * 128], XIN[:, j, c * 128:(c + 1) * 128], IDF, [XIN.b])
                    cp("act", xT[:, c, :], bk[:], [bk.b], [xT.b])
                    if c % 4 == 3:
                        yield
                for m in range(8):
                    bk = fbank()
                    for kc in range(8):
                        mm(bk, Wo[:, kc, m * 128:(m + 1) * 128], MIX[:, kc, :], kc == 0, kc == 7, [Wo.b, MIX.b])
                    tt("dve", xT[:, m, :], bk[:], xT[:, m, :], ALU.add, [bk.b, xT.b], [xT.b])
                    if m % 2 == 1:
                        yield
                yield from ln_gen(xT, G2, A)
                for h in range(4):
                    bq = [fbank(), fbank()]
                    for dh in range(2):
                        for kc in range(8):
                            mm(bq[dh], Wq[:, kc, (2 * h + dh) * 128:(2 * h + dh + 1) * 128], A[:, kc, :], kc == 0,
                               kc == 7, [Wq.b, A.b])
                        act(sq2[dh][:], bq[dh][:], AF.Square, [bq[dh].b], [sq2[dh].b])
                    yield
                    b2 = fbank()
                    for dh in range(2):
                        mm(b2, ON256[:], sq2[dh][:], dh == 0, dh == 1, [ON256.b, sq2[dh].b])
                    act(rstd[:], b2[:], AF.Sqrt, [b2.b], [rstd.b], bias=EPS)
                    recip(rstd[:], rstd[:], [rstd.b], [rstd.b])
                    for dh in range(2):
                        stt(Bb[:, 2 * h + dh, :], bq[dh][:], CQG[:, dh:dh + 1], rstd[:], ALU.mult, ALU.mult,
                            [bq[dh].b, CQG.b, rstd.b], [Bb.b])
                    yield
                for h in range(4):
                    for mb in range(2):
                        bs = fbank()
                        for dh in range(2):
                            mm(bs, MKT[:, 2 * h + dh, mb * 128:(mb + 1) * 128], Bb[:, 2 * h + dh, :], dh == 0, dh == 1,
                               [MKT.b, Bb.b])
                        act(Pm[mb][:], bs[:], AF.Exp, [bs.b], [Pm[mb].b], scale=1.0 / 16)
                    yield
                    bd = fbank()
                    for mb in range(2):
                        mm(bd, ONE1[:], Pm[mb][:], mb == 0, mb == 1, [ONE1.b, Pm[mb].b])
                    recip(rden[:], bd[:], [bd.b], [rden.b])
                    for dvc in range(2):
                        bo = fbank()
                        for mb in range(2):
                            mm(bo, MV[:, mb, (2 * h + dvc) * 128:(2 * h + dvc + 1) * 128], Pm[mb][:], mb == 0, mb == 1,
                               [MV.b, Pm[mb].b])
                        tt("dve", A[:, 2 * h + dvc, :], bo[:], rden[:], ALU.mult, [bo.b, rden.b], [A.b])
                    yield
                for m in range(8):
                    bk = fbank()
                    for kc in range(8):
                        mm(bk, Wo2[:, kc, m * 128:(m + 1) * 128], A[:, kc, :], kc == 0, kc == 7, [Wo2.b, A.b])
                    tt("dve", xT[:, m, :], bk[:], xT[:, m, :], ALU.add, [bk.b, xT.b], [xT.b])
                    if m % 2 == 1:
                        yield
                yield from ln_gen(xT, G3, Bb)

            def ffn(it):
                t0 = it * NT
                xT = xTs_[it % 2]; Bb = B3[it % 2]
                for fc in range(NFC):
                    wg = WG[fc % 2]; wv = WV[fc % 2]; HG = HGs_[fc % 2]; cv = cvs_[fc % 2]
                    S.dma("sp", wg[:], wg_s[:, fc], writes=[wg.b])
                    S.dma("sp", wv[:], wv_s[:, fc], writes=[wv.b])
                    if fc == NFC - 4:
                        for m in range(2):
                            S.dma("sp", WD[m][:], wd_s[:, m], writes=[WD[m].b])
                    bg = gbank(); bv = gbank()
                    for kc in range(8):
                        mm(bg, wg[:, kc, :], Bb[:, kc, :], kc == 0, kc == 7, [wg.b, Bb.b])
                    for kc in range(8):
                        mm(bv, wv[:, kc, :], Bb[:, kc, :], kc == 0, kc == 7, [wv.b, Bb.b])
                    cp("dve", HG[:, 0:2], CARRY[:, fc, :], [CARRY.b], [HG.b])
                    cp("act", HG[:, 2:NT + 2], bg[:], [bg.b], [HG.b])
                    cp("dve", CARRY[:, fc, :], HG[:, NT:NT + 2], [HG.b], [CARRY.b])
                    act(cv[:], HG[:, 2:NT + 2], AF.Identity, [HG.b, CW.b, CB.b], [cv.b], scale=CW[:, 2, fc:fc + 1],
                        bias=CB[:, fc:fc + 1])
                    stt(cv[:], HG[:, 1:NT + 1], CW[:, 1, fc:fc + 1], cv[:], ALU.mult, ALU.add, [HG.b, CW.b, cv.b],
                        [cv.b])
                    stt(cv[:], HG[:, 0:NT], CW[:, 0, fc:fc + 1], cv[:], ALU.mult, ALU.add, [HG.b, CW.b, cv.b], [cv.b])
                    act(cv[:], cv[:], AF.Silu, [cv.b], [cv.b])
                    tt("dve", hT[:, fc, :], cv[:], bv[:], ALU.mult, [cv.b, bv.b], [hT.b])
                    yield
                for m in range(8):
                    wd = WD[m % 2]
                    if m >= 2:
                        S.dma("sp", wd[:], wd_s[:, m], writes=[wd.b])
                    bk = gbank()
                    for fc in range(NFC):
                        mm(bk, wd[:, fc, :], hT[:, fc, :], fc == 0, fc == NFC - 1, [wd.b, hT.b])
                    tt("dve", xT[:, m, :], bk[:], xT[:, m, :], ALU.add, [bk.b, xT.b], [xT.b])
                    yield
                for j in range(4):
                    for c0 in (0, 4):
                        bk = gbank()
                        for c in range(c0, c0 + 4):
                            tr(bk, bk[:, (c - c0) * 128:(c - c0 + 1) * 128], xT[:, c, j * 128:(j + 1) * 128], IDF,
                               [xT.b])
                        cp("act", YST[:, c0 * 128:(c0 + 4) * 128], bk[:], [bk.b], [YST.b])
                    S.dma("pool", y_p[t0 + j * 128:t0 + (j + 1) * 128, :], YST[:], reads=[YST.b])
                    yield

            def drain(g, n):
                for _ in range(n):
                    try:
                        next(g)
                    except StopIteration:
                        return False
                return True

            d_loads(0)
            drain(front(0), 10 ** 6)
            for it in range(NTILES):
                nxt = None
                if it + 1 < NTILES:
                    d_loads(it + 1)
                    nxt = front(it + 1)
                for _ in ffn(it):
                    if nxt is not None:
                        drain(nxt, 1)
                if nxt is not None:
                    drain(nxt, 10 ** 6)
            for j in range(2):
                S.dma("sp", conv_p[j].rearrange("(c p) -> p c", p=128), CARRY[:, :, j], reads=[CARRY.b], **SLOW)
            S.barrier()

        with ExitStack() as ph:
          if 'S' in PH and 'DS' in PH:
            X = sb(ph, "Xds", [128, 1, 1024], F32)
            xT = sb(ph, "xTs", [128, 8, TS], F32)
            A = sb(ph, "actAs", [128, 8, TS], BF16); Bb = sb(ph, "actBs", [128, 8, TS], BF16)
            Wo = sb(ph, "Wos", [128, 8, 1024], BF16); Wq = sb(ph, "Wqs", [128, 8, 1024], BF16)
            Wo2 = sb(ph, "Wo2s", [128, 8, 1024], BF16)
            sq = sb(ph, "sqds", [128, 2, TS], BF16); rstd = sb(ph, "rstdds", [128, TS], F32)
            hT = sb(ph, "hTs", [128, NFC, TS], BF16)
            WG = [sb(ph, "WGs%d" % i, [128, 8, 128], BF16) for i in range(2)]
            WV = [sb(ph, "WVs%d" % i, [128, 8, 128], BF16) for i in range(2)]
            WD = [sb(ph, "WDs%d" % i, [128, NFC, 128], BF16) for i in range(2)]
            S.dma("pool", Wo[:], w_out.rearrange("(c p) n -> p c n", p=128), writes=[Wo.b])
            S.dma("pool", Wq[:], ca_wq.rearrange("(c p) n -> p c n", p=128), writes=[Wq.b])
            S.dma("pool", Wo2[:], ca_wo.rearrange("(c p) n -> p c n", p=128), writes=[Wo2.b])
            wrr = 0
            n = TS
            CMKts = [sb(ph, "CMKt%d" % i, [128, 2, 1024], F32) for i in range(2)]
            CMVb = [sb(ph, "CMVb%d" % i, [128, 2, 4, 260], BF16) for i in range(2)]
            MKs = sb(ph, "MKs", [128, 8, 256], BF16)
            PCs = sb(ph, "PCs", [128, 32], BF16)
            RDc = sb(ph, "RDc", [4, 4], F32)
            COn = sb(ph, "COn", [4, 4, 256], F32)
            SCt = sb(ph, "SCt", [32, F], F32)
            SCV = sb(ph, "SCV", [128, NFC, 32], F32)
            CVS = sb(ph, "CVS", [128, NFC, 32], F32)
            HGs = sb(ph, "HGs", [128, NS, 6], F32)
            cvs = sb(ph, "cvs", [128, NS, 4], F32)
            ID4 = TB(IDF.t[0:4, 0:4], IDF.b); ID32 = TB(IDF.t[0:32, 0:32], IDF.b)
            for i in range(2):
                memset("dve", CMVb[i][:], 1.0, [CMVb[i].b])
            S.dma("sp", SCt[:], st_conv, writes=[SCt.b])
            for f0 in range(0, NFC, 16):
                bk = bank()
                for fc in range(f0, min(f0 + 16, NFC)):
                    tr(bk, bk[:, (fc - f0) * 32:(fc - f0 + 1) * 32], SCt[0:32, fc * 128:(fc + 1) * 128], ID32, [SCt.b])
                nf = min(f0 + 16, NFC) - f0
                cp("act", SCV[:, f0:f0 + nf, :], bk[:, 0:nf * 32].rearrange("p (f x) -> p f x", x=32), [bk.b], [SCV.b])
            for c in range(8):
                cp("dve", xT[:, c, 0:n], xsT[:, c, :], [xsT.b], [xT.b])
            for m in range(8):
                bk = bank()
                for kc in range(8):
                    mm(bk, Wo[:, kc, m * 128:(m + 1) * 128], MIXs[:, kc, :], kc == 0, kc == 7, [Wo.b, MIXs.b],
                       out=bk[:, 0:n])
                tt("dve", xT[:, m, 0:n], bk[:, 0:n], xT[:, m, 0:n], ALU.add, [bk.b, xT.b], [xT.b])
            ln_fm(xT, G2, A, n, sq, rstd)
            for h in range(4):
                bq = [bank(), bank()]
                for dh in range(2):
                    for kc in range(8):
                        mm(bq[dh], Wq[:, kc, (2 * h + dh) * 128:(2 * h + dh + 1) * 128], A[:, kc, 0:n], kc == 0,
                           kc == 7, [Wq.b, A.b], out=bq[dh][:, 0:n])
                b2 = bank()
                for dh in range(2):
                    act(sq[:, dh, 0:n], bq[dh][:, 0:n], AF.Square, [bq[dh].b], [sq.b])
                    mm(b2, ON256[:], sq[:, dh, 0:n], dh == 0, dh == 1, [ON256.b, sq.b], out=b2[:, 0:n])
                act(rstd[:, 0:n], b2[:, 0:n], AF.Sqrt, [b2.b], [rstd.b], bias=EPS)
                recip(rstd[:, 0:n], rstd[:, 0:n], [rstd.b], [rstd.b])
                for dh in range(2):
                    stt(Bb[:, 2 * h + dh, 0:n], bq[dh][:, 0:n], CQG[:, dh:dh + 1], rstd[:, 0:n], ALU.mult, ALU.mult,
                        [bq[dh].b, CQG.b, rstd.b], [Bb.b])
            for sq_ in range(NS):
                CMV_ = CMVb[sq_ % 2]; CMKt = CMKts[sq_ % 2]
                S.dma("sp", CMKt[:], cmk[sq_].rearrange("(mb p) f -> p mb f", p=128), writes=[CMKt.b])
                for mb in range(2):
                    S.dma("pool", CMV_[:, mb, :, 0:256], cmv[sq_, mb * 128:(mb + 1) * 128, :].rearrange(
                        "p (h d) -> p h d", d=256), writes=[CMV_.b])
                for mb in range(2):
                    for c0 in (0, 4):
                        bk = bank()
                        for c8 in range(c0, c0 + 4):
                            tr(bk, bk[:, (c8 - c0) * 128:(c8 - c0 + 1) * 128], CMKt[:, mb, c8 * 128:(c8 + 1) * 128], IDF,
                               [CMKt.b])
                        cp("act" if c0 == 0 else "dve", MKs[:, c0:c0 + 4, mb * 128:(mb + 1) * 128],
                           bk[:].rearrange("p (c n) -> p c n", n=128), [bk.b], [MKs.b])
                SBc = bank()
                for mb in range(2):
                    for h in range(4):
                        for dh in range(2):
                            mm(SBc, MKs[:, 2 * h + dh, mb * 128:(mb + 1) * 128], Bb[:, 2 * h + dh, sq_ * 4:(sq_ + 1) * 4],
                               dh == 0, dh == 1, [MKs.b, Bb.b], out=SBc[:, (mb * 4 + h) * 4:(mb * 4 + h + 1) * 4])
                act(PCs[:], SBc[:, 0:32], AF.Exp, [SBc.b], [PCs.b], scale=1.0 / 16)
                Oc = [bank(), bank(), bank(), bank()]
                for h in range(4):
                    for mb in range(2):
                        mm(Oc[h], PCs[:, (mb * 4 + h) * 4:(mb * 4 + h + 1) * 4], CMV_[:, mb, h, 0:257], mb == 0, mb == 1,
                           [PCs.b, CMV_.b], out=Oc[h][0:4, 0:257])
                for h in range(4):
                    recip(RDc[:, h:h + 1], Oc[h][0:4, 256:257], [Oc[h].b], [RDc.b])
                    ts("dve", COn[:, h, :], Oc[h][0:4, 0:256], RDc[:, h:h + 1], None, ALU.mult, None,
                       [Oc[h].b, RDc.b], [COn.b])
                TBk = bank()
                cof = COn[:, :, :].rearrange("p h d -> p (h d)")
                for c8 in range(8):
                    tr(TBk, TBk[:, c8 * 4:(c8 + 1) * 4], cof[0:4, c8 * 128:(c8 + 1) * 128], ID4, [COn.b])
                cp("act", A[:, :, sq_ * 4:(sq_ + 1) * 4], TBk[:, 0:32].rearrange("p (c t) -> p c t", t=4), [TBk.b],
                   [A.b])
            for m in range(8):
                bk = bank()
                for kc in range(8):
                    mm(bk, Wo2[:, kc, m * 128:(m + 1) * 128], A[:, kc, 0:n], kc == 0, kc == 7, [Wo2.b, A.b],
                       out=bk[:, 0:n])
                tt("dve", xT[:, m, 0:n], bk[:, 0:n], xT[:, m, 0:n], ALU.add, [bk.b, xT.b], [xT.b])
            ln_fm(xT, G3, Bb, n, sq, rstd)
            for fc in range(NFC):
                wg = WG[wrr % 2]; wv = WV[wrr % 2]; wrr += 1
                S.dma("sp", wg[:], wg_s[:, fc],
                      writes=[wg.b])
                S.dma("sp", wv[:], wv_s[:, fc],
                      writes=[wv.b])
                bg = bank(); bv = bank()
                for kc in range(8):
                    mm(bg, wg[:, kc, :], Bb[:, kc, 0:n], kc == 0, kc == 7, [wg.b, Bb.b], out=bg[:, 0:n])
                for kc in range(8):
                    mm(bv, wv[:, kc, :], Bb[:, kc, 0:n], kc == 0, kc == 7, [wv.b, Bb.b], out=bv[:, 0:n])
                cp("dve", HGs[:, :, 0:2], SCV[:, fc, :].rearrange("p (s j) -> p s j", j=2), [SCV.b], [HGs.b])
                cp("act", HGs[:, :, 2:6], bg[:, 0:n].rearrange("p (s t) -> p s t", t=4), [bg.b], [HGs.b])
                cp("dve", CVS[:, fc, :].rearrange("p (s j) -> p s j", j=2), HGs[:, :, 4:6], [HGs.b], [CVS.b])
                act(cvs[:], HGs[:, :, 2:6], AF.Identity, [HGs.b, CW.b, CB.b], [cvs.b], scale=CW[:, 2, fc:fc + 1],
                    bias=CB[:, fc:fc + 1])
                stt(cvs[:], HGs[:, :, 1:5], CW[:, 1, fc:fc + 1], cvs[:], ALU.mult, ALU.add, [HGs.b, CW.b, cvs.b],
                    [cvs.b])
                stt(cvs[:], HGs[:, :, 0:4], CW[:, 0, fc:fc + 1], cvs[:], ALU.mult, ALU.add, [HGs.b, CW.b, cvs.b],
                    [cvs.b])
                act(cvs[:], cvs[:], AF.Silu, [cvs.b], [cvs.b])
                tt("dve", hT[:, fc, 0:n], cvs[:, :, :].rearrange("p s t -> p (s t)"), bv[:, 0:n], ALU.mult,
                   [cvs.b, bv.b], [hT.b])
            for m in range(8):
                wd = WD[m % 2]
                S.dma("sp", wd[:], wd_s[:, m],
                      writes=[wd.b])
                bk = bank()
                for fc in range(NFC):
                    mm(bk, wd[:, fc, :], hT[:, fc, 0:n], fc == 0, fc == NFC - 1, [wd.b, hT.b], out=bk[:, 0:n])
                tt("dve", xT[:, m, 0:n], bk[:, 0:n], xT[:, m, 0:n], ALU.add, [bk.b, xT.b], [xT.b])
            store_tm(xT, 8, n, y_s, X)
            for f0 in range(0, NFC, 4):
                bk = bank()
                nf = min(f0 + 4, NFC) - f0
                for fc in range(f0, f0 + nf):
                    tr(bk, bk[0:32, (fc - f0) * 128:(fc - f0 + 1) * 128], CVS[:, fc, :], IDF, [CVS.b])
                cp("act", SCt[0:32, f0 * 128:(f0 + nf) * 128], bk[0:32, 0:nf * 128], [bk.b], [SCt.b])
            S.dma("sp", conv_s, SCt[:], reads=[SCt.b])

            S.barrier()

        S.barrier()
        S.emit()
    return nc


_NC_CACHE = {}


def _consts():
    ident = np.eye(128, dtype=np.float32)
    bd64 = np.zeros((128, 128), np.float32); bd64[:64, :64] = 1 / 64; bd64[64:, 64:] = 1 / 64
    m16 = np.kron(np.eye(8, dtype=np.float32), np.ones((16, 16), np.float32))
    caus = np.triu(np.ones((128, 128), np.float32))
    pwn = np.array(PW_N, np.float32)
    kpos = np.arange(128, dtype=np.float32)
    g2m = np.zeros((128, 2), np.float32)
    for p in range(128):
        g2m[p, (p // 16) % 2] = 1.0
    return dict(c_ident=ident, c_bd64=bd64, c_mask16=m16, c_caus=caus, c_pwn=pwn, c_kpos=kpos, c_g2m=g2m)


WNAMES = ["ln1_g", "ln2_g", "ln3_g", "mem_norm_g", "w_in", "ssm_a_re", "ssm_a_im", "ssm_b_re", "ssm_b_im",
          "ssm_c_re", "ssm_c_im", "ssm_d", "ssm_log_dt", "ssm_glu_w", "q_norm_g", "k_norm_g", "lam_q1", "lam_k1",
          "lam_q2", "lam_k2", "subln_g", "w_out", "ca_wq", "ca_wk", "ca_wv", "ca_q_norm_g", "ca_k_norm_g", "ca_wo",
          "ffn_wg", "ffn_wv", "ffn_wd", "ffn_conv_w", "ffn_conv_b"]


def make_in_maps(inp, cores):
    cst = _consts()
    shared = {n: np.ascontiguousarray(np.asarray(inp[n])[0]) for n in WNAMES}
    maps = []
    for c in cores:
        b = c % 4
        m = dict(shared)
        m.update(cst)
        m["xp"] = np.ascontiguousarray(np.asarray(inp["x_prompt"])[b])
        m["memp"] = np.ascontiguousarray(np.asarray(inp["mem_prompt"])[b])
        sl = slice(c * NS, (c + 1) * NS)
        m["xs"] = np.ascontiguousarray(np.asarray(inp["x_sample"])[sl]).reshape(TS, 1024)
        m["st_re"] = np.ascontiguousarray(np.asarray(inp["state_ssm_re"])[0, sl]).reshape(NS, 2048)
        m["st_im"] = np.ascontiguousarray(np.asarray(inp["state_ssm_im"])[0, sl]).reshape(NS, 2048)
        m["st_conv"] = np.ascontiguousarray(np.asarray(inp["state_conv"])[0, sl]).reshape(NS * 2, F)
        m["cmk"] = np.ascontiguousarray(np.asarray(inp["cache_mem_k"])[0, sl]).reshape(NS, 256, 1024)
        m["cmv"] = np.ascontiguousarray(np.asarray(inp["cache_mem_v"])[0, sl]).reshape(NS, 256, 1024)
        m["cache_k"] = np.asarray(inp["cache_k"]).reshape(2560 * 128, 512)
        m["cache_v"] = np.asarray(inp["cache_v"]).reshape(2560 * 128, 512)
        m["ptab"] = np.ascontiguousarray(np.asarray(inp["page_table"])[sl]).astype(np.int32)
        maps.append(m)
    return maps


def kernel(**inp):
    nc = build_nc()
    cores = list(range(NCORES))
    maps = make_in_maps(inp, cores)
    res = run_bass_kernel_spmd(nc, maps, core_ids=cores)
    R = res.results
    f32 = np.float32
    y_prompt = np.stack([R[b]["y_p"] for b in range(4)]).astype(f32)
    k_prompt = np.stack([R[b]["k_p"] for b in range(4)]).reshape(1, 4, T, 4, 2, 64).astype(f32)
    v_prompt = np.stack([R[b]["v_p"] for b in range(4)]).reshape(1, 4, T, 4, 128).astype(f32)
    hre = np.stack([R[b]["hre_p"] for b in range(4)]).reshape(1, 4, 32, 64).astype(f32)
    him = np.stack([R[b]["him_p"] for b in range(4)]).reshape(1, 4, 32, 64).astype(f32)
    conv_prompt = np.stack([R[b]["conv_p"] for b in range(4)]).reshape(1, 4, 2, F).astype(f32)
    mk = np.stack([R[b]["mk_p"] for b in range(4)]).reshape(1, 4, 256, 4, 256).astype(f32)
    mv = np.stack([R[b]["mv_p"] for b in range(4)]).reshape(1, 4, 256, 4, 256).astype(f32)
    y_sample = np.concatenate([R[c]["y_s"] for c in range(NCORES)]).reshape(128, 4, 1024).astype(f32)
    k_sample = np.concatenate([R[c]["k_s"] for c in range(NCORES)]).reshape(1, 128, 4, 4, 2, 64).astype(f32)
    v_sample = np.concatenate([R[c]["v_s"] for c in range(NCORES)]).reshape(1, 128, 4, 4, 128).astype(f32)
    hre_s = np.concatenate([R[c]["hre_s"] for c in range(NCORES)]).reshape(1, 128, 32, 64).astype(f32)
    him_s = np.concatenate([R[c]["him_s"] for c in range(NCORES)]).reshape(1, 128, 32, 64).astype(f32)
    conv_s = np.concatenate([R[c]["conv_s"] for c in range(NCORES)]).reshape(1, 128, 2, F).astype(f32)
    return (y_prompt, y_sample, k_prompt, v_prompt, k_sample, v_sample, hre, him, hre_s, him_s,
            conv_prompt, conv_s, mk, mv)
```
